# Optimizing a Trainium2 kernel written in Bass

```python
import jax, jax.numpy as jnp
from jax import lax
import numpy as np

D_MODEL = 1024
BATCH = 1
SEQ = 16384
DEPTH = 2

CHUNK = 64
Q_BLOCK = 128
N_MEM = 256
EPS = 1e-6
FOX_HEADS = 8
FOX_HEAD_DIM = 64
FOX_WIDTH = FOX_HEADS * FOX_HEAD_DIM
GDN_HEADS = 4
GDN_HEAD_DIM = 128
GDN_WIDTH = GDN_HEADS * GDN_HEAD_DIM
CONV_K = 4
MEM_HEADS = 4
MEM_HEAD_DIM = 128
MEM_WIDTH = MEM_HEADS * MEM_HEAD_DIM
N_BRANCH = 3
BRANCH_WIDTH = 512

IN_SIZES = (FOX_WIDTH, FOX_WIDTH, FOX_WIDTH, FOX_HEADS, FOX_WIDTH,
            GDN_WIDTH, GDN_WIDTH, GDN_WIDTH, GDN_HEADS, GDN_HEADS, GDN_WIDTH,
            MEM_WIDTH, MEM_WIDTH,
            N_BRANCH * D_MODEL)
N_IN = sum(IN_SIZES)

kernel_name = 'hybrid_fox_gdn_memory_gated_merge'


def _split_cols(z, sizes):
    idx = []
    acc = 0
    for s in sizes[:-1]:
        acc += s
        idx.append(acc)
    return jnp.split(z, idx, axis=-1)


def rms_norm(x, g):
    xf = x.astype(jnp.float32)
    y = xf * lax.rsqrt(jnp.mean(xf * xf, axis=-1, keepdims=True) + EPS)
    return (y * g.astype(jnp.float32)).astype(x.dtype)


def l2_normalize(x):
    return x * lax.rsqrt(jnp.sum(x * x, axis=-1, keepdims=True) + EPS)


def forgetting_attention(q, k, v, f_logit):
    B, S, H, d = q.shape
    log_f = jax.nn.log_sigmoid(f_logit.astype(jnp.float32))
    F = jnp.cumsum(log_f, axis=1).transpose(0, 2, 1)
    nb = S // Q_BLOCK
    pos = jnp.arange(S)
    q_blocks = q.reshape(B, nb, Q_BLOCK, H, d).swapaxes(0, 1)
    F_blocks = F.reshape(B, H, nb, Q_BLOCK).transpose(2, 0, 1, 3)
    p_blocks = pos.reshape(nb, Q_BLOCK)
    scale = d ** -0.5

    def block(args):
        q_blk, F_blk, p_blk = args
        s = jnp.einsum('bqhd,bkhd->bhqk', q_blk, k,
                       preferred_element_type=jnp.float32) * scale
        s = s + F_blk[..., :, None] - F[:, :, None, :]
        s = jnp.where(pos[None, None, None, :] <= p_blk[None, None, :, None], s, -jnp.inf)
        p = jax.nn.softmax(s, axis=-1)
        return jnp.einsum('bhqk,bkhd->bqhd', p.astype(v.dtype), v)

    o = lax.map(block, (q_blocks, F_blocks, p_blocks))
    return o.swapaxes(0, 1).reshape(B, S, H * d)


def causal_dwconv(x, w):
    K = w.shape[0]
    return lax.conv_general_dilated(
        x, w[:, None, :].astype(x.dtype), window_strides=(1,),
        padding=[(K - 1, 0)], dimension_numbers=('NWC', 'WIO', 'NWC'),
        feature_group_count=x.shape[-1])


def gated_delta_rule(q, k, v, g, beta):
    B, S, H, dk = q.shape
    dv = v.shape[-1]
    N = S // CHUNK
    C = CHUNK

    def chunks(t):
        t = t.reshape((B, N, C, H) + t.shape[3:])
        return jnp.moveaxis(t, 3, 1)

    qc, kc, vc = chunks(q), chunks(k), chunks(v)
    gc, bc = chunks(g), chunks(beta)
    G = jnp.cumsum(gc, axis=-1)
    tril_incl = jnp.tril(jnp.ones((C, C), dtype=bool))
    tril_strict = jnp.tril(jnp.ones((C, C), dtype=bool), -1)
    gamma = jnp.exp(jnp.where(tril_incl, G[..., :, None] - G[..., None, :], -jnp.inf))
    kb = kc * bc[..., None]
    A = jnp.where(tril_strict, jnp.einsum('bhncd,bhnsd->bhncs', kb, kc) * gamma, 0.0)
    eye = jnp.eye(C, dtype=A.dtype)
    rhs = jnp.concatenate([vc * bc[..., None], kb * jnp.exp(G)[..., None]], axis=-1)
    sol = lax.linalg.triangular_solve(A + eye, rhs, left_side=True, lower=True,
                                      unit_diagonal=True)
    u, w = sol[..., :dv], sol[..., dv:]
    a_qk = jnp.einsum('bhncd,bhnsd->bhncs', qc, kc) * gamma
    q_dec = qc * jnp.exp(G)[..., None]
    G_last = G[..., -1]
    k_dec = kc * jnp.exp(G_last[..., None] - G)[..., None]

    def step(state, inp):
        dq, kd, uu, ww, aqk, gl = inp
        v_new = uu - jnp.einsum('bhcd,bhde->bhce', ww, state)
        o = (jnp.einsum('bhcd,bhde->bhce', dq, state)
             + jnp.einsum('bhcs,bhse->bhce', aqk, v_new))
        state = state * jnp.exp(gl)[..., None, None] + jnp.einsum('bhcd,bhce->bhde', kd, v_new)
        return state, o

    xs = tuple(jnp.moveaxis(t, 2, 0) for t in (q_dec, k_dec, u, w, a_qk, G_last))
    state0 = jnp.zeros((B, H, dk, dv), jnp.float32)
    _, o = lax.scan(step, state0, xs)
    o = jnp.moveaxis(o, 0, 2)
    return jnp.moveaxis(o, 1, 3).reshape(B, S, H, dv)


def hybrid_layer(x, mem, norm_g, w_in, b_fg, b_merge, conv_w, a_log, dt_bias,
                 gdn_norm_g, mem_norm_g, w_mem_kv, w_branch, w_out):
    B, S, D = x.shape
    dt = x.dtype
    h = rms_norm(x, norm_g)
    z = h @ w_in
    (aq, ak, av, af, az, bq, bk, bv, ba, bb, bz, mq, mz, gates) = _split_cols(z, IN_SIZES)

    o_a = forgetting_attention(aq.reshape(B, S, FOX_HEADS, FOX_HEAD_DIM),
                               ak.reshape(B, S, FOX_HEADS, FOX_HEAD_DIM),
                               av.reshape(B, S, FOX_HEADS, FOX_HEAD_DIM),
                               af + b_fg)
    y_a = (o_a * jax.nn.silu(az)).astype(dt)

    qkv = jax.nn.silu(causal_dwconv(jnp.concatenate([bq, bk, bv], axis=-1), conv_w))
    gq, gk, gv = jnp.split(qkv.astype(jnp.float32), 3, axis=-1)
    gq = l2_normalize(gq.reshape(B, S, GDN_HEADS, GDN_HEAD_DIM)) * (GDN_HEAD_DIM ** -0.5)
    gk = l2_normalize(gk.reshape(B, S, GDN_HEADS, GDN_HEAD_DIM))
    gv = gv.reshape(B, S, GDN_HEADS, GDN_HEAD_DIM)
    g_log = -jnp.exp(a_log.astype(jnp.float32)) * jax.nn.softplus(
        ba.astype(jnp.float32) + dt_bias.astype(jnp.float32))
    beta = jax.nn.sigmoid(bb.astype(jnp.float32))
    o_b = gated_delta_rule(gq, gk, gv, g_log, beta)
    y_b = (rms_norm(o_b, gdn_norm_g).reshape(B, S, GDN_WIDTH) * jax.nn.silu(bz)).astype(dt)

    mem_n = rms_norm(mem, mem_norm_g)
    mk, mv = jnp.split(mem_n @ w_mem_kv, 2, axis=-1)
    M = mem.shape[1]
    mk = mk.reshape(B, M, MEM_HEADS, MEM_HEAD_DIM)
    mv = mv.reshape(B, M, MEM_HEADS, MEM_HEAD_DIM)
    s_m = jnp.einsum('bshd,bmhd->bhsm', mq.reshape(B, S, MEM_HEADS, MEM_HEAD_DIM), mk,
                     preferred_element_type=jnp.float32) * (MEM_HEAD_DIM ** -0.5)
    p_m = jax.nn.softmax(s_m, axis=-1)
    o_m = jnp.einsum('bhsm,bmhd->bshd', p_m.astype(mv.dtype), mv).reshape(B, S, MEM_WIDTH)
    y_m = (o_m * jax.nn.silu(mz)).astype(dt)

    ys = jnp.stack([y_a, y_b, y_m], axis=2)
    proj = jnp.einsum('bsnc,ncd->bsnd', ys, w_branch)
    gate = jax.nn.sigmoid(gates + b_merge).reshape(B, S, N_BRANCH, D)
    merged = jnp.sum(gate * proj, axis=2)
    return x + merged @ w_out


def setup_inputs(seed: int = 0) -> dict:
    key = jax.random.key(seed)
    ks = jax.random.split(key, 16)
    f32 = jnp.float32
    x = jax.random.normal(ks[0], (BATCH, SEQ, D_MODEL), f32)
    mem = jax.random.normal(ks[1], (BATCH, N_MEM, D_MODEL), f32)
    norm_g = 1.0 + 0.02 * jax.random.normal(ks[2], (DEPTH, D_MODEL), f32)
    w_in = jax.random.normal(ks[3], (DEPTH, D_MODEL, N_IN), f32) * (D_MODEL ** -0.5)
    b_fg = 1.0 + 3.0 * jax.random.uniform(ks[4], (DEPTH, FOX_HEADS), f32)
    b_merge = 0.02 * jax.random.normal(ks[5], (DEPTH, N_BRANCH * D_MODEL), f32)
    conv_w = jax.random.normal(ks[6], (DEPTH, CONV_K, 3 * GDN_WIDTH), f32) * (CONV_K ** -0.5)
    a_log = jnp.log(jax.random.uniform(ks[7], (DEPTH, GDN_HEADS), f32, 1.0, 16.0))
    dt0 = jnp.exp(jax.random.uniform(ks[8], (DEPTH, GDN_HEADS), f32,
                                     float(np.log(1e-3)), float(np.log(1e-1))))
    dt_bias = dt0 + jnp.log(-jnp.expm1(-dt0))
    gdn_norm_g = 1.0 + 0.02 * jax.random.normal(ks[9], (DEPTH, GDN_HEAD_DIM), f32)
    mem_norm_g = 1.0 + 0.02 * jax.random.normal(ks[10], (DEPTH, D_MODEL), f32)
    w_mem_kv = jax.random.normal(ks[11], (DEPTH, D_MODEL, 2 * MEM_WIDTH), f32) * (D_MODEL ** -0.5)
    w_branch = jax.random.normal(ks[12], (DEPTH, N_BRANCH, BRANCH_WIDTH, D_MODEL), f32) * (BRANCH_WIDTH ** -0.5)
    w_out = jax.random.normal(ks[13], (DEPTH, D_MODEL, D_MODEL), f32) * (0.5 * D_MODEL ** -0.5)
    final_norm_g = 1.0 + 0.02 * jax.random.normal(ks[14], (D_MODEL,), f32)
    return {'x': x, 'mem': mem, 'norm_g': norm_g, 'w_in': w_in, 'b_fg': b_fg,
            'b_merge': b_merge, 'conv_w': conv_w, 'a_log': a_log, 'dt_bias': dt_bias,
            'gdn_norm_g': gdn_norm_g, 'mem_norm_g': mem_norm_g, 'w_mem_kv': w_mem_kv,
            'w_branch': w_branch, 'w_out': w_out, 'final_norm_g': final_norm_g}


def reference(x, mem, norm_g, w_in, b_fg, b_merge, conv_w, a_log, dt_bias,
              gdn_norm_g, mem_norm_g, w_mem_kv, w_branch, w_out, final_norm_g):
    for l in range(DEPTH):
        x = hybrid_layer(x, mem, norm_g[l], w_in[l], b_fg[l], b_merge[l], conv_w[l],
                         a_log[l], dt_bias[l], gdn_norm_g[l], mem_norm_g[l],
                         w_mem_kv[l], w_branch[l], w_out[l])
    return rms_norm(x, final_norm_g)
```

```python
import contextlib
import numpy as np
import ml_dtypes
import concourse.bass as bass
import concourse.mybir as mybir
from concourse.bass_utils import run_bass_kernel_spmd

F32 = mybir.dt.float32
BF16 = mybir.dt.bfloat16
AF = mybir.ActivationFunctionType
ALU = mybir.AluOpType

D = 1024
S_FULL = 16384
NCORES = 8
EPS = 1e-6
N_IN = 8208
OFF = dict(aq=0, ak=512, av=1024, af=1536, az=1544, bq=2056, bk=2568, bv=3080,
           ba=3592, bb=3596, bz=3600, mq=4112, mz=4624, gates=5136)


def _is_psum_key(k):
    if isinstance(k, str):
        return k.startswith('ps')
    if isinstance(k, tuple) and len(k) >= 2:
        return k[0] in ('pb', 'pS', 'pO') or k[1] == 'ps'
    return False


class Sched:
    ENGS = ['pe', 'act', 'dve', 'pool', 'sp']

    def __init__(self, nc, same_engine_sync=True):
        self.nc = nc
        self.ops = []
        self.lastw = {}
        self.readers = {}
        self.slot_count = {}
        self.same = same_engine_sync
        self.stack = contextlib.ExitStack()
        self.esem = {e: self.stack.enter_context(nc.semaphore("sem_" + e)) for e in self.ENGS}
        self.ssem = {}
        self.cnt = {e: 0 for e in self.ENGS}

    def add(self, eng, fn, reads=(), writes=(), slot=None):
        op = dict(eng=eng, fn=fn, deps=[], slot=slot, inc=False, id=len(self.ops))
        deps = {}
        for k in reads:
            w = self.lastw.get(k)
            if w is not None:
                deps[w['id']] = w
            if _is_psum_key(k):
                for r in self.readers.get(k, ()):
                    if r['eng'] != eng:
                        deps[r['id']] = r
        for k in writes:
            w = self.lastw.get(k)
            if w is not None:
                deps[w['id']] = w
            for r in self.readers.get(k, ()):
                deps[r['id']] = r
        for d in deps.values():
            if d is op:
                continue
            op['deps'].append(d)
            if d['slot'] is None:
                if d['eng'] != eng or (self.same and eng != 'pe') or slot is not None:
                    d['inc'] = True
        for k in writes:
            self.lastw[k] = op
            self.readers[k] = []
        for k in reads:
            self.readers.setdefault(k, []).append(op)
        if slot is not None:
            if slot not in self.ssem:
                self.ssem[slot] = self.stack.enter_context(self.nc.semaphore("sl_%d" % len(self.ssem)))
            self.slot_count[slot] = self.slot_count.get(slot, 0) + 1
            op['slot_val'] = self.slot_count[slot] * 16
        self.ops.append(op)
        return op

    def flush(self):
        nc = self.nc
        for op in self.ops:
            if op['slot'] is None and op['inc']:
                self.cnt[op['eng']] += 1
                op['count'] = self.cnt[op['eng']]
        ops = self.ops
        esem, ssem = self.esem, self.ssem
        final = dict(self.slot_count)
        with nc.Block() as block:
            def run(ename, eng):
                known = {}
                for op in ops:
                    if op['eng'] != ename:
                        continue
                    waits = {}
                    for d in op['deps']:
                        if d['slot'] is not None:
                            key = ('s', d['slot'])
                            v = d['slot_val']
                            sem = ssem[d['slot']]
                        else:
                            if d['eng'] == ename and op['slot'] is None and \
                                    (ename == 'pe' or not self.same):
                                continue
                            key = ('e', d['eng'])
                            v = d['count']
                            sem = esem[d['eng']]
                        if waits.get(key, (None, -1))[1] < v:
                            waits[key] = (sem, v)
                    for key, (sem, v) in waits.items():
                        if known.get(key, -1) >= v:
                            continue
                        known[key] = v
                        eng.wait_ge(sem, v)
                    ins = op['fn'](eng)
                    if op['slot'] is not None:
                        ins.then_inc(ssem[op['slot']], 16)
                    elif op['inc']:
                        ins.then_inc(esem[ename], 1)
                if ename == 'sp':
                    for s, n in final.items():
                        eng.wait_ge(ssem[s], n * 16)

            block.tensor(lambda e: run('pe', e))
            block.scalar(lambda e: run('act', e))
            block.vector(lambda e: run('dve', e))
            block.gpsimd(lambda e: run('pool', e))
            block.sync(lambda e: run('sp', e))
        self.ops = []
        self.lastw = {}
        self.readers = {}

    def close(self):
        self.flush()
        self.stack.close()


_NAME = [0]


class Ctx:
    def __init__(self, nc):
        self.nc = nc
        self.st = contextlib.ExitStack()

    def sb(self, shape, dt, name=None):
        _NAME[0] += 1
        return self.st.enter_context(self.nc.sbuf_tensor(name or ("t%d" % _NAME[0]), list(shape), dt))

    def ps(self, shape, dt, name=None):
        _NAME[0] += 1
        return self.st.enter_context(self.nc.psum_tensor(name or ("p%d" % _NAME[0]), list(shape), dt))


def emit_rmsnorm(sc, x_sb, xkey, g_sb, ones_f, out_sb, outkey, TT, sq, ps, rstd, tag, dim=D, gkey='g'):
    for k in range(8):
        sc.add('act', lambda e, k=k: e.activation(sq[:, k, :], x_sb[:, k, :], AF.Square),
               reads=[xkey], writes=[(tag, 'sq', k)])

    def mm(e):
        r = None
        for k in range(8):
            r = e.matmul(ps[:, 0:TT], ones_f[:, :], sq[:, k, :], start=(k == 0), stop=(k == 7))
        return r
    sc.add('pe', mm, reads=[(tag, 'sq', k) for k in range(8)] + ['ones_f'], writes=[(tag, 'ps')])
    sc.add('act', lambda e: e.activation(rstd[:, :], ps[:, 0:TT], AF.Sqrt, bias=EPS, scale=1.0 / dim),
           reads=[(tag, 'ps')], writes=[(tag, 'rstd')])
    sc.add('dve', lambda e: e.reciprocal(rstd[:, :], rstd[:, :]),
           reads=[(tag, 'rstd')], writes=[(tag, 'rstd')])
    for k in range(8):
        sc.add('dve',
               lambda e, k=k: e.scalar_tensor_tensor(out_sb[:, k, :], x_sb[:, k, :], g_sb[:, k:k + 1],
                                                     rstd[:, :], ALU.mult, ALU.mult),
               reads=[xkey, (tag, 'rstd'), gkey], writes=[outkey])


def build_P(TS):
    nc = bass.Bass("TRN2", target_bir_lowering=False)
    xT = nc.dram_tensor("xT", [D, TS], F32, kind="ExternalInput").ap()
    g = nc.dram_tensor("g", [128, 8], F32, kind="ExternalInput").ap()
    hT = nc.dram_tensor("hT", [D, TS], BF16, kind="ExternalOutput").ap()
    TT = 512
    cx = Ctx(nc)
    with cx.st:
        sc = Sched(nc)
        ones_f = cx.sb([128, 128], F32)
        g_sb = cx.sb([128, 8], F32)
        xs = [cx.sb([128, 8, TT], F32) for _ in range(2)]
        hs = [cx.sb([128, 8, TT], BF16) for _ in range(2)]
        sq = cx.sb([128, 8, TT], F32)
        rstd = cx.sb([128, TT], F32)
        ps = cx.ps([128, 512], F32)
        sc.add('pool', lambda e: e.memset(ones_f[:, :], 1.0), writes=['ones_f'])
        sc.add('sp', lambda e: e.dma_start(out=g_sb[:, :], in_=g[:, :]), writes=['g'], slot='g')
        xv = xT.rearrange("(k p) t -> p k t", p=128)
        hv = hT.rearrange("(k p) t -> p k t", p=128)
        for i in range(TS // TT):
            b = i % 2
            sc.add('sp', lambda e, i=i, b=b: e.dma_start(out=xs[b][:, :, :], in_=xv[:, :, i * TT:(i + 1) * TT]),
                   writes=[('x', b)], slot=('x', b))
            emit_rmsnorm(sc, xs[b], ('x', b), g_sb, ones_f, hs[b], ('h', b), TT, sq, ps, rstd, 'n')
            sc.add('sp', lambda e, i=i, b=b: e.dma_start(out=hv[:, :, i * TT:(i + 1) * TT], in_=hs[b][:, :, :]),
                   reads=[('h', b)], writes=[('hout', i)], slot=('ho', b))
        sc.close()
    return nc


def make_consts(sc, cx):
    c = {}
    c['ones'] = cx.sb([128, 128], F32)
    c['ident'] = cx.sb([128, 128], F32)
    c['uincl'] = cx.sb([128, 128], F32)
    c['ustrict'] = cx.sb([128, 128], F32)
    c['e0'] = cx.sb([128, 128], F32)
    c['ones_bf'] = cx.sb([128, 128], BF16)
    c['ident_bf'] = cx.sb([128, 128], BF16)
    c['zeros'] = cx.sb([128, 128], F32)
    sc.add('pool', lambda e: e.memset(c['ones'][:, :], 1.0), writes=['c_ones'])
    sc.add('pool', lambda e: e.memset(c['zeros'][:, :], 0.0), writes=['c_zeros'])
    sc.add('pool', lambda e: e.memset(c['ones_bf'][:, :], 1.0), writes=['c_ones_bf'])
    sc.add('pool', lambda e: e.affine_select(c['ident'][:, :], c['zeros'][:, :], [[1, 128]], ALU.not_equal, 1.0,
                                             base=0, channel_multiplier=-1),
           reads=['c_zeros'], writes=['c_ident'])
    sc.add('pool', lambda e: e.tensor_copy(c['ident_bf'][:, :], c['ident'][:, :]),
           reads=['c_ident'], writes=['c_ident_bf'])
    sc.add('pool', lambda e: e.affine_select(c['uincl'][:, :], c['ones'][:, :], [[1, 128]], ALU.is_ge, 0.0,
                                             base=0, channel_multiplier=-1),
           reads=['c_ones'], writes=['c_uincl'])
    sc.add('pool', lambda e: e.affine_select(c['ustrict'][:, :], c['ones'][:, :], [[1, 128]], ALU.is_gt, 0.0,
                                             base=0, channel_multiplier=-1),
           reads=['c_ones'], writes=['c_ustrict'])
    sc.add('pool', lambda e: e.affine_select(c['e0'][:, :], c['ones'][:, :], [[0, 128]], ALU.is_ge, 0.0,
                                             base=0, channel_multiplier=-1),
           reads=['c_ones'], writes=['c_e0'])
    return c


def load_cast(sc, dst_bf, dstkey, src_ap, stage, stagekey, eng_dma='sp', eng_cast='pool', slot=None):
    sc.add(eng_dma, lambda e: e.dma_start(out=stage, in_=src_ap), writes=[stagekey], slot=slot or stagekey)
    sc.add(eng_cast, lambda e: e.tensor_copy(dst_bf, stage), reads=[stagekey], writes=[dstkey])


def fox_phase(nc, sc, cx0, c, S, hT, wf, bfg, oaT, scr, psb, stop=99):
    NT = S // 128
    NG = S // 512
    cx = Ctx(nc)
    with cx.st:
        wq = cx.sb([128, 8, 64], BF16)
        wk = cx.sb([128, 8, 64], BF16)
        wv = cx.sb([128, 8, 65], BF16)
        wst = cx.sb([128, 8, 193], F32)
        QT = cx.sb([65, S], BF16)
        KT = cx.sb([65, S], BF16)
        V = cx.sb([128, NT, 65], BF16)
        lfr = cx.sb([128, NT], F32)
        lfn = cx.sb([128, NT], F32)
        Fn = cx.sb([128, NT], F32)
        frefB = cx.sb([128, NG], F32)
        ctok = cx.sb([128, NT], F32)
        cTT = cx.sb([128, 128], BF16)
        totT = cx.sb([128, 1], F32)
        X = cx.sb([128, 128], F32)
        negb = cx.sb([128, 1], F32)
        biasg = [cx.sb([128, NT], F32) for _ in range(2)]
        ht = [cx.sb([128, 8, 512], BF16) for _ in range(2)]
        Pb = [cx.sb([128, 512], BF16) for _ in range(4)]
        oun = cx.sb([65, 512], F32)
        rl = cx.sb([65, 512], F32)
        ofin = [cx.sb([64, 512], F32) for _ in range(2)]

        sc.add('sp', lambda e: e.dma_start(out=wst[:, :, :], in_=wf.rearrange("(k p) c -> p k c", p=128)),
               writes=['wst'], slot='wst')
        sc.add('pool', lambda e: e.tensor_copy(wq[:, :, :], wst[:, :, 0:64]), reads=['wst'], writes=['wq'])
        sc.add('pool', lambda e: e.tensor_copy(wk[:, :, :], wst[:, :, 64:128]), reads=['wst'], writes=['wk'])
        sc.add('pool', lambda e: e.tensor_copy(wv[:, :, :], wst[:, :, 128:193]), reads=['wst'], writes=['wv'])
        sc.add('sp', lambda e: e.dma_start(out=negb[:, :], in_=bfg[:, :]), writes=['negb'], slot='negb')
        sc.add('dve', lambda e: e.tensor_scalar(negb[:, :], negb[:, :], -1.0, None, ALU.mult),
               reads=['negb'], writes=['negb'])
        sc.add('pool', lambda e: e.memset(KT[64:65, :], 1.0), writes=['KTrow'])
        sc.add('pool', lambda e: e.memset(V[:, :, 64:65], 1.0), writes=['Vones'])

        if stop <= 0:
            sc.flush()
            return
        hv = hT.rearrange("(k p) t -> p k t", p=128)
        psq, psk, psv = psb[0], psb[1], psb[2]
        for i in range(NG):
            b = i % 2
            sc.add('sp', lambda e, i=i, b=b: e.dma_start(out=ht[b][:, :, :], in_=hv[:, :, i * 512:(i + 1) * 512]),
                   writes=[('ht', b)], slot=('ht', b))

            def mmq(e, b=b):
                r = None
                for k in range(8):
                    r = e.matmul(psq[0:64, :], wq[:, k, :], ht[b][:, k, :], start=(k == 0), stop=(k == 7))
                return r
            import os
            DBG = int(os.environ.get('FOXDBG', '15'))
            if DBG & 1:
              sc.add('pe', mmq, reads=[('ht', b), 'wq'], writes=['psq'])
            if DBG & 1:
              sc.add('act', lambda e, i=i: e.activation(QT[0:64, i * 512:(i + 1) * 512], psq[0:64, :], AF.Copy,
                                                      scale=0.125),
                   reads=['psq'], writes=[('QT', i)])

            def mmk(e, b=b):
                r = None
                for k in range(8):
                    r = e.matmul(psk[0:64, :], wk[:, k, :], ht[b][:, k, :], start=(k == 0), stop=(k == 7))
                return r
            if DBG & 2:
              sc.add('pe', mmk, reads=[('ht', b), 'wk'], writes=['psk'])
              sc.add('dve', lambda e, i=i: e.tensor_copy(KT[0:64, i * 512:(i + 1) * 512], psk[0:64, :]),
                   reads=['psk'], writes=[('KT', i)])

            def mmv(e, b=b):
                r = None
                for sub in range(4):
                    for k in range(8):
                        r = e.matmul(psv[:, sub * 128:sub * 128 + 65], ht[b][:, k, sub * 128:(sub + 1) * 128],
                                     wv[:, k, :], start=(k == 0), stop=(k == 7))
                return r
            pv3 = psv[:, :].rearrange("p (s c) -> p s c", c=128)
            if DBG & 4:
              sc.add('pe', mmv, reads=[('ht', b), 'wv'], writes=['psv'])
              sc.add('dve', lambda e, i=i, pv3=pv3: e.tensor_copy(V[:, 4 * i:4 * i + 4, 0:64], pv3[:, :, 0:64]),
                   reads=['psv', 'Vones'], writes=[('V', i)])
            if DBG & 8:
              sc.add('dve', lambda e, i=i, pv3=pv3: e.tensor_copy(lfr[:, 4 * i:4 * i + 4], pv3[:, :, 64]),
                   reads=['psv'], writes=[('lfr', i)])

        if stop <= 1:
            sc.flush()
            return
        allfr = [('lfr', i) for i in range(NG)]
        sc.add('act', lambda e: e.activation(lfn[:, :], lfr[:, :], AF.Exp, bias=negb[:, 0:1], scale=-1.0),
               reads=allfr + ['negb'], writes=['lfn'])
        sc.add('act', lambda e: e.activation(lfn[:, :], lfn[:, :], AF.Ln, bias=1.0, scale=1.0),
               reads=['lfn'], writes=['lfn'])
        pt = psb[0]
        sc.add('pe', lambda e: e.matmul(pt[0:NT, 0:1], lfn[:, :], c['ones'][:, 0:1], start=True, stop=True),
               reads=['lfn', 'c_ones', 'psq'], writes=['psq'])
        sc.add('dve', lambda e: e.tensor_copy(totT[0:NT, :], pt[0:NT, 0:1]), reads=['psq'], writes=['totT'])
        sc.add('dve', lambda e: e.tensor_scalar(X[0:NT, 0:NT], c['ustrict'][0:NT, 0:NT], totT[0:NT, 0:1], None,
                                                ALU.mult),
               reads=['totT', 'c_ustrict'], writes=['X'])
        pf = psb[1]

        def mmF(e):
            e.matmul(pf[:, 0:NT], c['uincl'][:, :], lfn[:, :], start=True, stop=False)
            return e.matmul(pf[:, 0:NT], c['ones'][0:NT, :], X[0:NT, 0:NT], start=False, stop=True)
        sc.add('pe', mmF, reads=['lfn', 'X', 'c_uincl', 'c_ones', 'psk'], writes=['psk'])
        sc.add('dve', lambda e: e.tensor_copy(Fn[:, :], pf[:, 0:NT]), reads=['psk'], writes=['Fn'])
        pr = psb[2]
        sc.add('pe', lambda e: e.matmul(pr[:, 0:NG], c['e0'][:, :], Fn[:, 0:NT:4], start=True, stop=True),
               reads=['Fn', 'c_e0', 'psv'], writes=['psv'])
        sc.add('dve', lambda e: e.tensor_copy(frefB[:, :], pr[:, 0:NG]), reads=['psv'], writes=['frefB'])
        for r in range(4):
            sc.add('dve', lambda e, r=r: e.tensor_tensor(ctok[:, r:NT:4], frefB[:, :], Fn[:, r:NT:4], ALU.subtract),
                   reads=['frefB', 'Fn'], writes=[('ctok', r)])
        pc = psb[3]
        sc.add('pe', lambda e: e.transpose(pc[0:NT, 0:128], ctok[:, :], c['ident'][:, :]),
               reads=[('ctok', r) for r in range(4)] + ['c_ident'], writes=['ps3'])
        sc.add('dve', lambda e: e.tensor_copy(cTT[0:NT, :], pc[0:NT, 0:128]), reads=['ps3'], writes=['cTT'])
        sc.add('sp', lambda e: e.dma_start(out=scr[0:NT, :], in_=cTT[0:NT, :]), reads=['cTT'], writes=['scr'],
               slot='scr')
        sc.add('sp', lambda e: e.dma_start(out=QT[64:65, :], in_=scr[0:NT, :].rearrange("(o j) p -> o (j p)", o=1)),
               reads=['scr'], writes=['QTrow'], slot='qtrow')

        if stop <= 2:
            sc.flush()
            return
        allQ = ['QTrow']
        pS = [psb[4], psb[5]]
        pO = [psb[6], psb[7]]
        blk = 0
        for g in range(NG):
            gb = g % 2
            nj = 4 * g + 4
            sc.add('dve', lambda e, g=g, gb=gb, nj=nj: e.tensor_scalar(biasg[gb][:, 0:nj], Fn[:, 0:nj],
                                                                       frefB[:, g:g + 1], None,
                                                                       ALU.subtract),
                   reads=['Fn', 'frefB'], writes=[('biasg', gb)])
            for j in range(nj):
                r = j - 4 * g
                c0 = 0 if r < 0 else r * 128
                N = 512 - c0
                sb_ = blk % 2
                pb_ = blk % 4
                blk += 1
                sc.add('pe', lambda e, j=j, g=g, c0=c0, N=N, sb_=sb_: e.matmul(
                    pS[sb_][:, 0:N], KT[0:65, j * 128:(j + 1) * 128], QT[0:65, g * 512 + c0:(g + 1) * 512],
                    start=True, stop=True),
                    reads=[('KT', j // 4), ('QT', g), 'QTrow', 'KTrow'], writes=[('pS', sb_)])
                sc.add('act', lambda e, j=j, gb=gb, N=N, sb_=sb_, pb_=pb_: e.activation(
                    Pb[pb_][:, 0:N], pS[sb_][:, 0:N], AF.Exp, bias=biasg[gb][:, j:j + 1], scale=1.0),
                    reads=[('pS', sb_), ('biasg', gb)], writes=[('P', pb_)])
                if r >= 0:
                    sc.add('pool', lambda e, pb_=pb_: e.affine_select(
                        Pb[pb_][:, 0:128], Pb[pb_][:, 0:128], [[1, 128]], ALU.is_ge, 0.0, base=0,
                        channel_multiplier=-1),
                        reads=[('P', pb_)], writes=[('P', pb_)])
                sc.add('pe', lambda e, j=j, gb=gb, c0=c0, N=N, pb_=pb_, nj=nj: e.matmul(
                    pO[gb][0:65, c0:512], V[:, j, 0:65], Pb[pb_][:, 0:N], start=(j == 0), stop=(j == nj - 1),
                    skip_group_check=True),
                    reads=[('P', pb_), ('V', j // 4), 'Vones'], writes=[('pO', gb)])
            sc.add('act', lambda e, gb=gb: e.activation(oun[0:65, :], pO[gb][0:65, :], AF.Copy),
                   reads=[('pO', gb)], writes=['oun'])
            sc.add('dve', lambda e: e.reciprocal(rl[64:65, :], oun[64:65, :]), reads=['oun'], writes=['rl'])
            sc.add('pe', lambda e: e.matmul(psb[3][0:64, :], c['ones'][64:65, 0:64], rl[64:65, :], start=True,
                                            stop=True),
                   reads=['rl', 'c_ones'], writes=['ps3'])
            sc.add('dve', lambda e, gb=gb: e.tensor_tensor(ofin[gb][:, :], oun[0:64, :], psb[3][0:64, :], ALU.mult),
                   reads=['oun', 'ps3'], writes=[('ofin', gb)])
            sc.add('sp', lambda e, g=g, gb=gb: e.dma_start(out=oaT[:, g * 512:(g + 1) * 512], in_=ofin[gb][:, :]),
                   reads=[('ofin', gb)], writes=[('oaT', g)], slot=('oa', gb))
        sc.flush()


def gdn_phase(nc, sc, c, S, hT, wg, cw, gpar, obT, psb):
    import os
    GSTOP = int(os.environ.get('GSTOP', '99'))
    GSKIP = int(os.environ.get('GSKIP', '0'))
    NSEG = S // 512
    A = sc.add
    cx = Ctx(nc)
    PB = lambda n: ('pb', n)
    with cx.st:
        f32t = lambda *sh: cx.sb(list(sh), F32)
        bft = lambda *sh: cx.sb(list(sh), BF16)
        wst = f32t(128, 8, 322)
        wq, wk, wv, wab = bft(128, 8, 128), bft(128, 8, 128), bft(128, 8, 64), bft(128, 8, 2)
        cw_sb, gp_sb = f32t(128, 12), f32t(128, 2)
        negA = f32t(128, 1)
        M_s, M_i = f32t(128, 4, 128), f32t(128, 4, 128)
        E63, E127, EL = f32t(128, 128), f32t(128, 128), f32t(128, 128)
        ht = [bft(128, 8, 512) for _ in range(2)]
        rq, rk, rv = f32t(128, 515), f32t(128, 515), f32t(64, 515)
        cq, ck, cv = f32t(128, 512), f32t(128, 512), f32t(64, 512)
        sq2, rn = f32t(128, 512), f32t(128, 512)
        qn_f, kn_f = f32t(128, 512), f32t(128, 512)
        qn_bf, kT_bf = bft(128, 512), bft(128, 512)
        a_sb, b_sb, g_tok, G_tok, eG, ekd, dl, dh, negbt, glo = [f32t(128, 4) for _ in range(10)]
        diagG, EGrow, Dm, Gam, Gs, Gi = [f32t(128, 512) for _ in range(6)]
        qdec = bft(128, 512)
        ktok = f32t(128, 4, 128)
        kg, kdec = bft(128, 4, 128), bft(128, 4, 128)
        vtok = bft(128, 4, 64)
        B_f = f32t(128, 512)
        Bb = [bft(128, 512) for _ in range(2)]
        Pb_ = [bft(128, 512) for _ in range(2)]
        S_f, S_b = f32t(128, 512), bft(128, 512)
        aqk = bft(128, 512)
        ybu = f32t(128, 4, 64)
        ywT = bft(128, 512)
        St_f, St_b = f32t(128, 64), bft(128, 64)
        vnew = bft(128, 64)
        o_sb = f32t(64, 512)

        A('sp', lambda e: e.dma_start(out=wst[:, :, :], in_=wg.rearrange("(k p) c -> p k c", p=128)),
          writes=['gwst'], slot='gwst')
        A('pool', lambda e: e.tensor_copy(wq[:, :, :], wst[:, :, 0:128]), reads=['gwst'], writes=['gwq'])
        A('pool', lambda e: e.tensor_copy(wk[:, :, :], wst[:, :, 128:256]), reads=['gwst'], writes=['gwk'])
        A('pool', lambda e: e.tensor_copy(wv[:, :, :], wst[:, :, 256:320]), reads=['gwst'], writes=['gwv'])
        A('pool', lambda e: e.tensor_copy(wab[:, :, :], wst[:, :, 320:322]), reads=['gwst'], writes=['gwab'])
        A('sp', lambda e: e.dma_start(out=cw_sb[:, :], in_=cw[:, :]), writes=['cw'], slot='cw')
        A('sp', lambda e: e.dma_start(out=gp_sb[:, :], in_=gpar[:, :]), writes=['gp'], slot='gp')
        A('act', lambda e: e.activation(negA[:, :], gp_sb[:, 0:1], AF.Exp), reads=['gp'], writes=['negA'])
        A('dve', lambda e: e.tensor_scalar(negA[:, :], negA[:, :], -1.0, None, ALU.mult), reads=['negA'],
          writes=['negA'])
        A('pool', lambda e: e.memset(M_s[:, :, :], 1.0), writes=['M_s'])
        A('pool', lambda e: e.memset(M_i[:, :, :], 1.0), writes=['M_i'])
        A('pool', lambda e: e.affine_select(M_s[:, :, :], M_s[:, :, :], [[0, 4], [1, 128]], ALU.is_gt, 0.0, base=0,
                                            channel_multiplier=-1), reads=['M_s'], writes=['M_s'])
        A('pool', lambda e: e.affine_select(M_i[:, :, :], M_i[:, :, :], [[0, 4], [1, 128]], ALU.is_ge, 0.0, base=0,
                                            channel_multiplier=-1), reads=['M_i'], writes=['M_i'])
        A('pool', lambda e: e.memset(M_s[0:64, :, 64:128], 0.0), reads=['M_s'], writes=['M_s'])
        A('pool', lambda e: e.memset(M_i[0:64, :, 64:128], 0.0), reads=['M_i'], writes=['M_i'])
        A('pool', lambda e: e.affine_select(E63[:, :], c['zeros'][:, :], [[0, 128]], ALU.not_equal, 1.0, base=-63,
                                            channel_multiplier=1), reads=['c_zeros'], writes=['E63'])
        A('pool', lambda e: e.affine_select(E127[:, :], c['zeros'][:, :], [[0, 128]], ALU.not_equal, 1.0, base=-127,
                                            channel_multiplier=1), reads=['c_zeros'], writes=['E127'])
        A('pool', lambda e: e.tensor_copy(EL[:, 0:64], E63[:, 0:64]), reads=['E63'], writes=['EL'])
        A('pool', lambda e: e.tensor_copy(EL[:, 64:128], E127[:, 64:128]), reads=['E127', 'EL'], writes=['EL'])
        A('pool', lambda e: e.memset(rq[:, 0:3], 0.0), writes=['rq'])
        A('pool', lambda e: e.memset(rk[:, 0:3], 0.0), writes=['rk'])
        A('pool', lambda e: e.memset(rv[:, 0:3], 0.0), writes=['rv'])
        A('pool', lambda e: e.memset(St_f[:, :], 0.0), writes=['St_f'])
        A('pool', lambda e: e.memset(St_b[:, :], 0.0), writes=['St_b'])

        if GSTOP == 1:
            sc.flush()
            return
        hv = hT.rearrange("(k p) t -> p k t", p=128)
        ones, ident = c['ones'], c['ident']
        for i in range(NSEG):
            b = i % 2
            A('sp', lambda e, i=i, b=b: e.dma_start(out=ht[b][:, :, :], in_=hv[:, :, i * 512:(i + 1) * 512]),
              writes=[('ght', b)], slot=('ght', b))
            for (w_, M, bank, raw, key) in ((wq, 128, 0, rq, 'rq'), (wk, 128, 1, rk, 'rk'), (wv, 64, 2, rv, 'rv')):
                def mm(e, w_=w_, M=M, bank=bank, b=b):
                    r = None
                    for k in range(8):
                        r = e.matmul(psb[bank][0:M, :], w_[:, k, :], ht[b][:, k, :], start=(k == 0), stop=(k == 7))
                    return r
                A('pe', mm, reads=[('ght', b), 'gwq', 'gwk', 'gwv'], writes=[PB(bank)])
                A('dve', lambda e, M=M, bank=bank, raw=raw: e.tensor_copy(raw[0:M, 3:515], psb[bank][0:M, :]),
                  reads=[PB(bank)], writes=[key])

            def mmab(e, b=b):
                r = None
                for sub in range(4):
                    for k in range(8):
                        r = e.matmul(psb[3][:, sub * 2:sub * 2 + 2], ht[b][:, k, sub * 128:(sub + 1) * 128],
                                     wab[:, k, :], start=(k == 0), stop=(k == 7))
                return r
            A('pe', mmab, reads=[('ght', b), 'gwab'], writes=[PB(3)])
            p3 = psb[3][:, 0:8].rearrange("p (s c) -> p s c", c=2)
            A('dve', lambda e, p3=p3: e.tensor_copy(a_sb[:, :], p3[:, :, 0]), reads=[PB(3)], writes=['a_sb'])
            A('dve', lambda e, p3=p3: e.tensor_copy(b_sb[:, :], p3[:, :, 1]), reads=[PB(3)], writes=['b_sb'])
            if GSTOP == 2:
                sc.flush()
                return
            for which, (raw, cv_, M, key, ckey) in enumerate(((rq, cq, 128, 'rq', 'cq'), (rk, ck, 128, 'rk', 'ck'),
                                                              (rv, cv, 64, 'rv', 'cv'))):
                A('pool', lambda e, raw=raw, cv_=cv_, M=M, which=which: e.tensor_scalar(
                    cv_[0:M, :], raw[0:M, 0:512], cw_sb[0:M, which * 4:which * 4 + 1], None, ALU.mult),
                    reads=[key, 'cw'], writes=[ckey])
                for tap in range(1, 4):
                    A('dve', lambda e, raw=raw, cv_=cv_, M=M, which=which, tap=tap: e.scalar_tensor_tensor(
                        cv_[0:M, :], raw[0:M, tap:tap + 512], cw_sb[0:M, which * 4 + tap:which * 4 + tap + 1],
                        cv_[0:M, :], ALU.mult, ALU.add),
                        reads=[key, 'cw', ckey], writes=[ckey])
                A('pool', lambda e, raw=raw, M=M: e.tensor_copy(raw[0:M, 0:3], raw[0:M, 512:515]),
                  reads=[key, ckey], writes=[key])
                A('act', lambda e, cv_=cv_, M=M: e.activation(cv_[0:M, :], cv_[0:M, :], AF.Silu),
                  reads=[ckey], writes=[ckey])
            if GSTOP == 3:
                sc.flush()
                return
            for (cv_, ckey, bank, outf, okey, mul) in ((cq, 'cq', 4, qn_f, 'qn_f', 128.0 ** -0.5),
                                                      (ck, 'ck', 5, kn_f, 'kn_f', 1.0)):
                A('act', lambda e, cv_=cv_: e.activation(sq2[:, :], cv_[:, :], AF.Square), reads=[ckey],
                  writes=['sq2'])
                A('pe', lambda e, bank=bank: e.matmul(psb[bank][:, :], ones[:, :], sq2[:, :], start=True, stop=True),
                  reads=['sq2', 'c_ones'], writes=[PB(bank)])
                A('act', lambda e, bank=bank: e.activation(rn[:, :], psb[bank][:, :], AF.Sqrt, bias=EPS, scale=1.0),
                  reads=[PB(bank)], writes=['rn'])
                A('dve', lambda e: e.reciprocal(rn[:, :], rn[:, :]), reads=['rn'], writes=['rn'])
                A('dve', lambda e, cv_=cv_, outf=outf, mul=mul: e.scalar_tensor_tensor(
                    outf[:, :], cv_[:, :], mul, rn[:, :], ALU.mult, ALU.mult), reads=[ckey, 'rn'], writes=[okey])
            A('pool', lambda e: e.tensor_copy(qn_bf[:, :], qn_f[:, :]), reads=['qn_f'], writes=['qn_bf'])
            A('pool', lambda e: e.tensor_copy(kT_bf[:, :], kn_f[:, :]), reads=['kn_f'], writes=['kT_bf'])
            if GSTOP == 4:
                sc.flush()
                return
            A('act', lambda e: e.activation(g_tok[:, :], a_sb[:, :], AF.Exp, bias=gp_sb[:, 1:2], scale=1.0),
              reads=['a_sb', 'gp'], writes=['g_tok'])
            A('act', lambda e: e.activation(g_tok[:, :], g_tok[:, :], AF.Ln, bias=1.0, scale=1.0),
              reads=['g_tok'], writes=['g_tok'])
            A('dve', lambda e: e.tensor_scalar(g_tok[:, :], g_tok[:, :], negA[:, 0:1], None, ALU.mult),
              reads=['g_tok', 'negA'], writes=['g_tok'])
            A('act', lambda e: e.activation(b_sb[:, :], b_sb[:, :], AF.Sigmoid), reads=['b_sb'], writes=['b_sb'])
            A('dve', lambda e: e.tensor_scalar(negbt[:, :], b_sb[:, :], -1.0, None, ALU.mult), reads=['b_sb'],
              writes=['negbt'])
            A('pe', lambda e: e.matmul(psb[3][:, 0:4], M_i[:, 0, :], g_tok[:, :], start=True, stop=True),
              reads=['g_tok', 'M_i'], writes=[PB(3)])
            A('dve', lambda e: e.tensor_copy(G_tok[:, :], psb[3][:, 0:4]), reads=[PB(3)], writes=['G_tok'])
            A('act', lambda e: e.activation(eG[:, :], G_tok[:, :], AF.Exp), reads=['G_tok'], writes=['eG'])
            A('pe', lambda e: e.matmul(psb[3][:, 0:4], EL[:, :], G_tok[:, :], start=True, stop=True),
              reads=['G_tok', 'EL'], writes=[PB(3)])
            A('dve', lambda e: e.tensor_tensor(glo[:, :], psb[3][:, 0:4], G_tok[:, :], ALU.subtract),
              reads=[PB(3), 'G_tok'], writes=['glo'])
            A('act', lambda e: e.activation(ekd[:, :], glo[:, :], AF.Exp), reads=['glo'], writes=['ekd'])
            A('pe', lambda e: e.matmul(psb[3][:, 0:4], E63[:, :], G_tok[:, :], start=True, stop=True),
              reads=['G_tok', 'E63'], writes=[PB(3)])
            A('dve', lambda e: e.tensor_copy(dl[:, :], psb[3][:, 0:4]), reads=[PB(3)], writes=['dl'])
            A('act', lambda e: e.activation(dl[:, :], dl[:, :], AF.Exp), reads=['dl'], writes=['dl'])
            A('pe', lambda e: e.matmul(psb[3][:, 0:4], E127[:, :], G_tok[:, :], start=True, stop=True),
              reads=['G_tok', 'E127'], writes=[PB(3)])
            A('dve', lambda e: e.tensor_copy(dh[:, :], psb[3][:, 0:4]), reads=[PB(3)], writes=['dh'])
            A('act', lambda e: e.activation(dh[:, :], dh[:, :], AF.Exp), reads=['dh'], writes=['dh'])
            if GSTOP == 5:
                sc.flush()
                return
            for sub in range(4):
                A('dve', lambda e, sub=sub: e.tensor_scalar(diagG[:, sub * 128:(sub + 1) * 128], ident[:, :],
                                                            G_tok[:, sub:sub + 1], None, ALU.mult),
                  reads=['G_tok', 'c_ident'], writes=[('diagG', sub)])
                A('pe', lambda e, sub=sub: e.matmul(psb[6][:, sub * 128:(sub + 1) * 128], ones[:, :],
                                                    diagG[:, sub * 128:(sub + 1) * 128], start=True, stop=True),
                  reads=[('diagG', sub), 'c_ones'], writes=[PB(6)])
            A('act', lambda e: e.activation(EGrow[:, :], psb[6][:, :], AF.Exp), reads=[PB(6)], writes=['EGrow'])
            if not (GSKIP & 1):
              A('pool', lambda e: e.tensor_tensor(qdec[:, :], qn_f[:, :], EGrow[:, :], ALU.mult),
              reads=['qn_f', 'EGrow'], writes=['qdec'])
            for sub in range(4 if not (GSKIP & 2) else 0):
                A('dve', lambda e, sub=sub: e.tensor_scalar(Dm[:, sub * 128:(sub + 1) * 128],
                                                            psb[6][:, sub * 128:(sub + 1) * 128],
                                                            G_tok[:, sub:sub + 1], None, ALU.subtract),
                  reads=[PB(6), 'G_tok'], writes=['Dm'])
            if not (GSKIP & 4):
              A('pool', lambda e: e.tensor_scalar(Dm[:, :], Dm[:, :], 0.0, None, ALU.min), reads=['Dm'], writes=['Dm'])
            if not (GSKIP & 8):
              A('act', lambda e: e.activation(Gam[:, :], Dm[:, :], AF.Exp), reads=['Dm'], writes=['Gam'])
            if not (GSKIP & 16):
              A('pool', lambda e: e.tensor_tensor(Gs[:, :], Gam[:, :], M_s[:, :, :].rearrange("p s c -> p (s c)"),
                                                ALU.mult), reads=['Gam', 'M_s'], writes=['Gs'])
            if not (GSKIP & 16):
              A('pool', lambda e: e.tensor_tensor(Gi[:, :], Gam[:, :], M_i[:, :, :].rearrange("p s c -> p (s c)"),
                                                ALU.mult), reads=['Gam', 'M_i'], writes=['Gi'])
            if GSTOP == 6:
                sc.flush()
                return
            for sub in range(4):
                A('pe', lambda e, sub=sub: e.transpose(psb[7][:, sub * 128:(sub + 1) * 128],
                                                       kn_f[:, sub * 128:(sub + 1) * 128], ident[:, :]),
                  reads=['kn_f', 'c_ident'], writes=[PB(7)])
            A('dve', lambda e: e.tensor_copy(ktok[:, :, :], psb[7][:, :].rearrange("p (s c) -> p s c", c=128)),
              reads=[PB(7)], writes=['ktok'])
            for sub in range(4):
                A('pool', lambda e, sub=sub: e.tensor_scalar(kg[:, sub, :], ktok[:, sub, :], eG[:, sub:sub + 1], None,
                                                             ALU.mult), reads=['ktok', 'eG'], writes=['kg'])
                A('pool', lambda e, sub=sub: e.tensor_scalar(kdec[:, sub, :], ktok[:, sub, :], ekd[:, sub:sub + 1],
                                                             None, ALU.mult), reads=['ktok', 'ekd'], writes=['kdec'])
            for sub in range(4):
                A('pe', lambda e, sub=sub: e.transpose(psb[0][:, sub * 64:(sub + 1) * 64],
                                                       cv[0:64, sub * 128:(sub + 1) * 128], ident[0:64, 0:64]),
                  reads=['cv', 'c_ident'], writes=[PB(0)])
            A('dve', lambda e: e.tensor_copy(vtok[:, :, :], psb[0][:, 0:256].rearrange("p (s c) -> p s c", c=64)),
              reads=[PB(0)], writes=['vtok'])
            if GSTOP == 7:
                sc.flush()
                return
            for sub in range(4):
                cs = slice(sub * 128, (sub + 1) * 128)
                A('pe', lambda e, cs=cs: e.matmul(psb[1][:, cs], kT_bf[:, cs], kT_bf[:, cs], start=True, stop=True),
                  reads=['kT_bf'], writes=[PB(1)])
                A('pe', lambda e, cs=cs: e.matmul(psb[2][:, cs], kT_bf[:, cs], qn_bf[:, cs], start=True, stop=True),
                  reads=['kT_bf', 'qn_bf'], writes=[PB(2)])
                A('dve', lambda e, cs=cs, sub=sub: e.scalar_tensor_tensor(
                    B_f[:, cs], psb[1][:, cs], negbt[:, sub:sub + 1], Gs[:, cs], ALU.mult, ALU.mult),
                    reads=[PB(1), 'negbt', 'Gs'], writes=['B_f'])
            A('dve', lambda e: e.tensor_tensor(aqk[:, :], psb[2][:, :], Gi[:, :], ALU.mult), reads=[PB(2), 'Gi'],
              writes=['aqk'])
            A('pool', lambda e: e.tensor_copy(Bb[0][:, :], B_f[:, :]), reads=['B_f'], writes=[('Bb', 0)])
            for sub in range(4):
                cs = slice(sub * 128, (sub + 1) * 128)
                A('pe', lambda e, cs=cs: e.transpose(psb[4][:, cs], B_f[:, cs], ident[:, :]),
                  reads=['B_f', 'c_ident'], writes=[PB(4)])
            A('dve', lambda e: e.tensor_copy(Pb_[0][:, :], psb[4][:, :]), reads=[PB(4)], writes=[('Pb', 0)])
            for sub in range(4):
                cs = slice(sub * 128, (sub + 1) * 128)
                A('pool', lambda e, cs=cs: e.tensor_tensor(S_f[:, cs], B_f[:, cs], ident[:, :], ALU.add),
                  reads=['B_f', 'c_ident'], writes=['S_f'])
            A('pool', lambda e: e.tensor_copy(S_b[:, :], S_f[:, :]), reads=['S_f'], writes=['S_b'])
            if GSTOP == 8:
                sc.flush()
                return
            for j in range(5):
                cur, nxt = j % 2, (j + 1) % 2
                for sub in range(4):
                    cs = slice(sub * 128, (sub + 1) * 128)
                    A('pe', lambda e, cs=cs, cur=cur: e.matmul(psb[5][:, cs], Pb_[cur][:, cs], Bb[cur][:, cs],
                                                               start=True, stop=True),
                      reads=[('Pb', cur), ('Bb', cur)], writes=[PB(5)])
                A('dve', lambda e, nxt=nxt: e.tensor_copy(Bb[nxt][:, :], psb[5][:, :]), reads=[PB(5)],
                  writes=[('Bb', nxt)])
                for sub in range(4):
                    cs = slice(sub * 128, (sub + 1) * 128)
                    A('pe', lambda e, cs=cs, cur=cur: e.matmul(psb[4][:, cs], Bb[cur][:, cs], Pb_[cur][:, cs],
                                                               start=True, stop=True),
                      reads=[('Pb', cur), ('Bb', cur)], writes=[PB(4)])
                A('dve', lambda e, nxt=nxt: e.tensor_copy(Pb_[nxt][:, :], psb[4][:, :]), reads=[PB(4)],
                  writes=[('Pb', nxt)])
                for sub in range(4):
                    cs = slice(sub * 128, (sub + 1) * 128)
                    A('pe', lambda e, cs=cs, nxt=nxt: e.matmul(psb[7][:, cs], Pb_[nxt][:, cs], S_b[:, cs],
                                                               start=True, stop=True),
                      reads=[('Pb', nxt), 'S_b'], writes=[PB(7)])
                A('dve', lambda e: e.tensor_tensor(S_f[:, :], S_f[:, :], psb[7][:, :], ALU.add),
                  reads=['S_f', PB(7)], writes=['S_f'])
                A('pool', lambda e: e.tensor_copy(S_b[:, :], S_f[:, :]), reads=['S_f'], writes=['S_b'])
            if GSTOP == 9:
                sc.flush()
                return
            for sub in range(4):
                cs = slice(sub * 128, (sub + 1) * 128)
                A('pe', lambda e, cs=cs, sub=sub: e.matmul(psb[0][:, sub * 64:(sub + 1) * 64], S_b[:, cs],
                                                           vtok[:, sub, :], start=True, stop=True),
                  reads=['S_b', 'vtok'], writes=[PB(0)])
                A('pe', lambda e, cs=cs, sub=sub: e.matmul(psb[1][:, cs], kg[:, sub, :], S_b[:, cs], start=True,
                                                           stop=True),
                  reads=['S_b', 'kg'], writes=[PB(1)])
                A('dve', lambda e, sub=sub: e.tensor_scalar(ybu[:, sub, :], psb[0][:, sub * 64:(sub + 1) * 64],
                                                            b_sb[:, sub:sub + 1], None, ALU.mult),
                  reads=[PB(0), 'b_sb'], writes=['ybu'])
            A('dve', lambda e: e.tensor_copy(ywT[:, :], psb[1][:, :]), reads=[PB(1)], writes=['ywT'])
            if GSTOP == 10:
                sc.flush()
                return
            for ch in range(8):
                sub, hf = ch // 2, ch % 2
                rs = slice(hf * 64, hf * 64 + 64)
                cs = slice(sub * 128, (sub + 1) * 128)
                cc = slice(ch * 64, (ch + 1) * 64)
                pv_ = psb[2 + (ch % 2)]
                pS_ = psb[4 + (ch % 2)]
                A('pe', lambda e, cs=cs, pv_=pv_: e.matmul(pv_[:, 0:64], ywT[:, cs], St_b[:, :], start=True,
                                                           stop=True),
                  reads=['ywT', 'St_b'], writes=[PB(2 + (ch % 2))])
                A('dve', lambda e, rs=rs, sub=sub, pv_=pv_: e.scalar_tensor_tensor(
                    vnew[rs, :], pv_[rs, 0:64], negbt[rs, sub:sub + 1], ybu[rs, sub, :], ALU.mult, ALU.add),
                    reads=[PB(2 + (ch % 2)), 'negbt', 'ybu'], writes=['vnew'])

                def mmo(e, cc=cc, rs=rs):
                    e.matmul(psb[6][0:64, cc], St_b[:, :], qdec[:, cc], start=True, stop=False)
                    return e.matmul(psb[6][0:64, cc], vnew[rs, :], aqk[rs, cc], start=False, stop=True)
                A('pe', mmo, reads=['St_b', 'qdec', 'vnew', 'aqk'], writes=[PB(6)])
                A('pe', lambda e, rs=rs, sub=sub, pS_=pS_: e.matmul(pS_[:, 0:64], kdec[rs, sub, :], vnew[rs, :],
                                                                    start=True, stop=True),
                  reads=['kdec', 'vnew'], writes=[PB(4 + (ch % 2))])
                dsc = (dl if hf == 0 else dh)
                A('dve', lambda e, sub=sub, pS_=pS_, dsc=dsc: e.scalar_tensor_tensor(
                    St_b[:, :], St_f[:, :], dsc[:, sub:sub + 1], pS_[:, 0:64], ALU.mult, ALU.add),
                    reads=['St_f', PB(4 + (ch % 2)), 'dl', 'dh'], writes=['St_b'])
                A('dve', lambda e, sub=sub, pS_=pS_, dsc=dsc: e.scalar_tensor_tensor(
                    St_f[:, :], St_f[:, :], dsc[:, sub:sub + 1], pS_[:, 0:64], ALU.mult, ALU.add),
                    reads=['St_f', PB(4 + (ch % 2)), 'dl', 'dh'], writes=['St_f'])
            A('act', lambda e: e.activation(o_sb[:, :], psb[6][0:64, :], AF.Copy), reads=[PB(6)], writes=['o_sb'])
            A('sp', lambda e, i=i: e.dma_start(out=obT[:, i * 512:(i + 1) * 512], in_=o_sb[:, :]),
              reads=['o_sb'], writes=[('obT', i)], slot='ob')
        sc.flush()


def build_M(S, do_fox=True, do_gdn=True, stop=99):
    nc = bass.Bass("TRN2", target_bir_lowering=False)
    hT = nc.dram_tensor("hT", [D, S], BF16, kind="ExternalInput").ap()
    wf = nc.dram_tensor("wf", [D, 193], F32, kind="ExternalInput").ap()
    bfg = nc.dram_tensor("bfg", [128, 1], F32, kind="ExternalInput").ap()
    wg = nc.dram_tensor("wg", [D, 322], F32, kind="ExternalInput").ap()
    cw = nc.dram_tensor("cw", [128, 12], F32, kind="ExternalInput").ap()
    gpar = nc.dram_tensor("gpar", [128, 2], F32, kind="ExternalInput").ap()
    oaT = nc.dram_tensor("oaT", [64, S], F32, kind="ExternalOutput").ap()
    obT = nc.dram_tensor("obT", [64, S], F32, kind="ExternalOutput").ap()
    scr = nc.dram_tensor("scr", [128, 128], BF16).ap()
    cx = Ctx(nc)
    with cx.st:
        sc = Sched(nc)
        c = make_consts(sc, cx)
        psb = [cx.ps([128, 512], F32) for _ in range(8)]
        if do_gdn:
            gdn_phase(nc, sc, c, S, hT, wg, cw, gpar, obT, psb)
        if do_fox:
            fox_phase(nc, sc, cx, c, S, hT, wf, bfg, oaT, scr, psb, stop=stop)
        sc.close()
    return nc


def build_T(TS, last):
    nc = bass.Bass("TRN2", target_bir_lowering=False)
    TT = 256
    NTT = TS // TT
    xT = nc.dram_tensor("xT", [D, TS], F32, kind="ExternalInput").ap()
    hT = nc.dram_tensor("hT", [D, TS], BF16, kind="ExternalInput").ap()
    oaT = nc.dram_tensor("oaT", [512, TS], F32, kind="ExternalInput").ap()
    obT = nc.dram_tensor("obT", [512, TS], F32, kind="ExternalInput").ap()
    w_in = nc.dram_tensor("w_in", [D, N_IN], F32, kind="ExternalInput").ap()
    w_br = nc.dram_tensor("w_br", [1536, D], F32, kind="ExternalInput").ap()
    w_out = nc.dram_tensor("w_out", [D, D], F32, kind="ExternalInput").ap()
    w_kv = nc.dram_tensor("w_kv", [D, 1024], F32, kind="ExternalInput").ap()
    memT = nc.dram_tensor("memT", [D, 256], F32, kind="ExternalInput").ap()
    mem_g = nc.dram_tensor("mem_g", [128, 8], F32, kind="ExternalInput").ap()
    b_mg = nc.dram_tensor("b_mg", [128, 24], F32, kind="ExternalInput").ap()
    gdn_g = nc.dram_tensor("gdn_g", [128, 1], F32, kind="ExternalInput").ap()
    next_g = nc.dram_tensor("next_g", [128, 8], F32, kind="ExternalInput").ap()
    xoT = nc.dram_tensor("xoT", [D, TS], F32, kind="ExternalOutput").ap()
    if not last:
        hoT = nc.dram_tensor("hoT", [D, TS], BF16, kind="ExternalOutput").ap()
    cx = Ctx(nc)
    with cx.st:
        sc = Sched(nc)
        ones_f = cx.sb([128, 128], F32)
        ones_bf = cx.sb([128, 128], BF16)
        sc.add('pool', lambda e: e.memset(ones_f[:, :], 1.0), writes=['ones_f'])
        sc.add('pool', lambda e: e.memset(ones_bf[:, :], 1.0), writes=['ones_bf'])
        psb = [cx.ps([128, 512], F32) for _ in range(8)]

        BM = {(0, 0): 0, (0, 1): 1, (1, 0): 2, (1, 1): 2, (2, 0): 3, (2, 1): 4, (3, 0): 5, (3, 1): 6,
              (4, 0): 2, (4, 1): 3, (5, 0): 4, (5, 1): 5, (6, 0): 6, (6, 1): 7, (7, 0): 0, (7, 1): 1}

        def half(bk, h):
            return psb[BM[(bk, h)]][:, 0:TT]

        def hk(bk, h):
            return ('pb', BM[(bk, h)])
        Wz = cx.sb([128, 8, 5120], BF16)
        Wbr = cx.sb([128, 12, 1024], BF16)
        Wout = cx.sb([128, 8, 1024], BF16)
        mkT = cx.sb([128, 4, 256], BF16)
        mv = cx.sb([128, 2, 512], BF16)
        stage = [cx.sb([128, 1024], F32) for _ in range(2)]
        memg_sb = cx.sb([128, 8], F32)
        bm_sb = cx.sb([128, 24], F32)
        gg_sb = cx.sb([128, 1], F32)
        ng_sb = cx.sb([128, 8], F32)
        for i, (dst, srcap) in enumerate([(memg_sb, mem_g), (bm_sb, b_mg), (gg_sb, gdn_g), (ng_sb, next_g)]):
            sc.add('sp', lambda e, dst=dst, srcap=srcap: e.dma_start(out=dst[:, :], in_=srcap[:, :]),
                   writes=[('par', i)], slot=('par', i))
        nst = [0]

        def ldw(dst, dkey, srcap):
            b = nst[0] % 2
            nst[0] += 1
            wd = srcap.shape[-1]
            sc.add('sp', lambda e: e.dma_start(out=stage[b][:, 0:wd], in_=srcap), writes=[('stage', b)],
                   slot=('stage', b))
            sc.add('pool' if b else 'dve', lambda e: e.tensor_copy(dst, stage[b][:, 0:wd]),
                   reads=[('stage', b)], writes=[dkey])

        pcx = Ctx(nc)
        with pcx.st:
            Wkv = pcx.sb([128, 8, 1024], BF16)
            mt = pcx.sb([128, 8, 256], F32)
            mn = pcx.sb([128, 8, 256], BF16)
            sqm = pcx.sb([128, 8, 256], F32)
            rstm = pcx.sb([128, 256], F32)
            for k in range(8):
                ldw(Wkv[:, k, :], 'Wkv', w_kv[k * 128:(k + 1) * 128, :])
            sc.add('sp', lambda e: e.dma_start(out=mt[:, :, :], in_=memT.rearrange("(k p) m -> p k m", p=128)),
                   writes=['mt'], slot='mt')
            sc.ops[-1]
            saved = {'g': None}
            emit_rmsnorm(sc, mt, 'mt', memg_sb, ones_f, mn, 'mn', 256, sqm, psb[0], rstm, 'mnorm', gkey=('par', 0))
            for hh in range(4):
                def mmk(e, hh=hh):
                    r = None
                    for k in range(8):
                        r = e.matmul(half(1, hh % 2), Wkv[:, k, hh * 128:(hh + 1) * 128], mn[:, k, :],
                                     start=(k == 0), stop=(k == 7))
                    return r
                sc.add('pe', mmk, reads=['Wkv', 'mn'], writes=[hk(1, hh % 2)])
                sc.add('dve', lambda e, hh=hh: e.tensor_copy(mkT[:, hh, :], half(1, hh % 2)),
                       reads=[hk(1, hh % 2)], writes=['mkT'])
            for mc in range(2):
                def mmv(e, mc=mc):
                    r = None
                    for k in range(8):
                        r = e.matmul(psb[2 + mc][:, :], mn[:, k, mc * 128:(mc + 1) * 128], Wkv[:, k, 512:1024],
                                     start=(k == 0), stop=(k == 7))
                    return r
                sc.add('pe', mmv, reads=['Wkv', 'mn'], writes=[('pb', 2 + mc)])
                sc.add('dve', lambda e, mc=mc: e.tensor_copy(mv[:, mc, :], psb[2 + mc][:, :]),
                       reads=[('pb', 2 + mc)], writes=['mv'])
            sc.flush()

        for k in range(8):
            ldw(Wz[:, k, 0:512], 'Wz', w_in[k * 128:(k + 1) * 128, OFF['az']:OFF['az'] + 512])
            ldw(Wz[:, k, 512:1024], 'Wz', w_in[k * 128:(k + 1) * 128, OFF['bz']:OFF['bz'] + 512])
            for cb in range(4):
                ldw(Wz[:, k, 1024 + cb * 1024:2048 + cb * 1024], 'Wz',
                    w_in[k * 128:(k + 1) * 128, OFF['mq'] + cb * 1024:OFF['mq'] + (cb + 1) * 1024])
        for k in range(12):
            ldw(Wbr[:, k, :], 'Wbr', w_br[k * 128:(k + 1) * 128, :])
        for k in range(8):
            ldw(Wout[:, k, :], 'Wout', w_out[k * 128:(k + 1) * 128, :])

        ht = [cx.sb([128, 8, TT], BF16) for _ in range(2)]
        xt = cx.sb([128, 8, TT], F32)
        oat = cx.sb([128, 4, TT], F32)
        obt = cx.sb([128, 4, TT], F32)
        yT = cx.sb([128, 12, TT], BF16)
        mg = cx.sb([128, 8, TT], BF16)
        hout = cx.sb([128, 8, TT], BF16 if not last else F32)
        sqs = [cx.sb([128, TT], F32) for _ in range(2)]
        sil = [cx.sb([128, TT], F32) for _ in range(2)]
        tmp = [cx.sb([128, TT], F32) for _ in range(2)]
        rstd = cx.sb([128, TT], F32)
        rden = cx.sb([128, TT], F32)
        mqs = cx.sb([128, TT], BF16)
        pT = [cx.sb([128, TT], BF16) for _ in range(2)]
        gs = [cx.sb([128, TT], F32) for _ in range(3)]
        acc = [cx.sb([128, TT], F32) for _ in range(2)]
        hv = hT.rearrange("(k p) t -> p k t", p=128)
        xv = xT.rearrange("(k p) t -> p k t", p=128)
        oav = oaT.rearrange("(k p) t -> p k t", p=128)
        obv = obT.rearrange("(k p) t -> p k t", p=128)
        xov = xoT.rearrange("(k p) t -> p k t", p=128)
        if not last:
            hov = hoT.rearrange("(k p) t -> p k t", p=128)

        zcnt = [0]

        def zproj(col0, b):
            s = zcnt[0] % 2
            zcnt[0] += 1
            dst = half(0, s)

            def mm(e):
                r = None
                for k in range(8):
                    r = e.matmul(dst, Wz[:, k, col0:col0 + 128], ht[b][:, k, :], start=(k == 0), stop=(k == 7))
                return r
            sc.add('pe', mm, reads=['Wz', ('ht', b)], writes=[hk(0, s)])
            return dst, hk(0, s)

        for it in range(NTT):
            b = it % 2
            t0, t1 = it * TT, (it + 1) * TT
            sc.add('sp', lambda e, b=b, t0=t0, t1=t1: e.dma_start(out=ht[b][:, :, :], in_=hv[:, :, t0:t1]),
                   writes=[('ht', b)], slot=('ht', b))
            sc.add('sp', lambda e, t0=t0, t1=t1: e.dma_start(out=xt[:, :, :], in_=xv[:, :, t0:t1]),
                   writes=['xt'] + [('xn', dc) for dc in range(8)], slot='xt')
            sc.add('sp', lambda e, t0=t0, t1=t1: e.dma_start(out=oat[:, :, :], in_=oav[:, :, t0:t1]),
                   writes=['oat'], slot='oat')
            sc.add('sp', lambda e, t0=t0, t1=t1: e.dma_start(out=obt[:, :, :], in_=obv[:, :, t0:t1]),
                   writes=['obt'], slot='obt')
            for fc in range(4):
                zp, zk = zproj(fc * 128, b)
                s = fc % 2
                sc.add('act', lambda e, zp=zp, s=s: e.activation(sil[s][:, :], zp, AF.Silu),
                       reads=[zk], writes=[('sil', s)])
                sc.add('pool', lambda e, fc=fc, s=s: e.tensor_tensor(yT[:, fc, :], oat[:, fc, :], sil[s][:, :],
                                                                    ALU.mult),
                       reads=['oat', ('sil', s)], writes=[('yT', fc)])
            for hd in range(4):
                s = hd % 2
                sc.add('act', lambda e, hd=hd, s=s: e.activation(sqs[s][:, :], obt[:, hd, :], AF.Square),
                       reads=['obt'], writes=[('sqs', s)])
                sc.add('pe', lambda e, s=s: e.matmul(half(1, 0), ones_f[:, :], sqs[s][:, :], start=True, stop=True),
                       reads=[('sqs', s), 'ones_f'], writes=[hk(1, 0)])
                sc.add('act', lambda e: e.activation(rstd[:, :], half(1, 0), AF.Sqrt, bias=EPS, scale=1.0 / 128),
                       reads=[hk(1, 0)], writes=['rstd'])
                sc.add('dve', lambda e: e.reciprocal(rstd[:, :], rstd[:, :]), reads=['rstd'], writes=['rstd'])
                sc.add('dve', lambda e, hd=hd, s=s: e.scalar_tensor_tensor(tmp[s][:, :], obt[:, hd, :],
                                                                          gg_sb[:, 0:1], rstd[:, :],
                                                                          ALU.mult, ALU.mult),
                       reads=['obt', 'rstd', ('par', 2)], writes=[('tmp', s)])
                zp, zk = zproj(512 + hd * 128, b)
                sc.add('act', lambda e, zp=zp, s=s: e.activation(sil[s][:, :], zp, AF.Silu),
                       reads=[zk], writes=[('sil', s)])
                sc.add('pool', lambda e, hd=hd, s=s: e.tensor_tensor(yT[:, 4 + hd, :], tmp[s][:, :], sil[s][:, :],
                                                                    ALU.mult),
                       reads=[('tmp', s), ('sil', s)], writes=[('yT', 4 + hd)])
            for hh in range(4):
                s = hh % 2
                zp, zk = zproj(1024 + hh * 128, b)
                sc.add('dve', lambda e, zp=zp: e.tensor_copy(mqs[:, :], zp), reads=[zk], writes=['mqs'])
                for mc in range(2):
                    sc.add('pe', lambda e, hh=hh, mc=mc: e.matmul(half(2, mc), mkT[:, hh, mc * 128:(mc + 1) * 128],
                                                                 mqs[:, :], start=True, stop=True),
                           reads=['mkT', 'mqs'], writes=[hk(2, mc)])
                    sc.add('act', lambda e, mc=mc: e.activation(pT[mc][:, :], half(2, mc), AF.Exp,
                                                                scale=128.0 ** -0.5),
                           reads=[hk(2, mc)], writes=[('pT', mc)])

                def mmn(e, hh=hh):
                    e.matmul(half(3, 0), mv[:, 0, hh * 128:(hh + 1) * 128], pT[0][:, :], start=True, stop=False)
                    return e.matmul(half(3, 0), mv[:, 1, hh * 128:(hh + 1) * 128], pT[1][:, :], start=False,
                                    stop=True)
                sc.add('pe', mmn, reads=['mv', ('pT', 0), ('pT', 1)], writes=[hk(3, 0)])

                def mmd(e):
                    e.matmul(half(3, 1), ones_bf[:, :], pT[0][:, :], start=True, stop=False)
                    return e.matmul(half(3, 1), ones_bf[:, :], pT[1][:, :], start=False, stop=True)
                sc.add('pe', mmd, reads=['ones_bf', ('pT', 0), ('pT', 1)], writes=[hk(3, 1)])
                sc.add('dve', lambda e: e.reciprocal(rden[:, :], half(3, 1)), reads=[hk(3, 1)], writes=['rden'])
                sc.add('dve', lambda e, s=s: e.tensor_tensor(tmp[s][:, :], half(3, 0), rden[:, :], ALU.mult),
                       reads=[hk(3, 0), 'rden'], writes=[('tmp', s)])
                zp, zk = zproj(1536 + hh * 128, b)
                sc.add('act', lambda e, zp=zp, s=s: e.activation(sil[s][:, :], zp, AF.Silu),
                       reads=[zk], writes=[('sil', s)])
                sc.add('pool', lambda e, hh=hh, s=s: e.tensor_tensor(yT[:, 8 + hh, :], tmp[s][:, :], sil[s][:, :],
                                                                    ALU.mult),
                       reads=[('tmp', s), ('sil', s)], writes=[('yT', 8 + hh)])
            for dc in range(8):
                for n in range(3):
                    pslot = [(4, 0), (4, 1), (5, 0)][n]
                    gslot = [(5, 1), (6, 0), (6, 1)][n]

                    def mmp(e, n=n, dc=dc, pslot=pslot):
                        r = None
                        for kc in range(4):
                            r = e.matmul(half(*pslot), Wbr[:, n * 4 + kc, dc * 128:(dc + 1) * 128],
                                         yT[:, n * 4 + kc, :], start=(kc == 0), stop=(kc == 3))
                        return r
                    sc.add('pe', mmp, reads=['Wbr'] + [('yT', n * 4 + kc) for kc in range(4)],
                           writes=[hk(*pslot)])

                    def mmg(e, n=n, dc=dc, gslot=gslot, b=b):
                        r = None
                        for k in range(8):
                            c0 = 2048 + n * 1024 + dc * 128
                            r = e.matmul(half(*gslot), Wz[:, k, c0:c0 + 128], ht[b][:, k, :], start=(k == 0),
                                         stop=(k == 7))
                        return r
                    sc.add('pe', mmg, reads=['Wz', ('ht', b)], writes=[hk(*gslot)])
                    sc.add('act', lambda e, n=n, dc=dc, gslot=gslot: e.activation(
                        gs[n][:, :], half(*gslot), AF.Sigmoid, bias=bm_sb[:, n * 8 + dc:n * 8 + dc + 1], scale=1.0),
                        reads=[hk(*gslot), ('par', 1)], writes=[('gs', n)])
                sc.add('dve', lambda e: e.tensor_tensor(acc[0][:, :], half(4, 0), gs[0][:, :], ALU.mult),
                       reads=[hk(4, 0), ('gs', 0)], writes=[('acc', 0)])
                sc.add('dve', lambda e: e.tensor_tensor(acc[1][:, :], half(4, 1), gs[1][:, :], ALU.mult),
                       reads=[hk(4, 1), ('gs', 1)], writes=[('acc', 1)])
                sc.add('pool', lambda e: e.tensor_tensor(acc[0][:, :], acc[0][:, :], acc[1][:, :], ALU.add),
                       reads=[('acc', 0), ('acc', 1)], writes=[('acc', 0)])
                sc.add('dve', lambda e: e.tensor_tensor(acc[1][:, :], half(5, 0), gs[2][:, :], ALU.mult),
                       reads=[hk(5, 0), ('gs', 2)], writes=[('acc', 1)])
                sc.add('pool', lambda e, dc=dc: e.tensor_tensor(mg[:, dc, :], acc[0][:, :], acc[1][:, :], ALU.add),
                       reads=[('acc', 0), ('acc', 1)], writes=[('mg', dc)])
            for dc in range(8):
                s = dc % 2

                def mmo(e, dc=dc, s=s):
                    r = None
                    for k in range(8):
                        r = e.matmul(half(7, s), Wout[:, k, dc * 128:(dc + 1) * 128], mg[:, k, :], start=(k == 0),
                                     stop=(k == 7))
                    return r
                sc.add('pe', mmo, reads=['Wout'] + [('mg', k) for k in range(8)], writes=[hk(7, s)])
                sc.add('dve', lambda e, dc=dc, s=s: e.tensor_tensor(xt[:, dc, :], xt[:, dc, :], half(7, s), ALU.add),
                       reads=['xt', hk(7, s)], writes=[('xn', dc)])
            allxn = [('xn', dc) for dc in range(8)]
            if not last:
                sc.add('sp', lambda e, t0=t0, t1=t1: e.dma_start(out=xov[:, :, t0:t1], in_=xt[:, :, :]),
                       reads=allxn, writes=[('xo', it)], slot='xo')
            for k in range(8):
                s = k % 2
                sc.add('act', lambda e, k=k, s=s: e.activation(sqs[s][:, :], xt[:, k, :], AF.Square),
                       reads=[('xn', k)], writes=[('sqs', s)])
                sc.add('pe', lambda e, k=k, s=s: e.matmul(half(1, 1), ones_f[:, :], sqs[s][:, :], start=(k == 0),
                                                         stop=(k == 7)),
                       reads=[('sqs', s), 'ones_f'], writes=[hk(1, 1)])
            sc.add('act', lambda e: e.activation(rstd[:, :], half(1, 1), AF.Sqrt, bias=EPS, scale=1.0 / D),
                   reads=[hk(1, 1)], writes=['rstd'])
            sc.add('dve', lambda e: e.reciprocal(rstd[:, :], rstd[:, :]), reads=['rstd'], writes=['rstd'])
            for k in range(8):
                sc.add('dve', lambda e, k=k: e.scalar_tensor_tensor(hout[:, k, :], xt[:, k, :], ng_sb[:, k:k + 1],
                                                                   rstd[:, :], ALU.mult, ALU.mult),
                       reads=[('xn', k), 'rstd', ('par', 3)], writes=['hout'])
            if last:
                sc.add('sp', lambda e, t0=t0, t1=t1: e.dma_start(out=xov[:, :, t0:t1], in_=hout[:, :, :]),
                       reads=['hout'], writes=[('xo', it)], slot='xo')
            else:
                sc.add('sp', lambda e, t0=t0, t1=t1: e.dma_start(out=hov[:, :, t0:t1], in_=hout[:, :, :]),
                       reads=['hout'], writes=[('ho', it)], slot='ho')
        sc.close()
    return nc


def mixer_inputs(c, hT, w_in_l, b_fg_l, conv_w_l, a_log_l, dt_bias_l):
    hd, half = c // 2, c % 2
    wf = np.concatenate([w_in_l[:, OFF['aq'] + c * 64:OFF['aq'] + (c + 1) * 64],
                         w_in_l[:, OFF['ak'] + c * 64:OFF['ak'] + (c + 1) * 64],
                         w_in_l[:, OFF['av'] + c * 64:OFF['av'] + (c + 1) * 64],
                         w_in_l[:, OFF['af'] + c:OFF['af'] + c + 1]], axis=1)
    vo = hd * 128 + half * 64
    wg = np.concatenate([w_in_l[:, OFF['bq'] + hd * 128:OFF['bq'] + (hd + 1) * 128],
                         w_in_l[:, OFF['bk'] + hd * 128:OFF['bk'] + (hd + 1) * 128],
                         w_in_l[:, OFF['bv'] + vo:OFF['bv'] + vo + 64],
                         w_in_l[:, OFF['ba'] + hd:OFF['ba'] + hd + 1],
                         w_in_l[:, OFF['bb'] + hd:OFF['bb'] + hd + 1]], axis=1)
    cw = np.zeros((128, 12), np.float32)
    cw[:, 0:4] = conv_w_l[:, hd * 128:(hd + 1) * 128].T
    cw[:, 4:8] = conv_w_l[:, 512 + hd * 128:512 + (hd + 1) * 128].T
    cw[0:64, 8:12] = conv_w_l[:, 1024 + vo:1024 + vo + 64].T
    gpar = np.empty((128, 2), np.float32)
    gpar[:, 0] = a_log_l[hd]
    gpar[:, 1] = dt_bias_l[hd]
    return dict(hT=hT, wf=np.ascontiguousarray(wf), bfg=np.full((128, 1), b_fg_l[c], np.float32),
                wg=np.ascontiguousarray(wg), cw=cw, gpar=gpar)


def _lay8(v):
    return np.ascontiguousarray(np.asarray(v, np.float32).reshape(-1, 128).T)


_PROGS = {}


def _prog(name, fn):
    if name not in _PROGS:
        _PROGS[name] = fn()
    return _PROGS[name]


def kernel(x, mem, norm_g, w_in, b_fg, b_merge, conv_w, a_log, dt_bias, gdn_norm_g, mem_norm_g, w_mem_kv,
           w_branch, w_out, final_norm_g):
    f = lambda a: np.asarray(a, np.float32)
    x, mem, norm_g, w_in, b_fg, b_merge, conv_w = map(f, (x, mem, norm_g, w_in, b_fg, b_merge, conv_w))
    a_log, dt_bias, gdn_norm_g, mem_norm_g = map(f, (a_log, dt_bias, gdn_norm_g, mem_norm_g))
    w_mem_kv, w_branch, w_out, final_norm_g = map(f, (w_mem_kv, w_branch, w_out, final_norm_g))
    S = x.shape[1]
    TS = S // NCORES
    cores = list(range(NCORES))
    xT = np.ascontiguousarray(x[0].T)
    memT = np.ascontiguousarray(mem[0].T)
    sh = lambda a, c: np.ascontiguousarray(a[:, c * TS:(c + 1) * TS])
    ncP = _prog('P', lambda: build_P(TS))
    res = run_bass_kernel_spmd(ncP, [dict(xT=sh(xT, c), g=_lay8(norm_g[0])) for c in cores], core_ids=cores)
    hT = np.concatenate([np.asarray(r["hT"]) for r in res.results], axis=1)
    depth = w_in.shape[0]
    for l in range(depth):
        last = (l == depth - 1)
        ncM = _prog('M', lambda: build_M(S))
        hTc = np.ascontiguousarray(hT)
        res = run_bass_kernel_spmd(
            ncM, [mixer_inputs(c, hTc, w_in[l], b_fg[l], conv_w[l], a_log[l], dt_bias[l]) for c in cores],
            core_ids=cores)
        oaT = np.concatenate([np.asarray(r["oaT"]) for r in res.results], axis=0)
        obT = np.concatenate([np.asarray(r["obT"]) for r in res.results], axis=0)
        ncT = _prog('T%d' % int(last), lambda: build_T(TS, last))
        ng = final_norm_g if last else norm_g[l + 1]
        maps = []
        for c in cores:
            maps.append(dict(xT=sh(xT, c), hT=sh(hT, c), oaT=sh(oaT, c), obT=sh(obT, c),
                             w_in=np.ascontiguousarray(w_in[l]),
                             w_br=np.ascontiguousarray(w_branch[l].reshape(1536, D)),
                             w_out=np.ascontiguousarray(w_out[l]), w_kv=np.ascontiguousarray(w_mem_kv[l]),
                             memT=memT, mem_g=_lay8(mem_norm_g[l]), b_mg=_lay8(b_merge[l]),
                             gdn_g=np.ascontiguousarray(gdn_norm_g[l].reshape(128, 1)), next_g=_lay8(ng)))
        res = run_bass_kernel_spmd(ncT, maps, core_ids=cores)
        xT = np.concatenate([np.asarray(r["xoT"]) for r in res.results], axis=1)
        if not last:
            hT = np.concatenate([np.asarray(r["hoT"]) for r in res.results], axis=1)
    out = np.ascontiguousarray(xT.T).reshape(1, S, D).astype(np.float32)
    return out
```

```python
import contextlib
import numpy as np
import ml_dtypes
import concourse.bass as bass
import concourse.mybir as mybir
from concourse.bass_utils import run_bass_kernel_spmd

F32 = mybir.dt.float32
BF16 = mybir.dt.bfloat16
AF = mybir.ActivationFunctionType
ALU = mybir.AluOpType

D = 1024
S_FULL = 16384
NCORES = 8
EPS = 1e-6
N_IN = 8208
import os as _os
SAME_ENGINE_SYNC = bool(int(_os.environ.get('SAME_SYNC', '1')))
OFF = dict(aq=0, ak=512, av=1024, af=1536, az=1544, bq=2056, bk=2568, bv=3080,
           ba=3592, bb=3596, bz=3600, mq=4112, mz=4624, gates=5136)


def _is_psum_key(k):
    if isinstance(k, str):
        return k.startswith('ps')
    if isinstance(k, tuple) and len(k) >= 2:
        return k[0] in ('pb', 'pS', 'pO') or k[1] == 'ps'
    return False


class Sched:
    ENGS = ['pe', 'act', 'dve', 'pool', 'sp']

    def __init__(self, nc, same_engine_sync=None):
        if same_engine_sync is None:
            same_engine_sync = SAME_ENGINE_SYNC
        self.nc = nc
        self.ops = []
        self.lastw = {}
        self.readers = {}
        self.slot_count = {}
        self.same = same_engine_sync
        self.stack = contextlib.ExitStack()
        self.esem = {e: self.stack.enter_context(nc.semaphore("sem_" + e)) for e in self.ENGS}
        self.ssem = {}
        self.cnt = {e: 0 for e in self.ENGS}

    def _needs_same(self, eng):
        if eng == 'pe':
            return False
        if eng == 'pool':
            return True
        return self.same

    def add(self, eng, fn, reads=(), writes=(), slot=None):
        op = dict(eng=eng, fn=fn, deps=[], slot=slot, inc=False, id=len(self.ops))
        deps = {}
        for k in reads:
            w = self.lastw.get(k)
            if w is not None:
                deps[w['id']] = w
            if _is_psum_key(k):
                for r in self.readers.get(k, ()):
                    if r['eng'] != eng:
                        deps[r['id']] = r
        for k in writes:
            w = self.lastw.get(k)
            if w is not None:
                deps[w['id']] = w
            for r in self.readers.get(k, ()):
                deps[r['id']] = r
        for d in deps.values():
            if d is op:
                continue
            op['deps'].append(d)
            if d['slot'] is None:
                if d['eng'] != eng or self._needs_same(eng) or slot is not None:
                    d['inc'] = True
        for k in writes:
            self.lastw[k] = op
            self.readers[k] = []
        for k in reads:
            self.readers.setdefault(k, []).append(op)
        if slot is not None:
            if slot not in self.ssem:
                self.ssem[slot] = self.stack.enter_context(self.nc.semaphore("sl_%d" % len(self.ssem)))
            self.slot_count[slot] = self.slot_count.get(slot, 0) + 1
            op['slot_val'] = self.slot_count[slot] * 16
        self.ops.append(op)
        return op

    def flush(self):
        nc = self.nc
        for op in self.ops:
            if op['slot'] is None and op['inc']:
                self.cnt[op['eng']] += 1
                op['count'] = self.cnt[op['eng']]
        ops = self.ops
        esem, ssem = self.esem, self.ssem
        final = dict(self.slot_count)
        with nc.Block() as block:
            def run(ename, eng):
                known = {}
                for op in ops:
                    if op['eng'] != ename:
                        continue
                    waits = {}
                    for d in op['deps']:
                        if d['slot'] is not None:
                            key = ('s', d['slot'])
                            v = d['slot_val']
                            sem = ssem[d['slot']]
                        else:
                            if d['eng'] == ename and op['slot'] is None and not self._needs_same(ename):
                                continue
                            key = ('e', d['eng'])
                            v = d['count']
                            sem = esem[d['eng']]
                        if waits.get(key, (None, -1))[1] < v:
                            waits[key] = (sem, v)
                    for key, (sem, v) in waits.items():
                        if known.get(key, -1) >= v:
                            continue
                        known[key] = v
                        eng.wait_ge(sem, v)
                    ins = op['fn'](eng)
                    if op['slot'] is not None:
                        ins.then_inc(ssem[op['slot']], 16)
                    elif op['inc']:
                        ins.then_inc(esem[ename], 1)
                if ename == 'sp':
                    for s, n in final.items():
                        eng.wait_ge(ssem[s], n * 16)

            block.tensor(lambda e: run('pe', e))
            block.scalar(lambda e: run('act', e))
            block.vector(lambda e: run('dve', e))
            block.gpsimd(lambda e: run('pool', e))
            block.sync(lambda e: run('sp', e))
        self.ops = []
        self.lastw = {}
        self.readers = {}

    def collective(self, kind, op, src_ap, dst_ap, reads=(), writes=(), slot='cc', ncores=NCORES):
        self.flush()
        if slot not in self.ssem:
            self.ssem[slot] = self.stack.enter_context(self.nc.semaphore("sl_%d" % len(self.ssem)))
        self.slot_count[slot] = self.slot_count.get(slot, 0) + 1
        ins = self.nc.gpsimd.collective_compute(kind, op, replica_groups=[list(range(ncores))],
                                                ins=[src_ap], outs=[dst_ap])
        ins.then_inc(self.ssem[slot], 16)
        pseudo = dict(eng='pool', fn=None, deps=[], slot=slot, inc=False, id=-1,
                      slot_val=self.slot_count[slot] * 16)
        for k in writes:
            self.lastw[k] = pseudo
            self.readers[k] = []

    def close(self):
        self.flush()
        self.stack.close()


_NAME = [0]


class Ctx:
    def __init__(self, nc):
        self.nc = nc
        self.st = contextlib.ExitStack()

    def sb(self, shape, dt, name=None):
        _NAME[0] += 1
        return self.st.enter_context(self.nc.sbuf_tensor(name or ("t%d" % _NAME[0]), list(shape), dt))

    def ps(self, shape, dt, name=None):
        _NAME[0] += 1
        return self.st.enter_context(self.nc.psum_tensor(name or ("p%d" % _NAME[0]), list(shape), dt))


def emit_rmsnorm(sc, x_sb, xkey, g_sb, ones_f, out_sb, outkey, TT, sq, ps, rstd, tag, dim=D, gkey='g'):
    for k in range(8):
        sc.add('act', lambda e, k=k: e.activation(sq[:, k, :], x_sb[:, k, :], AF.Square),
               reads=[xkey], writes=[(tag, 'sq', k)])

    def mm(e):
        r = None
        for k in range(8):
            r = e.matmul(ps[:, 0:TT], ones_f[:, :], sq[:, k, :], start=(k == 0), stop=(k == 7))
        return r
    sc.add('pe', mm, reads=[(tag, 'sq', k) for k in range(8)] + ['ones_f'], writes=[(tag, 'ps')])
    sc.add('act', lambda e: e.activation(rstd[:, :], ps[:, 0:TT], AF.Sqrt, bias=EPS, scale=1.0 / dim),
           reads=[(tag, 'ps')], writes=[(tag, 'rstd')])
    sc.add('dve', lambda e: e.reciprocal(rstd[:, :], rstd[:, :]),
           reads=[(tag, 'rstd')], writes=[(tag, 'rstd')])
    for k in range(8):
        sc.add('dve',
               lambda e, k=k: e.scalar_tensor_tensor(out_sb[:, k, :], x_sb[:, k, :], g_sb[:, k:k + 1],
                                                     rstd[:, :], ALU.mult, ALU.mult),
               reads=[xkey, (tag, 'rstd'), gkey], writes=[outkey])


def build_P(TS):
    nc = bass.Bass("TRN2", target_bir_lowering=False)
    xT = nc.dram_tensor("xT", [D, TS], F32, kind="ExternalInput").ap()
    g = nc.dram_tensor("g", [128, 8], F32, kind="ExternalInput").ap()
    hT = nc.dram_tensor("hT", [D, TS], BF16, kind="ExternalOutput").ap()
    TT = 512
    cx = Ctx(nc)
    with cx.st:
        sc = Sched(nc)
        ones_f = cx.sb([128, 128], F32)
        g_sb = cx.sb([128, 8], F32)
        xs = [cx.sb([128, 8, TT], F32) for _ in range(2)]
        hs = [cx.sb([128, 8, TT], BF16) for _ in range(2)]
        sq = cx.sb([128, 8, TT], F32)
        rstd = cx.sb([128, TT], F32)
        ps = cx.ps([128, 512], F32)
        sc.add('pool', lambda e: e.memset(ones_f[:, :], 1.0), writes=['ones_f'])
        sc.add('sp', lambda e: e.dma_start(out=g_sb[:, :], in_=g[:, :]), writes=['g'], slot='g')
        xv = xT.rearrange("(k p) t -> p k t", p=128)
        hv = hT.rearrange("(k p) t -> p k t", p=128)
        for i in range(TS // TT):
            b = i % 2
            sc.add('sp', lambda e, i=i, b=b: e.dma_start(out=xs[b][:, :, :], in_=xv[:, :, i * TT:(i + 1) * TT]),
                   writes=[('x', b)], slot=('x', b))
            emit_rmsnorm(sc, xs[b], ('x', b), g_sb, ones_f, hs[b], ('h', b), TT, sq, ps, rstd, 'n')
            sc.add('sp', lambda e, i=i, b=b: e.dma_start(out=hv[:, :, i * TT:(i + 1) * TT], in_=hs[b][:, :, :]),
                   reads=[('h', b)], writes=[('hout', i)], slot=('ho', b))
        sc.close()
    return nc


def make_consts(sc, cx):
    c = {}
    c['ones'] = cx.sb([128, 128], F32)
    c['ident'] = cx.sb([128, 128], F32)
    c['uincl'] = cx.sb([128, 128], F32)
    c['ustrict'] = cx.sb([128, 128], F32)
    c['e0'] = cx.sb([128, 128], F32)
    c['ones_bf'] = cx.sb([128, 128], BF16)
    c['ident_bf'] = cx.sb([128, 128], BF16)
    c['zeros'] = cx.sb([128, 128], F32)
    sc.add('pool', lambda e: e.memset(c['ones'][:, :], 1.0), writes=['c_ones'])
    sc.add('pool', lambda e: e.memset(c['zeros'][:, :], 0.0), writes=['c_zeros'])
    sc.add('pool', lambda e: e.memset(c['ones_bf'][:, :], 1.0), writes=['c_ones_bf'])
    sc.add('pool', lambda e: e.affine_select(c['ident'][:, :], c['zeros'][:, :], [[1, 128]], ALU.not_equal, 1.0,
                                             base=0, channel_multiplier=-1),
           reads=['c_zeros'], writes=['c_ident'])
    sc.add('pool', lambda e: e.tensor_copy(c['ident_bf'][:, :], c['ident'][:, :]),
           reads=['c_ident'], writes=['c_ident_bf'])
    sc.add('pool', lambda e: e.affine_select(c['uincl'][:, :], c['ones'][:, :], [[1, 128]], ALU.is_ge, 0.0,
                                             base=0, channel_multiplier=-1),
           reads=['c_ones'], writes=['c_uincl'])
    sc.add('pool', lambda e: e.affine_select(c['ustrict'][:, :], c['ones'][:, :], [[1, 128]], ALU.is_gt, 0.0,
                                             base=0, channel_multiplier=-1),
           reads=['c_ones'], writes=['c_ustrict'])
    sc.add('pool', lambda e: e.affine_select(c['e0'][:, :], c['ones'][:, :], [[0, 128]], ALU.is_ge, 0.0,
                                             base=0, channel_multiplier=-1),
           reads=['c_ones'], writes=['c_e0'])
    return c


def load_cast(sc, dst_bf, dstkey, src_ap, stage, stagekey, eng_dma='sp', eng_cast='pool', slot=None):
    sc.add(eng_dma, lambda e: e.dma_start(out=stage, in_=src_ap), writes=[stagekey], slot=slot or stagekey)
    sc.add(eng_cast, lambda e: e.tensor_copy(dst_bf, stage), reads=[stagekey], writes=[dstkey])


def fox_phase(nc, sc, cx0, c, S, hT, wf, bfg, oaT, scr, psb, stop=99):
    NT = S // 128
    NG = S // 512
    cx = Ctx(nc)
    with cx.st:
        wq = cx.sb([128, 8, 64], BF16)
        wk = cx.sb([128, 8, 64], BF16)
        wv = cx.sb([128, 8, 65], BF16)
        wst = cx.sb([128, 8, 193], F32)
        QT = cx.sb([65, S], BF16)
        KT = cx.sb([65, S], BF16)
        V = cx.sb([128, NT, 65], BF16)
        lfr = cx.sb([128, NT], F32)
        lfn = cx.sb([128, NT], F32)
        Fn = cx.sb([128, NT], F32)
        frefB = cx.sb([128, NG], F32)
        ctok = cx.sb([128, NT], F32)
        cTT = cx.sb([128, 128], BF16)
        totT = cx.sb([128, 1], F32)
        X = cx.sb([128, 128], F32)
        negb = cx.sb([128, 1], F32)
        biasg = [cx.sb([128, NT], F32) for _ in range(2)]
        ht = [cx.sb([128, 8, 512], BF16) for _ in range(2)]
        Pb = [cx.sb([128, 512], BF16) for _ in range(4)]
        oun = cx.sb([65, 512], F32)
        rl = cx.sb([65, 512], F32)
        ofin = [cx.sb([64, 512], F32) for _ in range(2)]

        sc.add('sp', lambda e: e.dma_start(out=wst[:, :, :], in_=wf.rearrange("(k p) c -> p k c", p=128)),
               writes=['wst'], slot='wst')
        sc.add('pool', lambda e: e.tensor_copy(wq[:, :, :], wst[:, :, 0:64]), reads=['wst'], writes=['wq'])
        sc.add('pool', lambda e: e.tensor_copy(wk[:, :, :], wst[:, :, 64:128]), reads=['wst'], writes=['wk'])
        sc.add('pool', lambda e: e.tensor_copy(wv[:, :, :], wst[:, :, 128:193]), reads=['wst'], writes=['wv'])
        sc.add('sp', lambda e: e.dma_start(out=negb[:, :], in_=bfg[:, :]), writes=['negb'], slot='negb')
        sc.add('dve', lambda e: e.tensor_scalar(negb[:, :], negb[:, :], -1.0, None, ALU.mult),
               reads=['negb'], writes=['negb'])
        sc.add('pool', lambda e: e.memset(KT[64:65, :], 1.0), writes=['KTrow'])
        sc.add('pool', lambda e: e.memset(V[:, :, 64:65], 1.0), writes=['Vones'])

        if stop <= 0:
            sc.flush()
            return
        hv = hT.rearrange("(k p) t -> p k t", p=128)
        psq, psk, psv = psb[0], psb[1], psb[2]
        for i in range(NG):
            b = i % 2
            sc.add('sp', lambda e, i=i, b=b: e.dma_start(out=ht[b][:, :, :], in_=hv[:, :, i * 512:(i + 1) * 512]),
                   writes=[('ht', b)], slot=('ht', b))

            def mmq(e, b=b):
                r = None
                for k in range(8):
                    r = e.matmul(psq[0:64, :], wq[:, k, :], ht[b][:, k, :], start=(k == 0), stop=(k == 7))
                return r
            import os
            DBG = int(os.environ.get('FOXDBG', '15'))
            if DBG & 1:
              sc.add('pe', mmq, reads=[('ht', b), 'wq'], writes=['psq'])
            if DBG & 1:
              sc.add('act', lambda e, i=i: e.activation(QT[0:64, i * 512:(i + 1) * 512], psq[0:64, :], AF.Copy,
                                                      scale=0.125),
                   reads=['psq'], writes=[('QT', i)])

            def mmk(e, b=b):
                r = None
                for k in range(8):
                    r = e.matmul(psk[0:64, :], wk[:, k, :], ht[b][:, k, :], start=(k == 0), stop=(k == 7))
                return r
            if DBG & 2:
              sc.add('pe', mmk, reads=[('ht', b), 'wk'], writes=['psk'])
              sc.add('dve', lambda e, i=i: e.tensor_copy(KT[0:64, i * 512:(i + 1) * 512], psk[0:64, :]),
                   reads=['psk'], writes=[('KT', i)])

            def mmv(e, b=b):
                r = None
                for sub in range(4):
                    for k in range(8):
                        r = e.matmul(psv[:, sub * 128:sub * 128 + 65], ht[b][:, k, sub * 128:(sub + 1) * 128],
                                     wv[:, k, :], start=(k == 0), stop=(k == 7))
                return r
            pv3 = psv[:, :].rearrange("p (s c) -> p s c", c=128)
            if DBG & 4:
              sc.add('pe', mmv, reads=[('ht', b), 'wv'], writes=['psv'])
              sc.add('dve', lambda e, i=i, pv3=pv3: e.tensor_copy(V[:, 4 * i:4 * i + 4, 0:64], pv3[:, :, 0:64]),
                   reads=['psv', 'Vones'], writes=[('V', i)])
            if DBG & 8:
              sc.add('dve', lambda e, i=i, pv3=pv3: e.tensor_copy(lfr[:, 4 * i:4 * i + 4], pv3[:, :, 64]),
                   reads=['psv'], writes=[('lfr', i)])

        if stop <= 1:
            sc.flush()
            return
        allfr = [('lfr', i) for i in range(NG)]
        sc.add('act', lambda e: e.activation(lfn[:, :], lfr[:, :], AF.Exp, bias=negb[:, 0:1], scale=-1.0),
               reads=allfr + ['negb'], writes=['lfn'])
        sc.add('act', lambda e: e.activation(lfn[:, :], lfn[:, :], AF.Ln, bias=1.0, scale=1.0),
               reads=['lfn'], writes=['lfn'])
        pt = psb[0]
        sc.add('pe', lambda e: e.matmul(pt[0:NT, 0:1], lfn[:, :], c['ones'][:, 0:1], start=True, stop=True),
               reads=['lfn', 'c_ones', 'psq'], writes=['psq'])
        sc.add('dve', lambda e: e.tensor_copy(totT[0:NT, :], pt[0:NT, 0:1]), reads=['psq'], writes=['totT'])
        sc.add('dve', lambda e: e.tensor_scalar(X[0:NT, 0:NT], c['ustrict'][0:NT, 0:NT], totT[0:NT, 0:1], None,
                                                ALU.mult),
               reads=['totT', 'c_ustrict'], writes=['X'])
        pf = psb[1]

        def mmF(e):
            e.matmul(pf[:, 0:NT], c['uincl'][:, :], lfn[:, :], start=True, stop=False)
            return e.matmul(pf[:, 0:NT], c['ones'][0:NT, :], X[0:NT, 0:NT], start=False, stop=True)
        sc.add('pe', mmF, reads=['lfn', 'X', 'c_uincl', 'c_ones', 'psk'], writes=['psk'])
        sc.add('dve', lambda e: e.tensor_copy(Fn[:, :], pf[:, 0:NT]), reads=['psk'], writes=['Fn'])
        pr = psb[2]
        sc.add('pe', lambda e: e.matmul(pr[:, 0:NG], c['e0'][:, :], Fn[:, 0:NT:4], start=True, stop=True),
               reads=['Fn', 'c_e0', 'psv'], writes=['psv'])
        sc.add('dve', lambda e: e.tensor_copy(frefB[:, :], pr[:, 0:NG]), reads=['psv'], writes=['frefB'])
        for r in range(4):
            sc.add('dve', lambda e, r=r: e.tensor_tensor(ctok[:, r:NT:4], frefB[:, :], Fn[:, r:NT:4], ALU.subtract),
                   reads=['frefB', 'Fn'], writes=[('ctok', r)])
        pc = psb[3]
        sc.add('pe', lambda e: e.transpose(pc[0:NT, 0:128], ctok[:, :], c['ident'][:, :]),
               reads=[('ctok', r) for r in range(4)] + ['c_ident'], writes=['ps3'])
        sc.add('dve', lambda e: e.tensor_copy(cTT[0:NT, :], pc[0:NT, 0:128]), reads=['ps3'], writes=['cTT'])
        sc.add('sp', lambda e: e.dma_start(out=scr[0:NT, :], in_=cTT[0:NT, :]), reads=['cTT'], writes=['scr'],
               slot='scr')
        sc.add('sp', lambda e: e.dma_start(out=QT[64:65, :], in_=scr[0:NT, :].rearrange("(o j) p -> o (j p)", o=1)),
               reads=['scr'], writes=['QTrow'], slot='qtrow')

        if stop <= 2:
            sc.flush()
            return
        sc.flush()
        LA = 2
        pS = [psb[0], psb[1], psb[2], psb[3]]
        pO = [psb[6], psb[7]]
        pbc = psb[4]
        blocks = []
        for g in range(NG):
            nj = 4 * g + 4
            for j in range(nj):
                r = j - 4 * g
                c0 = 0 if r < 0 else r * 128
                blocks.append((g, j, r, c0, 512 - c0, nj))
        NB = len(blocks)

        def emit_front(bi):
            g, j, r, c0, N, nj = blocks[bi]
            gb = g % 2
            sb_ = bi % 4
            if j == 0:
                sc.add('dve', lambda e: e.tensor_scalar(biasg[gb][:, 0:nj], Fn[:, 0:nj], frefB[:, g:g + 1], None,
                                                        ALU.subtract),
                       reads=['Fn', 'frefB'], writes=[('biasg', gb)])
            sc.add('pe', lambda e: e.matmul(pS[sb_][:, 0:N], KT[0:65, j * 128:(j + 1) * 128],
                                            QT[0:65, g * 512 + c0:(g + 1) * 512], start=True, stop=True),
                   reads=['QT', 'KT'], writes=[('pS', sb_)])
            sc.add('act', lambda e: e.activation(Pb[sb_][:, 0:N], pS[sb_][:, 0:N], AF.Exp,
                                                 bias=biasg[gb][:, j:j + 1], scale=1.0),
                   reads=[('pS', sb_), ('biasg', gb)], writes=[('P', sb_)])
            if r >= 0:
                sc.add('pool', lambda e: e.affine_select(Pb[sb_][:, 0:128], Pb[sb_][:, 0:128], [[1, 128]], ALU.is_ge,
                                                         0.0, base=0, channel_multiplier=-1),
                       reads=[('P', sb_)], writes=[('P', sb_)])

        def emit_back(bi):
            g, j, r, c0, N, nj = blocks[bi]
            gb = g % 2
            sb_ = bi % 4
            sc.add('pe', lambda e: e.matmul(pO[gb][0:65, c0:512], V[:, j, 0:65], Pb[sb_][:, 0:N], start=(j == 0),
                                            stop=(j == nj - 1), skip_group_check=True),
                   reads=[('P', sb_), 'V'], writes=[('pO', gb)])
            if j == nj - 1:
                sc.add('dve', lambda e: e.tensor_copy(oun[0:65, :], pO[gb][0:65, :]),
                       reads=[('pO', gb)], writes=['oun'])
                sc.add('dve', lambda e: e.reciprocal(rl[64:65, :], oun[64:65, :]), reads=['oun'], writes=['rl'])
                sc.add('pe', lambda e: e.matmul(pbc[0:64, :], c['ones'][64:65, 0:64], rl[64:65, :], start=True,
                                                stop=True),
                       reads=['rl', 'c_ones'], writes=[('pb', 4)])
                sc.add('dve', lambda e: e.tensor_tensor(ofin[gb][:, :], oun[0:64, :], pbc[0:64, :], ALU.mult),
                       reads=['oun', ('pb', 4)], writes=[('ofin', gb)])
                sc.add('sp', lambda e: e.dma_start(out=oaT[:, g * 512:(g + 1) * 512], in_=ofin[gb][:, :]),
                       reads=[('ofin', gb)], writes=[('oaT', g)], slot=('oa', gb))

        for bi in range(NB + LA):
            if bi < NB:
                emit_front(bi)
            if bi - LA >= 0:
                emit_back(bi - LA)
        sc.flush()


def gdn_phase(nc, sc, c, S, hT, wg, cw, gpar, obT, psb):
    import os
    GSTOP = int(os.environ.get('GSTOP', '99'))
    GSKIP = int(os.environ.get('GSKIP', '0'))
    NSEG = S // 512
    A = sc.add
    cx = Ctx(nc)
    PB = lambda n: ('pb', n)
    with cx.st:
        f32t = lambda *sh: cx.sb(list(sh), F32)
        bft = lambda *sh: cx.sb(list(sh), BF16)
        wst = f32t(128, 8, 322)
        wq, wk, wv, wab = bft(128, 8, 128), bft(128, 8, 128), bft(128, 8, 64), bft(128, 8, 2)
        cw_sb, gp_sb = f32t(128, 12), f32t(128, 2)
        negA = f32t(128, 1)
        M_s, M_i = f32t(128, 4, 128), f32t(128, 4, 128)
        E63, E127, EL = f32t(128, 128), f32t(128, 128), f32t(128, 128)
        ht = [bft(128, 8, 512) for _ in range(2)]
        rq, rk, rv = f32t(128, 515), f32t(128, 515), f32t(64, 515)
        cq, ck, cv = f32t(128, 512), f32t(128, 512), f32t(64, 512)
        sq2, rn = f32t(128, 512), f32t(128, 512)
        qn_f, kn_f = f32t(128, 512), f32t(128, 512)
        qn_bf, kT_bf = bft(128, 512), bft(128, 512)
        a_sb, b_sb, g_tok, G_tok, eG, ekd, dl, dh, negbt, glo = [f32t(128, 4) for _ in range(10)]
        diagG, EGrow, Dm, Gam, Gs, Gi = [f32t(128, 512) for _ in range(6)]
        qdec = bft(128, 512)
        ktok = f32t(128, 4, 128)
        kg, kdec = bft(128, 4, 128), bft(128, 4, 128)
        vtok = bft(128, 4, 64)
        B_f = f32t(128, 512)
        Bb = [bft(128, 512) for _ in range(2)]
        Pb_ = [bft(128, 512) for _ in range(2)]
        S_f, S_b = f32t(128, 512), bft(128, 512)
        aqk = bft(128, 512)
        ybu = f32t(128, 4, 64)
        ywT = bft(128, 512)
        St_f, St_b = f32t(128, 64), bft(128, 64)
        vnew = bft(128, 64)
        o_sb = f32t(64, 512)

        A('sp', lambda e: e.dma_start(out=wst[:, :, :], in_=wg.rearrange("(k p) c -> p k c", p=128)),
          writes=['gwst'], slot='gwst')
        A('pool', lambda e: e.tensor_copy(wq[:, :, :], wst[:, :, 0:128]), reads=['gwst'], writes=['gwq'])
        A('pool', lambda e: e.tensor_copy(wk[:, :, :], wst[:, :, 128:256]), reads=['gwst'], writes=['gwk'])
        A('pool', lambda e: e.tensor_copy(wv[:, :, :], wst[:, :, 256:320]), reads=['gwst'], writes=['gwv'])
        A('pool', lambda e: e.tensor_copy(wab[:, :, :], wst[:, :, 320:322]), reads=['gwst'], writes=['gwab'])
        A('sp', lambda e: e.dma_start(out=cw_sb[:, :], in_=cw[:, :]), writes=['cw'], slot='cw')
        A('sp', lambda e: e.dma_start(out=gp_sb[:, :], in_=gpar[:, :]), writes=['gp'], slot='gp')
        A('act', lambda e: e.activation(negA[:, :], gp_sb[:, 0:1], AF.Exp), reads=['gp'], writes=['negA'])
        A('dve', lambda e: e.tensor_scalar(negA[:, :], negA[:, :], -1.0, None, ALU.mult), reads=['negA'],
          writes=['negA'])
        A('pool', lambda e: e.memset(M_s[:, :, :], 1.0), writes=['M_s'])
        A('pool', lambda e: e.memset(M_i[:, :, :], 1.0), writes=['M_i'])
        A('pool', lambda e: e.affine_select(M_s[:, :, :], M_s[:, :, :], [[0, 4], [1, 128]], ALU.is_gt, 0.0, base=0,
                                            channel_multiplier=-1), reads=['M_s'], writes=['M_s'])
        A('pool', lambda e: e.affine_select(M_i[:, :, :], M_i[:, :, :], [[0, 4], [1, 128]], ALU.is_ge, 0.0, base=0,
                                            channel_multiplier=-1), reads=['M_i'], writes=['M_i'])
        A('pool', lambda e: e.memset(M_s[0:64, :, 64:128], 0.0), reads=['M_s'], writes=['M_s'])
        A('pool', lambda e: e.memset(M_i[0:64, :, 64:128], 0.0), reads=['M_i'], writes=['M_i'])
        A('pool', lambda e: e.affine_select(E63[:, :], c['zeros'][:, :], [[0, 128]], ALU.not_equal, 1.0, base=-63,
                                            channel_multiplier=1), reads=['c_zeros'], writes=['E63'])
        A('pool', lambda e: e.affine_select(E127[:, :], c['zeros'][:, :], [[0, 128]], ALU.not_equal, 1.0, base=-127,
                                            channel_multiplier=1), reads=['c_zeros'], writes=['E127'])
        A('pool', lambda e: e.tensor_copy(EL[:, 0:64], E63[:, 0:64]), reads=['E63'], writes=['EL'])
        A('pool', lambda e: e.tensor_copy(EL[:, 64:128], E127[:, 64:128]), reads=['E127', 'EL'], writes=['EL'])
        A('pool', lambda e: e.memset(rq[:, 0:3], 0.0), writes=['rq'])
        A('pool', lambda e: e.memset(rk[:, 0:3], 0.0), writes=['rk'])
        A('pool', lambda e: e.memset(rv[:, 0:3], 0.0), writes=['rv'])
        A('pool', lambda e: e.memset(St_f[:, :], 0.0), writes=['St_f'])
        A('pool', lambda e: e.memset(St_b[:, :], 0.0), writes=['St_b'])

        if GSTOP == 1:
            sc.flush()
            return
        hv = hT.rearrange("(k p) t -> p k t", p=128)
        ones, ident = c['ones'], c['ident']
        for i in range(NSEG):
            b = i % 2
            A('sp', lambda e, i=i, b=b: e.dma_start(out=ht[b][:, :, :], in_=hv[:, :, i * 512:(i + 1) * 512]),
              writes=[('ght', b)], slot=('ght', b))
            for (w_, M, bank, raw, key) in ((wq, 128, 0, rq, 'rq'), (wk, 128, 1, rk, 'rk'), (wv, 64, 2, rv, 'rv')):
                def mm(e, w_=w_, M=M, bank=bank, b=b):
                    r = None
                    for k in range(8):
                        r = e.matmul(psb[bank][0:M, :], w_[:, k, :], ht[b][:, k, :], start=(k == 0), stop=(k == 7))
                    return r
                A('pe', mm, reads=[('ght', b), 'gwq', 'gwk', 'gwv'], writes=[PB(bank)])
                A('dve', lambda e, M=M, bank=bank, raw=raw: e.tensor_copy(raw[0:M, 3:515], psb[bank][0:M, :]),
                  reads=[PB(bank)], writes=[key])

            def mmab(e, b=b):
                r = None
                for sub in range(4):
                    for k in range(8):
                        r = e.matmul(psb[3][:, sub * 2:sub * 2 + 2], ht[b][:, k, sub * 128:(sub + 1) * 128],
                                     wab[:, k, :], start=(k == 0), stop=(k == 7))
                return r
            A('pe', mmab, reads=[('ght', b), 'gwab'], writes=[PB(3)])
            p3 = psb[3][:, 0:8].rearrange("p (s c) -> p s c", c=2)
            A('dve', lambda e, p3=p3: e.tensor_copy(a_sb[:, :], p3[:, :, 0]), reads=[PB(3)], writes=['a_sb'])
            A('dve', lambda e, p3=p3: e.tensor_copy(b_sb[:, :], p3[:, :, 1]), reads=[PB(3)], writes=['b_sb'])
            if GSTOP == 2:
                sc.flush()
                return
            for which, (raw, cv_, M, key, ckey) in enumerate(((rq, cq, 128, 'rq', 'cq'), (rk, ck, 128, 'rk', 'ck'),
                                                              (rv, cv, 64, 'rv', 'cv'))):
                A('pool', lambda e, raw=raw, cv_=cv_, M=M, which=which: e.tensor_scalar(
                    cv_[0:M, :], raw[0:M, 0:512], cw_sb[0:M, which * 4:which * 4 + 1], None, ALU.mult),
                    reads=[key, 'cw'], writes=[ckey])
                for tap in range(1, 4):
                    A('dve', lambda e, raw=raw, cv_=cv_, M=M, which=which, tap=tap: e.scalar_tensor_tensor(
                        cv_[0:M, :], raw[0:M, tap:tap + 512], cw_sb[0:M, which * 4 + tap:which * 4 + tap + 1],
                        cv_[0:M, :], ALU.mult, ALU.add),
                        reads=[key, 'cw', ckey], writes=[ckey])
                A('pool', lambda e, raw=raw, M=M: e.tensor_copy(raw[0:M, 0:3], raw[0:M, 512:515]),
                  reads=[key, ckey], writes=[key])
                A('act', lambda e, cv_=cv_, M=M: e.activation(cv_[0:M, :], cv_[0:M, :], AF.Silu),
                  reads=[ckey], writes=[ckey])
            if GSTOP == 3:
                sc.flush()
                return
            for (cv_, ckey, bank, outf, okey, mul) in ((cq, 'cq', 4, qn_f, 'qn_f', 128.0 ** -0.5),
                                                      (ck, 'ck', 5, kn_f, 'kn_f', 1.0)):
                A('act', lambda e, cv_=cv_: e.activation(sq2[:, :], cv_[:, :], AF.Square), reads=[ckey],
                  writes=['sq2'])
                A('pe', lambda e, bank=bank: e.matmul(psb[bank][:, :], ones[:, :], sq2[:, :], start=True, stop=True),
                  reads=['sq2', 'c_ones'], writes=[PB(bank)])
                A('act', lambda e, bank=bank: e.activation(rn[:, :], psb[bank][:, :], AF.Sqrt, bias=EPS, scale=1.0),
                  reads=[PB(bank)], writes=['rn'])
                A('dve', lambda e: e.reciprocal(rn[:, :], rn[:, :]), reads=['rn'], writes=['rn'])
                A('dve', lambda e, cv_=cv_, outf=outf, mul=mul: e.scalar_tensor_tensor(
                    outf[:, :], cv_[:, :], mul, rn[:, :], ALU.mult, ALU.mult), reads=[ckey, 'rn'], writes=[okey])
            A('pool', lambda e: e.tensor_copy(qn_bf[:, :], qn_f[:, :]), reads=['qn_f'], writes=['qn_bf'])
            A('pool', lambda e: e.tensor_copy(kT_bf[:, :], kn_f[:, :]), reads=['kn_f'], writes=['kT_bf'])
            if GSTOP == 4:
                sc.flush()
                return
            A('act', lambda e: e.activation(g_tok[:, :], a_sb[:, :], AF.Exp, bias=gp_sb[:, 1:2], scale=1.0),
              reads=['a_sb', 'gp'], writes=['g_tok'])
            A('act', lambda e: e.activation(g_tok[:, :], g_tok[:, :], AF.Ln, bias=1.0, scale=1.0),
              reads=['g_tok'], writes=['g_tok'])
            A('dve', lambda e: e.tensor_scalar(g_tok[:, :], g_tok[:, :], negA[:, 0:1], None, ALU.mult),
              reads=['g_tok', 'negA'], writes=['g_tok'])
            A('act', lambda e: e.activation(b_sb[:, :], b_sb[:, :], AF.Sigmoid), reads=['b_sb'], writes=['b_sb'])
            A('dve', lambda e: e.tensor_scalar(negbt[:, :], b_sb[:, :], -1.0, None, ALU.mult), reads=['b_sb'],
              writes=['negbt'])
            A('pe', lambda e: e.matmul(psb[3][:, 0:4], M_i[:, 0, :], g_tok[:, :], start=True, stop=True),
              reads=['g_tok', 'M_i'], writes=[PB(3)])
            A('dve', lambda e: e.tensor_copy(G_tok[:, :], psb[3][:, 0:4]), reads=[PB(3)], writes=['G_tok'])
            A('act', lambda e: e.activation(eG[:, :], G_tok[:, :], AF.Exp), reads=['G_tok'], writes=['eG'])
            A('pe', lambda e: e.matmul(psb[3][:, 0:4], EL[:, :], G_tok[:, :], start=True, stop=True),
              reads=['G_tok', 'EL'], writes=[PB(3)])
            A('dve', lambda e: e.tensor_tensor(glo[:, :], psb[3][:, 0:4], G_tok[:, :], ALU.subtract),
              reads=[PB(3), 'G_tok'], writes=['glo'])
            A('act', lambda e: e.activation(ekd[:, :], glo[:, :], AF.Exp), reads=['glo'], writes=['ekd'])
            A('pe', lambda e: e.matmul(psb[3][:, 0:4], E63[:, :], G_tok[:, :], start=True, stop=True),
              reads=['G_tok', 'E63'], writes=[PB(3)])
            A('dve', lambda e: e.tensor_copy(dl[:, :], psb[3][:, 0:4]), reads=[PB(3)], writes=['dl'])
            A('act', lambda e: e.activation(dl[:, :], dl[:, :], AF.Exp), reads=['dl'], writes=['dl'])
            A('pe', lambda e: e.matmul(psb[3][:, 0:4], E127[:, :], G_tok[:, :], start=True, stop=True),
              reads=['G_tok', 'E127'], writes=[PB(3)])
            A('dve', lambda e: e.tensor_copy(dh[:, :], psb[3][:, 0:4]), reads=[PB(3)], writes=['dh'])
            A('act', lambda e: e.activation(dh[:, :], dh[:, :], AF.Exp), reads=['dh'], writes=['dh'])
            if GSTOP == 5:
                sc.flush()
                return
            for sub in range(4):
                A('dve', lambda e, sub=sub: e.tensor_scalar(diagG[:, sub * 128:(sub + 1) * 128], ident[:, :],
                                                            G_tok[:, sub:sub + 1], None, ALU.mult),
                  reads=['G_tok', 'c_ident'], writes=[('diagG', sub)])
                A('pe', lambda e, sub=sub: e.matmul(psb[6][:, sub * 128:(sub + 1) * 128], ones[:, :],
                                                    diagG[:, sub * 128:(sub + 1) * 128], start=True, stop=True),
                  reads=[('diagG', sub), 'c_ones'], writes=[PB(6)])
            A('act', lambda e: e.activation(EGrow[:, :], psb[6][:, :], AF.Exp), reads=[PB(6)], writes=['EGrow'])
            if not (GSKIP & 1):
              A('pool', lambda e: e.tensor_tensor(qdec[:, :], qn_f[:, :], EGrow[:, :], ALU.mult),
              reads=['qn_f', 'EGrow'], writes=['qdec'])
            for sub in range(4 if not (GSKIP & 2) else 0):
                A('dve', lambda e, sub=sub: e.tensor_scalar(Dm[:, sub * 128:(sub + 1) * 128],
                                                            psb[6][:, sub * 128:(sub + 1) * 128],
                                                            G_tok[:, sub:sub + 1], None, ALU.subtract),
                  reads=[PB(6), 'G_tok'], writes=['Dm'])
            if not (GSKIP & 4):
              A('pool', lambda e: e.tensor_scalar(Dm[:, :], Dm[:, :], 0.0, None, ALU.min), reads=['Dm'], writes=['Dm'])
            if not (GSKIP & 8):
              A('act', lambda e: e.activation(Gam[:, :], Dm[:, :], AF.Exp), reads=['Dm'], writes=['Gam'])
            if not (GSKIP & 16):
              A('pool', lambda e: e.tensor_tensor(Gs[:, :], Gam[:, :], M_s[:, :, :].rearrange("p s c -> p (s c)"),
                                                ALU.mult), reads=['Gam', 'M_s'], writes=['Gs'])
            if not (GSKIP & 16):
              A('pool', lambda e: e.tensor_tensor(Gi[:, :], Gam[:, :], M_i[:, :, :].rearrange("p s c -> p (s c)"),
                                                ALU.mult), reads=['Gam', 'M_i'], writes=['Gi'])
            if GSTOP == 6:
                sc.flush()
                return
            for sub in range(4):
                A('pe', lambda e, sub=sub: e.transpose(psb[7][:, sub * 128:(sub + 1) * 128],
                                                       kn_f[:, sub * 128:(sub + 1) * 128], ident[:, :]),
                  reads=['kn_f', 'c_ident'], writes=[PB(7)])
            A('dve', lambda e: e.tensor_copy(ktok[:, :, :], psb[7][:, :].rearrange("p (s c) -> p s c", c=128)),
              reads=[PB(7)], writes=['ktok'])
            for sub in range(4):
                A('pool', lambda e, sub=sub: e.tensor_scalar(kg[:, sub, :], ktok[:, sub, :], eG[:, sub:sub + 1], None,
                                                             ALU.mult), reads=['ktok', 'eG'], writes=['kg'])
                A('pool', lambda e, sub=sub: e.tensor_scalar(kdec[:, sub, :], ktok[:, sub, :], ekd[:, sub:sub + 1],
                                                             None, ALU.mult), reads=['ktok', 'ekd'], writes=['kdec'])
            for sub in range(4):
                A('pe', lambda e, sub=sub: e.transpose(psb[0][:, sub * 64:(sub + 1) * 64],
                                                       cv[0:64, sub * 128:(sub + 1) * 128], ident[0:64, 0:64]),
                  reads=['cv', 'c_ident'], writes=[PB(0)])
            A('dve', lambda e: e.tensor_copy(vtok[:, :, :], psb[0][:, 0:256].rearrange("p (s c) -> p s c", c=64)),
              reads=[PB(0)], writes=['vtok'])
            if GSTOP == 7:
                sc.flush()
                return
            for sub in range(4):
                cs = slice(sub * 128, (sub + 1) * 128)
                A('pe', lambda e, cs=cs: e.matmul(psb[1][:, cs], kT_bf[:, cs], kT_bf[:, cs], start=True, stop=True),
                  reads=['kT_bf'], writes=[PB(1)])
                A('pe', lambda e, cs=cs: e.matmul(psb[2][:, cs], kT_bf[:, cs], qn_bf[:, cs], start=True, stop=True),
                  reads=['kT_bf', 'qn_bf'], writes=[PB(2)])
                A('dve', lambda e, cs=cs, sub=sub: e.scalar_tensor_tensor(
                    B_f[:, cs], psb[1][:, cs], negbt[:, sub:sub + 1], Gs[:, cs], ALU.mult, ALU.mult),
                    reads=[PB(1), 'negbt', 'Gs'], writes=['B_f'])
            A('dve', lambda e: e.tensor_tensor(aqk[:, :], psb[2][:, :], Gi[:, :], ALU.mult), reads=[PB(2), 'Gi'],
              writes=['aqk'])
            A('pool', lambda e: e.tensor_copy(Bb[0][:, :], B_f[:, :]), reads=['B_f'], writes=[('Bb', 0)])
            for sub in range(4):
                cs = slice(sub * 128, (sub + 1) * 128)
                A('pe', lambda e, cs=cs: e.transpose(psb[4][:, cs], B_f[:, cs], ident[:, :]),
                  reads=['B_f', 'c_ident'], writes=[PB(4)])
            A('dve', lambda e: e.tensor_copy(Pb_[0][:, :], psb[4][:, :]), reads=[PB(4)], writes=[('Pb', 0)])
            for sub in range(4):
                cs = slice(sub * 128, (sub + 1) * 128)
                A('pool', lambda e, cs=cs: e.tensor_tensor(S_f[:, cs], B_f[:, cs], ident[:, :], ALU.add),
                  reads=['B_f', 'c_ident'], writes=['S_f'])
            A('pool', lambda e: e.tensor_copy(S_b[:, :], S_f[:, :]), reads=['S_f'], writes=['S_b'])
            if GSTOP == 8:
                sc.flush()
                return
            for j in range(5):
                cur, nxt = j % 2, (j + 1) % 2
                for sub in range(4):
                    cs = slice(sub * 128, (sub + 1) * 128)
                    A('pe', lambda e, cs=cs, cur=cur: e.matmul(psb[5][:, cs], Pb_[cur][:, cs], Bb[cur][:, cs],
                                                               start=True, stop=True),
                      reads=[('Pb', cur), ('Bb', cur)], writes=[PB(5)])
                A('dve', lambda e, nxt=nxt: e.tensor_copy(Bb[nxt][:, :], psb[5][:, :]), reads=[PB(5)],
                  writes=[('Bb', nxt)])
                for sub in range(4):
                    cs = slice(sub * 128, (sub + 1) * 128)
                    A('pe', lambda e, cs=cs, cur=cur: e.matmul(psb[4][:, cs], Bb[cur][:, cs], Pb_[cur][:, cs],
                                                               start=True, stop=True),
                      reads=[('Pb', cur), ('Bb', cur)], writes=[PB(4)])
                A('dve', lambda e, nxt=nxt: e.tensor_copy(Pb_[nxt][:, :], psb[4][:, :]), reads=[PB(4)],
                  writes=[('Pb', nxt)])
                for sub in range(4):
                    cs = slice(sub * 128, (sub + 1) * 128)
                    A('pe', lambda e, cs=cs, nxt=nxt: e.matmul(psb[7][:, cs], Pb_[nxt][:, cs], S_b[:, cs],
                                                               start=True, stop=True),
                      reads=[('Pb', nxt), 'S_b'], writes=[PB(7)])
                A('dve', lambda e: e.tensor_tensor(S_f[:, :], S_f[:, :], psb[7][:, :], ALU.add),
                  reads=['S_f', PB(7)], writes=['S_f'])
                A('pool', lambda e: e.tensor_copy(S_b[:, :], S_f[:, :]), reads=['S_f'], writes=['S_b'])
            if GSTOP == 9:
                sc.flush()
                return
            for sub in range(4):
                cs = slice(sub * 128, (sub + 1) * 128)
                A('pe', lambda e, cs=cs, sub=sub: e.matmul(psb[0][:, sub * 64:(sub + 1) * 64], S_b[:, cs],
                                                           vtok[:, sub, :], start=True, stop=True),
                  reads=['S_b', 'vtok'], writes=[PB(0)])
                A('pe', lambda e, cs=cs, sub=sub: e.matmul(psb[1][:, cs], kg[:, sub, :], S_b[:, cs], start=True,
                                                           stop=True),
                  reads=['S_b', 'kg'], writes=[PB(1)])
                A('dve', lambda e, sub=sub: e.tensor_scalar(ybu[:, sub, :], psb[0][:, sub * 64:(sub + 1) * 64],
                                                            b_sb[:, sub:sub + 1], None, ALU.mult),
                  reads=[PB(0), 'b_sb'], writes=['ybu'])
            A('dve', lambda e: e.tensor_copy(ywT[:, :], psb[1][:, :]), reads=[PB(1)], writes=['ywT'])
            if GSTOP == 10:
                sc.flush()
                return
            for ch in range(8):
                sub, hf = ch // 2, ch % 2
                rs = slice(hf * 64, hf * 64 + 64)
                cs = slice(sub * 128, (sub + 1) * 128)
                cc = slice(ch * 64, (ch + 1) * 64)
                pv_ = psb[2 + (ch % 2)]
                pS_ = psb[4 + (ch % 2)]
                A('pe', lambda e, cs=cs, pv_=pv_: e.matmul(pv_[:, 0:64], ywT[:, cs], St_b[:, :], start=True,
                                                           stop=True),
                  reads=['ywT', 'St_b'], writes=[PB(2 + (ch % 2))])
                A('dve', lambda e, rs=rs, sub=sub, pv_=pv_: e.scalar_tensor_tensor(
                    vnew[rs, :], pv_[rs, 0:64], negbt[rs, sub:sub + 1], ybu[rs, sub, :], ALU.mult, ALU.add),
                    reads=[PB(2 + (ch % 2)), 'negbt', 'ybu'], writes=['vnew'])

                def mmo(e, cc=cc, rs=rs):
                    e.matmul(psb[6][0:64, cc], St_b[:, :], qdec[:, cc], start=True, stop=False)
                    return e.matmul(psb[6][0:64, cc], vnew[rs, :], aqk[rs, cc], start=False, stop=True)
                A('pe', mmo, reads=['St_b', 'qdec', 'vnew', 'aqk'], writes=[PB(6)])
                A('pe', lambda e, rs=rs, sub=sub, pS_=pS_: e.matmul(pS_[:, 0:64], kdec[rs, sub, :], vnew[rs, :],
                                                                    start=True, stop=True),
                  reads=['kdec', 'vnew'], writes=[PB(4 + (ch % 2))])
                dsc = (dl if hf == 0 else dh)
                A('dve', lambda e, sub=sub, pS_=pS_, dsc=dsc: e.scalar_tensor_tensor(
                    St_b[:, :], St_f[:, :], dsc[:, sub:sub + 1], pS_[:, 0:64], ALU.mult, ALU.add),
                    reads=['St_f', PB(4 + (ch % 2)), 'dl', 'dh'], writes=['St_b'])
                A('dve', lambda e, sub=sub, pS_=pS_, dsc=dsc: e.scalar_tensor_tensor(
                    St_f[:, :], St_f[:, :], dsc[:, sub:sub + 1], pS_[:, 0:64], ALU.mult, ALU.add),
                    reads=['St_f', PB(4 + (ch % 2)), 'dl', 'dh'], writes=['St_f'])
            A('act', lambda e: e.activation(o_sb[:, :], psb[6][0:64, :], AF.Copy), reads=[PB(6)], writes=['o_sb'])
            A('sp', lambda e, i=i: e.dma_start(out=obT[:, i * 512:(i + 1) * 512], in_=o_sb[:, :]),
              reads=['o_sb'], writes=[('obT', i)], slot='ob')
        sc.flush()


def build_M(S, do_fox=True, do_gdn=True, stop=99):
    nc = bass.Bass("TRN2", target_bir_lowering=False)
    hT = nc.dram_tensor("hT", [D, S], BF16, kind="ExternalInput").ap()
    wf = nc.dram_tensor("wf", [D, 193], F32, kind="ExternalInput").ap()
    bfg = nc.dram_tensor("bfg", [128, 1], F32, kind="ExternalInput").ap()
    wg = nc.dram_tensor("wg", [D, 322], F32, kind="ExternalInput").ap()
    cw = nc.dram_tensor("cw", [128, 12], F32, kind="ExternalInput").ap()
    gpar = nc.dram_tensor("gpar", [128, 2], F32, kind="ExternalInput").ap()
    oaT = nc.dram_tensor("oaT", [64, S], F32, kind="ExternalOutput").ap()
    obT = nc.dram_tensor("obT", [64, S], F32, kind="ExternalOutput").ap()
    scr = nc.dram_tensor("scr", [128, 128], BF16).ap()
    cx = Ctx(nc)
    with cx.st:
        sc = Sched(nc)
        c = make_consts(sc, cx)
        psb = [cx.ps([128, 512], F32) for _ in range(8)]
        if do_gdn:
            gdn_phase(nc, sc, c, S, hT, wg, cw, gpar, obT, psb)
        if do_fox:
            fox_phase(nc, sc, cx, c, S, hT, wf, bfg, oaT, scr, psb, stop=stop)
        sc.close()
    return nc


def build_T(TS, last):
    nc = bass.Bass("TRN2", target_bir_lowering=False)
    TT = 256
    NTT = TS // TT
    xT = nc.dram_tensor("xT", [D, TS], F32, kind="ExternalInput").ap()
    hT = nc.dram_tensor("hT", [D, TS], BF16, kind="ExternalInput").ap()
    oaT = nc.dram_tensor("oaT", [512, TS], F32, kind="ExternalInput").ap()
    obT = nc.dram_tensor("obT", [512, TS], F32, kind="ExternalInput").ap()
    w_in = nc.dram_tensor("w_in", [D, N_IN], F32, kind="ExternalInput").ap()
    w_br = nc.dram_tensor("w_br", [1536, D], F32, kind="ExternalInput").ap()
    w_out = nc.dram_tensor("w_out", [D, D], F32, kind="ExternalInput").ap()
    w_kv = nc.dram_tensor("w_kv", [D, 1024], F32, kind="ExternalInput").ap()
    memT = nc.dram_tensor("memT", [D, 256], F32, kind="ExternalInput").ap()
    mem_g = nc.dram_tensor("mem_g", [128, 8], F32, kind="ExternalInput").ap()
    b_mg = nc.dram_tensor("b_mg", [128, 24], F32, kind="ExternalInput").ap()
    gdn_g = nc.dram_tensor("gdn_g", [128, 1], F32, kind="ExternalInput").ap()
    next_g = nc.dram_tensor("next_g", [128, 8], F32, kind="ExternalInput").ap()
    xoT = nc.dram_tensor("xoT", [D, TS], F32, kind="ExternalOutput").ap()
    if not last:
        hoT = nc.dram_tensor("hoT", [D, TS], BF16, kind="ExternalOutput").ap()
    cx = Ctx(nc)
    with cx.st:
        sc = Sched(nc)
        ones_f = cx.sb([128, 128], F32)
        ones_bf = cx.sb([128, 128], BF16)
        sc.add('pool', lambda e: e.memset(ones_f[:, :], 1.0), writes=['ones_f'])
        sc.add('pool', lambda e: e.memset(ones_bf[:, :], 1.0), writes=['ones_bf'])
        psb = [cx.ps([128, 512], F32) for _ in range(8)]

        BM = {(0, 0): 0, (0, 1): 1, (1, 0): 2, (1, 1): 2, (2, 0): 3, (2, 1): 4, (3, 0): 5, (3, 1): 6,
              (4, 0): 2, (4, 1): 3, (5, 0): 4, (5, 1): 5, (6, 0): 6, (6, 1): 7, (7, 0): 0, (7, 1): 1}

        def half(bk, h):
            return psb[BM[(bk, h)]][:, 0:TT]

        def hk(bk, h):
            return ('pb', BM[(bk, h)])
        Wz = cx.sb([128, 8, 5120], BF16)
        Wbr = cx.sb([128, 12, 1024], BF16)
        Wout = cx.sb([128, 8, 1024], BF16)
        mkT = cx.sb([128, 4, 256], BF16)
        mv = cx.sb([128, 2, 512], BF16)
        stage = [cx.sb([128, 1024], F32) for _ in range(2)]
        memg_sb = cx.sb([128, 8], F32)
        bm_sb = cx.sb([128, 24], F32)
        gg_sb = cx.sb([128, 1], F32)
        ng_sb = cx.sb([128, 8], F32)
        for i, (dst, srcap) in enumerate([(memg_sb, mem_g), (bm_sb, b_mg), (gg_sb, gdn_g), (ng_sb, next_g)]):
            sc.add('sp', lambda e, dst=dst, srcap=srcap: e.dma_start(out=dst[:, :], in_=srcap[:, :]),
                   writes=[('par', i)], slot=('par', i))
        nst = [0]

        def ldw(dst, dkey, srcap):
            b = nst[0] % 2
            nst[0] += 1
            wd = srcap.shape[-1]
            sc.add('sp', lambda e: e.dma_start(out=stage[b][:, 0:wd], in_=srcap), writes=[('stage', b)],
                   slot=('stage', b))
            sc.add('pool' if b else 'dve', lambda e: e.tensor_copy(dst, stage[b][:, 0:wd]),
                   reads=[('stage', b)], writes=[dkey])

        pcx = Ctx(nc)
        with pcx.st:
            Wkv = pcx.sb([128, 8, 1024], BF16)
            mt = pcx.sb([128, 8, 256], F32)
            mn = pcx.sb([128, 8, 256], BF16)
            sqm = pcx.sb([128, 8, 256], F32)
            rstm = pcx.sb([128, 256], F32)
            for k in range(8):
                ldw(Wkv[:, k, :], 'Wkv', w_kv[k * 128:(k + 1) * 128, :])
            sc.add('sp', lambda e: e.dma_start(out=mt[:, :, :], in_=memT.rearrange("(k p) m -> p k m", p=128)),
                   writes=['mt'], slot='mt')
            sc.ops[-1]
            saved = {'g': None}
            emit_rmsnorm(sc, mt, 'mt', memg_sb, ones_f, mn, 'mn', 256, sqm, psb[0], rstm, 'mnorm', gkey=('par', 0))
            for hh in range(4):
                def mmk(e, hh=hh):
                    r = None
                    for k in range(8):
                        r = e.matmul(half(1, hh % 2), Wkv[:, k, hh * 128:(hh + 1) * 128], mn[:, k, :],
                                     start=(k == 0), stop=(k == 7))
                    return r
                sc.add('pe', mmk, reads=['Wkv', 'mn'], writes=[hk(1, hh % 2)])
                sc.add('dve', lambda e, hh=hh: e.tensor_copy(mkT[:, hh, :], half(1, hh % 2)),
                       reads=[hk(1, hh % 2)], writes=['mkT'])
            for mc in range(2):
                def mmv(e, mc=mc):
                    r = None
                    for k in range(8):
                        r = e.matmul(psb[2 + mc][:, :], mn[:, k, mc * 128:(mc + 1) * 128], Wkv[:, k, 512:1024],
                                     start=(k == 0), stop=(k == 7))
                    return r
                sc.add('pe', mmv, reads=['Wkv', 'mn'], writes=[('pb', 2 + mc)])
                sc.add('dve', lambda e, mc=mc: e.tensor_copy(mv[:, mc, :], psb[2 + mc][:, :]),
                       reads=[('pb', 2 + mc)], writes=['mv'])
            sc.flush()

        for k in range(8):
            ldw(Wz[:, k, 0:512], 'Wz', w_in[k * 128:(k + 1) * 128, OFF['az']:OFF['az'] + 512])
            ldw(Wz[:, k, 512:1024], 'Wz', w_in[k * 128:(k + 1) * 128, OFF['bz']:OFF['bz'] + 512])
            for cb in range(4):
                ldw(Wz[:, k, 1024 + cb * 1024:2048 + cb * 1024], 'Wz',
                    w_in[k * 128:(k + 1) * 128, OFF['mq'] + cb * 1024:OFF['mq'] + (cb + 1) * 1024])
        for k in range(12):
            ldw(Wbr[:, k, :], 'Wbr', w_br[k * 128:(k + 1) * 128, :])
        for k in range(8):
            ldw(Wout[:, k, :], 'Wout', w_out[k * 128:(k + 1) * 128, :])

        ht = [cx.sb([128, 8, TT], BF16) for _ in range(2)]
        xt = cx.sb([128, 8, TT], F32)
        oat = cx.sb([128, 4, TT], F32)
        obt = cx.sb([128, 4, TT], F32)
        yT = cx.sb([128, 12, TT], BF16)
        mg = cx.sb([128, 8, TT], BF16)
        hout = cx.sb([128, 8, TT], BF16 if not last else F32)
        sqs = [cx.sb([128, TT], F32) for _ in range(2)]
        sil = [cx.sb([128, TT], F32) for _ in range(2)]
        tmp = [cx.sb([128, TT], F32) for _ in range(2)]
        rstd = cx.sb([128, TT], F32)
        rden = cx.sb([128, TT], F32)
        mqs = cx.sb([128, TT], BF16)
        pT = [cx.sb([128, TT], BF16) for _ in range(2)]
        gs = [cx.sb([128, TT], F32) for _ in range(3)]
        acc = [cx.sb([128, TT], F32) for _ in range(2)]
        hv = hT.rearrange("(k p) t -> p k t", p=128)
        xv = xT.rearrange("(k p) t -> p k t", p=128)
        oav = oaT.rearrange("(k p) t -> p k t", p=128)
        obv = obT.rearrange("(k p) t -> p k t", p=128)
        xov = xoT.rearrange("(k p) t -> p k t", p=128)
        if not last:
            hov = hoT.rearrange("(k p) t -> p k t", p=128)

        zcnt = [0]

        def zproj(col0, b):
            s = zcnt[0] % 2
            zcnt[0] += 1
            dst = half(0, s)

            def mm(e):
                r = None
                for k in range(8):
                    r = e.matmul(dst, Wz[:, k, col0:col0 + 128], ht[b][:, k, :], start=(k == 0), stop=(k == 7))
                return r
            sc.add('pe', mm, reads=['Wz', ('ht', b)], writes=[hk(0, s)])
            return dst, hk(0, s)

        for it in range(NTT):
            b = it % 2
            t0, t1 = it * TT, (it + 1) * TT
            sc.add('sp', lambda e, b=b, t0=t0, t1=t1: e.dma_start(out=ht[b][:, :, :], in_=hv[:, :, t0:t1]),
                   writes=[('ht', b)], slot=('ht', b))
            sc.add('sp', lambda e, t0=t0, t1=t1: e.dma_start(out=xt[:, :, :], in_=xv[:, :, t0:t1]),
                   writes=['xt'] + [('xn', dc) for dc in range(8)], slot='xt')
            sc.add('sp', lambda e, t0=t0, t1=t1: e.dma_start(out=oat[:, :, :], in_=oav[:, :, t0:t1]),
                   writes=['oat'], slot='oat')
            sc.add('sp', lambda e, t0=t0, t1=t1: e.dma_start(out=obt[:, :, :], in_=obv[:, :, t0:t1]),
                   writes=['obt'], slot='obt')
            for fc in range(4):
                zp, zk = zproj(fc * 128, b)
                s = fc % 2
                sc.add('act', lambda e, zp=zp, s=s: e.activation(sil[s][:, :], zp, AF.Silu),
                       reads=[zk], writes=[('sil', s)])
                sc.add('pool', lambda e, fc=fc, s=s: e.tensor_tensor(yT[:, fc, :], oat[:, fc, :], sil[s][:, :],
                                                                    ALU.mult),
                       reads=['oat', ('sil', s)], writes=[('yT', fc)])
            for hd in range(4):
                s = hd % 2
                sc.add('act', lambda e, hd=hd, s=s: e.activation(sqs[s][:, :], obt[:, hd, :], AF.Square),
                       reads=['obt'], writes=[('sqs', s)])
                sc.add('pe', lambda e, s=s: e.matmul(half(1, 0), ones_f[:, :], sqs[s][:, :], start=True, stop=True),
                       reads=[('sqs', s), 'ones_f'], writes=[hk(1, 0)])
                sc.add('act', lambda e: e.activation(rstd[:, :], half(1, 0), AF.Sqrt, bias=EPS, scale=1.0 / 128),
                       reads=[hk(1, 0)], writes=['rstd'])
                sc.add('dve', lambda e: e.reciprocal(rstd[:, :], rstd[:, :]), reads=['rstd'], writes=['rstd'])
                sc.add('dve', lambda e, hd=hd, s=s: e.scalar_tensor_tensor(tmp[s][:, :], obt[:, hd, :],
                                                                          gg_sb[:, 0:1], rstd[:, :],
                                                                          ALU.mult, ALU.mult),
                       reads=['obt', 'rstd', ('par', 2)], writes=[('tmp', s)])
                zp, zk = zproj(512 + hd * 128, b)
                sc.add('act', lambda e, zp=zp, s=s: e.activation(sil[s][:, :], zp, AF.Silu),
                       reads=[zk], writes=[('sil', s)])
                sc.add('pool', lambda e, hd=hd, s=s: e.tensor_tensor(yT[:, 4 + hd, :], tmp[s][:, :], sil[s][:, :],
                                                                    ALU.mult),
                       reads=[('tmp', s), ('sil', s)], writes=[('yT', 4 + hd)])
            for hh in range(4):
                s = hh % 2
                zp, zk = zproj(1024 + hh * 128, b)
                sc.add('dve', lambda e, zp=zp: e.tensor_copy(mqs[:, :], zp), reads=[zk], writes=['mqs'])
                for mc in range(2):
                    sc.add('pe', lambda e, hh=hh, mc=mc: e.matmul(half(2, mc), mkT[:, hh, mc * 128:(mc + 1) * 128],
                                                                 mqs[:, :], start=True, stop=True),
                           reads=['mkT', 'mqs'], writes=[hk(2, mc)])
                    sc.add('act', lambda e, mc=mc: e.activation(pT[mc][:, :], half(2, mc), AF.Exp,
                                                                scale=128.0 ** -0.5),
                           reads=[hk(2, mc)], writes=[('pT', mc)])

                def mmn(e, hh=hh):
                    e.matmul(half(3, 0), mv[:, 0, hh * 128:(hh + 1) * 128], pT[0][:, :], start=True, stop=False)
                    return e.matmul(half(3, 0), mv[:, 1, hh * 128:(hh + 1) * 128], pT[1][:, :], start=False,
                                    stop=True)
                sc.add('pe', mmn, reads=['mv', ('pT', 0), ('pT', 1)], writes=[hk(3, 0)])

                def mmd(e):
                    e.matmul(half(3, 1), ones_bf[:, :], pT[0][:, :], start=True, stop=False)
                    return e.matmul(half(3, 1), ones_bf[:, :], pT[1][:, :], start=False, stop=True)
                sc.add('pe', mmd, reads=['ones_bf', ('pT', 0), ('pT', 1)], writes=[hk(3, 1)])
                sc.add('dve', lambda e: e.reciprocal(rden[:, :], half(3, 1)), reads=[hk(3, 1)], writes=['rden'])
                sc.add('dve', lambda e, s=s: e.tensor_tensor(tmp[s][:, :], half(3, 0), rden[:, :], ALU.mult),
                       reads=[hk(3, 0), 'rden'], writes=[('tmp', s)])
                zp, zk = zproj(1536 + hh * 128, b)
                sc.add('act', lambda e, zp=zp, s=s: e.activation(sil[s][:, :], zp, AF.Silu),
                       reads=[zk], writes=[('sil', s)])
                sc.add('pool', lambda e, hh=hh, s=s: e.tensor_tensor(yT[:, 8 + hh, :], tmp[s][:, :], sil[s][:, :],
                                                                    ALU.mult),
                       reads=[('tmp', s), ('sil', s)], writes=[('yT', 8 + hh)])
            for dc in range(8):
                for n in range(3):
                    pslot = [(4, 0), (4, 1), (5, 0)][n]
                    gslot = [(5, 1), (6, 0), (6, 1)][n]

                    def mmp(e, n=n, dc=dc, pslot=pslot):
                        r = None
                        for kc in range(4):
                            r = e.matmul(half(*pslot), Wbr[:, n * 4 + kc, dc * 128:(dc + 1) * 128],
                                         yT[:, n * 4 + kc, :], start=(kc == 0), stop=(kc == 3))
                        return r
                    sc.add('pe', mmp, reads=['Wbr'] + [('yT', n * 4 + kc) for kc in range(4)],
                           writes=[hk(*pslot)])

                    def mmg(e, n=n, dc=dc, gslot=gslot, b=b):
                        r = None
                        for k in range(8):
                            c0 = 2048 + n * 1024 + dc * 128
                            r = e.matmul(half(*gslot), Wz[:, k, c0:c0 + 128], ht[b][:, k, :], start=(k == 0),
                                         stop=(k == 7))
                        return r
                    sc.add('pe', mmg, reads=['Wz', ('ht', b)], writes=[hk(*gslot)])
                    sc.add('act', lambda e, n=n, dc=dc, gslot=gslot: e.activation(
                        gs[n][:, :], half(*gslot), AF.Sigmoid, bias=bm_sb[:, n * 8 + dc:n * 8 + dc + 1], scale=1.0),
                        reads=[hk(*gslot), ('par', 1)], writes=[('gs', n)])
                sc.add('dve', lambda e: e.tensor_tensor(acc[0][:, :], half(4, 0), gs[0][:, :], ALU.mult),
                       reads=[hk(4, 0), ('gs', 0)], writes=[('acc', 0)])
                sc.add('dve', lambda e: e.tensor_tensor(acc[1][:, :], half(4, 1), gs[1][:, :], ALU.mult),
                       reads=[hk(4, 1), ('gs', 1)], writes=[('acc', 1)])
                sc.add('pool', lambda e: e.tensor_tensor(acc[0][:, :], acc[0][:, :], acc[1][:, :], ALU.add),
                       reads=[('acc', 0), ('acc', 1)], writes=[('acc', 0)])
                sc.add('dve', lambda e: e.tensor_tensor(acc[1][:, :], half(5, 0), gs[2][:, :], ALU.mult),
                       reads=[hk(5, 0), ('gs', 2)], writes=[('acc', 1)])
                sc.add('pool', lambda e, dc=dc: e.tensor_tensor(mg[:, dc, :], acc[0][:, :], acc[1][:, :], ALU.add),
                       reads=[('acc', 0), ('acc', 1)], writes=[('mg', dc)])
            for dc in range(8):
                s = dc % 2

                def mmo(e, dc=dc, s=s):
                    r = None
                    for k in range(8):
                        r = e.matmul(half(7, s), Wout[:, k, dc * 128:(dc + 1) * 128], mg[:, k, :], start=(k == 0),
                                     stop=(k == 7))
                    return r
                sc.add('pe', mmo, reads=['Wout'] + [('mg', k) for k in range(8)], writes=[hk(7, s)])
                sc.add('dve', lambda e, dc=dc, s=s: e.tensor_tensor(xt[:, dc, :], xt[:, dc, :], half(7, s), ALU.add),
                       reads=['xt', hk(7, s)], writes=[('xn', dc)])
            allxn = [('xn', dc) for dc in range(8)]
            if not last:
                sc.add('sp', lambda e, t0=t0, t1=t1: e.dma_start(out=xov[:, :, t0:t1], in_=xt[:, :, :]),
                       reads=allxn, writes=[('xo', it)], slot='xo')
            for k in range(8):
                s = k % 2
                sc.add('act', lambda e, k=k, s=s: e.activation(sqs[s][:, :], xt[:, k, :], AF.Square),
                       reads=[('xn', k)], writes=[('sqs', s)])
                sc.add('pe', lambda e, k=k, s=s: e.matmul(half(1, 1), ones_f[:, :], sqs[s][:, :], start=(k == 0),
                                                         stop=(k == 7)),
                       reads=[('sqs', s), 'ones_f'], writes=[hk(1, 1)])
            sc.add('act', lambda e: e.activation(rstd[:, :], half(1, 1), AF.Sqrt, bias=EPS, scale=1.0 / D),
                   reads=[hk(1, 1)], writes=['rstd'])
            sc.add('dve', lambda e: e.reciprocal(rstd[:, :], rstd[:, :]), reads=['rstd'], writes=['rstd'])
            for k in range(8):
                sc.add('dve', lambda e, k=k: e.scalar_tensor_tensor(hout[:, k, :], xt[:, k, :], ng_sb[:, k:k + 1],
                                                                   rstd[:, :], ALU.mult, ALU.mult),
                       reads=[('xn', k), 'rstd', ('par', 3)], writes=['hout'])
            if last:
                sc.add('sp', lambda e, t0=t0, t1=t1: e.dma_start(out=xov[:, :, t0:t1], in_=hout[:, :, :]),
                       reads=['hout'], writes=[('xo', it)], slot='xo')
            else:
                sc.add('sp', lambda e, t0=t0, t1=t1: e.dma_start(out=hov[:, :, t0:t1], in_=hout[:, :, :]),
                       reads=['hout'], writes=[('ho', it)], slot='ho')
        sc.close()
    return nc


def mixer_inputs(c, hT, w_in_l, b_fg_l, conv_w_l, a_log_l, dt_bias_l):
    hd, half = c // 2, c % 2
    wf = np.concatenate([w_in_l[:, OFF['aq'] + c * 64:OFF['aq'] + (c + 1) * 64],
                         w_in_l[:, OFF['ak'] + c * 64:OFF['ak'] + (c + 1) * 64],
                         w_in_l[:, OFF['av'] + c * 64:OFF['av'] + (c + 1) * 64],
                         w_in_l[:, OFF['af'] + c:OFF['af'] + c + 1]], axis=1)
    vo = hd * 128 + half * 64
    wg = np.concatenate([w_in_l[:, OFF['bq'] + hd * 128:OFF['bq'] + (hd + 1) * 128],
                         w_in_l[:, OFF['bk'] + hd * 128:OFF['bk'] + (hd + 1) * 128],
                         w_in_l[:, OFF['bv'] + vo:OFF['bv'] + vo + 64],
                         w_in_l[:, OFF['ba'] + hd:OFF['ba'] + hd + 1],
                         w_in_l[:, OFF['bb'] + hd:OFF['bb'] + hd + 1]], axis=1)
    cw = np.zeros((128, 12), np.float32)
    cw[:, 0:4] = conv_w_l[:, hd * 128:(hd + 1) * 128].T
    cw[:, 4:8] = conv_w_l[:, 512 + hd * 128:512 + (hd + 1) * 128].T
    cw[0:64, 8:12] = conv_w_l[:, 1024 + vo:1024 + vo + 64].T
    gpar = np.empty((128, 2), np.float32)
    gpar[:, 0] = a_log_l[hd]
    gpar[:, 1] = dt_bias_l[hd]
    return dict(hT=hT, wf=np.ascontiguousarray(wf), bfg=np.full((128, 1), b_fg_l[c], np.float32),
                wg=np.ascontiguousarray(wg), cw=cw, gpar=gpar)


def _lay8(v):
    return np.ascontiguousarray(np.asarray(v, np.float32).reshape(-1, 128).T)


_PROGS = {}


def _prog(name, fn):
    if name not in _PROGS:
        _PROGS[name] = fn()
    return _PROGS[name]


def kernel(x, mem, norm_g, w_in, b_fg, b_merge, conv_w, a_log, dt_bias, gdn_norm_g, mem_norm_g, w_mem_kv,
           w_branch, w_out, final_norm_g):
    f = lambda a: np.asarray(a, np.float32)
    x, mem, norm_g, w_in, b_fg, b_merge, conv_w = map(f, (x, mem, norm_g, w_in, b_fg, b_merge, conv_w))
    a_log, dt_bias, gdn_norm_g, mem_norm_g = map(f, (a_log, dt_bias, gdn_norm_g, mem_norm_g))
    w_mem_kv, w_branch, w_out, final_norm_g = map(f, (w_mem_kv, w_branch, w_out, final_norm_g))
    S = x.shape[1]
    TS = S // NCORES
    cores = list(range(NCORES))
    xT = np.ascontiguousarray(x[0].T)
    memT = np.ascontiguousarray(mem[0].T)
    sh = lambda a, c: np.ascontiguousarray(a[:, c * TS:(c + 1) * TS])
    ncP = _prog('P', lambda: build_P(TS))
    res = run_bass_kernel_spmd(ncP, [dict(xT=sh(xT, c), g=_lay8(norm_g[0])) for c in cores], core_ids=cores)
    hT = np.concatenate([np.asarray(r["hT"]) for r in res.results], axis=1)
    depth = w_in.shape[0]
    for l in range(depth):
        last = (l == depth - 1)
        ncM = _prog('M', lambda: build_M(S))
        hTc = np.ascontiguousarray(hT)
        res = run_bass_kernel_spmd(
            ncM, [mixer_inputs(c, hTc, w_in[l], b_fg[l], conv_w[l], a_log[l], dt_bias[l]) for c in cores],
            core_ids=cores)
        oaT = np.concatenate([np.asarray(r["oaT"]) for r in res.results], axis=0)
        obT = np.concatenate([np.asarray(r["obT"]) for r in res.results], axis=0)
        ncT = _prog('T%d' % int(last), lambda: build_T(TS, last))
        ng = final_norm_g if last else norm_g[l + 1]
        maps = []
        for c in cores:
            maps.append(dict(xT=sh(xT, c), hT=sh(hT, c), oaT=sh(oaT, c), obT=sh(obT, c),
                             w_in=np.ascontiguousarray(w_in[l]),
                             w_br=np.ascontiguousarray(w_branch[l].reshape(1536, D)),
                             w_out=np.ascontiguousarray(w_out[l]), w_kv=np.ascontiguousarray(w_mem_kv[l]),
                             memT=memT, mem_g=_lay8(mem_norm_g[l]), b_mg=_lay8(b_merge[l]),
                             gdn_g=np.ascontiguousarray(gdn_norm_g[l].reshape(128, 1)), next_g=_lay8(ng)))
        res = run_bass_kernel_spmd(ncT, maps, core_ids=cores)
        xT = np.concatenate([np.asarray(r["xoT"]) for r in res.results], axis=1)
        if not last:
            hT = np.concatenate([np.asarray(r["hoT"]) for r in res.results], axis=1)
    out = np.ascontiguousarray(xT.T).reshape(1, S, D).astype(np.float32)
    return out
```

```python
import contextlib
import numpy as np
import ml_dtypes
import concourse.bass as bass
import concourse.mybir as mybir
from concourse.bass_utils import run_bass_kernel_spmd

F32 = mybir.dt.float32
BF16 = mybir.dt.bfloat16
AF = mybir.ActivationFunctionType
ALU = mybir.AluOpType

D = 1024
S_FULL = 16384
NCORES = 8
EPS = 1e-6
N_IN = 8208
import os as _os
SAME_ENGINE_SYNC = bool(int(_os.environ.get('SAME_SYNC', '1')))
OFF = dict(aq=0, ak=512, av=1024, af=1536, az=1544, bq=2056, bk=2568, bv=3080,
           ba=3592, bb=3596, bz=3600, mq=4112, mz=4624, gates=5136)


def _is_psum_key(k):
    if isinstance(k, str):
        return k.startswith('ps')
    if isinstance(k, tuple) and len(k) >= 2:
        return k[0] in ('pb', 'pS', 'pO') or k[1] == 'ps'
    return False


class Sched:
    ENGS = ['pe', 'act', 'dve', 'pool', 'sp']

    def __init__(self, nc, same_engine_sync=None):
        if same_engine_sync is None:
            same_engine_sync = SAME_ENGINE_SYNC
        self.nc = nc
        self.ops = []
        self.lastw = {}
        self.readers = {}
        self.slot_count = {}
        self.same = same_engine_sync
        self.stack = contextlib.ExitStack()
        self.esem = {e: self.stack.enter_context(nc.semaphore("sem_" + e)) for e in self.ENGS}
        self.ssem = {}
        self.cnt = {e: 0 for e in self.ENGS}

    def _needs_same(self, eng):
        if eng == 'pe':
            return False
        if eng == 'pool':
            return True
        return self.same

    def add(self, eng, fn, reads=(), writes=(), slot=None):
        op = dict(eng=eng, fn=fn, deps=[], slot=slot, inc=False, id=len(self.ops))
        deps = {}
        for k in reads:
            w = self.lastw.get(k)
            if w is not None:
                deps[w['id']] = w
            if _is_psum_key(k):
                for r in self.readers.get(k, ()):
                    if r['eng'] != eng:
                        deps[r['id']] = r
        for k in writes:
            w = self.lastw.get(k)
            if w is not None:
                deps[w['id']] = w
            for r in self.readers.get(k, ()):
                deps[r['id']] = r
        for d in deps.values():
            if d is op:
                continue
            op['deps'].append(d)
            if d['slot'] is None:
                if d['eng'] != eng or self._needs_same(eng) or slot is not None:
                    d['inc'] = True
        for k in writes:
            self.lastw[k] = op
            self.readers[k] = []
        for k in reads:
            self.readers.setdefault(k, []).append(op)
        if slot is not None:
            if slot not in self.ssem:
                self.ssem[slot] = self.stack.enter_context(self.nc.semaphore("sl_%d" % len(self.ssem)))
            self.slot_count[slot] = self.slot_count.get(slot, 0) + 1
            op['slot_val'] = self.slot_count[slot] * 16
        self.ops.append(op)
        return op

    def flush(self):
        nc = self.nc
        for op in self.ops:
            if op['slot'] is None and op['inc']:
                self.cnt[op['eng']] += 1
                op['count'] = self.cnt[op['eng']]
        ops = self.ops
        esem, ssem = self.esem, self.ssem
        final = dict(self.slot_count)
        with nc.Block() as block:
            def run(ename, eng):
                known = {}
                for op in ops:
                    if op['eng'] != ename:
                        continue
                    waits = {}
                    for d in op['deps']:
                        if d['slot'] is not None:
                            key = ('s', d['slot'])
                            v = d['slot_val']
                            sem = ssem[d['slot']]
                        else:
                            if d['eng'] == ename and op['slot'] is None and not self._needs_same(ename):
                                continue
                            key = ('e', d['eng'])
                            v = d['count']
                            sem = esem[d['eng']]
                        if waits.get(key, (None, -1))[1] < v:
                            waits[key] = (sem, v)
                    for key, (sem, v) in waits.items():
                        if known.get(key, -1) >= v:
                            continue
                        known[key] = v
                        eng.wait_ge(sem, v)
                    ins = op['fn'](eng)
                    if op['slot'] is not None:
                        ins.then_inc(ssem[op['slot']], 16)
                    elif op['inc']:
                        ins.then_inc(esem[ename], 1)
                if ename == 'sp':
                    for s, n in final.items():
                        eng.wait_ge(ssem[s], n * 16)

            block.tensor(lambda e: run('pe', e))
            block.scalar(lambda e: run('act', e))
            block.vector(lambda e: run('dve', e))
            block.gpsimd(lambda e: run('pool', e))
            block.sync(lambda e: run('sp', e))
        self.ops = []
        self.lastw = {}
        self.readers = {}

    def collective(self, kind, op, src_ap, dst_ap, reads=(), writes=(), slot='cc', ncores=NCORES):
        self.flush()
        if slot not in self.ssem:
            self.ssem[slot] = self.stack.enter_context(self.nc.semaphore("sl_%d" % len(self.ssem)))
        self.slot_count[slot] = self.slot_count.get(slot, 0) + 1
        ins = self.nc.gpsimd.collective_compute(kind, op, replica_groups=[list(range(ncores))],
                                                ins=[src_ap], outs=[dst_ap])
        ins.then_inc(self.ssem[slot], 16)
        pseudo = dict(eng='pool', fn=None, deps=[], slot=slot, inc=False, id=-1,
                      slot_val=self.slot_count[slot] * 16)
        for k in writes:
            self.lastw[k] = pseudo
            self.readers[k] = []

    def close(self):
        self.flush()
        self.stack.close()


_NAME = [0]


class Ctx:
    def __init__(self, nc):
        self.nc = nc
        self.st = contextlib.ExitStack()

    def sb(self, shape, dt, name=None):
        _NAME[0] += 1
        return self.st.enter_context(self.nc.sbuf_tensor(name or ("t%d" % _NAME[0]), list(shape), dt))

    def ps(self, shape, dt, name=None):
        _NAME[0] += 1
        return self.st.enter_context(self.nc.psum_tensor(name or ("p%d" % _NAME[0]), list(shape), dt))


def emit_rmsnorm(sc, x_sb, xkey, g_sb, ones_f, out_sb, outkey, TT, sq, ps, rstd, tag, dim=D, gkey='g'):
    for k in range(8):
        sc.add('act', lambda e, k=k: e.activation(sq[:, k, :], x_sb[:, k, :], AF.Square),
               reads=[xkey], writes=[(tag, 'sq', k)])

    def mm(e):
        r = None
        for k in range(8):
            r = e.matmul(ps[:, 0:TT], ones_f[:, :], sq[:, k, :], start=(k == 0), stop=(k == 7))
        return r
    sc.add('pe', mm, reads=[(tag, 'sq', k) for k in range(8)] + ['ones_f'], writes=[(tag, 'ps')])
    sc.add('act', lambda e: e.activation(rstd[:, :], ps[:, 0:TT], AF.Sqrt, bias=EPS, scale=1.0 / dim),
           reads=[(tag, 'ps')], writes=[(tag, 'rstd')])
    sc.add('dve', lambda e: e.reciprocal(rstd[:, :], rstd[:, :]),
           reads=[(tag, 'rstd')], writes=[(tag, 'rstd')])
    for k in range(8):
        sc.add('dve',
               lambda e, k=k: e.scalar_tensor_tensor(out_sb[:, k, :], x_sb[:, k, :], g_sb[:, k:k + 1],
                                                     rstd[:, :], ALU.mult, ALU.mult),
               reads=[xkey, (tag, 'rstd'), gkey], writes=[outkey])


def build_P(TS):
    nc = bass.Bass("TRN2", target_bir_lowering=False)
    xT = nc.dram_tensor("xT", [D, TS], F32, kind="ExternalInput").ap()
    g = nc.dram_tensor("g", [128, 8], F32, kind="ExternalInput").ap()
    hT = nc.dram_tensor("hT", [D, TS], BF16, kind="ExternalOutput").ap()
    TT = 512
    cx = Ctx(nc)
    with cx.st:
        sc = Sched(nc)
        ones_f = cx.sb([128, 128], F32)
        g_sb = cx.sb([128, 8], F32)
        xs = [cx.sb([128, 8, TT], F32) for _ in range(2)]
        hs = [cx.sb([128, 8, TT], BF16) for _ in range(2)]
        sq = cx.sb([128, 8, TT], F32)
        rstd = cx.sb([128, TT], F32)
        ps = cx.ps([128, 512], F32)
        sc.add('pool', lambda e: e.memset(ones_f[:, :], 1.0), writes=['ones_f'])
        sc.add('sp', lambda e: e.dma_start(out=g_sb[:, :], in_=g[:, :]), writes=['g'], slot='g')
        xv = xT.rearrange("(k p) t -> p k t", p=128)
        hv = hT.rearrange("(k p) t -> p k t", p=128)
        for i in range(TS // TT):
            b = i % 2
            sc.add('sp', lambda e, i=i, b=b: e.dma_start(out=xs[b][:, :, :], in_=xv[:, :, i * TT:(i + 1) * TT]),
                   writes=[('x', b)], slot=('x', b))
            emit_rmsnorm(sc, xs[b], ('x', b), g_sb, ones_f, hs[b], ('h', b), TT, sq, ps, rstd, 'n')
            sc.add('sp', lambda e, i=i, b=b: e.dma_start(out=hv[:, :, i * TT:(i + 1) * TT], in_=hs[b][:, :, :]),
                   reads=[('h', b)], writes=[('hout', i)], slot=('ho', b))
        sc.close()
    return nc


def make_consts(sc, cx):
    c = {}
    c['ones'] = cx.sb([128, 128], F32)
    c['ident'] = cx.sb([128, 128], F32)
    c['uincl'] = cx.sb([128, 128], F32)
    c['ustrict'] = cx.sb([128, 128], F32)
    c['e0'] = cx.sb([128, 128], F32)
    c['ones_bf'] = cx.sb([128, 128], BF16)
    c['ident_bf'] = cx.sb([128, 128], BF16)
    c['zeros'] = cx.sb([128, 128], F32)
    sc.add('pool', lambda e: e.memset(c['ones'][:, :], 1.0), writes=['c_ones'])
    sc.add('pool', lambda e: e.memset(c['zeros'][:, :], 0.0), writes=['c_zeros'])
    sc.add('pool', lambda e: e.memset(c['ones_bf'][:, :], 1.0), writes=['c_ones_bf'])
    sc.add('pool', lambda e: e.affine_select(c['ident'][:, :], c['zeros'][:, :], [[1, 128]], ALU.not_equal, 1.0,
                                             base=0, channel_multiplier=-1),
           reads=['c_zeros'], writes=['c_ident'])
    sc.add('pool', lambda e: e.tensor_copy(c['ident_bf'][:, :], c['ident'][:, :]),
           reads=['c_ident'], writes=['c_ident_bf'])
    sc.add('pool', lambda e: e.affine_select(c['uincl'][:, :], c['ones'][:, :], [[1, 128]], ALU.is_ge, 0.0,
                                             base=0, channel_multiplier=-1),
           reads=['c_ones'], writes=['c_uincl'])
    sc.add('pool', lambda e: e.affine_select(c['ustrict'][:, :], c['ones'][:, :], [[1, 128]], ALU.is_gt, 0.0,
                                             base=0, channel_multiplier=-1),
           reads=['c_ones'], writes=['c_ustrict'])
    sc.add('pool', lambda e: e.affine_select(c['e0'][:, :], c['ones'][:, :], [[0, 128]], ALU.is_ge, 0.0,
                                             base=0, channel_multiplier=-1),
           reads=['c_ones'], writes=['c_e0'])
    return c


def load_cast(sc, dst_bf, dstkey, src_ap, stage, stagekey, eng_dma='sp', eng_cast='pool', slot=None):
    sc.add(eng_dma, lambda e: e.dma_start(out=stage, in_=src_ap), writes=[stagekey], slot=slot or stagekey)
    sc.add(eng_cast, lambda e: e.tensor_copy(dst_bf, stage), reads=[stagekey], writes=[dstkey])


def fox_phase(nc, sc, cx0, c, S, hT, wf, bfg, oaT, scr, psb, stop=99):
    NT = S // 128
    NG = S // 512
    cx = Ctx(nc)
    with cx.st:
        wq = cx.sb([128, 8, 64], BF16)
        wk = cx.sb([128, 8, 64], BF16)
        wv = cx.sb([128, 8, 65], BF16)
        wst = cx.sb([128, 8, 193], F32)
        QT = cx.sb([65, S], BF16)
        KT = cx.sb([65, S], BF16)
        V = cx.sb([128, NT, 65], BF16)
        lfr = cx.sb([128, NT], F32)
        lfn = cx.sb([128, NT], F32)
        Fn = cx.sb([128, NT], F32)
        frefB = cx.sb([128, NG], F32)
        ctok = cx.sb([128, NT], F32)
        cTT = cx.sb([128, 128], BF16)
        totT = cx.sb([128, 1], F32)
        X = cx.sb([128, 128], F32)
        negb = cx.sb([128, 1], F32)
        biasg = [cx.sb([128, NT], F32) for _ in range(2)]
        ht = [cx.sb([128, 8, 512], BF16) for _ in range(2)]
        Pb = [cx.sb([128, 512], BF16) for _ in range(4)]
        oun = cx.sb([65, 512], F32)
        rl = cx.sb([65, 512], F32)
        ofin = [cx.sb([64, 512], F32) for _ in range(2)]

        sc.add('sp', lambda e: e.dma_start(out=wst[:, :, :], in_=wf.rearrange("(k p) c -> p k c", p=128)),
               writes=['wst'], slot='wst')
        sc.add('pool', lambda e: e.tensor_copy(wq[:, :, :], wst[:, :, 0:64]), reads=['wst'], writes=['wq'])
        sc.add('pool', lambda e: e.tensor_copy(wk[:, :, :], wst[:, :, 64:128]), reads=['wst'], writes=['wk'])
        sc.add('pool', lambda e: e.tensor_copy(wv[:, :, :], wst[:, :, 128:193]), reads=['wst'], writes=['wv'])
        sc.add('sp', lambda e: e.dma_start(out=negb[:, :], in_=bfg[:, :]), writes=['negb'], slot='negb')
        sc.add('dve', lambda e: e.tensor_scalar(negb[:, :], negb[:, :], -1.0, None, ALU.mult),
               reads=['negb'], writes=['negb'])
        sc.add('pool', lambda e: e.memset(KT[64:65, :], 1.0), writes=['KTrow'])
        sc.add('pool', lambda e: e.memset(V[:, :, 64:65], 1.0), writes=['Vones'])

        if stop <= 0:
            sc.flush()
            return
        hv = hT.rearrange("(k p) t -> p k t", p=128)
        psq, psk, psv = psb[0], psb[1], psb[2]
        for i in range(NG):
            b = i % 2
            sc.add('sp', lambda e, i=i, b=b: e.dma_start(out=ht[b][:, :, :], in_=hv[:, :, i * 512:(i + 1) * 512]),
                   writes=[('ht', b)], slot=('ht', b))

            def mmq(e, b=b):
                r = None
                for k in range(8):
                    r = e.matmul(psq[0:64, :], wq[:, k, :], ht[b][:, k, :], start=(k == 0), stop=(k == 7))
                return r
            import os
            DBG = int(os.environ.get('FOXDBG', '15'))
            if DBG & 1:
              sc.add('pe', mmq, reads=[('ht', b), 'wq'], writes=['psq'])
            if DBG & 1:
              sc.add('act', lambda e, i=i: e.activation(QT[0:64, i * 512:(i + 1) * 512], psq[0:64, :], AF.Copy,
                                                      scale=0.125),
                   reads=['psq'], writes=[('QT', i)])

            def mmk(e, b=b):
                r = None
                for k in range(8):
                    r = e.matmul(psk[0:64, :], wk[:, k, :], ht[b][:, k, :], start=(k == 0), stop=(k == 7))
                return r
            if DBG & 2:
              sc.add('pe', mmk, reads=[('ht', b), 'wk'], writes=['psk'])
              sc.add('dve', lambda e, i=i: e.tensor_copy(KT[0:64, i * 512:(i + 1) * 512], psk[0:64, :]),
                   reads=['psk'], writes=[('KT', i)])

            def mmv(e, b=b):
                r = None
                for sub in range(4):
                    for k in range(8):
                        r = e.matmul(psv[:, sub * 128:sub * 128 + 65], ht[b][:, k, sub * 128:(sub + 1) * 128],
                                     wv[:, k, :], start=(k == 0), stop=(k == 7))
                return r
            pv3 = psv[:, :].rearrange("p (s c) -> p s c", c=128)
            if DBG & 4:
              sc.add('pe', mmv, reads=[('ht', b), 'wv'], writes=['psv'])
              sc.add('dve', lambda e, i=i, pv3=pv3: e.tensor_copy(V[:, 4 * i:4 * i + 4, 0:64], pv3[:, :, 0:64]),
                   reads=['psv', 'Vones'], writes=[('V', i)])
            if DBG & 8:
              sc.add('dve', lambda e, i=i, pv3=pv3: e.tensor_copy(lfr[:, 4 * i:4 * i + 4], pv3[:, :, 64]),
                   reads=['psv'], writes=[('lfr', i)])

        if stop <= 1:
            sc.flush()
            return
        allfr = [('lfr', i) for i in range(NG)]
        sc.add('act', lambda e: e.activation(lfn[:, :], lfr[:, :], AF.Exp, bias=negb[:, 0:1], scale=-1.0),
               reads=allfr + ['negb'], writes=['lfn'])
        sc.add('act', lambda e: e.activation(lfn[:, :], lfn[:, :], AF.Ln, bias=1.0, scale=1.0),
               reads=['lfn'], writes=['lfn'])
        pt = psb[0]
        sc.add('pe', lambda e: e.matmul(pt[0:NT, 0:1], lfn[:, :], c['ones'][:, 0:1], start=True, stop=True),
               reads=['lfn', 'c_ones', 'psq'], writes=['psq'])
        sc.add('dve', lambda e: e.tensor_copy(totT[0:NT, :], pt[0:NT, 0:1]), reads=['psq'], writes=['totT'])
        sc.add('dve', lambda e: e.tensor_scalar(X[0:NT, 0:NT], c['ustrict'][0:NT, 0:NT], totT[0:NT, 0:1], None,
                                                ALU.mult),
               reads=['totT', 'c_ustrict'], writes=['X'])
        pf = psb[1]

        def mmF(e):
            e.matmul(pf[:, 0:NT], c['uincl'][:, :], lfn[:, :], start=True, stop=False)
            return e.matmul(pf[:, 0:NT], c['ones'][0:NT, :], X[0:NT, 0:NT], start=False, stop=True)
        sc.add('pe', mmF, reads=['lfn', 'X', 'c_uincl', 'c_ones', 'psk'], writes=['psk'])
        sc.add('dve', lambda e: e.tensor_copy(Fn[:, :], pf[:, 0:NT]), reads=['psk'], writes=['Fn'])
        pr = psb[2]
        sc.add('pe', lambda e: e.matmul(pr[:, 0:NG], c['e0'][:, :], Fn[:, 0:NT:4], start=True, stop=True),
               reads=['Fn', 'c_e0', 'psv'], writes=['psv'])
        sc.add('dve', lambda e: e.tensor_copy(frefB[:, :], pr[:, 0:NG]), reads=['psv'], writes=['frefB'])
        for r in range(4):
            sc.add('dve', lambda e, r=r: e.tensor_tensor(ctok[:, r:NT:4], frefB[:, :], Fn[:, r:NT:4], ALU.subtract),
                   reads=['frefB', 'Fn'], writes=[('ctok', r)])
        pc = psb[3]
        sc.add('pe', lambda e: e.transpose(pc[0:NT, 0:128], ctok[:, :], c['ident'][:, :]),
               reads=[('ctok', r) for r in range(4)] + ['c_ident'], writes=['ps3'])
        sc.add('dve', lambda e: e.tensor_copy(cTT[0:NT, :], pc[0:NT, 0:128]), reads=['ps3'], writes=['cTT'])
        sc.add('sp', lambda e: e.dma_start(out=scr[0:NT, :], in_=cTT[0:NT, :]), reads=['cTT'], writes=['scr'],
               slot='scr')
        sc.add('sp', lambda e: e.dma_start(out=QT[64:65, :], in_=scr[0:NT, :].rearrange("(o j) p -> o (j p)", o=1)),
               reads=['scr'], writes=['QTrow'], slot='qtrow')

        if stop <= 2:
            sc.flush()
            return
        sc.flush()
        LA = 2
        pS = [psb[0], psb[1], psb[2], psb[3]]
        pO = [psb[6], psb[7]]
        pbc = psb[4]
        blocks = []
        for g in range(NG):
            nj = 4 * g + 4
            for j in range(nj):
                r = j - 4 * g
                c0 = 0 if r < 0 else r * 128
                blocks.append((g, j, r, c0, 512 - c0, nj))
        NB = len(blocks)

        def emit_front(bi):
            g, j, r, c0, N, nj = blocks[bi]
            gb = g % 2
            sb_ = bi % 4
            if j == 0:
                sc.add('dve', lambda e: e.tensor_scalar(biasg[gb][:, 0:nj], Fn[:, 0:nj], frefB[:, g:g + 1], None,
                                                        ALU.subtract),
                       reads=['Fn', 'frefB'], writes=[('biasg', gb)])
            sc.add('pe', lambda e: e.matmul(pS[sb_][:, 0:N], KT[0:65, j * 128:(j + 1) * 128],
                                            QT[0:65, g * 512 + c0:(g + 1) * 512], start=True, stop=True),
                   reads=['QT', 'KT'], writes=[('pS', sb_)])
            sc.add('act', lambda e: e.activation(Pb[sb_][:, 0:N], pS[sb_][:, 0:N], AF.Exp,
                                                 bias=biasg[gb][:, j:j + 1], scale=1.0),
                   reads=[('pS', sb_), ('biasg', gb)], writes=[('P', sb_)])
            if r >= 0:
                sc.add('pool', lambda e: e.affine_select(Pb[sb_][:, 0:128], Pb[sb_][:, 0:128], [[1, 128]], ALU.is_ge,
                                                         0.0, base=0, channel_multiplier=-1),
                       reads=[('P', sb_)], writes=[('P', sb_)])

        def emit_back(bi):
            g, j, r, c0, N, nj = blocks[bi]
            gb = g % 2
            sb_ = bi % 4
            sc.add('pe', lambda e: e.matmul(pO[gb][0:65, c0:512], V[:, j, 0:65], Pb[sb_][:, 0:N], start=(j == 0),
                                            stop=(j == nj - 1), skip_group_check=True),
                   reads=[('P', sb_), 'V'], writes=[('pO', gb)])
            if j == nj - 1:
                sc.add('dve', lambda e: e.tensor_copy(oun[0:65, :], pO[gb][0:65, :]),
                       reads=[('pO', gb)], writes=['oun'])
                sc.add('dve', lambda e: e.reciprocal(rl[64:65, :], oun[64:65, :]), reads=['oun'], writes=['rl'])
                sc.add('pe', lambda e: e.matmul(pbc[0:64, :], c['ones'][64:65, 0:64], rl[64:65, :], start=True,
                                                stop=True),
                       reads=['rl', 'c_ones'], writes=[('pb', 4)])
                sc.add('dve', lambda e: e.tensor_tensor(ofin[gb][:, :], oun[0:64, :], pbc[0:64, :], ALU.mult),
                       reads=['oun', ('pb', 4)], writes=[('ofin', gb)])
                sc.add('sp', lambda e: e.dma_start(out=oaT[:, g * 512:(g + 1) * 512], in_=ofin[gb][:, :]),
                       reads=[('ofin', gb)], writes=[('oaT', g)], slot=('oa', gb))

        for bi in range(NB + LA):
            if bi < NB:
                emit_front(bi)
            if bi - LA >= 0:
                emit_back(bi - LA)
        sc.flush()


def gdn_phase(nc, sc, c, S, hT, wg, cw, gpar, obT, psb):
    NSEG = S // 512
    A = sc.add
    cx = Ctx(nc)
    PB = lambda n: ('pb', n)
    with cx.st:
        f32t = lambda *sh: cx.sb(list(sh), F32)
        bft = lambda *sh: cx.sb(list(sh), BF16)
        wst = f32t(128, 8, 322)
        wq, wk, wv, wab = bft(128, 8, 128), bft(128, 8, 128), bft(128, 8, 64), bft(128, 8, 2)
        cw_sb, gp_sb = f32t(128, 12), f32t(128, 2)
        negA = f32t(128, 1)
        M_s, M_i = f32t(128, 4, 128), f32t(128, 4, 128)
        E63, E127, EL = f32t(128, 128), f32t(128, 128), f32t(128, 128)
        ht = [bft(128, 8, 512) for _ in range(2)]
        rq, rk, rv = f32t(128, 515), f32t(128, 515), f32t(64, 515)
        cq, ck = f32t(128, 512), f32t(128, 512)
        sq2, rn, sq2b, rnb = f32t(128, 512), f32t(128, 512), f32t(128, 512), f32t(128, 512)
        g_tok, G_tok, eG, ekd, glo = [f32t(128, 4) for _ in range(5)]
        diagG, EGrow, Dm, Gam, Gs, Gi = [f32t(128, 512) for _ in range(6)]
        ktok = f32t(128, 4, 128)
        Bb = [bft(128, 512) for _ in range(2)]
        Pb_ = [bft(128, 512) for _ in range(2)]
        S_f, S_b = f32t(128, 512), bft(128, 512)
        St_f, St_b = f32t(128, 64), bft(128, 64)
        vnew = bft(128, 64)
        o_sb = f32t(64, 512)

        A('sp', lambda e: e.dma_start(out=wst[:, :, :], in_=wg.rearrange("(k p) c -> p k c", p=128)),
          writes=['gwst'], slot='gwst')
        A('pool', lambda e: e.tensor_copy(wq[:, :, :], wst[:, :, 0:128]), reads=['gwst'], writes=['gwq'])
        A('pool', lambda e: e.tensor_copy(wk[:, :, :], wst[:, :, 128:256]), reads=['gwst'], writes=['gwk'])
        A('pool', lambda e: e.tensor_copy(wv[:, :, :], wst[:, :, 256:320]), reads=['gwst'], writes=['gwv'])
        A('pool', lambda e: e.tensor_copy(wab[:, :, :], wst[:, :, 320:322]), reads=['gwst'], writes=['gwab'])
        A('sp', lambda e: e.dma_start(out=cw_sb[:, :], in_=cw[:, :]), writes=['cw'], slot='cw')
        A('sp', lambda e: e.dma_start(out=gp_sb[:, :], in_=gpar[:, :]), writes=['gp'], slot='gp')
        A('act', lambda e: e.activation(negA[:, :], gp_sb[:, 0:1], AF.Exp), reads=['gp'], writes=['negA'])
        A('dve', lambda e: e.tensor_scalar(negA[:, :], negA[:, :], -1.0, None, ALU.mult), reads=['negA'],
          writes=['negA'])
        A('pool', lambda e: e.memset(M_s[:, :, :], 1.0), writes=['M_s'])
        A('pool', lambda e: e.memset(M_i[:, :, :], 1.0), writes=['M_i'])
        A('pool', lambda e: e.affine_select(M_s[:, :, :], M_s[:, :, :], [[0, 4], [1, 128]], ALU.is_gt, 0.0, base=0,
                                            channel_multiplier=-1), reads=['M_s'], writes=['M_s'])
        A('pool', lambda e: e.affine_select(M_i[:, :, :], M_i[:, :, :], [[0, 4], [1, 128]], ALU.is_ge, 0.0, base=0,
                                            channel_multiplier=-1), reads=['M_i'], writes=['M_i'])
        A('pool', lambda e: e.memset(M_s[0:64, :, 64:128], 0.0), reads=['M_s'], writes=['M_s'])
        A('pool', lambda e: e.memset(M_i[0:64, :, 64:128], 0.0), reads=['M_i'], writes=['M_i'])
        A('pool', lambda e: e.affine_select(E63[:, :], c['zeros'][:, :], [[0, 128]], ALU.not_equal, 1.0, base=-63,
                                            channel_multiplier=1), reads=['c_zeros'], writes=['E63'])
        A('pool', lambda e: e.affine_select(E127[:, :], c['zeros'][:, :], [[0, 128]], ALU.not_equal, 1.0, base=-127,
                                            channel_multiplier=1), reads=['c_zeros'], writes=['E127'])
        A('pool', lambda e: e.tensor_copy(EL[:, 0:64], E63[:, 0:64]), reads=['E63'], writes=['EL'])
        A('pool', lambda e: e.tensor_copy(EL[:, 64:128], E127[:, 64:128]), reads=['E127', 'EL'], writes=['EL'])
        A('pool', lambda e: e.memset(rq[:, 0:3], 0.0), writes=['rq'])
        A('pool', lambda e: e.memset(rk[:, 0:3], 0.0), writes=['rk'])
        A('pool', lambda e: e.memset(rv[:, 0:3], 0.0), writes=['rv'])
        A('pool', lambda e: e.memset(St_f[:, :], 0.0), writes=['St_f'])
        A('pool', lambda e: e.memset(St_b[:, :], 0.0), writes=['St_b'])

        hv = hT.rearrange("(k p) t -> p k t", p=128)
        ones, ident = c['ones'], c['ident']
        DEPTH = dict(qn_f=2, kn_f=2, qn_bf=2, kT_bf=2, cv=2, a_sb=2, b_sb=2, B_f=2, kg=2, vtok=2, bpos=2,
                     qdec=3, aqk=3, kdec=3, negbt=3, dl=3, dh=3, ybu=2, ywT=2)
        SHAPES = dict(qn_f=(F32, (128, 512)), kn_f=(F32, (128, 512)), qn_bf=(BF16, (128, 512)),
                      kT_bf=(BF16, (128, 512)), cv=(F32, (64, 512)), a_sb=(F32, (128, 4)), b_sb=(F32, (128, 4)),
                      B_f=(F32, (128, 512)), kg=(BF16, (128, 4, 128)), vtok=(BF16, (128, 4, 64)),
                      bpos=(F32, (128, 4)), qdec=(BF16, (128, 512)), aqk=(BF16, (128, 512)),
                      kdec=(BF16, (128, 4, 128)), negbt=(F32, (128, 4)), dl=(F32, (128, 4)), dh=(F32, (128, 4)),
                      ybu=(F32, (128, 4, 64)), ywT=(BF16, (128, 512)))
        ROT = {n: [cx.sb(list(SHAPES[n][1]), SHAPES[n][0]) for _ in range(DEPTH[n])] for n in DEPTH}

        def mkA(s, lst):
            def K(k):
                return (k, s % DEPTH[k]) if (isinstance(k, str) and k in DEPTH) else k

            def A_(eng, fn, reads=(), writes=(), slot=None):
                lst.append(((eng, fn), dict(reads=[K(k) for k in reads], writes=[K(k) for k in writes], slot=slot)))
            return A_

        def rb(s, n):
            return ROT[n][s % DEPTH[n]]

        def stage1(i, A):
            b = i % 2
            qn_f, kn_f, qn_bf, kT_bf, cv, a_sb, b_sb = [rb(i, n) for n in
                                                        ('qn_f', 'kn_f', 'qn_bf', 'kT_bf', 'cv', 'a_sb', 'b_sb')]
            A('sp', lambda e: e.dma_start(out=ht[b][:, :, :], in_=hv[:, :, i * 512:(i + 1) * 512]),
              writes=[('ght', b)], slot=('ght', b))
            for (w_, M, bank, raw, key) in ((wq, 128, 0, rq, 'rq'), (wk, 128, 1, rk, 'rk'), (wv, 64, 0, rv, 'rv')):
                def mm(e, w_=w_, M=M, bank=bank):
                    r = None
                    for k in range(8):
                        r = e.matmul(psb[bank][0:M, :], w_[:, k, :], ht[b][:, k, :], start=(k == 0), stop=(k == 7))
                    return r
                A('pe', mm, reads=[('ght', b), 'gwq', 'gwk', 'gwv'], writes=[PB(bank)])
                A('dve', lambda e, M=M, bank=bank, raw=raw: e.tensor_copy(raw[0:M, 3:515], psb[bank][0:M, :]),
                  reads=[PB(bank)], writes=[key])

            def mmab(e):
                r = None
                for sub in range(4):
                    for k in range(8):
                        r = e.matmul(psb[1][:, sub * 2:sub * 2 + 2], ht[b][:, k, sub * 128:(sub + 1) * 128],
                                     wab[:, k, :], start=(k == 0), stop=(k == 7))
                return r
            A('pe', mmab, reads=[('ght', b), 'gwab'], writes=[PB(1)])
            p3 = psb[1][:, 0:8].rearrange("p (s c) -> p s c", c=2)
            A('dve', lambda e: e.tensor_copy(a_sb[:, :], p3[:, :, 0]), reads=[PB(1)], writes=['a_sb'])
            A('dve', lambda e: e.tensor_copy(b_sb[:, :], p3[:, :, 1]), reads=[PB(1)], writes=['b_sb'])
            for which, (raw, cv_, M, key, ckey) in enumerate(((rq, cq, 128, 'rq', 'cq'), (rk, ck, 128, 'rk', 'ck'),
                                                              (rv, cv, 64, 'rv', 'cv'))):
                A('act', lambda e, raw=raw, cv_=cv_, M=M, which=which: e.activation(
                    cv_[0:M, :], raw[0:M, 0:512], AF.Copy, scale=cw_sb[0:M, which * 4:which * 4 + 1]),
                    reads=[key, 'cw'], writes=[ckey])
                for tap in range(1, 4):
                    A('dve', lambda e, raw=raw, cv_=cv_, M=M, which=which, tap=tap: e.scalar_tensor_tensor(
                        cv_[0:M, :], raw[0:M, tap:tap + 512], cw_sb[0:M, which * 4 + tap:which * 4 + tap + 1],
                        cv_[0:M, :], ALU.mult, ALU.add),
                        reads=[key, 'cw', ckey], writes=[ckey])
                A('pool', lambda e, raw=raw, M=M: e.tensor_copy(raw[0:M, 0:3], raw[0:M, 512:515]),
                  reads=[key, ckey], writes=[key])
                A('act', lambda e, cv_=cv_, M=M: e.activation(cv_[0:M, :], cv_[0:M, :], AF.Silu),
                  reads=[ckey], writes=[ckey])
            for (cv_, ckey, bank, outf, okey, mul) in ((cq, 'cq', 0, qn_f, 'qn_f', 128.0 ** -0.5),
                                                      (ck, 'ck', 1, kn_f, 'kn_f', 1.0)):
                sq_, rn_ = (sq2, rn) if bank == 0 else (sq2b, rnb)
                A('act', lambda e, cv_=cv_, sq_=sq_: e.activation(sq_[:, :], cv_[:, :], AF.Square), reads=[ckey],
                  writes=[('sq2', bank)])
                A('pe', lambda e, bank=bank, sq_=sq_: e.matmul(psb[bank][:, :], ones[:, :], sq_[:, :], start=True,
                                                               stop=True),
                  reads=[('sq2', bank), 'c_ones'], writes=[PB(bank)])
                A('act', lambda e, bank=bank, rn_=rn_: e.activation(rn_[:, :], psb[bank][:, :], AF.Sqrt, bias=EPS,
                                                                    scale=1.0),
                  reads=[PB(bank)], writes=[('rn', bank)])
                A('dve', lambda e, rn_=rn_: e.reciprocal(rn_[:, :], rn_[:, :]), reads=[('rn', bank)],
                  writes=[('rn', bank)])
                A('dve', lambda e, cv_=cv_, outf=outf, mul=mul, rn_=rn_: e.scalar_tensor_tensor(
                    outf[:, :], cv_[:, :], mul, rn_[:, :], ALU.mult, ALU.mult), reads=[ckey, ('rn', bank)],
                    writes=[okey])
            A('act', lambda e: e.activation(qn_bf[:, :], qn_f[:, :], AF.Copy), reads=['qn_f'], writes=['qn_bf'])
            A('act', lambda e: e.activation(kT_bf[:, :], kn_f[:, :], AF.Copy), reads=['kn_f'], writes=['kT_bf'])

        def stage2(i, A):
            qn_f, kn_f, qn_bf, kT_bf, cv, a_sb, b_sb = [rb(i, n) for n in
                                                        ('qn_f', 'kn_f', 'qn_bf', 'kT_bf', 'cv', 'a_sb', 'b_sb')]
            B_f, kg, vtok, bpos = [rb(i, n) for n in ('B_f', 'kg', 'vtok', 'bpos')]
            qdec, aqk, kdec, negbt, dl, dh = [rb(i, n) for n in ('qdec', 'aqk', 'kdec', 'negbt', 'dl', 'dh')]
            A('act', lambda e: e.activation(g_tok[:, :], a_sb[:, :], AF.Exp, bias=gp_sb[:, 1:2], scale=1.0),
              reads=['a_sb', 'gp'], writes=['g_tok'])
            A('act', lambda e: e.activation(g_tok[:, :], g_tok[:, :], AF.Ln, bias=1.0, scale=1.0),
              reads=['g_tok'], writes=['g_tok'])
            A('dve', lambda e: e.tensor_scalar(g_tok[:, :], g_tok[:, :], negA[:, 0:1], None, ALU.mult),
              reads=['g_tok', 'negA'], writes=['g_tok'])
            A('act', lambda e: e.activation(bpos[:, :], b_sb[:, :], AF.Sigmoid), reads=['b_sb'], writes=['bpos'])
            A('dve', lambda e: e.tensor_scalar(negbt[:, :], bpos[:, :], -1.0, None, ALU.mult), reads=['bpos'],
              writes=['negbt'])
            A('pe', lambda e: e.matmul(psb[2][:, 0:4], M_i[:, 0, :], g_tok[:, :], start=True, stop=True),
              reads=['g_tok', 'M_i'], writes=[PB(2)])
            A('dve', lambda e: e.tensor_copy(G_tok[:, :], psb[2][:, 0:4]), reads=[PB(2)], writes=['G_tok'])
            A('act', lambda e: e.activation(eG[:, :], G_tok[:, :], AF.Exp), reads=['G_tok'], writes=['eG'])
            A('pe', lambda e: e.matmul(psb[2][:, 0:4], EL[:, :], G_tok[:, :], start=True, stop=True),
              reads=['G_tok', 'EL'], writes=[PB(2)])
            A('dve', lambda e: e.tensor_tensor(glo[:, :], psb[2][:, 0:4], G_tok[:, :], ALU.subtract),
              reads=[PB(2), 'G_tok'], writes=['glo'])
            A('act', lambda e: e.activation(ekd[:, :], glo[:, :], AF.Exp), reads=['glo'], writes=['ekd'])
            A('pe', lambda e: e.matmul(psb[2][:, 0:4], E63[:, :], G_tok[:, :], start=True, stop=True),
              reads=['G_tok', 'E63'], writes=[PB(2)])
            A('dve', lambda e: e.tensor_copy(dl[:, :], psb[2][:, 0:4]), reads=[PB(2)], writes=['dl'])
            A('act', lambda e: e.activation(dl[:, :], dl[:, :], AF.Exp), reads=['dl'], writes=['dl'])
            A('pe', lambda e: e.matmul(psb[2][:, 0:4], E127[:, :], G_tok[:, :], start=True, stop=True),
              reads=['G_tok', 'E127'], writes=[PB(2)])
            A('dve', lambda e: e.tensor_copy(dh[:, :], psb[2][:, 0:4]), reads=[PB(2)], writes=['dh'])
            A('act', lambda e: e.activation(dh[:, :], dh[:, :], AF.Exp), reads=['dh'], writes=['dh'])
            for sub in range(4):
                A('dve', lambda e, sub=sub: e.tensor_scalar(diagG[:, sub * 128:(sub + 1) * 128], ident[:, :],
                                                            G_tok[:, sub:sub + 1], None, ALU.mult),
                  reads=['G_tok', 'c_ident'], writes=[('diagG', sub)])
                A('pe', lambda e, sub=sub: e.matmul(psb[3][:, sub * 128:(sub + 1) * 128], ones[:, :],
                                                    diagG[:, sub * 128:(sub + 1) * 128], start=True, stop=True),
                  reads=[('diagG', sub), 'c_ones'], writes=[PB(3)])
            A('act', lambda e: e.activation(EGrow[:, :], psb[3][:, :], AF.Exp), reads=[PB(3)], writes=['EGrow'])
            A('pool', lambda e: e.tensor_tensor(qdec[:, :], qn_f[:, :], EGrow[:, :], ALU.mult),
              reads=['qn_f', 'EGrow'], writes=['qdec'])
            for sub in range(4):
                A('dve', lambda e, sub=sub: e.tensor_scalar(Dm[:, sub * 128:(sub + 1) * 128],
                                                            psb[3][:, sub * 128:(sub + 1) * 128],
                                                            G_tok[:, sub:sub + 1], 0.0, ALU.subtract, ALU.min),
                  reads=[PB(3), 'G_tok'], writes=['Dm'])
            A('act', lambda e: e.activation(Gam[:, :], Dm[:, :], AF.Exp), reads=['Dm'], writes=['Gam'])
            A('pool', lambda e: e.tensor_tensor(Gs[:, :], Gam[:, :], M_s[:, :, :].rearrange("p s c -> p (s c)"),
                                                ALU.mult), reads=['Gam', 'M_s'], writes=['Gs'])
            A('pool', lambda e: e.tensor_tensor(Gi[:, :], Gam[:, :], M_i[:, :, :].rearrange("p s c -> p (s c)"),
                                                ALU.mult), reads=['Gam', 'M_i'], writes=['Gi'])
            for sub in range(4):
                A('pe', lambda e, sub=sub: e.transpose(psb[2][:, sub * 128:(sub + 1) * 128],
                                                       kn_f[:, sub * 128:(sub + 1) * 128], ident[:, :]),
                  reads=['kn_f', 'c_ident'], writes=[PB(2)])
            A('dve', lambda e: e.tensor_copy(ktok[:, :, :], psb[2][:, :].rearrange("p (s c) -> p s c", c=128)),
              reads=[PB(2)], writes=['ktok'])
            for sub in range(4):
                A('act', lambda e, sub=sub: e.activation(kg[:, sub, :], ktok[:, sub, :], AF.Copy,
                                                         scale=eG[:, sub:sub + 1]), reads=['ktok', 'eG'], writes=['kg'])
                A('act', lambda e, sub=sub: e.activation(kdec[:, sub, :], ktok[:, sub, :], AF.Copy,
                                                         scale=ekd[:, sub:sub + 1]), reads=['ktok', 'ekd'],
                  writes=['kdec'])
            for sub in range(4):
                A('pe', lambda e, sub=sub: e.transpose(psb[2][:, sub * 64:(sub + 1) * 64],
                                                       cv[0:64, sub * 128:(sub + 1) * 128], ident[0:64, 0:64]),
                  reads=['cv', 'c_ident'], writes=[PB(2)])
            A('dve', lambda e: e.tensor_copy(vtok[:, :, :], psb[2][:, 0:256].rearrange("p (s c) -> p s c", c=64)),
              reads=[PB(2)], writes=['vtok'])
            for sub in range(4):
                cs = slice(sub * 128, (sub + 1) * 128)
                A('pe', lambda e, cs=cs: e.matmul(psb[2][:, cs], kT_bf[:, cs], kT_bf[:, cs], start=True, stop=True),
                  reads=['kT_bf'], writes=[PB(2)])
                A('dve', lambda e, cs=cs, sub=sub: e.scalar_tensor_tensor(
                    B_f[:, cs], psb[2][:, cs], negbt[:, sub:sub + 1], Gs[:, cs], ALU.mult, ALU.mult),
                    reads=[PB(2), 'negbt', 'Gs'], writes=['B_f'])
            for sub in range(4):
                cs = slice(sub * 128, (sub + 1) * 128)
                A('pe', lambda e, cs=cs: e.matmul(psb[3][:, cs], kT_bf[:, cs], qn_bf[:, cs], start=True, stop=True),
                  reads=['kT_bf', 'qn_bf'], writes=[PB(3)])
            A('dve', lambda e: e.tensor_tensor(aqk[:, :], psb[3][:, :], Gi[:, :], ALU.mult), reads=[PB(3), 'Gi'],
              writes=['aqk'])

        def stage3(i, A):
            B_f, kg, vtok, bpos = [rb(i, n) for n in ('B_f', 'kg', 'vtok', 'bpos')]
            ybu, ywT = rb(i, 'ybu'), rb(i, 'ywT')
            A('act', lambda e: e.activation(Bb[0][:, :], B_f[:, :], AF.Copy), reads=['B_f'], writes=[('Bb', 0)])
            for sub in range(4):
                cs = slice(sub * 128, (sub + 1) * 128)
                A('pe', lambda e, cs=cs: e.transpose(psb[4][:, cs], B_f[:, cs], ident[:, :]),
                  reads=['B_f', 'c_ident'], writes=[PB(4)])
            A('dve', lambda e: e.tensor_copy(Pb_[0][:, :], psb[4][:, :]), reads=[PB(4)], writes=[('Pb', 0)])
            for sub in range(4):
                cs = slice(sub * 128, (sub + 1) * 128)
                A('pool', lambda e, cs=cs: e.tensor_tensor(S_f[:, cs], B_f[:, cs], ident[:, :], ALU.add),
                  reads=['B_f', 'c_ident'], writes=['S_f'])
            A('act', lambda e: e.activation(S_b[:, :], S_f[:, :], AF.Copy), reads=['S_f'], writes=['S_b'])
            for j in range(5):
                cur, nxt = j % 2, (j + 1) % 2
                for sub in range(4):
                    cs = slice(sub * 128, (sub + 1) * 128)
                    A('pe', lambda e, cs=cs, cur=cur: e.matmul(psb[5][:, cs], Pb_[cur][:, cs], Bb[cur][:, cs],
                                                               start=True, stop=True),
                      reads=[('Pb', cur), ('Bb', cur)], writes=[PB(5)])
                A('dve', lambda e, nxt=nxt: e.tensor_copy(Bb[nxt][:, :], psb[5][:, :]), reads=[PB(5)],
                  writes=[('Bb', nxt)])
                for sub in range(4):
                    cs = slice(sub * 128, (sub + 1) * 128)
                    A('pe', lambda e, cs=cs, cur=cur: e.matmul(psb[4][:, cs], Bb[cur][:, cs], Pb_[cur][:, cs],
                                                               start=True, stop=True),
                      reads=[('Pb', cur), ('Bb', cur)], writes=[PB(4)])
                A('act', lambda e, nxt=nxt: e.activation(Pb_[nxt][:, :], psb[4][:, :], AF.Copy), reads=[PB(4)],
                  writes=[('Pb', nxt)])
                for sub in range(4):
                    cs = slice(sub * 128, (sub + 1) * 128)
                    A('pe', lambda e, cs=cs, nxt=nxt: e.matmul(psb[5][:, cs], Pb_[nxt][:, cs], S_b[:, cs],
                                                               start=True, stop=True),
                      reads=[('Pb', nxt), 'S_b'], writes=[PB(5)])
                A('dve', lambda e: e.tensor_tensor(S_f[:, :], S_f[:, :], psb[5][:, :], ALU.add),
                  reads=['S_f', PB(5)], writes=['S_f'])
                A('act', lambda e: e.activation(S_b[:, :], S_f[:, :], AF.Copy), reads=['S_f'], writes=['S_b'])
            for sub in range(4):
                cs = slice(sub * 128, (sub + 1) * 128)
                A('pe', lambda e, cs=cs, sub=sub: e.matmul(psb[4][:, sub * 64:(sub + 1) * 64], S_b[:, cs],
                                                           vtok[:, sub, :], start=True, stop=True),
                  reads=['S_b', 'vtok'], writes=[PB(4)])
            for sub in range(4):
                A('dve', lambda e, sub=sub: e.tensor_scalar(ybu[:, sub, :], psb[4][:, sub * 64:(sub + 1) * 64],
                                                            bpos[:, sub:sub + 1], None, ALU.mult),
                  reads=[PB(4), 'bpos'], writes=['ybu'])
            for sub in range(4):
                cs = slice(sub * 128, (sub + 1) * 128)
                A('pe', lambda e, cs=cs, sub=sub: e.matmul(psb[5][:, cs], kg[:, sub, :], S_b[:, cs], start=True,
                                                           stop=True),
                  reads=['S_b', 'kg'], writes=[PB(5)])
            A('dve', lambda e: e.tensor_copy(ywT[:, :], psb[5][:, :]), reads=[PB(5)], writes=['ywT'])

        def stage4(i, A):
            qdec, aqk, kdec, negbt, dl, dh = [rb(i, n) for n in ('qdec', 'aqk', 'kdec', 'negbt', 'dl', 'dh')]
            ybu, ywT = rb(i, 'ybu'), rb(i, 'ywT')
            for ch in range(8):
                sub, hf = ch // 2, ch % 2
                rs = slice(hf * 64, hf * 64 + 64)
                cs = slice(sub * 128, (sub + 1) * 128)
                cc = slice(ch * 64, (ch + 1) * 64)
                A('pe', lambda e, cs=cs: e.matmul(psb[6][:, 0:64], ywT[:, cs], St_b[:, :], start=True, stop=True),
                  reads=['ywT', 'St_b'], writes=[PB(6)])
                A('dve', lambda e, rs=rs, sub=sub: e.scalar_tensor_tensor(
                    vnew[rs, :], psb[6][rs, 0:64], negbt[rs, sub:sub + 1], ybu[rs, sub, :], ALU.mult, ALU.add),
                    reads=[PB(6), 'negbt', 'ybu'], writes=['vnew'])

                def mmo(e, cc=cc, rs=rs):
                    e.matmul(psb[7][0:64, cc], St_b[:, :], qdec[:, cc], start=True, stop=False)
                    return e.matmul(psb[7][0:64, cc], vnew[rs, :], aqk[rs, cc], start=False, stop=True)
                A('pe', mmo, reads=['St_b', 'qdec', 'vnew', 'aqk'], writes=[PB(7)])
                A('pe', lambda e, rs=rs, sub=sub: e.matmul(psb[6][:, 64:128], kdec[rs, sub, :], vnew[rs, :],
                                                           start=True, stop=True),
                  reads=['kdec', 'vnew'], writes=[PB(6)])
                dsc = (dl if hf == 0 else dh)
                A('dve', lambda e, sub=sub, dsc=dsc: e.scalar_tensor_tensor(
                    St_b[:, :], St_f[:, :], dsc[:, sub:sub + 1], psb[6][:, 64:128], ALU.mult, ALU.add),
                    reads=['St_f', PB(6), 'dl', 'dh'], writes=['St_b'])
                A('dve', lambda e, sub=sub, dsc=dsc: e.scalar_tensor_tensor(
                    St_f[:, :], St_f[:, :], dsc[:, sub:sub + 1], psb[6][:, 64:128], ALU.mult, ALU.add),
                    reads=['St_f', PB(6), 'dl', 'dh'], writes=['St_f'])
            A('act', lambda e: e.activation(o_sb[:, :], psb[7][0:64, :], AF.Copy), reads=[PB(7)], writes=['o_sb'])
            A('sp', lambda e: e.dma_start(out=obT[:, i * 512:(i + 1) * 512], in_=o_sb[:, :]),
              reads=['o_sb'], writes=[('obT', i)], slot='ob')

        def merge_lists(lists):
            lists = [l for l in lists if l]
            pos = [0] * len(lists)
            out = []
            while True:
                best, bf = None, None
                for li, l in enumerate(lists):
                    if pos[li] < len(l):
                        fr = pos[li] / len(l)
                        if bf is None or fr < bf:
                            best, bf = li, fr
                if best is None:
                    break
                out.append(lists[best][pos[best]])
                pos[best] += 1
            return out

        stages = (stage1, stage2, stage3, stage4)
        for t in range(NSEG + 3):
            lists = []
            for si, st_ in enumerate(stages):
                s = t - si
                if 0 <= s < NSEG:
                    lst = []
                    st_(s, mkA(s, lst))
                    lists.append(lst)
            for (a_, k_) in merge_lists(lists[::-1]):
                sc.add(*a_, **k_)
        sc.flush()


def build_M(S, do_fox=True, do_gdn=True, stop=99):
    nc = bass.Bass("TRN2", target_bir_lowering=False)
    hT = nc.dram_tensor("hT", [D, S], BF16, kind="ExternalInput").ap()
    wf = nc.dram_tensor("wf", [D, 193], F32, kind="ExternalInput").ap()
    bfg = nc.dram_tensor("bfg", [128, 1], F32, kind="ExternalInput").ap()
    wg = nc.dram_tensor("wg", [D, 322], F32, kind="ExternalInput").ap()
    cw = nc.dram_tensor("cw", [128, 12], F32, kind="ExternalInput").ap()
    gpar = nc.dram_tensor("gpar", [128, 2], F32, kind="ExternalInput").ap()
    oaT = nc.dram_tensor("oaT", [64, S], F32, kind="ExternalOutput").ap()
    obT = nc.dram_tensor("obT", [64, S], F32, kind="ExternalOutput").ap()
    scr = nc.dram_tensor("scr", [128, 128], BF16).ap()
    cx = Ctx(nc)
    with cx.st:
        sc = Sched(nc)
        c = make_consts(sc, cx)
        psb = [cx.ps([128, 512], F32) for _ in range(8)]
        if do_gdn:
            gdn_phase(nc, sc, c, S, hT, wg, cw, gpar, obT, psb)
        if do_fox:
            fox_phase(nc, sc, cx, c, S, hT, wf, bfg, oaT, scr, psb, stop=stop)
        sc.close()
    return nc


def build_T(TS, last):
    nc = bass.Bass("TRN2", target_bir_lowering=False)
    TT = 256
    NTT = TS // TT
    xT = nc.dram_tensor("xT", [D, TS], F32, kind="ExternalInput").ap()
    hT = nc.dram_tensor("hT", [D, TS], BF16, kind="ExternalInput").ap()
    oaT = nc.dram_tensor("oaT", [512, TS], F32, kind="ExternalInput").ap()
    obT = nc.dram_tensor("obT", [512, TS], F32, kind="ExternalInput").ap()
    w_in = nc.dram_tensor("w_in", [D, N_IN], F32, kind="ExternalInput").ap()
    w_br = nc.dram_tensor("w_br", [1536, D], F32, kind="ExternalInput").ap()
    w_out = nc.dram_tensor("w_out", [D, D], F32, kind="ExternalInput").ap()
    w_kv = nc.dram_tensor("w_kv", [D, 1024], F32, kind="ExternalInput").ap()
    memT = nc.dram_tensor("memT", [D, 256], F32, kind="ExternalInput").ap()
    mem_g = nc.dram_tensor("mem_g", [128, 8], F32, kind="ExternalInput").ap()
    b_mg = nc.dram_tensor("b_mg", [128, 24], F32, kind="ExternalInput").ap()
    gdn_g = nc.dram_tensor("gdn_g", [128, 1], F32, kind="ExternalInput").ap()
    next_g = nc.dram_tensor("next_g", [128, 8], F32, kind="ExternalInput").ap()
    xoT = nc.dram_tensor("xoT", [D, TS], F32, kind="ExternalOutput").ap()
    if not last:
        hoT = nc.dram_tensor("hoT", [D, TS], BF16, kind="ExternalOutput").ap()
    cx = Ctx(nc)
    with cx.st:
        sc = Sched(nc)
        ones_f = cx.sb([128, 128], F32)
        ones_bf = cx.sb([128, 128], BF16)
        sc.add('pool', lambda e: e.memset(ones_f[:, :], 1.0), writes=['ones_f'])
        sc.add('pool', lambda e: e.memset(ones_bf[:, :], 1.0), writes=['ones_bf'])
        psb = [cx.ps([128, 512], F32) for _ in range(8)]

        BM = {(0, 0): 0, (0, 1): 1, (1, 0): 2, (1, 1): 2, (2, 0): 3, (2, 1): 4, (3, 0): 5, (3, 1): 6,
              (4, 0): 2, (4, 1): 3, (5, 0): 4, (5, 1): 5, (6, 0): 6, (6, 1): 7, (7, 0): 0, (7, 1): 1}

        def half(bk, h):
            return psb[BM[(bk, h)]][:, 0:TT]

        def hk(bk, h):
            return ('pb', BM[(bk, h)])
        Wz = cx.sb([128, 8, 5120], BF16)
        Wbr = cx.sb([128, 12, 1024], BF16)
        Wout = cx.sb([128, 8, 1024], BF16)
        mkT = cx.sb([128, 4, 256], BF16)
        mv = cx.sb([128, 2, 512], BF16)
        stage = [cx.sb([128, 1024], F32) for _ in range(2)]
        memg_sb = cx.sb([128, 8], F32)
        bm_sb = cx.sb([128, 24], F32)
        gg_sb = cx.sb([128, 1], F32)
        ng_sb = cx.sb([128, 8], F32)
        for i, (dst, srcap) in enumerate([(memg_sb, mem_g), (bm_sb, b_mg), (gg_sb, gdn_g), (ng_sb, next_g)]):
            sc.add('sp', lambda e, dst=dst, srcap=srcap: e.dma_start(out=dst[:, :], in_=srcap[:, :]),
                   writes=[('par', i)], slot=('par', i))
        nst = [0]

        def ldw(dst, dkey, srcap):
            b = nst[0] % 2
            nst[0] += 1
            wd = srcap.shape[-1]
            sc.add('sp', lambda e: e.dma_start(out=stage[b][:, 0:wd], in_=srcap), writes=[('stage', b)],
                   slot=('stage', b))
            sc.add('pool' if b else 'dve', lambda e: e.tensor_copy(dst, stage[b][:, 0:wd]),
                   reads=[('stage', b)], writes=[dkey])

        pcx = Ctx(nc)
        with pcx.st:
            Wkv = pcx.sb([128, 8, 1024], BF16)
            mt = pcx.sb([128, 8, 256], F32)
            mn = pcx.sb([128, 8, 256], BF16)
            sqm = pcx.sb([128, 8, 256], F32)
            rstm = pcx.sb([128, 256], F32)
            for k in range(8):
                ldw(Wkv[:, k, :], 'Wkv', w_kv[k * 128:(k + 1) * 128, :])
            sc.add('sp', lambda e: e.dma_start(out=mt[:, :, :], in_=memT.rearrange("(k p) m -> p k m", p=128)),
                   writes=['mt'], slot='mt')
            sc.ops[-1]
            saved = {'g': None}
            emit_rmsnorm(sc, mt, 'mt', memg_sb, ones_f, mn, 'mn', 256, sqm, psb[0], rstm, 'mnorm', gkey=('par', 0))
            for hh in range(4):
                def mmk(e, hh=hh):
                    r = None
                    for k in range(8):
                        r = e.matmul(half(1, hh % 2), Wkv[:, k, hh * 128:(hh + 1) * 128], mn[:, k, :],
                                     start=(k == 0), stop=(k == 7))
                    return r
                sc.add('pe', mmk, reads=['Wkv', 'mn'], writes=[hk(1, hh % 2)])
                sc.add('dve', lambda e, hh=hh: e.tensor_copy(mkT[:, hh, :], half(1, hh % 2)),
                       reads=[hk(1, hh % 2)], writes=['mkT'])
            for mc in range(2):
                def mmv(e, mc=mc):
                    r = None
                    for k in range(8):
                        r = e.matmul(psb[2 + mc][:, :], mn[:, k, mc * 128:(mc + 1) * 128], Wkv[:, k, 512:1024],
                                     start=(k == 0), stop=(k == 7))
                    return r
                sc.add('pe', mmv, reads=['Wkv', 'mn'], writes=[('pb', 2 + mc)])
                sc.add('dve', lambda e, mc=mc: e.tensor_copy(mv[:, mc, :], psb[2 + mc][:, :]),
                       reads=[('pb', 2 + mc)], writes=['mv'])
            sc.flush()

        for k in range(8):
            ldw(Wz[:, k, 0:512], 'Wz', w_in[k * 128:(k + 1) * 128, OFF['az']:OFF['az'] + 512])
            ldw(Wz[:, k, 512:1024], 'Wz', w_in[k * 128:(k + 1) * 128, OFF['bz']:OFF['bz'] + 512])
            for cb in range(4):
                ldw(Wz[:, k, 1024 + cb * 1024:2048 + cb * 1024], 'Wz',
                    w_in[k * 128:(k + 1) * 128, OFF['mq'] + cb * 1024:OFF['mq'] + (cb + 1) * 1024])
        for k in range(12):
            ldw(Wbr[:, k, :], 'Wbr', w_br[k * 128:(k + 1) * 128, :])
        for k in range(8):
            ldw(Wout[:, k, :], 'Wout', w_out[k * 128:(k + 1) * 128, :])

        ht = [cx.sb([128, 8, TT], BF16) for _ in range(2)]
        xt = cx.sb([128, 8, TT], F32)
        oat = cx.sb([128, 4, TT], F32)
        obt = cx.sb([128, 4, TT], F32)
        yT = cx.sb([128, 12, TT], BF16)
        mg = cx.sb([128, 8, TT], BF16)
        hout = cx.sb([128, 8, TT], BF16 if not last else F32)
        sqs = [cx.sb([128, TT], F32) for _ in range(2)]
        sil = [cx.sb([128, TT], F32) for _ in range(2)]
        tmp = [cx.sb([128, TT], F32) for _ in range(2)]
        rstd = cx.sb([128, TT], F32)
        rden = cx.sb([128, TT], F32)
        mqs = cx.sb([128, TT], BF16)
        pT = [cx.sb([128, TT], BF16) for _ in range(2)]
        gs = [cx.sb([128, TT], F32) for _ in range(3)]
        acc = [cx.sb([128, TT], F32) for _ in range(2)]
        hv = hT.rearrange("(k p) t -> p k t", p=128)
        xv = xT.rearrange("(k p) t -> p k t", p=128)
        oav = oaT.rearrange("(k p) t -> p k t", p=128)
        obv = obT.rearrange("(k p) t -> p k t", p=128)
        xov = xoT.rearrange("(k p) t -> p k t", p=128)
        if not last:
            hov = hoT.rearrange("(k p) t -> p k t", p=128)

        zcnt = [0]

        def zproj(col0, b):
            s = zcnt[0] % 2
            zcnt[0] += 1
            dst = half(0, s)

            def mm(e):
                r = None
                for k in range(8):
                    r = e.matmul(dst, Wz[:, k, col0:col0 + 128], ht[b][:, k, :], start=(k == 0), stop=(k == 7))
                return r
            sc.add('pe', mm, reads=['Wz', ('ht', b)], writes=[hk(0, s)])
            return dst, hk(0, s)

        for it in range(NTT):
            b = it % 2
            t0, t1 = it * TT, (it + 1) * TT
            sc.add('sp', lambda e, b=b, t0=t0, t1=t1: e.dma_start(out=ht[b][:, :, :], in_=hv[:, :, t0:t1]),
                   writes=[('ht', b)], slot=('ht', b))
            sc.add('sp', lambda e, t0=t0, t1=t1: e.dma_start(out=xt[:, :, :], in_=xv[:, :, t0:t1]),
                   writes=['xt'] + [('xn', dc) for dc in range(8)], slot='xt')
            sc.add('sp', lambda e, t0=t0, t1=t1: e.dma_start(out=oat[:, :, :], in_=oav[:, :, t0:t1]),
                   writes=['oat'], slot='oat')
            sc.add('sp', lambda e, t0=t0, t1=t1: e.dma_start(out=obt[:, :, :], in_=obv[:, :, t0:t1]),
                   writes=['obt'], slot='obt')
            for fc in range(4):
                zp, zk = zproj(fc * 128, b)
                s = fc % 2
                sc.add('act', lambda e, zp=zp, s=s: e.activation(sil[s][:, :], zp, AF.Silu),
                       reads=[zk], writes=[('sil', s)])
                sc.add('pool', lambda e, fc=fc, s=s: e.tensor_tensor(yT[:, fc, :], oat[:, fc, :], sil[s][:, :],
                                                                    ALU.mult),
                       reads=['oat', ('sil', s)], writes=[('yT', fc)])
            for hd in range(4):
                s = hd % 2
                sc.add('act', lambda e, hd=hd, s=s: e.activation(sqs[s][:, :], obt[:, hd, :], AF.Square),
                       reads=['obt'], writes=[('sqs', s)])
                sc.add('pe', lambda e, s=s: e.matmul(half(1, 0), ones_f[:, :], sqs[s][:, :], start=True, stop=True),
                       reads=[('sqs', s), 'ones_f'], writes=[hk(1, 0)])
                sc.add('act', lambda e: e.activation(rstd[:, :], half(1, 0), AF.Sqrt, bias=EPS, scale=1.0 / 128),
                       reads=[hk(1, 0)], writes=['rstd'])
                sc.add('dve', lambda e: e.reciprocal(rstd[:, :], rstd[:, :]), reads=['rstd'], writes=['rstd'])
                sc.add('dve', lambda e, hd=hd, s=s: e.scalar_tensor_tensor(tmp[s][:, :], obt[:, hd, :],
                                                                          gg_sb[:, 0:1], rstd[:, :],
                                                                          ALU.mult, ALU.mult),
                       reads=['obt', 'rstd', ('par', 2)], writes=[('tmp', s)])
                zp, zk = zproj(512 + hd * 128, b)
                sc.add('act', lambda e, zp=zp, s=s: e.activation(sil[s][:, :], zp, AF.Silu),
                       reads=[zk], writes=[('sil', s)])
                sc.add('pool', lambda e, hd=hd, s=s: e.tensor_tensor(yT[:, 4 + hd, :], tmp[s][:, :], sil[s][:, :],
                                                                    ALU.mult),
                       reads=[('tmp', s), ('sil', s)], writes=[('yT', 4 + hd)])
            for hh in range(4):
                s = hh % 2
                zp, zk = zproj(1024 + hh * 128, b)
                sc.add('dve', lambda e, zp=zp: e.tensor_copy(mqs[:, :], zp), reads=[zk], writes=['mqs'])
                for mc in range(2):
                    sc.add('pe', lambda e, hh=hh, mc=mc: e.matmul(half(2, mc), mkT[:, hh, mc * 128:(mc + 1) * 128],
                                                                 mqs[:, :], start=True, stop=True),
                           reads=['mkT', 'mqs'], writes=[hk(2, mc)])
                    sc.add('act', lambda e, mc=mc: e.activation(pT[mc][:, :], half(2, mc), AF.Exp,
                                                                scale=128.0 ** -0.5),
                           reads=[hk(2, mc)], writes=[('pT', mc)])

                def mmn(e, hh=hh):
                    e.matmul(half(3, 0), mv[:, 0, hh * 128:(hh + 1) * 128], pT[0][:, :], start=True, stop=False)
                    return e.matmul(half(3, 0), mv[:, 1, hh * 128:(hh + 1) * 128], pT[1][:, :], start=False,
                                    stop=True)
                sc.add('pe', mmn, reads=['mv', ('pT', 0), ('pT', 1)], writes=[hk(3, 0)])

                def mmd(e):
                    e.matmul(half(3, 1), ones_bf[:, :], pT[0][:, :], start=True, stop=False)
                    return e.matmul(half(3, 1), ones_bf[:, :], pT[1][:, :], start=False, stop=True)
                sc.add('pe', mmd, reads=['ones_bf', ('pT', 0), ('pT', 1)], writes=[hk(3, 1)])
                sc.add('dve', lambda e: e.reciprocal(rden[:, :], half(3, 1)), reads=[hk(3, 1)], writes=['rden'])
                sc.add('dve', lambda e, s=s: e.tensor_tensor(tmp[s][:, :], half(3, 0), rden[:, :], ALU.mult),
                       reads=[hk(3, 0), 'rden'], writes=[('tmp', s)])
                zp, zk = zproj(1536 + hh * 128, b)
                sc.add('act', lambda e, zp=zp, s=s: e.activation(sil[s][:, :], zp, AF.Silu),
                       reads=[zk], writes=[('sil', s)])
                sc.add('pool', lambda e, hh=hh, s=s: e.tensor_tensor(yT[:, 8 + hh, :], tmp[s][:, :], sil[s][:, :],
                                                                    ALU.mult),
                       reads=[('tmp', s), ('sil', s)], writes=[('yT', 8 + hh)])
            for dc in range(8):
                for n in range(3):
                    pslot = [(4, 0), (4, 1), (5, 0)][n]
                    gslot = [(5, 1), (6, 0), (6, 1)][n]

                    def mmp(e, n=n, dc=dc, pslot=pslot):
                        r = None
                        for kc in range(4):
                            r = e.matmul(half(*pslot), Wbr[:, n * 4 + kc, dc * 128:(dc + 1) * 128],
                                         yT[:, n * 4 + kc, :], start=(kc == 0), stop=(kc == 3))
                        return r
                    sc.add('pe', mmp, reads=['Wbr'] + [('yT', n * 4 + kc) for kc in range(4)],
                           writes=[hk(*pslot)])

                    def mmg(e, n=n, dc=dc, gslot=gslot, b=b):
                        r = None
                        for k in range(8):
                            c0 = 2048 + n * 1024 + dc * 128
                            r = e.matmul(half(*gslot), Wz[:, k, c0:c0 + 128], ht[b][:, k, :], start=(k == 0),
                                         stop=(k == 7))
                        return r
                    sc.add('pe', mmg, reads=['Wz', ('ht', b)], writes=[hk(*gslot)])
                    sc.add('act', lambda e, n=n, dc=dc, gslot=gslot: e.activation(
                        gs[n][:, :], half(*gslot), AF.Sigmoid, bias=bm_sb[:, n * 8 + dc:n * 8 + dc + 1], scale=1.0),
                        reads=[hk(*gslot), ('par', 1)], writes=[('gs', n)])
                sc.add('dve', lambda e: e.tensor_tensor(acc[0][:, :], half(4, 0), gs[0][:, :], ALU.mult),
                       reads=[hk(4, 0), ('gs', 0)], writes=[('acc', 0)])
                sc.add('dve', lambda e: e.tensor_tensor(acc[1][:, :], half(4, 1), gs[1][:, :], ALU.mult),
                       reads=[hk(4, 1), ('gs', 1)], writes=[('acc', 1)])
                sc.add('pool', lambda e: e.tensor_tensor(acc[0][:, :], acc[0][:, :], acc[1][:, :], ALU.add),
                       reads=[('acc', 0), ('acc', 1)], writes=[('acc', 0)])
                sc.add('dve', lambda e: e.tensor_tensor(acc[1][:, :], half(5, 0), gs[2][:, :], ALU.mult),
                       reads=[hk(5, 0), ('gs', 2)], writes=[('acc', 1)])
                sc.add('pool', lambda e, dc=dc: e.tensor_tensor(mg[:, dc, :], acc[0][:, :], acc[1][:, :], ALU.add),
                       reads=[('acc', 0), ('acc', 1)], writes=[('mg', dc)])
            for dc in range(8):
                s = dc % 2

                def mmo(e, dc=dc, s=s):
                    r = None
                    for k in range(8):
                        r = e.matmul(half(7, s), Wout[:, k, dc * 128:(dc + 1) * 128], mg[:, k, :], start=(k == 0),
                                     stop=(k == 7))
                    return r
                sc.add('pe', mmo, reads=['Wout'] + [('mg', k) for k in range(8)], writes=[hk(7, s)])
                sc.add('dve', lambda e, dc=dc, s=s: e.tensor_tensor(xt[:, dc, :], xt[:, dc, :], half(7, s), ALU.add),
                       reads=['xt', hk(7, s)], writes=[('xn', dc)])
            allxn = [('xn', dc) for dc in range(8)]
            if not last:
                sc.add('sp', lambda e, t0=t0, t1=t1: e.dma_start(out=xov[:, :, t0:t1], in_=xt[:, :, :]),
                       reads=allxn, writes=[('xo', it)], slot='xo')
            for k in range(8):
                s = k % 2
                sc.add('act', lambda e, k=k, s=s: e.activation(sqs[s][:, :], xt[:, k, :], AF.Square),
                       reads=[('xn', k)], writes=[('sqs', s)])
                sc.add('pe', lambda e, k=k, s=s: e.matmul(half(1, 1), ones_f[:, :], sqs[s][:, :], start=(k == 0),
                                                         stop=(k == 7)),
                       reads=[('sqs', s), 'ones_f'], writes=[hk(1, 1)])
            sc.add('act', lambda e: e.activation(rstd[:, :], half(1, 1), AF.Sqrt, bias=EPS, scale=1.0 / D),
                   reads=[hk(1, 1)], writes=['rstd'])
            sc.add('dve', lambda e: e.reciprocal(rstd[:, :], rstd[:, :]), reads=['rstd'], writes=['rstd'])
            for k in range(8):
                sc.add('dve', lambda e, k=k: e.scalar_tensor_tensor(hout[:, k, :], xt[:, k, :], ng_sb[:, k:k + 1],
                                                                   rstd[:, :], ALU.mult, ALU.mult),
                       reads=[('xn', k), 'rstd', ('par', 3)], writes=['hout'])
            if last:
                sc.add('sp', lambda e, t0=t0, t1=t1: e.dma_start(out=xov[:, :, t0:t1], in_=hout[:, :, :]),
                       reads=['hout'], writes=[('xo', it)], slot='xo')
            else:
                sc.add('sp', lambda e, t0=t0, t1=t1: e.dma_start(out=hov[:, :, t0:t1], in_=hout[:, :, :]),
                       reads=['hout'], writes=[('ho', it)], slot='ho')
        sc.close()
    return nc


def mixer_inputs(c, hT, w_in_l, b_fg_l, conv_w_l, a_log_l, dt_bias_l):
    hd, half = c // 2, c % 2
    wf = np.concatenate([w_in_l[:, OFF['aq'] + c * 64:OFF['aq'] + (c + 1) * 64],
                         w_in_l[:, OFF['ak'] + c * 64:OFF['ak'] + (c + 1) * 64],
                         w_in_l[:, OFF['av'] + c * 64:OFF['av'] + (c + 1) * 64],
                         w_in_l[:, OFF['af'] + c:OFF['af'] + c + 1]], axis=1)
    vo = hd * 128 + half * 64
    wg = np.concatenate([w_in_l[:, OFF['bq'] + hd * 128:OFF['bq'] + (hd + 1) * 128],
                         w_in_l[:, OFF['bk'] + hd * 128:OFF['bk'] + (hd + 1) * 128],
                         w_in_l[:, OFF['bv'] + vo:OFF['bv'] + vo + 64],
                         w_in_l[:, OFF['ba'] + hd:OFF['ba'] + hd + 1],
                         w_in_l[:, OFF['bb'] + hd:OFF['bb'] + hd + 1]], axis=1)
    cw = np.zeros((128, 12), np.float32)
    cw[:, 0:4] = conv_w_l[:, hd * 128:(hd + 1) * 128].T
    cw[:, 4:8] = conv_w_l[:, 512 + hd * 128:512 + (hd + 1) * 128].T
    cw[0:64, 8:12] = conv_w_l[:, 1024 + vo:1024 + vo + 64].T
    gpar = np.empty((128, 2), np.float32)
    gpar[:, 0] = a_log_l[hd]
    gpar[:, 1] = dt_bias_l[hd]
    return dict(hT=hT, wf=np.ascontiguousarray(wf), bfg=np.full((128, 1), b_fg_l[c], np.float32),
                wg=np.ascontiguousarray(wg), cw=cw, gpar=gpar)


def _lay8(v):
    return np.ascontiguousarray(np.asarray(v, np.float32).reshape(-1, 128).T)


_PROGS = {}


def _prog(name, fn):
    if name not in _PROGS:
        _PROGS[name] = fn()
    return _PROGS[name]


def kernel(x, mem, norm_g, w_in, b_fg, b_merge, conv_w, a_log, dt_bias, gdn_norm_g, mem_norm_g, w_mem_kv,
           w_branch, w_out, final_norm_g):
    f = lambda a: np.asarray(a, np.float32)
    x, mem, norm_g, w_in, b_fg, b_merge, conv_w = map(f, (x, mem, norm_g, w_in, b_fg, b_merge, conv_w))
    a_log, dt_bias, gdn_norm_g, mem_norm_g = map(f, (a_log, dt_bias, gdn_norm_g, mem_norm_g))
    w_mem_kv, w_branch, w_out, final_norm_g = map(f, (w_mem_kv, w_branch, w_out, final_norm_g))
    S = x.shape[1]
    TS = S // NCORES
    cores = list(range(NCORES))
    xT = np.ascontiguousarray(x[0].T)
    memT = np.ascontiguousarray(mem[0].T)
    sh = lambda a, c: np.ascontiguousarray(a[:, c * TS:(c + 1) * TS])
    ncP = _prog('P', lambda: build_P(TS))
    res = run_bass_kernel_spmd(ncP, [dict(xT=sh(xT, c), g=_lay8(norm_g[0])) for c in cores], core_ids=cores)
    hT = np.concatenate([np.asarray(r["hT"]) for r in res.results], axis=1)
    depth = w_in.shape[0]
    for l in range(depth):
        last = (l == depth - 1)
        ncM = _prog('M', lambda: build_M(S))
        hTc = np.ascontiguousarray(hT)
        res = run_bass_kernel_spmd(
            ncM, [mixer_inputs(c, hTc, w_in[l], b_fg[l], conv_w[l], a_log[l], dt_bias[l]) for c in cores],
            core_ids=cores)
        oaT = np.concatenate([np.asarray(r["oaT"]) for r in res.results], axis=0)
        obT = np.concatenate([np.asarray(r["obT"]) for r in res.results], axis=0)
        ncT = _prog('T%d' % int(last), lambda: build_T(TS, last))
        ng = final_norm_g if last else norm_g[l + 1]
        maps = []
        for c in cores:
            maps.append(dict(xT=sh(xT, c), hT=sh(hT, c), oaT=sh(oaT, c), obT=sh(obT, c),
                             w_in=np.ascontiguousarray(w_in[l]),
                             w_br=np.ascontiguousarray(w_branch[l].reshape(1536, D)),
                             w_out=np.ascontiguousarray(w_out[l]), w_kv=np.ascontiguousarray(w_mem_kv[l]),
                             memT=memT, mem_g=_lay8(mem_norm_g[l]), b_mg=_lay8(b_merge[l]),
                             gdn_g=np.ascontiguousarray(gdn_norm_g[l].reshape(128, 1)), next_g=_lay8(ng)))
        res = run_bass_kernel_spmd(ncT, maps, core_ids=cores)
        xT = np.concatenate([np.asarray(r["xoT"]) for r in res.results], axis=1)
        if not last:
            hT = np.concatenate([np.asarray(r["hoT"]) for r in res.results], axis=1)
    out = np.ascontiguousarray(xT.T).reshape(1, S, D).astype(np.float32)
    return out
```

```python
import contextlib
import numpy as np
import ml_dtypes
import concourse.bass as bass
import concourse.mybir as mybir
from concourse.bass_utils import run_bass_kernel_spmd

F32 = mybir.dt.float32
BF16 = mybir.dt.bfloat16
AF = mybir.ActivationFunctionType
ALU = mybir.AluOpType

D = 1024
S_FULL = 16384
NCORES = 8
EPS = 1e-6
N_IN = 8208
import os as _os
SAME_ENGINE_SYNC = bool(int(_os.environ.get('SAME_SYNC', '1')))
OFF = dict(aq=0, ak=512, av=1024, af=1536, az=1544, bq=2056, bk=2568, bv=3080,
           ba=3592, bb=3596, bz=3600, mq=4112, mz=4624, gates=5136)


def _is_psum_key(k):
    if isinstance(k, str):
        return k.startswith('ps')
    if isinstance(k, tuple) and len(k) >= 2:
        return k[0] in ('pb', 'pS', 'pO') or k[1] == 'ps'
    return False


class Sched:
    ENGS = ['pe', 'act', 'dve', 'pool', 'sp']

    def __init__(self, nc, same_engine_sync=None):
        if same_engine_sync is None:
            same_engine_sync = SAME_ENGINE_SYNC
        self.nc = nc
        self.ops = []
        self.lastw = {}
        self.readers = {}
        self.slot_count = {}
        self.same = same_engine_sync
        self.stack = contextlib.ExitStack()
        self.esem = {e: self.stack.enter_context(nc.semaphore("sem_" + e)) for e in self.ENGS}
        self.ssem = {}
        self.cnt = {e: 0 for e in self.ENGS}

    def _needs_same(self, eng):
        if eng == 'pe':
            return False
        if eng == 'pool':
            return True
        return self.same

    def add(self, eng, fn, reads=(), writes=(), slot=None):
        op = dict(eng=eng, fn=fn, deps=[], slot=slot, inc=False, id=len(self.ops))
        deps = {}
        for k in reads:
            w = self.lastw.get(k)
            if w is not None:
                deps[w['id']] = w
            if _is_psum_key(k):
                for r in self.readers.get(k, ()):
                    if r['eng'] != eng:
                        deps[r['id']] = r
        for k in writes:
            w = self.lastw.get(k)
            if w is not None:
                deps[w['id']] = w
            for r in self.readers.get(k, ()):
                deps[r['id']] = r
        for d in deps.values():
            if d is op:
                continue
            op['deps'].append(d)
            if d['slot'] is None:
                if d['eng'] != eng or self._needs_same(eng) or slot is not None:
                    d['inc'] = True
        for k in writes:
            self.lastw[k] = op
            self.readers[k] = []
        for k in reads:
            self.readers.setdefault(k, []).append(op)
        if slot is not None:
            if slot not in self.ssem:
                self.ssem[slot] = self.stack.enter_context(self.nc.semaphore("sl_%d" % len(self.ssem)))
            self.slot_count[slot] = self.slot_count.get(slot, 0) + 1
            op['slot_val'] = self.slot_count[slot] * 16
        self.ops.append(op)
        return op

    def flush(self):
        nc = self.nc
        for op in self.ops:
            if op['slot'] is None and op['inc']:
                self.cnt[op['eng']] += 1
                op['count'] = self.cnt[op['eng']]
        ops = self.ops
        esem, ssem = self.esem, self.ssem
        final = dict(self.slot_count)
        with nc.Block() as block:
            def run(ename, eng):
                known = {}
                for op in ops:
                    if op['eng'] != ename:
                        continue
                    waits = {}
                    for d in op['deps']:
                        if d['slot'] is not None:
                            key = ('s', d['slot'])
                            v = d['slot_val']
                            sem = ssem[d['slot']]
                        else:
                            if d['eng'] == ename and op['slot'] is None and not self._needs_same(ename):
                                continue
                            key = ('e', d['eng'])
                            v = d['count']
                            sem = esem[d['eng']]
                        if waits.get(key, (None, -1))[1] < v:
                            waits[key] = (sem, v)
                    for key, (sem, v) in waits.items():
                        if known.get(key, -1) >= v:
                            continue
                        known[key] = v
                        eng.wait_ge(sem, v)
                    ins = op['fn'](eng)
                    if op['slot'] is not None:
                        ins.then_inc(ssem[op['slot']], 16)
                    elif op['inc']:
                        ins.then_inc(esem[ename], 1)
                if ename == 'sp':
                    for s, n in final.items():
                        eng.wait_ge(ssem[s], n * 16)

            block.tensor(lambda e: run('pe', e))
            block.scalar(lambda e: run('act', e))
            block.vector(lambda e: run('dve', e))
            block.gpsimd(lambda e: run('pool', e))
            block.sync(lambda e: run('sp', e))
        self.ops = []
        self.lastw = {}
        self.readers = {}

    def collective(self, kind, op, src_ap, dst_ap, reads=(), writes=(), slot='cc', ncores=NCORES):
        self.flush()
        if slot not in self.ssem:
            self.ssem[slot] = self.stack.enter_context(self.nc.semaphore("sl_%d" % len(self.ssem)))
        self.slot_count[slot] = self.slot_count.get(slot, 0) + 1
        ins = self.nc.gpsimd.collective_compute(kind, op, replica_groups=[list(range(ncores))],
                                                ins=[src_ap], outs=[dst_ap])
        ins.then_inc(self.ssem[slot], 16)
        pseudo = dict(eng='pool', fn=None, deps=[], slot=slot, inc=False, id=-1,
                      slot_val=self.slot_count[slot] * 16)
        for k in writes:
            self.lastw[k] = pseudo
            self.readers[k] = []

    def close(self):
        self.flush()
        self.stack.close()


_NAME = [0]


class Ctx:
    def __init__(self, nc):
        self.nc = nc
        self.st = contextlib.ExitStack()

    def sb(self, shape, dt, name=None):
        _NAME[0] += 1
        return self.st.enter_context(self.nc.sbuf_tensor(name or ("t%d" % _NAME[0]), list(shape), dt))

    def ps(self, shape, dt, name=None):
        _NAME[0] += 1
        return self.st.enter_context(self.nc.psum_tensor(name or ("p%d" % _NAME[0]), list(shape), dt))


def emit_rmsnorm(sc, x_sb, xkey, g_sb, ones_f, out_sb, outkey, TT, sq, ps, rstd, tag, dim=D, gkey='g'):
    for k in range(8):
        sc.add('act', lambda e, k=k: e.activation(sq[:, k, :], x_sb[:, k, :], AF.Square),
               reads=[xkey], writes=[(tag, 'sq', k)])

    def mm(e):
        r = None
        for k in range(8):
            r = e.matmul(ps[:, 0:TT], ones_f[:, :], sq[:, k, :], start=(k == 0), stop=(k == 7))
        return r
    sc.add('pe', mm, reads=[(tag, 'sq', k) for k in range(8)] + ['ones_f'], writes=[(tag, 'ps')])
    sc.add('act', lambda e: e.activation(rstd[:, :], ps[:, 0:TT], AF.Sqrt, bias=EPS, scale=1.0 / dim),
           reads=[(tag, 'ps')], writes=[(tag, 'rstd')])
    sc.add('dve', lambda e: e.reciprocal(rstd[:, :], rstd[:, :]),
           reads=[(tag, 'rstd')], writes=[(tag, 'rstd')])
    for k in range(8):
        sc.add('dve',
               lambda e, k=k: e.scalar_tensor_tensor(out_sb[:, k, :], x_sb[:, k, :], g_sb[:, k:k + 1],
                                                     rstd[:, :], ALU.mult, ALU.mult),
               reads=[xkey, (tag, 'rstd'), gkey], writes=[outkey])


def build_P(TS):
    nc = bass.Bass("TRN2", target_bir_lowering=False)
    xT = nc.dram_tensor("xT", [D, TS], F32, kind="ExternalInput").ap()
    g = nc.dram_tensor("g", [128, 8], F32, kind="ExternalInput").ap()
    hT = nc.dram_tensor("hT", [D, TS], BF16, kind="ExternalOutput").ap()
    TT = 512
    cx = Ctx(nc)
    with cx.st:
        sc = Sched(nc)
        ones_f = cx.sb([128, 128], F32)
        g_sb = cx.sb([128, 8], F32)
        xs = [cx.sb([128, 8, TT], F32) for _ in range(2)]
        hs = [cx.sb([128, 8, TT], BF16) for _ in range(2)]
        sq = cx.sb([128, 8, TT], F32)
        rstd = cx.sb([128, TT], F32)
        ps = cx.ps([128, 512], F32)
        sc.add('pool', lambda e: e.memset(ones_f[:, :], 1.0), writes=['ones_f'])
        sc.add('sp', lambda e: e.dma_start(out=g_sb[:, :], in_=g[:, :]), writes=['g'], slot='g')
        xv = xT.rearrange("(k p) t -> p k t", p=128)
        hv = hT.rearrange("(k p) t -> p k t", p=128)
        for i in range(TS // TT):
            b = i % 2
            sc.add('sp', lambda e, i=i, b=b: e.dma_start(out=xs[b][:, :, :], in_=xv[:, :, i * TT:(i + 1) * TT]),
                   writes=[('x', b)], slot=('x', b))
            emit_rmsnorm(sc, xs[b], ('x', b), g_sb, ones_f, hs[b], ('h', b), TT, sq, ps, rstd, 'n')
            sc.add('sp', lambda e, i=i, b=b: e.dma_start(out=hv[:, :, i * TT:(i + 1) * TT], in_=hs[b][:, :, :]),
                   reads=[('h', b)], writes=[('hout', i)], slot=('ho', b))
        sc.close()
    return nc


def make_consts(sc, cx):
    c = {}
    c['ones'] = cx.sb([128, 128], F32)
    c['ident'] = cx.sb([128, 128], F32)
    c['uincl'] = cx.sb([128, 128], F32)
    c['ustrict'] = cx.sb([128, 128], F32)
    c['e0'] = cx.sb([128, 128], F32)
    c['ones_bf'] = cx.sb([128, 128], BF16)
    c['ident_bf'] = cx.sb([128, 128], BF16)
    c['zeros'] = cx.sb([128, 128], F32)
    sc.add('pool', lambda e: e.memset(c['ones'][:, :], 1.0), writes=['c_ones'])
    sc.add('pool', lambda e: e.memset(c['zeros'][:, :], 0.0), writes=['c_zeros'])
    sc.add('pool', lambda e: e.memset(c['ones_bf'][:, :], 1.0), writes=['c_ones_bf'])
    sc.add('pool', lambda e: e.affine_select(c['ident'][:, :], c['zeros'][:, :], [[1, 128]], ALU.not_equal, 1.0,
                                             base=0, channel_multiplier=-1),
           reads=['c_zeros'], writes=['c_ident'])
    sc.add('pool', lambda e: e.tensor_copy(c['ident_bf'][:, :], c['ident'][:, :]),
           reads=['c_ident'], writes=['c_ident_bf'])
    sc.add('pool', lambda e: e.affine_select(c['uincl'][:, :], c['ones'][:, :], [[1, 128]], ALU.is_ge, 0.0,
                                             base=0, channel_multiplier=-1),
           reads=['c_ones'], writes=['c_uincl'])
    sc.add('pool', lambda e: e.affine_select(c['ustrict'][:, :], c['ones'][:, :], [[1, 128]], ALU.is_gt, 0.0,
                                             base=0, channel_multiplier=-1),
           reads=['c_ones'], writes=['c_ustrict'])
    sc.add('pool', lambda e: e.affine_select(c['e0'][:, :], c['ones'][:, :], [[0, 128]], ALU.is_ge, 0.0,
                                             base=0, channel_multiplier=-1),
           reads=['c_ones'], writes=['c_e0'])
    return c


def load_cast(sc, dst_bf, dstkey, src_ap, stage, stagekey, eng_dma='sp', eng_cast='pool', slot=None):
    sc.add(eng_dma, lambda e: e.dma_start(out=stage, in_=src_ap), writes=[stagekey], slot=slot or stagekey)
    sc.add(eng_cast, lambda e: e.tensor_copy(dst_bf, stage), reads=[stagekey], writes=[dstkey])


def fox_phase(nc, sc, cx0, c, S, hT, wf, bfg, oaT, scr, psb, stop=99):
    NT = S // 128
    NG = S // 512
    cx = Ctx(nc)
    with cx.st:
        wq = cx.sb([128, 8, 64], BF16)
        wk = cx.sb([128, 8, 64], BF16)
        wv = cx.sb([128, 8, 65], BF16)
        wst = cx.sb([128, 8, 193], F32)
        QT = cx.sb([65, S], BF16)
        KT = cx.sb([65, S], BF16)
        V = cx.sb([128, NT, 65], BF16)
        lfr = cx.sb([128, NT], F32)
        lfn = cx.sb([128, NT], F32)
        Fn = cx.sb([128, NT], F32)
        frefB = cx.sb([128, NG], F32)
        ctok = cx.sb([128, NT], F32)
        cTT = cx.sb([128, 128], BF16)
        totT = cx.sb([128, 1], F32)
        X = cx.sb([128, 128], F32)
        negb = cx.sb([128, 1], F32)
        biasg = [cx.sb([128, NT], F32) for _ in range(2)]
        ht = [cx.sb([128, 8, 512], BF16) for _ in range(2)]
        Pb = [cx.sb([128, 512], BF16) for _ in range(4)]
        oun = cx.sb([65, 512], F32)
        rl = cx.sb([65, 512], F32)
        ofin = [cx.sb([64, 512], F32) for _ in range(2)]

        sc.add('sp', lambda e: e.dma_start(out=wst[:, :, :], in_=wf.rearrange("(k p) c -> p k c", p=128)),
               writes=['wst'], slot='wst')
        sc.add('pool', lambda e: e.tensor_copy(wq[:, :, :], wst[:, :, 0:64]), reads=['wst'], writes=['wq'])
        sc.add('pool', lambda e: e.tensor_copy(wk[:, :, :], wst[:, :, 64:128]), reads=['wst'], writes=['wk'])
        sc.add('pool', lambda e: e.tensor_copy(wv[:, :, :], wst[:, :, 128:193]), reads=['wst'], writes=['wv'])
        sc.add('sp', lambda e: e.dma_start(out=negb[:, :], in_=bfg[:, :]), writes=['negb'], slot='negb')
        sc.add('dve', lambda e: e.tensor_scalar(negb[:, :], negb[:, :], -1.0, None, ALU.mult),
               reads=['negb'], writes=['negb'])
        sc.add('pool', lambda e: e.memset(KT[64:65, :], 1.0), writes=['KTrow'])
        sc.add('pool', lambda e: e.memset(V[:, :, 64:65], 1.0), writes=['Vones'])

        if stop <= 0:
            sc.flush()
            return
        hv = hT.rearrange("(k p) t -> p k t", p=128)
        psq, psk, psv = psb[0], psb[1], psb[2]
        for i in range(NG):
            b = i % 2
            sc.add('sp', lambda e, i=i, b=b: e.dma_start(out=ht[b][:, :, :], in_=hv[:, :, i * 512:(i + 1) * 512]),
                   writes=[('ht', b)], slot=('ht', b))

            def mmq(e, b=b):
                r = None
                for k in range(8):
                    r = e.matmul(psq[0:64, :], wq[:, k, :], ht[b][:, k, :], start=(k == 0), stop=(k == 7))
                return r
            import os
            DBG = int(os.environ.get('FOXDBG', '15'))
            if DBG & 1:
              sc.add('pe', mmq, reads=[('ht', b), 'wq'], writes=['psq'])
            if DBG & 1:
              sc.add('act', lambda e, i=i: e.activation(QT[0:64, i * 512:(i + 1) * 512], psq[0:64, :], AF.Copy,
                                                      scale=0.125),
                   reads=['psq'], writes=[('QT', i)])

            def mmk(e, b=b):
                r = None
                for k in range(8):
                    r = e.matmul(psk[0:64, :], wk[:, k, :], ht[b][:, k, :], start=(k == 0), stop=(k == 7))
                return r
            if DBG & 2:
              sc.add('pe', mmk, reads=[('ht', b), 'wk'], writes=['psk'])
              sc.add('dve', lambda e, i=i: e.tensor_copy(KT[0:64, i * 512:(i + 1) * 512], psk[0:64, :]),
                   reads=['psk'], writes=[('KT', i)])

            def mmv(e, b=b):
                r = None
                for sub in range(4):
                    for k in range(8):
                        r = e.matmul(psv[:, sub * 128:sub * 128 + 65], ht[b][:, k, sub * 128:(sub + 1) * 128],
                                     wv[:, k, :], start=(k == 0), stop=(k == 7))
                return r
            pv3 = psv[:, :].rearrange("p (s c) -> p s c", c=128)
            if DBG & 4:
              sc.add('pe', mmv, reads=[('ht', b), 'wv'], writes=['psv'])
              sc.add('dve', lambda e, i=i, pv3=pv3: e.tensor_copy(V[:, 4 * i:4 * i + 4, 0:64], pv3[:, :, 0:64]),
                   reads=['psv', 'Vones'], writes=[('V', i)])
            if DBG & 8:
              sc.add('dve', lambda e, i=i, pv3=pv3: e.tensor_copy(lfr[:, 4 * i:4 * i + 4], pv3[:, :, 64]),
                   reads=['psv'], writes=[('lfr', i)])

        if stop <= 1:
            sc.flush()
            return
        allfr = [('lfr', i) for i in range(NG)]
        sc.add('act', lambda e: e.activation(lfn[:, :], lfr[:, :], AF.Exp, bias=negb[:, 0:1], scale=-1.0),
               reads=allfr + ['negb'], writes=['lfn'])
        sc.add('act', lambda e: e.activation(lfn[:, :], lfn[:, :], AF.Ln, bias=1.0, scale=1.0),
               reads=['lfn'], writes=['lfn'])
        pt = psb[0]
        sc.add('pe', lambda e: e.matmul(pt[0:NT, 0:1], lfn[:, :], c['ones'][:, 0:1], start=True, stop=True),
               reads=['lfn', 'c_ones', 'psq'], writes=['psq'])
        sc.add('dve', lambda e: e.tensor_copy(totT[0:NT, :], pt[0:NT, 0:1]), reads=['psq'], writes=['totT'])
        sc.add('dve', lambda e: e.tensor_scalar(X[0:NT, 0:NT], c['ustrict'][0:NT, 0:NT], totT[0:NT, 0:1], None,
                                                ALU.mult),
               reads=['totT', 'c_ustrict'], writes=['X'])
        pf = psb[1]

        def mmF(e):
            e.matmul(pf[:, 0:NT], c['uincl'][:, :], lfn[:, :], start=True, stop=False)
            return e.matmul(pf[:, 0:NT], c['ones'][0:NT, :], X[0:NT, 0:NT], start=False, stop=True)
        sc.add('pe', mmF, reads=['lfn', 'X', 'c_uincl', 'c_ones', 'psk'], writes=['psk'])
        sc.add('dve', lambda e: e.tensor_copy(Fn[:, :], pf[:, 0:NT]), reads=['psk'], writes=['Fn'])
        pr = psb[2]
        sc.add('pe', lambda e: e.matmul(pr[:, 0:NG], c['e0'][:, :], Fn[:, 0:NT:4], start=True, stop=True),
               reads=['Fn', 'c_e0', 'psv'], writes=['psv'])
        sc.add('dve', lambda e: e.tensor_copy(frefB[:, :], pr[:, 0:NG]), reads=['psv'], writes=['frefB'])
        for r in range(4):
            sc.add('dve', lambda e, r=r: e.tensor_tensor(ctok[:, r:NT:4], frefB[:, :], Fn[:, r:NT:4], ALU.subtract),
                   reads=['frefB', 'Fn'], writes=[('ctok', r)])
        pc = psb[3]
        sc.add('pe', lambda e: e.transpose(pc[0:NT, 0:128], ctok[:, :], c['ident'][:, :]),
               reads=[('ctok', r) for r in range(4)] + ['c_ident'], writes=['ps3'])
        sc.add('dve', lambda e: e.tensor_copy(cTT[0:NT, :], pc[0:NT, 0:128]), reads=['ps3'], writes=['cTT'])
        sc.add('sp', lambda e: e.dma_start(out=scr[0:NT, :], in_=cTT[0:NT, :]), reads=['cTT'], writes=['scr'],
               slot='scr')
        sc.add('sp', lambda e: e.dma_start(out=QT[64:65, :], in_=scr[0:NT, :].rearrange("(o j) p -> o (j p)", o=1)),
               reads=['scr'], writes=['QTrow'], slot='qtrow')

        if stop <= 2:
            sc.flush()
            return
        sc.flush()
        LA = 3
        pS = [psb[0], psb[1], psb[2], psb[3]]
        pO = [psb[6], psb[7]]
        pbc = psb[4]
        blocks = []
        for g in range(NG):
            nj = 4 * g + 4
            for j in range(nj):
                r = j - 4 * g
                c0 = 0 if r < 0 else r * 128
                blocks.append((g, j, r, c0, 512 - c0, nj))
        NB = len(blocks)

        def emit_front(bi):
            g, j, r, c0, N, nj = blocks[bi]
            gb = g % 2
            sb_ = bi % 4
            if j == 0:
                sc.add('dve', lambda e: e.tensor_scalar(biasg[gb][:, 0:nj], Fn[:, 0:nj], frefB[:, g:g + 1], None,
                                                        ALU.subtract),
                       reads=['Fn', 'frefB'], writes=[('biasg', gb)])
            sc.add('pe', lambda e: e.matmul(pS[sb_][:, 0:N], KT[0:65, j * 128:(j + 1) * 128],
                                            QT[0:65, g * 512 + c0:(g + 1) * 512], start=True, stop=True),
                   reads=['QT', 'KT'], writes=[('pS', sb_)])
            sc.add('act', lambda e: e.activation(Pb[sb_][:, 0:N], pS[sb_][:, 0:N], AF.Exp,
                                                 bias=biasg[gb][:, j:j + 1], scale=1.0),
                   reads=[('pS', sb_), ('biasg', gb)], writes=[('P', sb_)])
            if r >= 0:
                sc.add('pool', lambda e: e.affine_select(Pb[sb_][:, 0:128], Pb[sb_][:, 0:128], [[1, 128]], ALU.is_ge,
                                                         0.0, base=0, channel_multiplier=-1),
                       reads=[('P', sb_)], writes=[('P', sb_)])

        def emit_back(bi):
            g, j, r, c0, N, nj = blocks[bi]
            gb = g % 2
            sb_ = bi % 4
            sc.add('pe', lambda e: e.matmul(pO[gb][0:65, c0:512], V[:, j, 0:65], Pb[sb_][:, 0:N], start=(j == 0),
                                            stop=(j == nj - 1), skip_group_check=True),
                   reads=[('P', sb_), 'V'], writes=[('pO', gb)])
            if j == nj - 1:
                sc.add('dve', lambda e: e.tensor_copy(oun[0:65, :], pO[gb][0:65, :]),
                       reads=[('pO', gb)], writes=['oun'])
                sc.add('dve', lambda e: e.reciprocal(rl[64:65, :], oun[64:65, :]), reads=['oun'], writes=['rl'])
                pending.append((bi + 6, g, gb))

        def emit_fin(g, gb):
            sc.add('pe', lambda e: e.matmul(pbc[0:64, :], c['ones'][64:65, 0:64], rl[64:65, :], start=True,
                                            stop=True),
                   reads=['rl', 'c_ones'], writes=[('pb', 4)])
            sc.add('dve', lambda e: e.tensor_tensor(ofin[gb][:, :], oun[0:64, :], pbc[0:64, :], ALU.mult),
                   reads=['oun', ('pb', 4)], writes=[('ofin', gb)])
            sc.add('sp', lambda e: e.dma_start(out=oaT[:, g * 512:(g + 1) * 512], in_=ofin[gb][:, :]),
                   reads=[('ofin', gb)], writes=[('oaT', g)], slot=('oa', gb))

        pending = []
        for bi in range(NB + LA):
            if bi < NB:
                emit_front(bi)
            if bi - LA >= 0:
                emit_back(bi - LA)
            while pending and pending[0][0] <= bi - LA:
                _, g_, gb_ = pending.pop(0)
                emit_fin(g_, gb_)
        for _, g_, gb_ in pending:
            emit_fin(g_, gb_)
        sc.flush()


def gdn_phase(nc, sc, c, S, hT, wg, cw, gpar, obT, psb):
    NSEG = S // 512
    A = sc.add
    cx = Ctx(nc)
    PB = lambda n: ('pb', n)
    with cx.st:
        f32t = lambda *sh: cx.sb(list(sh), F32)
        bft = lambda *sh: cx.sb(list(sh), BF16)
        wst = f32t(128, 8, 322)
        wq, wk, wv, wab = bft(128, 8, 128), bft(128, 8, 128), bft(128, 8, 64), bft(128, 8, 2)
        cw_sb, gp_sb = f32t(128, 12), f32t(128, 2)
        negA = f32t(128, 1)
        M_s, M_i = f32t(128, 4, 128), f32t(128, 4, 128)
        E63, E127, EL = f32t(128, 128), f32t(128, 128), f32t(128, 128)
        ht = [bft(128, 8, 512) for _ in range(2)]
        rq, rk, rv = f32t(128, 515), f32t(128, 515), f32t(64, 515)
        cq, ck = f32t(128, 512), f32t(128, 512)
        sq2, sq2b = bft(128, 512), bft(128, 512)
        rn, rnb = f32t(128, 512), f32t(128, 512)
        g_tok, G_tok, eG, ekd, glo = [f32t(128, 4) for _ in range(5)]
        diagG, EGrow, Dm, Gam, Gs, Gi = [f32t(128, 512) for _ in range(6)]
        ktok = f32t(128, 4, 128)
        Bb = [bft(128, 512) for _ in range(2)]
        Pb_ = [bft(128, 512) for _ in range(2)]
        S_f, S_b = f32t(128, 512), bft(128, 512)
        St_f, St_b = f32t(128, 64), bft(128, 64)
        vnew = bft(128, 64)
        o_sb = f32t(64, 512)

        A('sp', lambda e: e.dma_start(out=wst[:, :, :], in_=wg.rearrange("(k p) c -> p k c", p=128)),
          writes=['gwst'], slot='gwst')
        A('pool', lambda e: e.tensor_copy(wq[:, :, :], wst[:, :, 0:128]), reads=['gwst'], writes=['gwq'])
        A('pool', lambda e: e.tensor_copy(wk[:, :, :], wst[:, :, 128:256]), reads=['gwst'], writes=['gwk'])
        A('pool', lambda e: e.tensor_copy(wv[:, :, :], wst[:, :, 256:320]), reads=['gwst'], writes=['gwv'])
        A('pool', lambda e: e.tensor_copy(wab[:, :, :], wst[:, :, 320:322]), reads=['gwst'], writes=['gwab'])
        A('sp', lambda e: e.dma_start(out=cw_sb[:, :], in_=cw[:, :]), writes=['cw'], slot='cw')
        A('sp', lambda e: e.dma_start(out=gp_sb[:, :], in_=gpar[:, :]), writes=['gp'], slot='gp')
        A('act', lambda e: e.activation(negA[:, :], gp_sb[:, 0:1], AF.Exp), reads=['gp'], writes=['negA'])
        A('dve', lambda e: e.tensor_scalar(negA[:, :], negA[:, :], -1.0, None, ALU.mult), reads=['negA'],
          writes=['negA'])
        A('pool', lambda e: e.memset(M_s[:, :, :], 1.0), writes=['M_s'])
        A('pool', lambda e: e.memset(M_i[:, :, :], 1.0), writes=['M_i'])
        A('pool', lambda e: e.affine_select(M_s[:, :, :], M_s[:, :, :], [[0, 4], [1, 128]], ALU.is_gt, 0.0, base=0,
                                            channel_multiplier=-1), reads=['M_s'], writes=['M_s'])
        A('pool', lambda e: e.affine_select(M_i[:, :, :], M_i[:, :, :], [[0, 4], [1, 128]], ALU.is_ge, 0.0, base=0,
                                            channel_multiplier=-1), reads=['M_i'], writes=['M_i'])
        A('pool', lambda e: e.memset(M_s[0:64, :, 64:128], 0.0), reads=['M_s'], writes=['M_s'])
        A('pool', lambda e: e.memset(M_i[0:64, :, 64:128], 0.0), reads=['M_i'], writes=['M_i'])
        A('pool', lambda e: e.affine_select(E63[:, :], c['zeros'][:, :], [[0, 128]], ALU.not_equal, 1.0, base=-63,
                                            channel_multiplier=1), reads=['c_zeros'], writes=['E63'])
        A('pool', lambda e: e.affine_select(E127[:, :], c['zeros'][:, :], [[0, 128]], ALU.not_equal, 1.0, base=-127,
                                            channel_multiplier=1), reads=['c_zeros'], writes=['E127'])
        A('pool', lambda e: e.tensor_copy(EL[:, 0:64], E63[:, 0:64]), reads=['E63'], writes=['EL'])
        A('pool', lambda e: e.tensor_copy(EL[:, 64:128], E127[:, 64:128]), reads=['E127', 'EL'], writes=['EL'])
        A('pool', lambda e: e.memset(rq[:, 0:3], 0.0), writes=['rq'])
        A('pool', lambda e: e.memset(rk[:, 0:3], 0.0), writes=['rk'])
        A('pool', lambda e: e.memset(rv[:, 0:3], 0.0), writes=['rv'])
        A('pool', lambda e: e.memset(St_f[:, :], 0.0), writes=['St_f'])
        A('pool', lambda e: e.memset(St_b[:, :], 0.0), writes=['St_b'])

        hv = hT.rearrange("(k p) t -> p k t", p=128)
        ones, ident = c['ones'], c['ident']
        DEPTH = dict(qn_f=2, kn_f=2, qn_bf=2, kT_bf=2, cv=2, a_sb=2, b_sb=2, B_f=2, kg=2, vtok=2, bpos=2,
                     qdec=3, aqk=3, kdec=3, negbt=3, dl=3, dh=3, ybu=2, ywT=2)
        SHAPES = dict(qn_f=(F32, (128, 512)), kn_f=(F32, (128, 512)), qn_bf=(BF16, (128, 512)),
                      kT_bf=(BF16, (128, 512)), cv=(F32, (64, 512)), a_sb=(F32, (128, 4)), b_sb=(F32, (128, 4)),
                      B_f=(F32, (128, 512)), kg=(BF16, (128, 4, 128)), vtok=(BF16, (128, 4, 64)),
                      bpos=(F32, (128, 4)), qdec=(BF16, (128, 512)), aqk=(BF16, (128, 512)),
                      kdec=(BF16, (128, 4, 128)), negbt=(F32, (128, 4)), dl=(F32, (128, 4)), dh=(F32, (128, 4)),
                      ybu=(F32, (128, 4, 64)), ywT=(BF16, (128, 512)))
        ROT = {n: [cx.sb(list(SHAPES[n][1]), SHAPES[n][0]) for _ in range(DEPTH[n])] for n in DEPTH}

        def mkA(s, lst):
            def K(k):
                return (k, s % DEPTH[k]) if (isinstance(k, str) and k in DEPTH) else k

            def A_(eng, fn, reads=(), writes=(), slot=None):
                lst.append(((eng, fn), dict(reads=[K(k) for k in reads], writes=[K(k) for k in writes], slot=slot)))
            return A_

        def rb(s, n):
            return ROT[n][s % DEPTH[n]]

        def stage1(i, A):
            b = i % 2
            qn_f, kn_f, qn_bf, kT_bf, cv, a_sb, b_sb = [rb(i, n) for n in
                                                        ('qn_f', 'kn_f', 'qn_bf', 'kT_bf', 'cv', 'a_sb', 'b_sb')]
            A('sp', lambda e: e.dma_start(out=ht[b][:, :, :], in_=hv[:, :, i * 512:(i + 1) * 512]),
              writes=[('ght', b)], slot=('ght', b))
            for (w_, M, bank, raw, key) in ((wq, 128, 0, rq, 'rq'), (wk, 128, 1, rk, 'rk'), (wv, 64, 0, rv, 'rv')):
                def mm(e, w_=w_, M=M, bank=bank):
                    r = None
                    for k in range(8):
                        r = e.matmul(psb[bank][0:M, :], w_[:, k, :], ht[b][:, k, :], start=(k == 0), stop=(k == 7))
                    return r
                A('pe', mm, reads=[('ght', b), 'gwq', 'gwk', 'gwv'], writes=[PB(bank)])
                A('dve', lambda e, M=M, bank=bank, raw=raw: e.tensor_copy(raw[0:M, 3:515], psb[bank][0:M, :]),
                  reads=[PB(bank)], writes=[key])

            def mmab(e):
                r = None
                for sub in range(4):
                    for k in range(8):
                        r = e.matmul(psb[1][:, sub * 2:sub * 2 + 2], ht[b][:, k, sub * 128:(sub + 1) * 128],
                                     wab[:, k, :], start=(k == 0), stop=(k == 7))
                return r
            A('pe', mmab, reads=[('ght', b), 'gwab'], writes=[PB(1)])
            p3 = psb[1][:, 0:8].rearrange("p (s c) -> p s c", c=2)
            A('dve', lambda e: e.tensor_copy(a_sb[:, :], p3[:, :, 0]), reads=[PB(1)], writes=['a_sb'])
            A('dve', lambda e: e.tensor_copy(b_sb[:, :], p3[:, :, 1]), reads=[PB(1)], writes=['b_sb'])
            for which, (raw, cv_, M, key, ckey) in enumerate(((rq, cq, 128, 'rq', 'cq'), (rk, ck, 128, 'rk', 'ck'),
                                                              (rv, cv, 64, 'rv', 'cv'))):
                A('act', lambda e, raw=raw, cv_=cv_, M=M, which=which: e.activation(
                    cv_[0:M, :], raw[0:M, 0:512], AF.Copy, scale=cw_sb[0:M, which * 4:which * 4 + 1]),
                    reads=[key, 'cw'], writes=[ckey])
                for tap in range(1, 4):
                    A('dve', lambda e, raw=raw, cv_=cv_, M=M, which=which, tap=tap: e.scalar_tensor_tensor(
                        cv_[0:M, :], raw[0:M, tap:tap + 512], cw_sb[0:M, which * 4 + tap:which * 4 + tap + 1],
                        cv_[0:M, :], ALU.mult, ALU.add),
                        reads=[key, 'cw', ckey], writes=[ckey])
                A('pool', lambda e, raw=raw, M=M: e.tensor_copy(raw[0:M, 0:3], raw[0:M, 512:515]),
                  reads=[key, ckey], writes=[key])
                A('act', lambda e, cv_=cv_, M=M: e.activation(cv_[0:M, :], cv_[0:M, :], AF.Silu),
                  reads=[ckey], writes=[ckey])
            for (cv_, ckey, bank, outf, okey, mul) in ((cq, 'cq', 0, qn_f, 'qn_f', 128.0 ** -0.5),
                                                      (ck, 'ck', 1, kn_f, 'kn_f', 1.0)):
                sq_, rn_ = (sq2, rn) if bank == 0 else (sq2b, rnb)
                A('act', lambda e, cv_=cv_, sq_=sq_: e.activation(sq_[:, :], cv_[:, :], AF.Square), reads=[ckey],
                  writes=[('sq2', bank)])
                A('pe', lambda e, bank=bank, sq_=sq_: e.matmul(psb[bank][:, :], c['ones_bf'][:, :], sq_[:, :],
                                                               start=True, stop=True),
                  reads=[('sq2', bank), 'c_ones_bf'], writes=[PB(bank)])
                A('act', lambda e, bank=bank, rn_=rn_: e.activation(rn_[:, :], psb[bank][:, :], AF.Ln, bias=EPS,
                                                                    scale=1.0),
                  reads=[PB(bank)], writes=[('rn', bank)])
                A('act', lambda e, rn_=rn_: e.activation(rn_[:, :], rn_[:, :], AF.Exp, scale=-0.5),
                  reads=[('rn', bank)], writes=[('rn', bank)])
                A('dve', lambda e, cv_=cv_, outf=outf, mul=mul, rn_=rn_: e.scalar_tensor_tensor(
                    outf[:, :], cv_[:, :], mul, rn_[:, :], ALU.mult, ALU.mult), reads=[ckey, ('rn', bank)],
                    writes=[okey])
            A('act', lambda e: e.activation(qn_bf[:, :], qn_f[:, :], AF.Copy), reads=['qn_f'], writes=['qn_bf'])
            A('act', lambda e: e.activation(kT_bf[:, :], kn_f[:, :], AF.Copy), reads=['kn_f'], writes=['kT_bf'])

        def stage2(i, A):
            qn_f, kn_f, qn_bf, kT_bf, cv, a_sb, b_sb = [rb(i, n) for n in
                                                        ('qn_f', 'kn_f', 'qn_bf', 'kT_bf', 'cv', 'a_sb', 'b_sb')]
            B_f, kg, vtok, bpos = [rb(i, n) for n in ('B_f', 'kg', 'vtok', 'bpos')]
            qdec, aqk, kdec, negbt, dl, dh = [rb(i, n) for n in ('qdec', 'aqk', 'kdec', 'negbt', 'dl', 'dh')]
            A('act', lambda e: e.activation(g_tok[:, :], a_sb[:, :], AF.Exp, bias=gp_sb[:, 1:2], scale=1.0),
              reads=['a_sb', 'gp'], writes=['g_tok'])
            A('act', lambda e: e.activation(g_tok[:, :], g_tok[:, :], AF.Ln, bias=1.0, scale=1.0),
              reads=['g_tok'], writes=['g_tok'])
            A('dve', lambda e: e.tensor_scalar(g_tok[:, :], g_tok[:, :], negA[:, 0:1], None, ALU.mult),
              reads=['g_tok', 'negA'], writes=['g_tok'])
            A('act', lambda e: e.activation(bpos[:, :], b_sb[:, :], AF.Exp, scale=-1.0), reads=['b_sb'],
              writes=['bpos'])
            A('dve', lambda e: e.tensor_scalar(bpos[:, :], bpos[:, :], 1.0, None, ALU.add), reads=['bpos'],
              writes=['bpos'])
            A('dve', lambda e: e.reciprocal(bpos[:, :], bpos[:, :]), reads=['bpos'], writes=['bpos'])
            A('dve', lambda e: e.tensor_scalar(negbt[:, :], bpos[:, :], -1.0, None, ALU.mult), reads=['bpos'],
              writes=['negbt'])
            A('pe', lambda e: e.matmul(psb[2][:, 0:4], M_i[:, 0, :], g_tok[:, :], start=True, stop=True),
              reads=['g_tok', 'M_i'], writes=[PB(2)])
            A('dve', lambda e: e.tensor_copy(G_tok[:, :], psb[2][:, 0:4]), reads=[PB(2)], writes=['G_tok'])
            A('act', lambda e: e.activation(eG[:, :], G_tok[:, :], AF.Exp), reads=['G_tok'], writes=['eG'])
            A('pe', lambda e: e.matmul(psb[2][:, 0:4], EL[:, :], G_tok[:, :], start=True, stop=True),
              reads=['G_tok', 'EL'], writes=[PB(2)])
            A('dve', lambda e: e.tensor_tensor(glo[:, :], psb[2][:, 0:4], G_tok[:, :], ALU.subtract),
              reads=[PB(2), 'G_tok'], writes=['glo'])
            A('act', lambda e: e.activation(ekd[:, :], glo[:, :], AF.Exp), reads=['glo'], writes=['ekd'])
            A('pe', lambda e: e.matmul(psb[2][:, 0:4], E63[:, :], G_tok[:, :], start=True, stop=True),
              reads=['G_tok', 'E63'], writes=[PB(2)])
            A('dve', lambda e: e.tensor_copy(dl[:, :], psb[2][:, 0:4]), reads=[PB(2)], writes=['dl'])
            A('act', lambda e: e.activation(dl[:, :], dl[:, :], AF.Exp), reads=['dl'], writes=['dl'])
            A('pe', lambda e: e.matmul(psb[2][:, 0:4], E127[:, :], G_tok[:, :], start=True, stop=True),
              reads=['G_tok', 'E127'], writes=[PB(2)])
            A('dve', lambda e: e.tensor_copy(dh[:, :], psb[2][:, 0:4]), reads=[PB(2)], writes=['dh'])
            A('act', lambda e: e.activation(dh[:, :], dh[:, :], AF.Exp), reads=['dh'], writes=['dh'])
            for sub in range(4):
                A('dve', lambda e, sub=sub: e.tensor_scalar(diagG[:, sub * 128:(sub + 1) * 128], ident[:, :],
                                                            G_tok[:, sub:sub + 1], None, ALU.mult),
                  reads=['G_tok', 'c_ident'], writes=[('diagG', sub)])
                A('pe', lambda e, sub=sub: e.matmul(psb[3][:, sub * 128:(sub + 1) * 128], ones[:, :],
                                                    diagG[:, sub * 128:(sub + 1) * 128], start=True, stop=True),
                  reads=[('diagG', sub), 'c_ones'], writes=[PB(3)])
            A('act', lambda e: e.activation(EGrow[:, :], psb[3][:, :], AF.Exp), reads=[PB(3)], writes=['EGrow'])
            A('pool', lambda e: e.tensor_tensor(qdec[:, :], qn_f[:, :], EGrow[:, :], ALU.mult),
              reads=['qn_f', 'EGrow'], writes=['qdec'])
            for sub in range(4):
                A('dve', lambda e, sub=sub: e.tensor_scalar(Dm[:, sub * 128:(sub + 1) * 128],
                                                            psb[3][:, sub * 128:(sub + 1) * 128],
                                                            G_tok[:, sub:sub + 1], 0.0, ALU.subtract, ALU.min),
                  reads=[PB(3), 'G_tok'], writes=['Dm'])
            A('act', lambda e: e.activation(Gam[:, :], Dm[:, :], AF.Exp), reads=['Dm'], writes=['Gam'])
            A('pool', lambda e: e.tensor_tensor(Gs[:, :], Gam[:, :], M_s[:, :, :].rearrange("p s c -> p (s c)"),
                                                ALU.mult), reads=['Gam', 'M_s'], writes=['Gs'])
            A('pool', lambda e: e.tensor_tensor(Gi[:, :], Gam[:, :], M_i[:, :, :].rearrange("p s c -> p (s c)"),
                                                ALU.mult), reads=['Gam', 'M_i'], writes=['Gi'])
            for sub in range(4):
                A('pe', lambda e, sub=sub: e.transpose(psb[2][:, sub * 128:(sub + 1) * 128],
                                                       kn_f[:, sub * 128:(sub + 1) * 128], ident[:, :]),
                  reads=['kn_f', 'c_ident'], writes=[PB(2)])
            A('dve', lambda e: e.tensor_copy(ktok[:, :, :], psb[2][:, :].rearrange("p (s c) -> p s c", c=128)),
              reads=[PB(2)], writes=['ktok'])
            for sub in range(4):
                A('act', lambda e, sub=sub: e.activation(kg[:, sub, :], ktok[:, sub, :], AF.Copy,
                                                         scale=eG[:, sub:sub + 1]), reads=['ktok', 'eG'], writes=['kg'])
                A('act', lambda e, sub=sub: e.activation(kdec[:, sub, :], ktok[:, sub, :], AF.Copy,
                                                         scale=ekd[:, sub:sub + 1]), reads=['ktok', 'ekd'],
                  writes=['kdec'])
            for sub in range(4):
                A('pe', lambda e, sub=sub: e.transpose(psb[2][:, sub * 64:(sub + 1) * 64],
                                                       cv[0:64, sub * 128:(sub + 1) * 128], ident[0:64, 0:64]),
                  reads=['cv', 'c_ident'], writes=[PB(2)])
            A('dve', lambda e: e.tensor_copy(vtok[:, :, :], psb[2][:, 0:256].rearrange("p (s c) -> p s c", c=64)),
              reads=[PB(2)], writes=['vtok'])
            for sub in range(4):
                cs = slice(sub * 128, (sub + 1) * 128)
                A('pe', lambda e, cs=cs: e.matmul(psb[2][:, cs], kT_bf[:, cs], kT_bf[:, cs], start=True, stop=True),
                  reads=['kT_bf'], writes=[PB(2)])
                A('dve', lambda e, cs=cs, sub=sub: e.scalar_tensor_tensor(
                    B_f[:, cs], psb[2][:, cs], negbt[:, sub:sub + 1], Gs[:, cs], ALU.mult, ALU.mult),
                    reads=[PB(2), 'negbt', 'Gs'], writes=['B_f'])
            for sub in range(4):
                cs = slice(sub * 128, (sub + 1) * 128)
                A('pe', lambda e, cs=cs: e.matmul(psb[3][:, cs], kT_bf[:, cs], qn_bf[:, cs], start=True, stop=True),
                  reads=['kT_bf', 'qn_bf'], writes=[PB(3)])
            A('dve', lambda e: e.tensor_tensor(aqk[:, :], psb[3][:, :], Gi[:, :], ALU.mult), reads=[PB(3), 'Gi'],
              writes=['aqk'])

        def stage3(i, A):
            B_f, kg, vtok, bpos = [rb(i, n) for n in ('B_f', 'kg', 'vtok', 'bpos')]
            ybu, ywT = rb(i, 'ybu'), rb(i, 'ywT')
            A('act', lambda e: e.activation(Bb[0][:, :], B_f[:, :], AF.Copy), reads=['B_f'], writes=[('Bb', 0)])
            for sub in range(4):
                cs = slice(sub * 128, (sub + 1) * 128)
                A('pe', lambda e, cs=cs: e.transpose(psb[4][:, cs], B_f[:, cs], ident[:, :]),
                  reads=['B_f', 'c_ident'], writes=[PB(4)])
            A('dve', lambda e: e.tensor_copy(Pb_[0][:, :], psb[4][:, :]), reads=[PB(4)], writes=[('Pb', 0)])
            for sub in range(4):
                cs = slice(sub * 128, (sub + 1) * 128)
                A('pool', lambda e, cs=cs: e.tensor_tensor(S_f[:, cs], B_f[:, cs], ident[:, :], ALU.add),
                  reads=['B_f', 'c_ident'], writes=['S_f'])
            A('act', lambda e: e.activation(S_b[:, :], S_f[:, :], AF.Copy), reads=['S_f'], writes=['S_b'])
            for j in range(5):
                cur, nxt = j % 2, (j + 1) % 2
                for sub in range(4):
                    cs = slice(sub * 128, (sub + 1) * 128)
                    A('pe', lambda e, cs=cs, cur=cur: e.matmul(psb[5][:, cs], Pb_[cur][:, cs], Bb[cur][:, cs],
                                                               start=True, stop=True),
                      reads=[('Pb', cur), ('Bb', cur)], writes=[PB(5)])
                A('dve', lambda e, nxt=nxt: e.tensor_copy(Bb[nxt][:, :], psb[5][:, :]), reads=[PB(5)],
                  writes=[('Bb', nxt)])
                for sub in range(4):
                    cs = slice(sub * 128, (sub + 1) * 128)
                    A('pe', lambda e, cs=cs, cur=cur: e.matmul(psb[4][:, cs], Bb[cur][:, cs], Pb_[cur][:, cs],
                                                               start=True, stop=True),
                      reads=[('Pb', cur), ('Bb', cur)], writes=[PB(4)])
                A('act', lambda e, nxt=nxt: e.activation(Pb_[nxt][:, :], psb[4][:, :], AF.Copy), reads=[PB(4)],
                  writes=[('Pb', nxt)])
                for sub in range(4):
                    cs = slice(sub * 128, (sub + 1) * 128)
                    A('pe', lambda e, cs=cs, nxt=nxt: e.matmul(psb[5][:, cs], Pb_[nxt][:, cs], S_b[:, cs],
                                                               start=True, stop=True),
                      reads=[('Pb', nxt), 'S_b'], writes=[PB(5)])
                A('dve', lambda e: e.tensor_tensor(S_f[:, :], S_f[:, :], psb[5][:, :], ALU.add),
                  reads=['S_f', PB(5)], writes=['S_f'])
                A('act', lambda e: e.activation(S_b[:, :], S_f[:, :], AF.Copy), reads=['S_f'], writes=['S_b'])
            for sub in range(4):
                cs = slice(sub * 128, (sub + 1) * 128)
                A('pe', lambda e, cs=cs, sub=sub: e.matmul(psb[4][:, sub * 64:(sub + 1) * 64], S_b[:, cs],
                                                           vtok[:, sub, :], start=True, stop=True),
                  reads=['S_b', 'vtok'], writes=[PB(4)])
            for sub in range(4):
                A('dve', lambda e, sub=sub: e.tensor_scalar(ybu[:, sub, :], psb[4][:, sub * 64:(sub + 1) * 64],
                                                            bpos[:, sub:sub + 1], None, ALU.mult),
                  reads=[PB(4), 'bpos'], writes=['ybu'])
            for sub in range(4):
                cs = slice(sub * 128, (sub + 1) * 128)
                A('pe', lambda e, cs=cs, sub=sub: e.matmul(psb[5][:, cs], kg[:, sub, :], S_b[:, cs], start=True,
                                                           stop=True),
                  reads=['S_b', 'kg'], writes=[PB(5)])
            A('dve', lambda e: e.tensor_copy(ywT[:, :], psb[5][:, :]), reads=[PB(5)], writes=['ywT'])

        def stage4(i, A):
            qdec, aqk, kdec, negbt, dl, dh = [rb(i, n) for n in ('qdec', 'aqk', 'kdec', 'negbt', 'dl', 'dh')]
            ybu, ywT = rb(i, 'ybu'), rb(i, 'ywT')
            for ch in range(8):
                sub, hf = ch // 2, ch % 2
                rs = slice(hf * 64, hf * 64 + 64)
                cs = slice(sub * 128, (sub + 1) * 128)
                cc = slice(ch * 64, (ch + 1) * 64)
                A('pe', lambda e, cs=cs: e.matmul(psb[6][:, 0:64], ywT[:, cs], St_b[:, :], start=True, stop=True),
                  reads=['ywT', 'St_b'], writes=[PB(6)])
                A('dve', lambda e, rs=rs, sub=sub: e.scalar_tensor_tensor(
                    vnew[rs, :], psb[6][rs, 0:64], negbt[rs, sub:sub + 1], ybu[rs, sub, :], ALU.mult, ALU.add),
                    reads=[PB(6), 'negbt', 'ybu'], writes=['vnew'])

                def mmo(e, cc=cc, rs=rs):
                    e.matmul(psb[7][0:64, cc], St_b[:, :], qdec[:, cc], start=True, stop=False)
                    return e.matmul(psb[7][0:64, cc], vnew[rs, :], aqk[rs, cc], start=False, stop=True)
                A('pe', mmo, reads=['St_b', 'qdec', 'vnew', 'aqk'], writes=[PB(7)])
                A('pe', lambda e, rs=rs, sub=sub: e.matmul(psb[6][:, 64:128], kdec[rs, sub, :], vnew[rs, :],
                                                           start=True, stop=True),
                  reads=['kdec', 'vnew'], writes=[PB(6)])
                dsc = (dl if hf == 0 else dh)
                A('dve', lambda e, sub=sub, dsc=dsc: e.scalar_tensor_tensor(
                    St_b[:, :], St_f[:, :], dsc[:, sub:sub + 1], psb[6][:, 64:128], ALU.mult, ALU.add),
                    reads=['St_f', PB(6), 'dl', 'dh'], writes=['St_b'])
                A('dve', lambda e, sub=sub, dsc=dsc: e.scalar_tensor_tensor(
                    St_f[:, :], St_f[:, :], dsc[:, sub:sub + 1], psb[6][:, 64:128], ALU.mult, ALU.add),
                    reads=['St_f', PB(6), 'dl', 'dh'], writes=['St_f'])
            A('act', lambda e: e.activation(o_sb[:, :], psb[7][0:64, :], AF.Copy), reads=[PB(7)], writes=['o_sb'])
            A('sp', lambda e: e.dma_start(out=obT[:, i * 512:(i + 1) * 512], in_=o_sb[:, :]),
              reads=['o_sb'], writes=[('obT', i)], slot='ob')

        def merge_lists(lists):
            lists = [l for l in lists if l]
            pos = [0] * len(lists)
            out = []
            while True:
                best, bf = None, None
                for li, l in enumerate(lists):
                    if pos[li] < len(l):
                        fr = pos[li] / len(l)
                        if bf is None or fr < bf:
                            best, bf = li, fr
                if best is None:
                    break
                out.append(lists[best][pos[best]])
                pos[best] += 1
            return out

        stages = (stage1, stage2, stage3, stage4)
        for t in range(NSEG + 3):
            lists = []
            for si, st_ in enumerate(stages):
                s = t - si
                if 0 <= s < NSEG:
                    lst = []
                    st_(s, mkA(s, lst))
                    lists.append(lst)
            for (a_, k_) in merge_lists(lists[::-1]):
                sc.add(*a_, **k_)
        sc.flush()


def build_M(S, do_fox=True, do_gdn=True, stop=99):
    nc = bass.Bass("TRN2", target_bir_lowering=False)
    hT = nc.dram_tensor("hT", [D, S], BF16, kind="ExternalInput").ap()
    wf = nc.dram_tensor("wf", [D, 193], F32, kind="ExternalInput").ap()
    bfg = nc.dram_tensor("bfg", [128, 1], F32, kind="ExternalInput").ap()
    wg = nc.dram_tensor("wg", [D, 322], F32, kind="ExternalInput").ap()
    cw = nc.dram_tensor("cw", [128, 12], F32, kind="ExternalInput").ap()
    gpar = nc.dram_tensor("gpar", [128, 2], F32, kind="ExternalInput").ap()
    oaT = nc.dram_tensor("oaT", [64, S], F32, kind="ExternalOutput").ap()
    obT = nc.dram_tensor("obT", [64, S], F32, kind="ExternalOutput").ap()
    scr = nc.dram_tensor("scr", [128, 128], BF16).ap()
    cx = Ctx(nc)
    with cx.st:
        sc = Sched(nc)
        c = make_consts(sc, cx)
        psb = [cx.ps([128, 512], F32) for _ in range(8)]
        if do_gdn:
            gdn_phase(nc, sc, c, S, hT, wg, cw, gpar, obT, psb)
        if do_fox:
            fox_phase(nc, sc, cx, c, S, hT, wf, bfg, oaT, scr, psb, stop=stop)
        sc.close()
    return nc


def build_T(TS, last):
    nc = bass.Bass("TRN2", target_bir_lowering=False)
    TT = 256
    NTT = TS // TT
    xT = nc.dram_tensor("xT", [D, TS], F32, kind="ExternalInput").ap()
    hT = nc.dram_tensor("hT", [D, TS], BF16, kind="ExternalInput").ap()
    oaT = nc.dram_tensor("oaT", [512, TS], F32, kind="ExternalInput").ap()
    obT = nc.dram_tensor("obT", [512, TS], F32, kind="ExternalInput").ap()
    w_in = nc.dram_tensor("w_in", [D, N_IN], F32, kind="ExternalInput").ap()
    w_br = nc.dram_tensor("w_br", [1536, D], F32, kind="ExternalInput").ap()
    w_out = nc.dram_tensor("w_out", [D, D], F32, kind="ExternalInput").ap()
    w_kv = nc.dram_tensor("w_kv", [D, 1024], F32, kind="ExternalInput").ap()
    memT = nc.dram_tensor("memT", [D, 256], F32, kind="ExternalInput").ap()
    mem_g = nc.dram_tensor("mem_g", [128, 8], F32, kind="ExternalInput").ap()
    b_mg = nc.dram_tensor("b_mg", [128, 24], F32, kind="ExternalInput").ap()
    gdn_g = nc.dram_tensor("gdn_g", [128, 1], F32, kind="ExternalInput").ap()
    next_g = nc.dram_tensor("next_g", [128, 8], F32, kind="ExternalInput").ap()
    xoT = nc.dram_tensor("xoT", [D, TS], F32, kind="ExternalOutput").ap()
    if not last:
        hoT = nc.dram_tensor("hoT", [D, TS], BF16, kind="ExternalOutput").ap()
    cx = Ctx(nc)
    with cx.st:
        sc = Sched(nc)
        ones_f = cx.sb([128, 128], F32)
        ones_bf = cx.sb([128, 128], BF16)
        sc.add('pool', lambda e: e.memset(ones_f[:, :], 1.0), writes=['ones_f'])
        sc.add('pool', lambda e: e.memset(ones_bf[:, :], 1.0), writes=['ones_bf'])
        psb = [cx.ps([128, 512], F32) for _ in range(8)]

        BM = {(0, 0): 0, (0, 1): 1, (1, 0): 2, (1, 1): 2, (2, 0): 3, (2, 1): 4, (3, 0): 5, (3, 1): 6,
              (4, 0): 2, (4, 1): 3, (5, 0): 4, (5, 1): 5, (6, 0): 6, (6, 1): 7, (7, 0): 0, (7, 1): 1}

        def half(bk, h):
            return psb[BM[(bk, h)]][:, 0:TT]

        def hk(bk, h):
            return ('pb', BM[(bk, h)])
        Wz = cx.sb([128, 8, 5120], BF16)
        Wbr = cx.sb([128, 12, 1024], BF16)
        Wout = cx.sb([128, 8, 1024], BF16)
        mkT = cx.sb([128, 4, 256], BF16)
        mv = cx.sb([128, 2, 512], BF16)
        stage = [cx.sb([128, 1024], F32) for _ in range(2)]
        memg_sb = cx.sb([128, 8], F32)
        bm_sb = cx.sb([128, 24], F32)
        gg_sb = cx.sb([128, 1], F32)
        ng_sb = cx.sb([128, 8], F32)
        for i, (dst, srcap) in enumerate([(memg_sb, mem_g), (bm_sb, b_mg), (gg_sb, gdn_g), (ng_sb, next_g)]):
            sc.add('sp', lambda e, dst=dst, srcap=srcap: e.dma_start(out=dst[:, :], in_=srcap[:, :]),
                   writes=[('par', i)], slot=('par', i))
        nst = [0]

        def ldw(dst, dkey, srcap):
            b = nst[0] % 2
            nst[0] += 1
            wd = srcap.shape[-1]
            sc.add('sp', lambda e: e.dma_start(out=stage[b][:, 0:wd], in_=srcap), writes=[('stage', b)],
                   slot=('stage', b))
            sc.add('pool' if b else 'dve', lambda e: e.tensor_copy(dst, stage[b][:, 0:wd]),
                   reads=[('stage', b)], writes=[dkey])

        pcx = Ctx(nc)
        with pcx.st:
            Wkv = pcx.sb([128, 8, 1024], BF16)
            mt = pcx.sb([128, 8, 256], F32)
            mn = pcx.sb([128, 8, 256], BF16)
            sqm = pcx.sb([128, 8, 256], F32)
            rstm = pcx.sb([128, 256], F32)
            for k in range(8):
                ldw(Wkv[:, k, :], 'Wkv', w_kv[k * 128:(k + 1) * 128, :])
            sc.add('sp', lambda e: e.dma_start(out=mt[:, :, :], in_=memT.rearrange("(k p) m -> p k m", p=128)),
                   writes=['mt'], slot='mt')
            sc.ops[-1]
            saved = {'g': None}
            emit_rmsnorm(sc, mt, 'mt', memg_sb, ones_f, mn, 'mn', 256, sqm, psb[0], rstm, 'mnorm', gkey=('par', 0))
            for hh in range(4):
                def mmk(e, hh=hh):
                    r = None
                    for k in range(8):
                        r = e.matmul(half(1, hh % 2), Wkv[:, k, hh * 128:(hh + 1) * 128], mn[:, k, :],
                                     start=(k == 0), stop=(k == 7))
                    return r
                sc.add('pe', mmk, reads=['Wkv', 'mn'], writes=[hk(1, hh % 2)])
                sc.add('dve', lambda e, hh=hh: e.tensor_copy(mkT[:, hh, :], half(1, hh % 2)),
                       reads=[hk(1, hh % 2)], writes=['mkT'])
            for mc in range(2):
                def mmv(e, mc=mc):
                    r = None
                    for k in range(8):
                        r = e.matmul(psb[2 + mc][:, :], mn[:, k, mc * 128:(mc + 1) * 128], Wkv[:, k, 512:1024],
                                     start=(k == 0), stop=(k == 7))
                    return r
                sc.add('pe', mmv, reads=['Wkv', 'mn'], writes=[('pb', 2 + mc)])
                sc.add('dve', lambda e, mc=mc: e.tensor_copy(mv[:, mc, :], psb[2 + mc][:, :]),
                       reads=[('pb', 2 + mc)], writes=['mv'])
            sc.flush()

        for k in range(8):
            ldw(Wz[:, k, 0:512], 'Wz', w_in[k * 128:(k + 1) * 128, OFF['az']:OFF['az'] + 512])
            ldw(Wz[:, k, 512:1024], 'Wz', w_in[k * 128:(k + 1) * 128, OFF['bz']:OFF['bz'] + 512])
            for cb in range(4):
                ldw(Wz[:, k, 1024 + cb * 1024:2048 + cb * 1024], 'Wz',
                    w_in[k * 128:(k + 1) * 128, OFF['mq'] + cb * 1024:OFF['mq'] + (cb + 1) * 1024])
        for k in range(12):
            ldw(Wbr[:, k, :], 'Wbr', w_br[k * 128:(k + 1) * 128, :])
        for k in range(8):
            ldw(Wout[:, k, :], 'Wout', w_out[k * 128:(k + 1) * 128, :])

        ht = [cx.sb([128, 8, TT], BF16) for _ in range(2)]
        xt = cx.sb([128, 8, TT], F32)
        oat = cx.sb([128, 4, TT], F32)
        obt = cx.sb([128, 4, TT], F32)
        yT = cx.sb([128, 12, TT], BF16)
        mg = cx.sb([128, 8, TT], BF16)
        hout = cx.sb([128, 8, TT], BF16 if not last else F32)
        sqs = [cx.sb([128, TT], F32) for _ in range(2)]
        sil = [cx.sb([128, TT], F32) for _ in range(2)]
        tmp = [cx.sb([128, TT], F32) for _ in range(2)]
        rstd = cx.sb([128, TT], F32)
        rden = cx.sb([128, TT], F32)
        mqs = cx.sb([128, TT], BF16)
        pT = [cx.sb([128, TT], BF16) for _ in range(2)]
        gs = [cx.sb([128, TT], F32) for _ in range(3)]
        acc = [cx.sb([128, TT], F32) for _ in range(2)]
        hv = hT.rearrange("(k p) t -> p k t", p=128)
        xv = xT.rearrange("(k p) t -> p k t", p=128)
        oav = oaT.rearrange("(k p) t -> p k t", p=128)
        obv = obT.rearrange("(k p) t -> p k t", p=128)
        xov = xoT.rearrange("(k p) t -> p k t", p=128)
        if not last:
            hov = hoT.rearrange("(k p) t -> p k t", p=128)

        zcnt = [0]

        def zproj(col0, b):
            s = zcnt[0] % 2
            zcnt[0] += 1
            dst = half(0, s)

            def mm(e):
                r = None
                for k in range(8):
                    r = e.matmul(dst, Wz[:, k, col0:col0 + 128], ht[b][:, k, :], start=(k == 0), stop=(k == 7))
                return r
            sc.add('pe', mm, reads=['Wz', ('ht', b)], writes=[hk(0, s)])
            return dst, hk(0, s)

        for it in range(NTT):
            b = it % 2
            t0, t1 = it * TT, (it + 1) * TT
            sc.add('sp', lambda e, b=b, t0=t0, t1=t1: e.dma_start(out=ht[b][:, :, :], in_=hv[:, :, t0:t1]),
                   writes=[('ht', b)], slot=('ht', b))
            sc.add('sp', lambda e, t0=t0, t1=t1: e.dma_start(out=xt[:, :, :], in_=xv[:, :, t0:t1]),
                   writes=['xt'] + [('xn', dc) for dc in range(8)], slot='xt')
            sc.add('sp', lambda e, t0=t0, t1=t1: e.dma_start(out=oat[:, :, :], in_=oav[:, :, t0:t1]),
                   writes=['oat'], slot='oat')
            sc.add('sp', lambda e, t0=t0, t1=t1: e.dma_start(out=obt[:, :, :], in_=obv[:, :, t0:t1]),
                   writes=['obt'], slot='obt')
            for fc in range(4):
                zp, zk = zproj(fc * 128, b)
                s = fc % 2
                sc.add('act', lambda e, zp=zp, s=s: e.activation(sil[s][:, :], zp, AF.Silu),
                       reads=[zk], writes=[('sil', s)])
                sc.add('pool', lambda e, fc=fc, s=s: e.tensor_tensor(yT[:, fc, :], oat[:, fc, :], sil[s][:, :],
                                                                    ALU.mult),
                       reads=['oat', ('sil', s)], writes=[('yT', fc)])
            for hd in range(4):
                s = hd % 2
                sc.add('act', lambda e, hd=hd, s=s: e.activation(sqs[s][:, :], obt[:, hd, :], AF.Square),
                       reads=['obt'], writes=[('sqs', s)])
                sc.add('pe', lambda e, s=s: e.matmul(half(1, 0), ones_f[:, :], sqs[s][:, :], start=True, stop=True),
                       reads=[('sqs', s), 'ones_f'], writes=[hk(1, 0)])
                sc.add('act', lambda e: e.activation(rstd[:, :], half(1, 0), AF.Sqrt, bias=EPS, scale=1.0 / 128),
                       reads=[hk(1, 0)], writes=['rstd'])
                sc.add('dve', lambda e: e.reciprocal(rstd[:, :], rstd[:, :]), reads=['rstd'], writes=['rstd'])
                sc.add('dve', lambda e, hd=hd, s=s: e.scalar_tensor_tensor(tmp[s][:, :], obt[:, hd, :],
                                                                          gg_sb[:, 0:1], rstd[:, :],
                                                                          ALU.mult, ALU.mult),
                       reads=['obt', 'rstd', ('par', 2)], writes=[('tmp', s)])
                zp, zk = zproj(512 + hd * 128, b)
                sc.add('act', lambda e, zp=zp, s=s: e.activation(sil[s][:, :], zp, AF.Silu),
                       reads=[zk], writes=[('sil', s)])
                sc.add('pool', lambda e, hd=hd, s=s: e.tensor_tensor(yT[:, 4 + hd, :], tmp[s][:, :], sil[s][:, :],
                                                                    ALU.mult),
                       reads=[('tmp', s), ('sil', s)], writes=[('yT', 4 + hd)])
            for hh in range(4):
                s = hh % 2
                zp, zk = zproj(1024 + hh * 128, b)
                sc.add('dve', lambda e, zp=zp: e.tensor_copy(mqs[:, :], zp), reads=[zk], writes=['mqs'])
                for mc in range(2):
                    sc.add('pe', lambda e, hh=hh, mc=mc: e.matmul(half(2, mc), mkT[:, hh, mc * 128:(mc + 1) * 128],
                                                                 mqs[:, :], start=True, stop=True),
                           reads=['mkT', 'mqs'], writes=[hk(2, mc)])
                    sc.add('act', lambda e, mc=mc: e.activation(pT[mc][:, :], half(2, mc), AF.Exp,
                                                                scale=128.0 ** -0.5),
                           reads=[hk(2, mc)], writes=[('pT', mc)])

                def mmn(e, hh=hh):
                    e.matmul(half(3, 0), mv[:, 0, hh * 128:(hh + 1) * 128], pT[0][:, :], start=True, stop=False)
                    return e.matmul(half(3, 0), mv[:, 1, hh * 128:(hh + 1) * 128], pT[1][:, :], start=False,
                                    stop=True)
                sc.add('pe', mmn, reads=['mv', ('pT', 0), ('pT', 1)], writes=[hk(3, 0)])

                def mmd(e):
                    e.matmul(half(3, 1), ones_bf[:, :], pT[0][:, :], start=True, stop=False)
                    return e.matmul(half(3, 1), ones_bf[:, :], pT[1][:, :], start=False, stop=True)
                sc.add('pe', mmd, reads=['ones_bf', ('pT', 0), ('pT', 1)], writes=[hk(3, 1)])
                sc.add('dve', lambda e: e.reciprocal(rden[:, :], half(3, 1)), reads=[hk(3, 1)], writes=['rden'])
                sc.add('dve', lambda e, s=s: e.tensor_tensor(tmp[s][:, :], half(3, 0), rden[:, :], ALU.mult),
                       reads=[hk(3, 0), 'rden'], writes=[('tmp', s)])
                zp, zk = zproj(1536 + hh * 128, b)
                sc.add('act', lambda e, zp=zp, s=s: e.activation(sil[s][:, :], zp, AF.Silu),
                       reads=[zk], writes=[('sil', s)])
                sc.add('pool', lambda e, hh=hh, s=s: e.tensor_tensor(yT[:, 8 + hh, :], tmp[s][:, :], sil[s][:, :],
                                                                    ALU.mult),
                       reads=[('tmp', s), ('sil', s)], writes=[('yT', 8 + hh)])
            for dc in range(8):
                for n in range(3):
                    pslot = [(4, 0), (4, 1), (5, 0)][n]
                    gslot = [(5, 1), (6, 0), (6, 1)][n]

                    def mmp(e, n=n, dc=dc, pslot=pslot):
                        r = None
                        for kc in range(4):
                            r = e.matmul(half(*pslot), Wbr[:, n * 4 + kc, dc * 128:(dc + 1) * 128],
                                         yT[:, n * 4 + kc, :], start=(kc == 0), stop=(kc == 3))
                        return r
                    sc.add('pe', mmp, reads=['Wbr'] + [('yT', n * 4 + kc) for kc in range(4)],
                           writes=[hk(*pslot)])

                    def mmg(e, n=n, dc=dc, gslot=gslot, b=b):
                        r = None
                        for k in range(8):
                            c0 = 2048 + n * 1024 + dc * 128
                            r = e.matmul(half(*gslot), Wz[:, k, c0:c0 + 128], ht[b][:, k, :], start=(k == 0),
                                         stop=(k == 7))
                        return r
                    sc.add('pe', mmg, reads=['Wz', ('ht', b)], writes=[hk(*gslot)])
                    sc.add('act', lambda e, n=n, dc=dc, gslot=gslot: e.activation(
                        gs[n][:, :], half(*gslot), AF.Sigmoid, bias=bm_sb[:, n * 8 + dc:n * 8 + dc + 1], scale=1.0),
                        reads=[hk(*gslot), ('par', 1)], writes=[('gs', n)])
                sc.add('dve', lambda e: e.tensor_tensor(acc[0][:, :], half(4, 0), gs[0][:, :], ALU.mult),
                       reads=[hk(4, 0), ('gs', 0)], writes=[('acc', 0)])
                sc.add('dve', lambda e: e.tensor_tensor(acc[1][:, :], half(4, 1), gs[1][:, :], ALU.mult),
                       reads=[hk(4, 1), ('gs', 1)], writes=[('acc', 1)])
                sc.add('pool', lambda e: e.tensor_tensor(acc[0][:, :], acc[0][:, :], acc[1][:, :], ALU.add),
                       reads=[('acc', 0), ('acc', 1)], writes=[('acc', 0)])
                sc.add('dve', lambda e: e.tensor_tensor(acc[1][:, :], half(5, 0), gs[2][:, :], ALU.mult),
                       reads=[hk(5, 0), ('gs', 2)], writes=[('acc', 1)])
                sc.add('pool', lambda e, dc=dc: e.tensor_tensor(mg[:, dc, :], acc[0][:, :], acc[1][:, :], ALU.add),
                       reads=[('acc', 0), ('acc', 1)], writes=[('mg', dc)])
            for dc in range(8):
                s = dc % 2

                def mmo(e, dc=dc, s=s):
                    r = None
                    for k in range(8):
                        r = e.matmul(half(7, s), Wout[:, k, dc * 128:(dc + 1) * 128], mg[:, k, :], start=(k == 0),
                                     stop=(k == 7))
                    return r
                sc.add('pe', mmo, reads=['Wout'] + [('mg', k) for k in range(8)], writes=[hk(7, s)])
                sc.add('dve', lambda e, dc=dc, s=s: e.tensor_tensor(xt[:, dc, :], xt[:, dc, :], half(7, s), ALU.add),
                       reads=['xt', hk(7, s)], writes=[('xn', dc)])
            allxn = [('xn', dc) for dc in range(8)]
            if not last:
                sc.add('sp', lambda e, t0=t0, t1=t1: e.dma_start(out=xov[:, :, t0:t1], in_=xt[:, :, :]),
                       reads=allxn, writes=[('xo', it)], slot='xo')
            for k in range(8):
                s = k % 2
                sc.add('act', lambda e, k=k, s=s: e.activation(sqs[s][:, :], xt[:, k, :], AF.Square),
                       reads=[('xn', k)], writes=[('sqs', s)])
                sc.add('pe', lambda e, k=k, s=s: e.matmul(half(1, 1), ones_f[:, :], sqs[s][:, :], start=(k == 0),
                                                         stop=(k == 7)),
                       reads=[('sqs', s), 'ones_f'], writes=[hk(1, 1)])
            sc.add('act', lambda e: e.activation(rstd[:, :], half(1, 1), AF.Sqrt, bias=EPS, scale=1.0 / D),
                   reads=[hk(1, 1)], writes=['rstd'])
            sc.add('dve', lambda e: e.reciprocal(rstd[:, :], rstd[:, :]), reads=['rstd'], writes=['rstd'])
            for k in range(8):
                sc.add('dve', lambda e, k=k: e.scalar_tensor_tensor(hout[:, k, :], xt[:, k, :], ng_sb[:, k:k + 1],
                                                                   rstd[:, :], ALU.mult, ALU.mult),
                       reads=[('xn', k), 'rstd', ('par', 3)], writes=['hout'])
            if last:
                sc.add('sp', lambda e, t0=t0, t1=t1: e.dma_start(out=xov[:, :, t0:t1], in_=hout[:, :, :]),
                       reads=['hout'], writes=[('xo', it)], slot='xo')
            else:
                sc.add('sp', lambda e, t0=t0, t1=t1: e.dma_start(out=hov[:, :, t0:t1], in_=hout[:, :, :]),
                       reads=['hout'], writes=[('ho', it)], slot='ho')
        sc.close()
    return nc


def mixer_inputs(c, hT, w_in_l, b_fg_l, conv_w_l, a_log_l, dt_bias_l):
    hd, half = c // 2, c % 2
    wf = np.concatenate([w_in_l[:, OFF['aq'] + c * 64:OFF['aq'] + (c + 1) * 64],
                         w_in_l[:, OFF['ak'] + c * 64:OFF['ak'] + (c + 1) * 64],
                         w_in_l[:, OFF['av'] + c * 64:OFF['av'] + (c + 1) * 64],
                         w_in_l[:, OFF['af'] + c:OFF['af'] + c + 1]], axis=1)
    vo = hd * 128 + half * 64
    wg = np.concatenate([w_in_l[:, OFF['bq'] + hd * 128:OFF['bq'] + (hd + 1) * 128],
                         w_in_l[:, OFF['bk'] + hd * 128:OFF['bk'] + (hd + 1) * 128],
                         w_in_l[:, OFF['bv'] + vo:OFF['bv'] + vo + 64],
                         w_in_l[:, OFF['ba'] + hd:OFF['ba'] + hd + 1],
                         w_in_l[:, OFF['bb'] + hd:OFF['bb'] + hd + 1]], axis=1)
    cw = np.zeros((128, 12), np.float32)
    cw[:, 0:4] = conv_w_l[:, hd * 128:(hd + 1) * 128].T
    cw[:, 4:8] = conv_w_l[:, 512 + hd * 128:512 + (hd + 1) * 128].T
    cw[0:64, 8:12] = conv_w_l[:, 1024 + vo:1024 + vo + 64].T
    gpar = np.empty((128, 2), np.float32)
    gpar[:, 0] = a_log_l[hd]
    gpar[:, 1] = dt_bias_l[hd]
    return dict(hT=hT, wf=np.ascontiguousarray(wf), bfg=np.full((128, 1), b_fg_l[c], np.float32),
                wg=np.ascontiguousarray(wg), cw=cw, gpar=gpar)


def _lay8(v):
    return np.ascontiguousarray(np.asarray(v, np.float32).reshape(-1, 128).T)


_PROGS = {}


def _prog(name, fn):
    if name not in _PROGS:
        _PROGS[name] = fn()
    return _PROGS[name]


def kernel(x, mem, norm_g, w_in, b_fg, b_merge, conv_w, a_log, dt_bias, gdn_norm_g, mem_norm_g, w_mem_kv,
           w_branch, w_out, final_norm_g):
    f = lambda a: np.asarray(a, np.float32)
    x, mem, norm_g, w_in, b_fg, b_merge, conv_w = map(f, (x, mem, norm_g, w_in, b_fg, b_merge, conv_w))
    a_log, dt_bias, gdn_norm_g, mem_norm_g = map(f, (a_log, dt_bias, gdn_norm_g, mem_norm_g))
    w_mem_kv, w_branch, w_out, final_norm_g = map(f, (w_mem_kv, w_branch, w_out, final_norm_g))
    S = x.shape[1]
    TS = S // NCORES
    cores = list(range(NCORES))
    xT = np.ascontiguousarray(x[0].T)
    memT = np.ascontiguousarray(mem[0].T)
    sh = lambda a, c: np.ascontiguousarray(a[:, c * TS:(c + 1) * TS])
    ncP = _prog('P', lambda: build_P(TS))
    res = run_bass_kernel_spmd(ncP, [dict(xT=sh(xT, c), g=_lay8(norm_g[0])) for c in cores], core_ids=cores)
    hT = np.concatenate([np.asarray(r["hT"]) for r in res.results], axis=1)
    depth = w_in.shape[0]
    for l in range(depth):
        last = (l == depth - 1)
        ncM = _prog('M', lambda: build_M(S))
        hTc = np.ascontiguousarray(hT)
        res = run_bass_kernel_spmd(
            ncM, [mixer_inputs(c, hTc, w_in[l], b_fg[l], conv_w[l], a_log[l], dt_bias[l]) for c in cores],
            core_ids=cores)
        oaT = np.concatenate([np.asarray(r["oaT"]) for r in res.results], axis=0)
        obT = np.concatenate([np.asarray(r["obT"]) for r in res.results], axis=0)
        ncT = _prog('T%d' % int(last), lambda: build_T(TS, last))
        ng = final_norm_g if last else norm_g[l + 1]
        maps = []
        for c in cores:
            maps.append(dict(xT=sh(xT, c), hT=sh(hT, c), oaT=sh(oaT, c), obT=sh(obT, c),
                             w_in=np.ascontiguousarray(w_in[l]),
                             w_br=np.ascontiguousarray(w_branch[l].reshape(1536, D)),
                             w_out=np.ascontiguousarray(w_out[l]), w_kv=np.ascontiguousarray(w_mem_kv[l]),
                             memT=memT, mem_g=_lay8(mem_norm_g[l]), b_mg=_lay8(b_merge[l]),
                             gdn_g=np.ascontiguousarray(gdn_norm_g[l].reshape(128, 1)), next_g=_lay8(ng)))
        res = run_bass_kernel_spmd(ncT, maps, core_ids=cores)
        xT = np.concatenate([np.asarray(r["xoT"]) for r in res.results], axis=1)
        if not last:
            hT = np.concatenate([np.asarray(r["hoT"]) for r in res.results], axis=1)
    out = np.ascontiguousarray(xT.T).reshape(1, S, D).astype(np.float32)
    return out
```

```python
import contextlib
import numpy as np
import ml_dtypes
import concourse.bass as bass
import concourse.mybir as mybir
from concourse.bass_utils import run_bass_kernel_spmd

F32 = mybir.dt.float32
BF16 = mybir.dt.bfloat16
AF = mybir.ActivationFunctionType
ALU = mybir.AluOpType

D = 1024
S_FULL = 16384
NCORES = 8
EPS = 1e-6
N_IN = 8208
import os as _os
SAME_ENGINE_SYNC = bool(int(_os.environ.get('SAME_SYNC', '1')))
OFF = dict(aq=0, ak=512, av=1024, af=1536, az=1544, bq=2056, bk=2568, bv=3080,
           ba=3592, bb=3596, bz=3600, mq=4112, mz=4624, gates=5136)


def _is_psum_key(k):
    if isinstance(k, str):
        return k.startswith('ps')
    if isinstance(k, tuple) and len(k) >= 2:
        return k[0] in ('pb', 'pS', 'pO') or k[1] == 'ps'
    return False


class Sched:
    ENGS = ['pe', 'act', 'dve', 'pool', 'sp']

    def __init__(self, nc, same_engine_sync=None):
        if same_engine_sync is None:
            same_engine_sync = SAME_ENGINE_SYNC
        self.nc = nc
        self.ops = []
        self.lastw = {}
        self.readers = {}
        self.slot_count = {}
        self.same = same_engine_sync
        self.stack = contextlib.ExitStack()
        self.esem = {e: self.stack.enter_context(nc.semaphore("sem_" + e)) for e in self.ENGS}
        self.ssem = {}
        self.cnt = {e: 0 for e in self.ENGS}

    def _needs_same(self, eng):
        if eng == 'pe':
            return False
        if eng == 'pool':
            return True
        return self.same

    def add(self, eng, fn, reads=(), writes=(), slot=None):
        op = dict(eng=eng, fn=fn, deps=[], slot=slot, inc=False, id=len(self.ops))
        deps = {}
        for k in reads:
            w = self.lastw.get(k)
            if w is not None:
                deps[w['id']] = w
            if _is_psum_key(k):
                for r in self.readers.get(k, ()):
                    if r['eng'] != eng:
                        deps[r['id']] = r
        for k in writes:
            w = self.lastw.get(k)
            if w is not None:
                deps[w['id']] = w
            for r in self.readers.get(k, ()):
                deps[r['id']] = r
        for d in deps.values():
            if d is op:
                continue
            op['deps'].append(d)
            if d['slot'] is None:
                if d['eng'] != eng or self._needs_same(eng) or slot is not None:
                    d['inc'] = True
        for k in writes:
            self.lastw[k] = op
            self.readers[k] = []
        for k in reads:
            self.readers.setdefault(k, []).append(op)
        if slot is not None:
            if slot not in self.ssem:
                self.ssem[slot] = self.stack.enter_context(self.nc.semaphore("sl_%d" % len(self.ssem)))
            self.slot_count[slot] = self.slot_count.get(slot, 0) + 1
            op['slot_val'] = self.slot_count[slot] * 16
        self.ops.append(op)
        return op

    def flush(self):
        nc = self.nc
        for op in self.ops:
            if op['slot'] is None and op['inc']:
                self.cnt[op['eng']] += 1
                op['count'] = self.cnt[op['eng']]
        ops = self.ops
        esem, ssem = self.esem, self.ssem
        final = dict(self.slot_count)
        with nc.Block() as block:
            def run(ename, eng):
                known = {}
                for op in ops:
                    if op['eng'] != ename:
                        continue
                    waits = {}
                    for d in op['deps']:
                        if d['slot'] is not None:
                            key = ('s', d['slot'])
                            v = d['slot_val']
                            sem = ssem[d['slot']]
                        else:
                            if d['eng'] == ename and op['slot'] is None and not self._needs_same(ename):
                                continue
                            key = ('e', d['eng'])
                            v = d['count']
                            sem = esem[d['eng']]
                        if waits.get(key, (None, -1))[1] < v:
                            waits[key] = (sem, v)
                    for key, (sem, v) in waits.items():
                        if known.get(key, -1) >= v:
                            continue
                        known[key] = v
                        eng.wait_ge(sem, v)
                    ins = op['fn'](eng)
                    if op['slot'] is not None:
                        ins.then_inc(ssem[op['slot']], 16)
                    elif op['inc']:
                        ins.then_inc(esem[ename], 1)
                if ename == 'sp':
                    for s, n in final.items():
                        eng.wait_ge(ssem[s], n * 16)

            block.tensor(lambda e: run('pe', e))
            block.scalar(lambda e: run('act', e))
            block.vector(lambda e: run('dve', e))
            block.gpsimd(lambda e: run('pool', e))
            block.sync(lambda e: run('sp', e))
        self.ops = []
        self.lastw = {}
        self.readers = {}

    def collective(self, kind, op, src_ap, dst_ap, reads=(), writes=(), slot='cc', ncores=NCORES):
        self.flush()
        if slot not in self.ssem:
            self.ssem[slot] = self.stack.enter_context(self.nc.semaphore("sl_%d" % len(self.ssem)))
        self.slot_count[slot] = self.slot_count.get(slot, 0) + 1
        ins = self.nc.gpsimd.collective_compute(kind, op, replica_groups=[list(range(ncores))],
                                                ins=[src_ap], outs=[dst_ap])
        ins.then_inc(self.ssem[slot], 16)
        pseudo = dict(eng='pool', fn=None, deps=[], slot=slot, inc=False, id=-1,
                      slot_val=self.slot_count[slot] * 16)
        for k in writes:
            self.lastw[k] = pseudo
            self.readers[k] = []

    def close(self):
        self.flush()
        self.stack.close()


_NAME = [0]


class Ctx:
    def __init__(self, nc):
        self.nc = nc
        self.st = contextlib.ExitStack()

    def sb(self, shape, dt, name=None):
        _NAME[0] += 1
        return self.st.enter_context(self.nc.sbuf_tensor(name or ("t%d" % _NAME[0]), list(shape), dt))

    def ps(self, shape, dt, name=None):
        _NAME[0] += 1
        return self.st.enter_context(self.nc.psum_tensor(name or ("p%d" % _NAME[0]), list(shape), dt))


def emit_rmsnorm(sc, x_sb, xkey, g_sb, ones_f, out_sb, outkey, TT, sq, ps, rstd, tag, dim=D, gkey='g'):
    for k in range(8):
        sc.add('act', lambda e, k=k: e.activation(sq[:, k, :], x_sb[:, k, :], AF.Square),
               reads=[xkey], writes=[(tag, 'sq', k)])

    def mm(e):
        r = None
        for k in range(8):
            r = e.matmul(ps[:, 0:TT], ones_f[:, :], sq[:, k, :], start=(k == 0), stop=(k == 7))
        return r
    sc.add('pe', mm, reads=[(tag, 'sq', k) for k in range(8)] + ['ones_f'], writes=[(tag, 'ps')])
    sc.add('act', lambda e: e.activation(rstd[:, :], ps[:, 0:TT], AF.Sqrt, bias=EPS, scale=1.0 / dim),
           reads=[(tag, 'ps')], writes=[(tag, 'rstd')])
    sc.add('dve', lambda e: e.reciprocal(rstd[:, :], rstd[:, :]),
           reads=[(tag, 'rstd')], writes=[(tag, 'rstd')])
    for k in range(8):
        sc.add('dve',
               lambda e, k=k: e.scalar_tensor_tensor(out_sb[:, k, :], x_sb[:, k, :], g_sb[:, k:k + 1],
                                                     rstd[:, :], ALU.mult, ALU.mult),
               reads=[xkey, (tag, 'rstd'), gkey], writes=[outkey])


def build_P(TS):
    nc = bass.Bass("TRN2", target_bir_lowering=False)
    xT = nc.dram_tensor("xT", [D, TS], F32, kind="ExternalInput").ap()
    g = nc.dram_tensor("g", [128, 8], F32, kind="ExternalInput").ap()
    hT = nc.dram_tensor("hT", [D, TS], BF16, kind="ExternalOutput").ap()
    TT = 512
    cx = Ctx(nc)
    with cx.st:
        sc = Sched(nc)
        ones_f = cx.sb([128, 128], F32)
        g_sb = cx.sb([128, 8], F32)
        xs = [cx.sb([128, 8, TT], F32) for _ in range(2)]
        hs = [cx.sb([128, 8, TT], BF16) for _ in range(2)]
        sq = cx.sb([128, 8, TT], F32)
        rstd = cx.sb([128, TT], F32)
        ps = cx.ps([128, 512], F32)
        sc.add('pool', lambda e: e.memset(ones_f[:, :], 1.0), writes=['ones_f'])
        sc.add('sp', lambda e: e.dma_start(out=g_sb[:, :], in_=g[:, :]), writes=['g'], slot='g')
        xv = xT.rearrange("(k p) t -> p k t", p=128)
        hv = hT.rearrange("(k p) t -> p k t", p=128)
        for i in range(TS // TT):
            b = i % 2
            sc.add('sp', lambda e, i=i, b=b: e.dma_start(out=xs[b][:, :, :], in_=xv[:, :, i * TT:(i + 1) * TT]),
                   writes=[('x', b)], slot=('x', b))
            emit_rmsnorm(sc, xs[b], ('x', b), g_sb, ones_f, hs[b], ('h', b), TT, sq, ps, rstd, 'n')
            sc.add('sp', lambda e, i=i, b=b: e.dma_start(out=hv[:, :, i * TT:(i + 1) * TT], in_=hs[b][:, :, :]),
                   reads=[('h', b)], writes=[('hout', i)], slot=('ho', b))
        sc.close()
    return nc


def make_consts(sc, cx):
    c = {}
    c['ones'] = cx.sb([128, 128], F32)
    c['ident'] = cx.sb([128, 128], F32)
    c['uincl'] = cx.sb([128, 128], F32)
    c['ustrict'] = cx.sb([128, 128], F32)
    c['e0'] = cx.sb([128, 128], F32)
    c['ones_bf'] = cx.sb([128, 128], BF16)
    c['ident_bf'] = cx.sb([128, 128], BF16)
    c['zeros'] = cx.sb([128, 128], F32)
    sc.add('pool', lambda e: e.memset(c['ones'][:, :], 1.0), writes=['c_ones'])
    sc.add('pool', lambda e: e.memset(c['zeros'][:, :], 0.0), writes=['c_zeros'])
    sc.add('pool', lambda e: e.memset(c['ones_bf'][:, :], 1.0), writes=['c_ones_bf'])
    sc.add('pool', lambda e: e.affine_select(c['ident'][:, :], c['zeros'][:, :], [[1, 128]], ALU.not_equal, 1.0,
                                             base=0, channel_multiplier=-1),
           reads=['c_zeros'], writes=['c_ident'])
    sc.add('pool', lambda e: e.tensor_copy(c['ident_bf'][:, :], c['ident'][:, :]),
           reads=['c_ident'], writes=['c_ident_bf'])
    sc.add('pool', lambda e: e.affine_select(c['uincl'][:, :], c['ones'][:, :], [[1, 128]], ALU.is_ge, 0.0,
                                             base=0, channel_multiplier=-1),
           reads=['c_ones'], writes=['c_uincl'])
    sc.add('pool', lambda e: e.affine_select(c['ustrict'][:, :], c['ones'][:, :], [[1, 128]], ALU.is_gt, 0.0,
                                             base=0, channel_multiplier=-1),
           reads=['c_ones'], writes=['c_ustrict'])
    sc.add('pool', lambda e: e.affine_select(c['e0'][:, :], c['ones'][:, :], [[0, 128]], ALU.is_ge, 0.0,
                                             base=0, channel_multiplier=-1),
           reads=['c_ones'], writes=['c_e0'])
    return c


def load_cast(sc, dst_bf, dstkey, src_ap, stage, stagekey, eng_dma='sp', eng_cast='pool', slot=None):
    sc.add(eng_dma, lambda e: e.dma_start(out=stage, in_=src_ap), writes=[stagekey], slot=slot or stagekey)
    sc.add(eng_cast, lambda e: e.tensor_copy(dst_bf, stage), reads=[stagekey], writes=[dstkey])


def fox_phase(nc, sc, cx0, c, S, hT, wf, bfg, oaT, scr, psb, stop=99):
    NT = S // 128
    NG = S // 512
    cx = Ctx(nc)
    with cx.st:
        wq = cx.sb([128, 8, 64], BF16)
        wk = cx.sb([128, 8, 64], BF16)
        wv = cx.sb([128, 8, 65], BF16)
        wst = cx.sb([128, 8, 193], F32)
        QT = cx.sb([65, S], BF16)
        KT = cx.sb([65, S], BF16)
        V = cx.sb([128, NT, 65], BF16)
        lfr = cx.sb([128, NT], F32)
        lfn = cx.sb([128, NT], F32)
        Fn = cx.sb([128, NT], F32)
        frefB = cx.sb([128, NG], F32)
        ctok = cx.sb([128, NT], F32)
        cTT = cx.sb([128, 128], BF16)
        totT = cx.sb([128, 1], F32)
        X = cx.sb([128, 128], F32)
        negb = cx.sb([128, 1], F32)
        biasg = [cx.sb([128, NT], F32) for _ in range(2)]
        ht = [cx.sb([128, 8, 512], BF16) for _ in range(2)]
        Pb = [cx.sb([128, 512], BF16) for _ in range(4)]
        oun = cx.sb([65, 512], F32)
        rl = cx.sb([65, 512], F32)
        ofin = [cx.sb([64, 512], F32) for _ in range(2)]

        sc.add('sp', lambda e: e.dma_start(out=wst[:, :, :], in_=wf.rearrange("(k p) c -> p k c", p=128)),
               writes=['wst'], slot='wst')
        sc.add('pool', lambda e: e.tensor_copy(wq[:, :, :], wst[:, :, 0:64]), reads=['wst'], writes=['wq'])
        sc.add('pool', lambda e: e.tensor_copy(wk[:, :, :], wst[:, :, 64:128]), reads=['wst'], writes=['wk'])
        sc.add('pool', lambda e: e.tensor_copy(wv[:, :, :], wst[:, :, 128:193]), reads=['wst'], writes=['wv'])
        sc.add('sp', lambda e: e.dma_start(out=negb[:, :], in_=bfg[:, :]), writes=['negb'], slot='negb')
        sc.add('dve', lambda e: e.tensor_scalar(negb[:, :], negb[:, :], -1.0, None, ALU.mult),
               reads=['negb'], writes=['negb'])
        sc.add('pool', lambda e: e.memset(KT[64:65, :], 1.0), writes=['KTrow'])
        sc.add('pool', lambda e: e.memset(V[:, :, 64:65], 1.0), writes=['Vones'])

        if stop <= 0:
            sc.flush()
            return
        hv = hT.rearrange("(k p) t -> p k t", p=128)
        psq, psk, psv = psb[0], psb[1], psb[2]
        for i in range(NG):
            b = i % 2
            sc.add('sp', lambda e, i=i, b=b: e.dma_start(out=ht[b][:, :, :], in_=hv[:, :, i * 512:(i + 1) * 512]),
                   writes=[('ht', b)], slot=('ht', b))

            def mmq(e, b=b):
                r = None
                for k in range(8):
                    r = e.matmul(psq[0:64, :], wq[:, k, :], ht[b][:, k, :], start=(k == 0), stop=(k == 7))
                return r
            import os
            DBG = int(os.environ.get('FOXDBG', '15'))
            if DBG & 1:
              sc.add('pe', mmq, reads=[('ht', b), 'wq'], writes=['psq'])
            if DBG & 1:
              sc.add('act', lambda e, i=i: e.activation(QT[0:64, i * 512:(i + 1) * 512], psq[0:64, :], AF.Copy,
                                                      scale=0.125),
                   reads=['psq'], writes=[('QT', i)])

            def mmk(e, b=b):
                r = None
                for k in range(8):
                    r = e.matmul(psk[0:64, :], wk[:, k, :], ht[b][:, k, :], start=(k == 0), stop=(k == 7))
                return r
            if DBG & 2:
              sc.add('pe', mmk, reads=[('ht', b), 'wk'], writes=['psk'])
              sc.add('dve', lambda e, i=i: e.tensor_copy(KT[0:64, i * 512:(i + 1) * 512], psk[0:64, :]),
                   reads=['psk'], writes=[('KT', i)])

            def mmv(e, b=b):
                r = None
                for sub in range(4):
                    for k in range(8):
                        r = e.matmul(psv[:, sub * 128:sub * 128 + 65], ht[b][:, k, sub * 128:(sub + 1) * 128],
                                     wv[:, k, :], start=(k == 0), stop=(k == 7))
                return r
            pv3 = psv[:, :].rearrange("p (s c) -> p s c", c=128)
            if DBG & 4:
              sc.add('pe', mmv, reads=[('ht', b), 'wv'], writes=['psv'])
              sc.add('dve', lambda e, i=i, pv3=pv3: e.tensor_copy(V[:, 4 * i:4 * i + 4, 0:64], pv3[:, :, 0:64]),
                   reads=['psv', 'Vones'], writes=[('V', i)])
            if DBG & 8:
              sc.add('dve', lambda e, i=i, pv3=pv3: e.tensor_copy(lfr[:, 4 * i:4 * i + 4], pv3[:, :, 64]),
                   reads=['psv'], writes=[('lfr', i)])

        if stop <= 1:
            sc.flush()
            return
        allfr = [('lfr', i) for i in range(NG)]
        sc.add('act', lambda e: e.activation(lfn[:, :], lfr[:, :], AF.Exp, bias=negb[:, 0:1], scale=-1.0),
               reads=allfr + ['negb'], writes=['lfn'])
        sc.add('act', lambda e: e.activation(lfn[:, :], lfn[:, :], AF.Ln, bias=1.0, scale=1.0),
               reads=['lfn'], writes=['lfn'])
        pt = psb[0]
        sc.add('pe', lambda e: e.matmul(pt[0:NT, 0:1], lfn[:, :], c['ones'][:, 0:1], start=True, stop=True),
               reads=['lfn', 'c_ones', 'psq'], writes=['psq'])
        sc.add('dve', lambda e: e.tensor_copy(totT[0:NT, :], pt[0:NT, 0:1]), reads=['psq'], writes=['totT'])
        sc.add('dve', lambda e: e.tensor_scalar(X[0:NT, 0:NT], c['ustrict'][0:NT, 0:NT], totT[0:NT, 0:1], None,
                                                ALU.mult),
               reads=['totT', 'c_ustrict'], writes=['X'])
        pf = psb[1]

        def mmF(e):
            e.matmul(pf[:, 0:NT], c['uincl'][:, :], lfn[:, :], start=True, stop=False)
            return e.matmul(pf[:, 0:NT], c['ones'][0:NT, :], X[0:NT, 0:NT], start=False, stop=True)
        sc.add('pe', mmF, reads=['lfn', 'X', 'c_uincl', 'c_ones', 'psk'], writes=['psk'])
        sc.add('dve', lambda e: e.tensor_copy(Fn[:, :], pf[:, 0:NT]), reads=['psk'], writes=['Fn'])
        pr = psb[2]
        sc.add('pe', lambda e: e.matmul(pr[:, 0:NG], c['e0'][:, :], Fn[:, 0:NT:4], start=True, stop=True),
               reads=['Fn', 'c_e0', 'psv'], writes=['psv'])
        sc.add('dve', lambda e: e.tensor_copy(frefB[:, :], pr[:, 0:NG]), reads=['psv'], writes=['frefB'])
        for r in range(4):
            sc.add('dve', lambda e, r=r: e.tensor_tensor(ctok[:, r:NT:4], frefB[:, :], Fn[:, r:NT:4], ALU.subtract),
                   reads=['frefB', 'Fn'], writes=[('ctok', r)])
        pc = psb[3]
        sc.add('pe', lambda e: e.transpose(pc[0:NT, 0:128], ctok[:, :], c['ident'][:, :]),
               reads=[('ctok', r) for r in range(4)] + ['c_ident'], writes=['ps3'])
        sc.add('dve', lambda e: e.tensor_copy(cTT[0:NT, :], pc[0:NT, 0:128]), reads=['ps3'], writes=['cTT'])
        sc.add('sp', lambda e: e.dma_start(out=scr[0:NT, :], in_=cTT[0:NT, :]), reads=['cTT'], writes=['scr'],
               slot='scr')
        sc.add('sp', lambda e: e.dma_start(out=QT[64:65, :], in_=scr[0:NT, :].rearrange("(o j) p -> o (j p)", o=1)),
               reads=['scr'], writes=['QTrow'], slot='qtrow')

        if stop <= 2:
            sc.flush()
            return
        sc.flush()
        LA = 3
        pS = [psb[0], psb[1], psb[2], psb[3]]
        pO = [psb[6], psb[7]]
        pbc = psb[4]
        blocks = []
        for g in range(NG):
            nj = 4 * g + 4
            for j in range(nj):
                r = j - 4 * g
                c0 = 0 if r < 0 else r * 128
                blocks.append((g, j, r, c0, 512 - c0, nj))
        NB = len(blocks)

        def emit_front(bi):
            g, j, r, c0, N, nj = blocks[bi]
            gb = g % 2
            sb_ = bi % 4
            if j == 0:
                sc.add('dve', lambda e: e.tensor_scalar(biasg[gb][:, 0:nj], Fn[:, 0:nj], frefB[:, g:g + 1], None,
                                                        ALU.subtract),
                       reads=['Fn', 'frefB'], writes=[('biasg', gb)])
            sc.add('pe', lambda e: e.matmul(pS[sb_][:, 0:N], KT[0:65, j * 128:(j + 1) * 128],
                                            QT[0:65, g * 512 + c0:(g + 1) * 512], start=True, stop=True),
                   reads=['QT', 'KT'], writes=[('pS', sb_)])
            sc.add('act', lambda e: e.activation(Pb[sb_][:, 0:N], pS[sb_][:, 0:N], AF.Exp,
                                                 bias=biasg[gb][:, j:j + 1], scale=1.0),
                   reads=[('pS', sb_), ('biasg', gb)], writes=[('P', sb_)])
            if r >= 0:
                sc.add('pool', lambda e: e.affine_select(Pb[sb_][:, 0:128], Pb[sb_][:, 0:128], [[1, 128]], ALU.is_ge,
                                                         0.0, base=0, channel_multiplier=-1),
                       reads=[('P', sb_)], writes=[('P', sb_)])

        def emit_back(bi):
            g, j, r, c0, N, nj = blocks[bi]
            gb = g % 2
            sb_ = bi % 4
            sc.add('pe', lambda e: e.matmul(pO[gb][0:65, c0:512], V[:, j, 0:65], Pb[sb_][:, 0:N], start=(j == 0),
                                            stop=(j == nj - 1), skip_group_check=True),
                   reads=[('P', sb_), 'V'], writes=[('pO', gb)])
            if j == nj - 1:
                sc.add('dve', lambda e: e.tensor_copy(oun[0:65, :], pO[gb][0:65, :]),
                       reads=[('pO', gb)], writes=['oun'])
                sc.add('dve', lambda e: e.reciprocal(rl[64:65, :], oun[64:65, :]), reads=['oun'], writes=['rl'])
                pending.append((bi + 6, g, gb))

        def emit_fin(g, gb):
            sc.add('pe', lambda e: e.matmul(pbc[0:64, :], c['ones'][64:65, 0:64], rl[64:65, :], start=True,
                                            stop=True),
                   reads=['rl', 'c_ones'], writes=[('pb', 4)])
            sc.add('dve', lambda e: e.tensor_tensor(ofin[gb][:, :], oun[0:64, :], pbc[0:64, :], ALU.mult),
                   reads=['oun', ('pb', 4)], writes=[('ofin', gb)])
            sc.add('sp', lambda e: e.dma_start(out=oaT[:, g * 512:(g + 1) * 512], in_=ofin[gb][:, :]),
                   reads=[('ofin', gb)], writes=[('oaT', g)], slot=('oa', gb))

        pending = []
        for bi in range(NB + LA):
            if bi < NB:
                emit_front(bi)
            if bi - LA >= 0:
                emit_back(bi - LA)
            while pending and pending[0][0] <= bi - LA:
                _, g_, gb_ = pending.pop(0)
                emit_fin(g_, gb_)
        for _, g_, gb_ in pending:
            emit_fin(g_, gb_)
        sc.flush()


def gdn_phase(nc, sc, c, S, hT, wg, cw, gpar, obT, psb):
    NSEG = S // 512
    A = sc.add
    cx = Ctx(nc)
    PB = lambda n: ('pb', n)
    with cx.st:
        f32t = lambda *sh: cx.sb(list(sh), F32)
        bft = lambda *sh: cx.sb(list(sh), BF16)
        wst = f32t(128, 8, 322)
        wq, wk, wv, wab = bft(128, 8, 128), bft(128, 8, 128), bft(128, 8, 64), bft(128, 8, 2)
        cw_sb, gp_sb = f32t(128, 12), f32t(128, 2)
        negA = f32t(128, 1)
        M_s, M_i = f32t(128, 4, 128), f32t(128, 4, 128)
        E63, E127, EL = f32t(128, 128), f32t(128, 128), f32t(128, 128)
        ht = [bft(128, 8, 512) for _ in range(2)]
        rq, rk, rv = f32t(128, 515), f32t(128, 515), f32t(64, 515)
        cq, ck = f32t(128, 512), f32t(128, 512)
        sq2, sq2b = bft(128, 512), bft(128, 512)
        rn, rnb = f32t(128, 512), f32t(128, 512)
        g_tok, G_tok, eG, ekd, glo = [f32t(128, 4) for _ in range(5)]
        diagG, EGrow, Dm, Gam, Gs, Gi = [f32t(128, 512) for _ in range(6)]
        ktok = f32t(128, 4, 128)
        Bb = [bft(128, 512) for _ in range(2)]
        Pb_ = [bft(128, 512) for _ in range(2)]
        S_f, S_b = f32t(128, 512), bft(128, 512)
        St_f, St_b = f32t(128, 64), bft(128, 64)
        vnew = bft(128, 64)
        o_sb = f32t(64, 512)

        A('sp', lambda e: e.dma_start(out=wst[:, :, :], in_=wg.rearrange("(k p) c -> p k c", p=128)),
          writes=['gwst'], slot='gwst')
        A('pool', lambda e: e.tensor_copy(wq[:, :, :], wst[:, :, 0:128]), reads=['gwst'], writes=['gwq'])
        A('pool', lambda e: e.tensor_copy(wk[:, :, :], wst[:, :, 128:256]), reads=['gwst'], writes=['gwk'])
        A('pool', lambda e: e.tensor_copy(wv[:, :, :], wst[:, :, 256:320]), reads=['gwst'], writes=['gwv'])
        A('pool', lambda e: e.tensor_copy(wab[:, :, :], wst[:, :, 320:322]), reads=['gwst'], writes=['gwab'])
        A('sp', lambda e: e.dma_start(out=cw_sb[:, :], in_=cw[:, :]), writes=['cw'], slot='cw')
        A('sp', lambda e: e.dma_start(out=gp_sb[:, :], in_=gpar[:, :]), writes=['gp'], slot='gp')
        A('act', lambda e: e.activation(negA[:, :], gp_sb[:, 0:1], AF.Exp), reads=['gp'], writes=['negA'])
        A('dve', lambda e: e.tensor_scalar(negA[:, :], negA[:, :], -1.0, None, ALU.mult), reads=['negA'],
          writes=['negA'])
        A('pool', lambda e: e.memset(M_s[:, :, :], 1.0), writes=['M_s'])
        A('pool', lambda e: e.memset(M_i[:, :, :], 1.0), writes=['M_i'])
        A('pool', lambda e: e.affine_select(M_s[:, :, :], M_s[:, :, :], [[0, 4], [1, 128]], ALU.is_gt, 0.0, base=0,
                                            channel_multiplier=-1), reads=['M_s'], writes=['M_s'])
        A('pool', lambda e: e.affine_select(M_i[:, :, :], M_i[:, :, :], [[0, 4], [1, 128]], ALU.is_ge, 0.0, base=0,
                                            channel_multiplier=-1), reads=['M_i'], writes=['M_i'])
        A('pool', lambda e: e.memset(M_s[0:64, :, 64:128], 0.0), reads=['M_s'], writes=['M_s'])
        A('pool', lambda e: e.memset(M_i[0:64, :, 64:128], 0.0), reads=['M_i'], writes=['M_i'])
        A('pool', lambda e: e.affine_select(E63[:, :], c['zeros'][:, :], [[0, 128]], ALU.not_equal, 1.0, base=-63,
                                            channel_multiplier=1), reads=['c_zeros'], writes=['E63'])
        A('pool', lambda e: e.affine_select(E127[:, :], c['zeros'][:, :], [[0, 128]], ALU.not_equal, 1.0, base=-127,
                                            channel_multiplier=1), reads=['c_zeros'], writes=['E127'])
        A('pool', lambda e: e.tensor_copy(EL[:, 0:64], E63[:, 0:64]), reads=['E63'], writes=['EL'])
        A('pool', lambda e: e.tensor_copy(EL[:, 64:128], E127[:, 64:128]), reads=['E127', 'EL'], writes=['EL'])
        A('pool', lambda e: e.memset(rq[:, 0:3], 0.0), writes=['rq'])
        A('pool', lambda e: e.memset(rk[:, 0:3], 0.0), writes=['rk'])
        A('pool', lambda e: e.memset(rv[:, 0:3], 0.0), writes=['rv'])
        A('pool', lambda e: e.memset(St_f[:, :], 0.0), writes=['St_f'])
        A('pool', lambda e: e.memset(St_b[:, :], 0.0), writes=['St_b'])

        hv = hT.rearrange("(k p) t -> p k t", p=128)
        ones, ident = c['ones'], c['ident']
        DEPTH = dict(qn_f=2, kn_f=2, qn_bf=2, kT_bf=2, cv=2, a_sb=2, b_sb=2, B_f=2, kg=2, vtok=2, bpos=2,
                     qdec=3, aqk=3, kdec=3, negbt=3, dl=3, dh=3, ybu=2, ywT=2)
        SHAPES = dict(qn_f=(F32, (128, 512)), kn_f=(F32, (128, 512)), qn_bf=(BF16, (128, 512)),
                      kT_bf=(BF16, (128, 512)), cv=(F32, (64, 512)), a_sb=(F32, (128, 4)), b_sb=(F32, (128, 4)),
                      B_f=(F32, (128, 512)), kg=(BF16, (128, 4, 128)), vtok=(BF16, (128, 4, 64)),
                      bpos=(F32, (128, 4)), qdec=(BF16, (128, 512)), aqk=(BF16, (128, 512)),
                      kdec=(BF16, (128, 4, 128)), negbt=(F32, (128, 4)), dl=(F32, (128, 4)), dh=(F32, (128, 4)),
                      ybu=(F32, (128, 4, 64)), ywT=(BF16, (128, 512)))
        ROT = {n: [cx.sb(list(SHAPES[n][1]), SHAPES[n][0]) for _ in range(DEPTH[n])] for n in DEPTH}

        def mkA(s, lst):
            def K(k):
                return (k, s % DEPTH[k]) if (isinstance(k, str) and k in DEPTH) else k

            def A_(eng, fn, reads=(), writes=(), slot=None):
                lst.append(((eng, fn), dict(reads=[K(k) for k in reads], writes=[K(k) for k in writes], slot=slot)))
            return A_

        def rb(s, n):
            return ROT[n][s % DEPTH[n]]

        def stage1(i, A):
            b = i % 2
            qn_f, kn_f, qn_bf, kT_bf, cv, a_sb, b_sb = [rb(i, n) for n in
                                                        ('qn_f', 'kn_f', 'qn_bf', 'kT_bf', 'cv', 'a_sb', 'b_sb')]
            A('sp', lambda e: e.dma_start(out=ht[b][:, :, :], in_=hv[:, :, i * 512:(i + 1) * 512]),
              writes=[('ght', b)], slot=('ght', b))
            for (w_, M, bank, raw, key) in ((wq, 128, 0, rq, 'rq'), (wk, 128, 1, rk, 'rk'), (wv, 64, 0, rv, 'rv')):
                def mm(e, w_=w_, M=M, bank=bank):
                    r = None
                    for k in range(8):
                        r = e.matmul(psb[bank][0:M, :], w_[:, k, :], ht[b][:, k, :], start=(k == 0), stop=(k == 7))
                    return r
                A('pe', mm, reads=[('ght', b), 'gwq', 'gwk', 'gwv'], writes=[PB(bank)])
                A('dve', lambda e, M=M, bank=bank, raw=raw: e.tensor_copy(raw[0:M, 3:515], psb[bank][0:M, :]),
                  reads=[PB(bank)], writes=[key])

            def mmab(e):
                r = None
                for sub in range(4):
                    for k in range(8):
                        r = e.matmul(psb[1][:, sub * 2:sub * 2 + 2], ht[b][:, k, sub * 128:(sub + 1) * 128],
                                     wab[:, k, :], start=(k == 0), stop=(k == 7))
                return r
            A('pe', mmab, reads=[('ght', b), 'gwab'], writes=[PB(1)])
            p3 = psb[1][:, 0:8].rearrange("p (s c) -> p s c", c=2)
            A('dve', lambda e: e.tensor_copy(a_sb[:, :], p3[:, :, 0]), reads=[PB(1)], writes=['a_sb'])
            A('dve', lambda e: e.tensor_copy(b_sb[:, :], p3[:, :, 1]), reads=[PB(1)], writes=['b_sb'])
            for which, (raw, cv_, M, key, ckey) in enumerate(((rq, cq, 128, 'rq', 'cq'), (rk, ck, 128, 'rk', 'ck'),
                                                              (rv, cv, 64, 'rv', 'cv'))):
                A('act', lambda e, raw=raw, cv_=cv_, M=M, which=which: e.activation(
                    cv_[0:M, :], raw[0:M, 0:512], AF.Copy, scale=cw_sb[0:M, which * 4:which * 4 + 1]),
                    reads=[key, 'cw'], writes=[ckey])
                for tap in range(1, 4):
                    A('dve', lambda e, raw=raw, cv_=cv_, M=M, which=which, tap=tap: e.scalar_tensor_tensor(
                        cv_[0:M, :], raw[0:M, tap:tap + 512], cw_sb[0:M, which * 4 + tap:which * 4 + tap + 1],
                        cv_[0:M, :], ALU.mult, ALU.add),
                        reads=[key, 'cw', ckey], writes=[ckey])
                A('pool', lambda e, raw=raw, M=M: e.tensor_copy(raw[0:M, 0:3], raw[0:M, 512:515]),
                  reads=[key, ckey], writes=[key])
                A('act', lambda e, cv_=cv_, M=M: e.activation(cv_[0:M, :], cv_[0:M, :], AF.Silu),
                  reads=[ckey], writes=[ckey])
            for (cv_, ckey, bank, outf, okey, mul) in ((cq, 'cq', 0, qn_f, 'qn_f', 128.0 ** -0.5),
                                                      (ck, 'ck', 1, kn_f, 'kn_f', 1.0)):
                sq_, rn_ = (sq2, rn) if bank == 0 else (sq2b, rnb)
                A('act', lambda e, cv_=cv_, sq_=sq_: e.activation(sq_[:, :], cv_[:, :], AF.Square), reads=[ckey],
                  writes=[('sq2', bank)])
                A('pe', lambda e, bank=bank, sq_=sq_: e.matmul(psb[bank][:, :], c['ones_bf'][:, :], sq_[:, :],
                                                               start=True, stop=True),
                  reads=[('sq2', bank), 'c_ones_bf'], writes=[PB(bank)])
                A('act', lambda e, bank=bank, rn_=rn_: e.activation(rn_[:, :], psb[bank][:, :], AF.Ln, bias=EPS,
                                                                    scale=1.0),
                  reads=[PB(bank)], writes=[('rn', bank)])
                A('act', lambda e, rn_=rn_: e.activation(rn_[:, :], rn_[:, :], AF.Exp, scale=-0.5),
                  reads=[('rn', bank)], writes=[('rn', bank)])
                A('dve', lambda e, cv_=cv_, outf=outf, mul=mul, rn_=rn_: e.scalar_tensor_tensor(
                    outf[:, :], cv_[:, :], mul, rn_[:, :], ALU.mult, ALU.mult), reads=[ckey, ('rn', bank)],
                    writes=[okey])
            A('act', lambda e: e.activation(qn_bf[:, :], qn_f[:, :], AF.Copy), reads=['qn_f'], writes=['qn_bf'])
            A('act', lambda e: e.activation(kT_bf[:, :], kn_f[:, :], AF.Copy), reads=['kn_f'], writes=['kT_bf'])

        def stage2(i, A):
            qn_f, kn_f, qn_bf, kT_bf, cv, a_sb, b_sb = [rb(i, n) for n in
                                                        ('qn_f', 'kn_f', 'qn_bf', 'kT_bf', 'cv', 'a_sb', 'b_sb')]
            B_f, kg, vtok, bpos = [rb(i, n) for n in ('B_f', 'kg', 'vtok', 'bpos')]
            qdec, aqk, kdec, negbt, dl, dh = [rb(i, n) for n in ('qdec', 'aqk', 'kdec', 'negbt', 'dl', 'dh')]
            A('act', lambda e: e.activation(g_tok[:, :], a_sb[:, :], AF.Exp, bias=gp_sb[:, 1:2], scale=1.0),
              reads=['a_sb', 'gp'], writes=['g_tok'])
            A('act', lambda e: e.activation(g_tok[:, :], g_tok[:, :], AF.Ln, bias=1.0, scale=1.0),
              reads=['g_tok'], writes=['g_tok'])
            A('dve', lambda e: e.tensor_scalar(g_tok[:, :], g_tok[:, :], negA[:, 0:1], None, ALU.mult),
              reads=['g_tok', 'negA'], writes=['g_tok'])
            A('act', lambda e: e.activation(bpos[:, :], b_sb[:, :], AF.Exp, scale=-1.0), reads=['b_sb'],
              writes=['bpos'])
            A('dve', lambda e: e.tensor_scalar(bpos[:, :], bpos[:, :], 1.0, None, ALU.add), reads=['bpos'],
              writes=['bpos'])
            A('dve', lambda e: e.reciprocal(bpos[:, :], bpos[:, :]), reads=['bpos'], writes=['bpos'])
            A('dve', lambda e: e.tensor_scalar(negbt[:, :], bpos[:, :], -1.0, None, ALU.mult), reads=['bpos'],
              writes=['negbt'])
            A('pe', lambda e: e.matmul(psb[2][:, 0:4], M_i[:, 0, :], g_tok[:, :], start=True, stop=True),
              reads=['g_tok', 'M_i'], writes=[PB(2)])
            A('dve', lambda e: e.tensor_copy(G_tok[:, :], psb[2][:, 0:4]), reads=[PB(2)], writes=['G_tok'])
            A('act', lambda e: e.activation(eG[:, :], G_tok[:, :], AF.Exp), reads=['G_tok'], writes=['eG'])
            A('pe', lambda e: e.matmul(psb[2][:, 0:4], EL[:, :], G_tok[:, :], start=True, stop=True),
              reads=['G_tok', 'EL'], writes=[PB(2)])
            A('dve', lambda e: e.tensor_tensor(glo[:, :], psb[2][:, 0:4], G_tok[:, :], ALU.subtract),
              reads=[PB(2), 'G_tok'], writes=['glo'])
            A('act', lambda e: e.activation(ekd[:, :], glo[:, :], AF.Exp), reads=['glo'], writes=['ekd'])
            A('pe', lambda e: e.matmul(psb[2][:, 0:4], E63[:, :], G_tok[:, :], start=True, stop=True),
              reads=['G_tok', 'E63'], writes=[PB(2)])
            A('dve', lambda e: e.tensor_copy(dl[:, :], psb[2][:, 0:4]), reads=[PB(2)], writes=['dl'])
            A('act', lambda e: e.activation(dl[:, :], dl[:, :], AF.Exp), reads=['dl'], writes=['dl'])
            A('pe', lambda e: e.matmul(psb[2][:, 0:4], E127[:, :], G_tok[:, :], start=True, stop=True),
              reads=['G_tok', 'E127'], writes=[PB(2)])
            A('dve', lambda e: e.tensor_copy(dh[:, :], psb[2][:, 0:4]), reads=[PB(2)], writes=['dh'])
            A('act', lambda e: e.activation(dh[:, :], dh[:, :], AF.Exp), reads=['dh'], writes=['dh'])
            for sub in range(4):
                A('dve', lambda e, sub=sub: e.tensor_scalar(diagG[:, sub * 128:(sub + 1) * 128], ident[:, :],
                                                            G_tok[:, sub:sub + 1], None, ALU.mult),
                  reads=['G_tok', 'c_ident'], writes=[('diagG', sub)])
                A('pe', lambda e, sub=sub: e.matmul(psb[3][:, sub * 128:(sub + 1) * 128], ones[:, :],
                                                    diagG[:, sub * 128:(sub + 1) * 128], start=True, stop=True),
                  reads=[('diagG', sub), 'c_ones'], writes=[PB(3)])
            A('act', lambda e: e.activation(EGrow[:, :], psb[3][:, :], AF.Exp), reads=[PB(3)], writes=['EGrow'])
            A('pool', lambda e: e.tensor_tensor(qdec[:, :], qn_f[:, :], EGrow[:, :], ALU.mult),
              reads=['qn_f', 'EGrow'], writes=['qdec'])
            for sub in range(4):
                A('dve', lambda e, sub=sub: e.tensor_scalar(Dm[:, sub * 128:(sub + 1) * 128],
                                                            psb[3][:, sub * 128:(sub + 1) * 128],
                                                            G_tok[:, sub:sub + 1], 0.0, ALU.subtract, ALU.min),
                  reads=[PB(3), 'G_tok'], writes=['Dm'])
            A('act', lambda e: e.activation(Gam[:, :], Dm[:, :], AF.Exp), reads=['Dm'], writes=['Gam'])
            A('pool', lambda e: e.tensor_tensor(Gs[:, :], Gam[:, :], M_s[:, :, :].rearrange("p s c -> p (s c)"),
                                                ALU.mult), reads=['Gam', 'M_s'], writes=['Gs'])
            A('pool', lambda e: e.tensor_tensor(Gi[:, :], Gam[:, :], M_i[:, :, :].rearrange("p s c -> p (s c)"),
                                                ALU.mult), reads=['Gam', 'M_i'], writes=['Gi'])
            for sub in range(4):
                A('pe', lambda e, sub=sub: e.transpose(psb[2][:, sub * 128:(sub + 1) * 128],
                                                       kn_f[:, sub * 128:(sub + 1) * 128], ident[:, :]),
                  reads=['kn_f', 'c_ident'], writes=[PB(2)])
            A('dve', lambda e: e.tensor_copy(ktok[:, :, :], psb[2][:, :].rearrange("p (s c) -> p s c", c=128)),
              reads=[PB(2)], writes=['ktok'])
            for sub in range(4):
                A('act', lambda e, sub=sub: e.activation(kg[:, sub, :], ktok[:, sub, :], AF.Copy,
                                                         scale=eG[:, sub:sub + 1]), reads=['ktok', 'eG'], writes=['kg'])
                A('act', lambda e, sub=sub: e.activation(kdec[:, sub, :], ktok[:, sub, :], AF.Copy,
                                                         scale=ekd[:, sub:sub + 1]), reads=['ktok', 'ekd'],
                  writes=['kdec'])
            for sub in range(4):
                A('pe', lambda e, sub=sub: e.transpose(psb[2][:, sub * 64:(sub + 1) * 64],
                                                       cv[0:64, sub * 128:(sub + 1) * 128], ident[0:64, 0:64]),
                  reads=['cv', 'c_ident'], writes=[PB(2)])
            A('dve', lambda e: e.tensor_copy(vtok[:, :, :], psb[2][:, 0:256].rearrange("p (s c) -> p s c", c=64)),
              reads=[PB(2)], writes=['vtok'])
            for sub in range(4):
                cs = slice(sub * 128, (sub + 1) * 128)
                A('pe', lambda e, cs=cs: e.matmul(psb[2][:, cs], kT_bf[:, cs], kT_bf[:, cs], start=True, stop=True),
                  reads=['kT_bf'], writes=[PB(2)])
                A('dve', lambda e, cs=cs, sub=sub: e.scalar_tensor_tensor(
                    B_f[:, cs], psb[2][:, cs], negbt[:, sub:sub + 1], Gs[:, cs], ALU.mult, ALU.mult),
                    reads=[PB(2), 'negbt', 'Gs'], writes=['B_f'])
            for sub in range(4):
                cs = slice(sub * 128, (sub + 1) * 128)
                A('pe', lambda e, cs=cs: e.matmul(psb[3][:, cs], kT_bf[:, cs], qn_bf[:, cs], start=True, stop=True),
                  reads=['kT_bf', 'qn_bf'], writes=[PB(3)])
            A('dve', lambda e: e.tensor_tensor(aqk[:, :], psb[3][:, :], Gi[:, :], ALU.mult), reads=[PB(3), 'Gi'],
              writes=['aqk'])

        def stage3(i, A):
            B_f, kg, vtok, bpos = [rb(i, n) for n in ('B_f', 'kg', 'vtok', 'bpos')]
            ybu, ywT = rb(i, 'ybu'), rb(i, 'ywT')
            A('act', lambda e: e.activation(Bb[0][:, :], B_f[:, :], AF.Copy), reads=['B_f'], writes=[('Bb', 0)])
            for sub in range(4):
                cs = slice(sub * 128, (sub + 1) * 128)
                A('pe', lambda e, cs=cs: e.transpose(psb[4][:, cs], B_f[:, cs], ident[:, :]),
                  reads=['B_f', 'c_ident'], writes=[PB(4)])
            A('dve', lambda e: e.tensor_copy(Pb_[0][:, :], psb[4][:, :]), reads=[PB(4)], writes=[('Pb', 0)])
            for sub in range(4):
                cs = slice(sub * 128, (sub + 1) * 128)
                A('pool', lambda e, cs=cs: e.tensor_tensor(S_f[:, cs], B_f[:, cs], ident[:, :], ALU.add),
                  reads=['B_f', 'c_ident'], writes=['S_f'])
            A('act', lambda e: e.activation(S_b[:, :], S_f[:, :], AF.Copy), reads=['S_f'], writes=['S_b'])
            for j in range(5):
                cur, nxt = j % 2, (j + 1) % 2
                for sub in range(4):
                    cs = slice(sub * 128, (sub + 1) * 128)
                    A('pe', lambda e, cs=cs, cur=cur: e.matmul(psb[5][:, cs], Pb_[cur][:, cs], Bb[cur][:, cs],
                                                               start=True, stop=True),
                      reads=[('Pb', cur), ('Bb', cur)], writes=[PB(5)])
                A('dve', lambda e, nxt=nxt: e.tensor_copy(Bb[nxt][:, :], psb[5][:, :]), reads=[PB(5)],
                  writes=[('Bb', nxt)])
                for sub in range(4):
                    cs = slice(sub * 128, (sub + 1) * 128)
                    A('pe', lambda e, cs=cs, cur=cur: e.matmul(psb[4][:, cs], Bb[cur][:, cs], Pb_[cur][:, cs],
                                                               start=True, stop=True),
                      reads=[('Pb', cur), ('Bb', cur)], writes=[PB(4)])
                A('act', lambda e, nxt=nxt: e.activation(Pb_[nxt][:, :], psb[4][:, :], AF.Copy), reads=[PB(4)],
                  writes=[('Pb', nxt)])
                for sub in range(4):
                    cs = slice(sub * 128, (sub + 1) * 128)
                    A('pe', lambda e, cs=cs, nxt=nxt: e.matmul(psb[5][:, cs], Pb_[nxt][:, cs], S_b[:, cs],
                                                               start=True, stop=True),
                      reads=[('Pb', nxt), 'S_b'], writes=[PB(5)])
                A('dve', lambda e: e.tensor_tensor(S_f[:, :], S_f[:, :], psb[5][:, :], ALU.add),
                  reads=['S_f', PB(5)], writes=['S_f'])
                A('act', lambda e: e.activation(S_b[:, :], S_f[:, :], AF.Copy), reads=['S_f'], writes=['S_b'])
            for sub in range(4):
                cs = slice(sub * 128, (sub + 1) * 128)
                A('pe', lambda e, cs=cs, sub=sub: e.matmul(psb[4][:, sub * 64:(sub + 1) * 64], S_b[:, cs],
                                                           vtok[:, sub, :], start=True, stop=True),
                  reads=['S_b', 'vtok'], writes=[PB(4)])
            for sub in range(4):
                A('dve', lambda e, sub=sub: e.tensor_scalar(ybu[:, sub, :], psb[4][:, sub * 64:(sub + 1) * 64],
                                                            bpos[:, sub:sub + 1], None, ALU.mult),
                  reads=[PB(4), 'bpos'], writes=['ybu'])
            for sub in range(4):
                cs = slice(sub * 128, (sub + 1) * 128)
                A('pe', lambda e, cs=cs, sub=sub: e.matmul(psb[5][:, cs], kg[:, sub, :], S_b[:, cs], start=True,
                                                           stop=True),
                  reads=['S_b', 'kg'], writes=[PB(5)])
            A('dve', lambda e: e.tensor_copy(ywT[:, :], psb[5][:, :]), reads=[PB(5)], writes=['ywT'])

        def stage4(i, A):
            qdec, aqk, kdec, negbt, dl, dh = [rb(i, n) for n in ('qdec', 'aqk', 'kdec', 'negbt', 'dl', 'dh')]
            ybu, ywT = rb(i, 'ybu'), rb(i, 'ywT')
            for ch in range(8):
                sub, hf = ch // 2, ch % 2
                rs = slice(hf * 64, hf * 64 + 64)
                cs = slice(sub * 128, (sub + 1) * 128)
                cc = slice(ch * 64, (ch + 1) * 64)
                A('pe', lambda e, cs=cs: e.matmul(psb[6][:, 0:64], ywT[:, cs], St_b[:, :], start=True, stop=True),
                  reads=['ywT', 'St_b'], writes=[PB(6)])
                A('dve', lambda e, rs=rs, sub=sub: e.scalar_tensor_tensor(
                    vnew[rs, :], psb[6][rs, 0:64], negbt[rs, sub:sub + 1], ybu[rs, sub, :], ALU.mult, ALU.add),
                    reads=[PB(6), 'negbt', 'ybu'], writes=['vnew'])

                def mmo(e, cc=cc, rs=rs):
                    e.matmul(psb[7][0:64, cc], St_b[:, :], qdec[:, cc], start=True, stop=False)
                    return e.matmul(psb[7][0:64, cc], vnew[rs, :], aqk[rs, cc], start=False, stop=True)
                A('pe', mmo, reads=['St_b', 'qdec', 'vnew', 'aqk'], writes=[PB(7)])
                A('pe', lambda e, rs=rs, sub=sub: e.matmul(psb[6][:, 64:128], kdec[rs, sub, :], vnew[rs, :],
                                                           start=True, stop=True),
                  reads=['kdec', 'vnew'], writes=[PB(6)])
                dsc = (dl if hf == 0 else dh)
                A('dve', lambda e, sub=sub, dsc=dsc: e.scalar_tensor_tensor(
                    St_b[:, :], St_f[:, :], dsc[:, sub:sub + 1], psb[6][:, 64:128], ALU.mult, ALU.add),
                    reads=['St_f', PB(6), 'dl', 'dh'], writes=['St_b'])
                A('dve', lambda e, sub=sub, dsc=dsc: e.scalar_tensor_tensor(
                    St_f[:, :], St_f[:, :], dsc[:, sub:sub + 1], psb[6][:, 64:128], ALU.mult, ALU.add),
                    reads=['St_f', PB(6), 'dl', 'dh'], writes=['St_f'])
            A('act', lambda e: e.activation(o_sb[:, :], psb[7][0:64, :], AF.Copy), reads=[PB(7)], writes=['o_sb'])
            A('sp', lambda e: e.dma_start(out=obT[:, i * 512:(i + 1) * 512], in_=o_sb[:, :]),
              reads=['o_sb'], writes=[('obT', i)], slot='ob')

        def merge_lists(lists):
            lists = [l for l in lists if l]
            pos = [0] * len(lists)
            out = []
            while True:
                best, bf = None, None
                for li, l in enumerate(lists):
                    if pos[li] < len(l):
                        fr = pos[li] / len(l)
                        if bf is None or fr < bf:
                            best, bf = li, fr
                if best is None:
                    break
                out.append(lists[best][pos[best]])
                pos[best] += 1
            return out

        stages = (stage1, stage2, stage3, stage4)
        for t in range(NSEG + 3):
            lists = []
            for si, st_ in enumerate(stages):
                s = t - si
                if 0 <= s < NSEG:
                    lst = []
                    st_(s, mkA(s, lst))
                    lists.append(lst)
            for (a_, k_) in merge_lists(lists[::-1]):
                sc.add(*a_, **k_)
        sc.flush()


def build_M(S, do_fox=True, do_gdn=True, stop=99):
    nc = bass.Bass("TRN2", target_bir_lowering=False)
    hT = nc.dram_tensor("hT", [D, S], BF16, kind="ExternalInput").ap()
    wf = nc.dram_tensor("wf", [D, 193], F32, kind="ExternalInput").ap()
    bfg = nc.dram_tensor("bfg", [128, 1], F32, kind="ExternalInput").ap()
    wg = nc.dram_tensor("wg", [D, 322], F32, kind="ExternalInput").ap()
    cw = nc.dram_tensor("cw", [128, 12], F32, kind="ExternalInput").ap()
    gpar = nc.dram_tensor("gpar", [128, 2], F32, kind="ExternalInput").ap()
    oaT = nc.dram_tensor("oaT", [64, S], F32, kind="ExternalOutput").ap()
    obT = nc.dram_tensor("obT", [64, S], F32, kind="ExternalOutput").ap()
    scr = nc.dram_tensor("scr", [128, 128], BF16).ap()
    cx = Ctx(nc)
    with cx.st:
        sc = Sched(nc)
        c = make_consts(sc, cx)
        psb = [cx.ps([128, 512], F32) for _ in range(8)]
        if do_gdn:
            gdn_phase(nc, sc, c, S, hT, wg, cw, gpar, obT, psb)
        if do_fox:
            fox_phase(nc, sc, cx, c, S, hT, wf, bfg, oaT, scr, psb, stop=stop)
        sc.close()
    return nc


def build_T(TS, last):
    nc = bass.Bass("TRN2", target_bir_lowering=False)
    TT = 256
    NTT = TS // TT
    xT = nc.dram_tensor("xT", [D, TS], F32, kind="ExternalInput").ap()
    hT = nc.dram_tensor("hT", [D, TS], BF16, kind="ExternalInput").ap()
    oaT = nc.dram_tensor("oaT", [512, TS], F32, kind="ExternalInput").ap()
    obT = nc.dram_tensor("obT", [512, TS], F32, kind="ExternalInput").ap()
    w_in = nc.dram_tensor("w_in", [D, N_IN], F32, kind="ExternalInput").ap()
    w_br = nc.dram_tensor("w_br", [1536, D], F32, kind="ExternalInput").ap()
    w_out = nc.dram_tensor("w_out", [D, D], F32, kind="ExternalInput").ap()
    w_kv = nc.dram_tensor("w_kv", [D, 1024], F32, kind="ExternalInput").ap()
    memT = nc.dram_tensor("memT", [D, 256], F32, kind="ExternalInput").ap()
    mem_g = nc.dram_tensor("mem_g", [128, 8], F32, kind="ExternalInput").ap()
    b_mg = nc.dram_tensor("b_mg", [128, 24], F32, kind="ExternalInput").ap()
    gdn_g = nc.dram_tensor("gdn_g", [128, 1], F32, kind="ExternalInput").ap()
    next_g = nc.dram_tensor("next_g", [128, 8], F32, kind="ExternalInput").ap()
    xoT = nc.dram_tensor("xoT", [D, TS], F32, kind="ExternalOutput").ap()
    if not last:
        hoT = nc.dram_tensor("hoT", [D, TS], BF16, kind="ExternalOutput").ap()
    cx = Ctx(nc)
    with cx.st:
        sc = Sched(nc)
        ones_f = cx.sb([128, 128], F32)
        ones_bf = cx.sb([128, 128], BF16)
        sc.add('pool', lambda e: e.memset(ones_f[:, :], 1.0), writes=['ones_f'])
        sc.add('pool', lambda e: e.memset(ones_bf[:, :], 1.0), writes=['ones_bf'])
        psb = [cx.ps([128, 512], F32) for _ in range(8)]

        BM = {(0, 0): 0, (0, 1): 1, (1, 0): 2, (1, 1): 2, (2, 0): 3, (2, 1): 4, (3, 0): 5, (3, 1): 6,
              (4, 0): 2, (4, 1): 3, (5, 0): 4, (5, 1): 5, (6, 0): 6, (6, 1): 7, (7, 0): 0, (7, 1): 1}

        def half(bk, h):
            return psb[BM[(bk, h)]][:, 0:TT]

        def hk(bk, h):
            return ('pb', BM[(bk, h)])
        Wz = cx.sb([128, 8, 5120], BF16)
        Wbr = cx.sb([128, 12, 1024], BF16)
        Wout = cx.sb([128, 8, 1024], BF16)
        mkT = cx.sb([128, 4, 256], BF16)
        mv = cx.sb([128, 2, 512], BF16)
        stage = [cx.sb([128, 1024], F32) for _ in range(2)]
        memg_sb = cx.sb([128, 8], F32)
        bm_sb = cx.sb([128, 24], F32)
        gg_sb = cx.sb([128, 1], F32)
        ng_sb = cx.sb([128, 8], F32)
        for i, (dst, srcap) in enumerate([(memg_sb, mem_g), (bm_sb, b_mg), (gg_sb, gdn_g), (ng_sb, next_g)]):
            sc.add('sp', lambda e, dst=dst, srcap=srcap: e.dma_start(out=dst[:, :], in_=srcap[:, :]),
                   writes=[('par', i)], slot=('par', i))
        nst = [0]

        def ldw(dst, dkey, srcap):
            b = nst[0] % 2
            nst[0] += 1
            wd = srcap.shape[-1]
            sc.add('sp', lambda e: e.dma_start(out=stage[b][:, 0:wd], in_=srcap), writes=[('stage', b)],
                   slot=('stage', b))
            sc.add('pool' if b else 'dve', lambda e: e.tensor_copy(dst, stage[b][:, 0:wd]),
                   reads=[('stage', b)], writes=[dkey])

        pcx = Ctx(nc)
        with pcx.st:
            Wkv = pcx.sb([128, 8, 1024], BF16)
            mt = pcx.sb([128, 8, 256], F32)
            mn = pcx.sb([128, 8, 256], BF16)
            sqm = pcx.sb([128, 8, 256], F32)
            rstm = pcx.sb([128, 256], F32)
            for k in range(8):
                ldw(Wkv[:, k, :], 'Wkv', w_kv[k * 128:(k + 1) * 128, :])
            sc.add('sp', lambda e: e.dma_start(out=mt[:, :, :], in_=memT.rearrange("(k p) m -> p k m", p=128)),
                   writes=['mt'], slot='mt')
            sc.ops[-1]
            saved = {'g': None}
            emit_rmsnorm(sc, mt, 'mt', memg_sb, ones_f, mn, 'mn', 256, sqm, psb[0], rstm, 'mnorm', gkey=('par', 0))
            for hh in range(4):
                def mmk(e, hh=hh):
                    r = None
                    for k in range(8):
                        r = e.matmul(half(1, hh % 2), Wkv[:, k, hh * 128:(hh + 1) * 128], mn[:, k, :],
                                     start=(k == 0), stop=(k == 7))
                    return r
                sc.add('pe', mmk, reads=['Wkv', 'mn'], writes=[hk(1, hh % 2)])
                sc.add('dve', lambda e, hh=hh: e.tensor_copy(mkT[:, hh, :], half(1, hh % 2)),
                       reads=[hk(1, hh % 2)], writes=['mkT'])
            for mc in range(2):
                def mmv(e, mc=mc):
                    r = None
                    for k in range(8):
                        r = e.matmul(psb[2 + mc][:, :], mn[:, k, mc * 128:(mc + 1) * 128], Wkv[:, k, 512:1024],
                                     start=(k == 0), stop=(k == 7))
                    return r
                sc.add('pe', mmv, reads=['Wkv', 'mn'], writes=[('pb', 2 + mc)])
                sc.add('dve', lambda e, mc=mc: e.tensor_copy(mv[:, mc, :], psb[2 + mc][:, :]),
                       reads=[('pb', 2 + mc)], writes=['mv'])
            sc.flush()

        for k in range(8):
            ldw(Wz[:, k, 0:512], 'Wz', w_in[k * 128:(k + 1) * 128, OFF['az']:OFF['az'] + 512])
            ldw(Wz[:, k, 512:1024], 'Wz', w_in[k * 128:(k + 1) * 128, OFF['bz']:OFF['bz'] + 512])
            for cb in range(4):
                ldw(Wz[:, k, 1024 + cb * 1024:2048 + cb * 1024], 'Wz',
                    w_in[k * 128:(k + 1) * 128, OFF['mq'] + cb * 1024:OFF['mq'] + (cb + 1) * 1024])
        for k in range(12):
            ldw(Wbr[:, k, :], 'Wbr', w_br[k * 128:(k + 1) * 128, :])
        for k in range(8):
            ldw(Wout[:, k, :], 'Wout', w_out[k * 128:(k + 1) * 128, :])

        ht = [cx.sb([128, 8, TT], BF16) for _ in range(2)]
        xt = cx.sb([128, 8, TT], F32)
        oat = cx.sb([128, 4, TT], F32)
        obt = cx.sb([128, 4, TT], F32)
        yT = cx.sb([128, 12, TT], BF16)
        mg = cx.sb([128, 8, TT], BF16)
        hout = cx.sb([128, 8, TT], BF16 if not last else F32)
        sqs = [cx.sb([128, TT], F32) for _ in range(2)]
        sil = [cx.sb([128, TT], F32) for _ in range(2)]
        tmpB = [cx.sb([128, TT], F32) for _ in range(4)]
        tmpM = [cx.sb([128, TT], F32) for _ in range(4)]
        rstd = cx.sb([128, TT], F32)
        rden = cx.sb([128, TT], F32)
        mqs = cx.sb([128, TT], BF16)
        pT = [cx.sb([128, TT], BF16) for _ in range(2)]
        gs = [cx.sb([128, TT], F32) for _ in range(3)]
        acc = [cx.sb([128, TT], F32) for _ in range(2)]
        hv = hT.rearrange("(k p) t -> p k t", p=128)
        xv = xT.rearrange("(k p) t -> p k t", p=128)
        oav = oaT.rearrange("(k p) t -> p k t", p=128)
        obv = obT.rearrange("(k p) t -> p k t", p=128)
        xov = xoT.rearrange("(k p) t -> p k t", p=128)
        if not last:
            hov = hoT.rearrange("(k p) t -> p k t", p=128)

        zcnt = [0]

        def zproj(col0, b):
            s = zcnt[0] % 2
            zcnt[0] += 1
            dst = half(0, s)

            def mm(e):
                r = None
                for k in range(8):
                    r = e.matmul(dst, Wz[:, k, col0:col0 + 128], ht[b][:, k, :], start=(k == 0), stop=(k == 7))
                return r
            sc.add('pe', mm, reads=['Wz', ('ht', b)], writes=[hk(0, s)])
            return dst, hk(0, s)

        def load_ht(it_):
            b_ = it_ % 2
            sc.add('sp', lambda e: e.dma_start(out=ht[b_][:, :, :], in_=hv[:, :, it_ * TT:(it_ + 1) * TT]),
                   writes=[('ht', b_)], slot=('ht', b_))

        def load_o(it_):
            sc.add('sp', lambda e: e.dma_start(out=oat[:, :, :], in_=oav[:, :, it_ * TT:(it_ + 1) * TT]),
                   writes=['oat'], slot='oat')
            sc.add('sp', lambda e: e.dma_start(out=obt[:, :, :], in_=obv[:, :, it_ * TT:(it_ + 1) * TT]),
                   writes=['obt'], slot='obt')

        def phase12(it):
            b = it % 2
            t0, t1 = it * TT, (it + 1) * TT
            for hd in range(4):
                s = hd % 2
                sbk = 7
                sc.add('act', lambda e, hd=hd, s=s: e.activation(sqs[s][:, :], obt[:, hd, :], AF.Square),
                       reads=['obt'], writes=[('sqs', s)])
                sc.add('pe', lambda e, s=s, sbk=sbk: e.matmul(psb[sbk][:, 0:TT], ones_f[:, :], sqs[s][:, :],
                                                              start=True, stop=True),
                       reads=[('sqs', s), 'ones_f'], writes=[('pb', sbk)])
                sc.add('act', lambda e, hd=hd, sbk=sbk: e.activation(tmpB[hd][:, :], psb[sbk][:, 0:TT], AF.Ln,
                                                                     bias=EPS, scale=1.0 / 128),
                       reads=[('pb', sbk)], writes=[('tmpB', hd)])
                sc.add('act', lambda e, hd=hd: e.activation(tmpB[hd][:, :], tmpB[hd][:, :], AF.Exp, scale=-0.5),
                       reads=[('tmpB', hd)], writes=[('tmpB', hd)])
                sc.add('dve', lambda e, hd=hd: e.scalar_tensor_tensor(tmpB[hd][:, :], obt[:, hd, :], gg_sb[:, 0:1],
                                                                     tmpB[hd][:, :], ALU.mult, ALU.mult),
                       reads=['obt', ('tmpB', hd), ('par', 2)], writes=[('tmpB', hd)])
            for hh in range(4):
                zp, zk = zproj(1024 + hh * 128, b)
                sc.add('dve', lambda e, zp=zp: e.tensor_copy(mqs[:, :], zp), reads=[zk], writes=['mqs'])
                for mc in range(2):
                    sc.add('pe', lambda e, hh=hh, mc=mc: e.matmul(half(2, mc), mkT[:, hh, mc * 128:(mc + 1) * 128],
                                                                 mqs[:, :], start=True, stop=True),
                           reads=['mkT', 'mqs'], writes=[hk(2, mc)])
                    sc.add('act', lambda e, mc=mc: e.activation(pT[mc][:, :], half(2, mc), AF.Exp,
                                                                scale=128.0 ** -0.5),
                           reads=[hk(2, mc)], writes=[('pT', mc)])

                def mmn(e, hh=hh):
                    e.matmul(half(3, 0), mv[:, 0, hh * 128:(hh + 1) * 128], pT[0][:, :], start=True, stop=False)
                    return e.matmul(half(3, 0), mv[:, 1, hh * 128:(hh + 1) * 128], pT[1][:, :], start=False,
                                    stop=True)
                sc.add('pe', mmn, reads=['mv', ('pT', 0), ('pT', 1)], writes=[hk(3, 0)])

                def mmd(e):
                    e.matmul(half(3, 1), ones_bf[:, :], pT[0][:, :], start=True, stop=False)
                    return e.matmul(half(3, 1), ones_bf[:, :], pT[1][:, :], start=False, stop=True)
                sc.add('pe', mmd, reads=['ones_bf', ('pT', 0), ('pT', 1)], writes=[hk(3, 1)])
                sc.add('dve', lambda e: e.reciprocal(rden[:, :], half(3, 1)), reads=[hk(3, 1)], writes=['rden'])
                sc.add('dve', lambda e, hh=hh: e.tensor_tensor(tmpM[hh][:, :], half(3, 0), rden[:, :], ALU.mult),
                       reads=[hk(3, 0), 'rden'], writes=[('tmpM', hh)])
            for ci in range(12):
                s = ci % 2
                col0 = [0, 512, 1536][ci // 4] + (ci % 4) * 128
                zp, zk = zproj(col0, b)
                sc.add('act', lambda e, zp=zp, s=s: e.activation(sil[s][:, :], zp, AF.Silu),
                       reads=[zk], writes=[('sil', s)])
                if ci < 4:
                    srcb, skey = oat[:, ci, :], 'oat'
                elif ci < 8:
                    srcb, skey = tmpB[ci - 4][:, :], ('tmpB', ci - 4)
                else:
                    srcb, skey = tmpM[ci - 8][:, :], ('tmpM', ci - 8)
                sc.add('pool', lambda e, ci=ci, s=s, srcb=srcb: e.tensor_tensor(yT[:, ci, :], srcb, sil[s][:, :],
                                                                               ALU.mult),
                       reads=[skey, ('sil', s)], writes=[('yT', ci)])

        def merge_out(it):
            b = it % 2
            t0, t1 = it * TT, (it + 1) * TT
            for dc in range(8):
                for n in range(3):
                    pslot = [(4, 0), (4, 1), (5, 0)][n]
                    gslot = [(5, 1), (6, 0), (6, 1)][n]

                    def mmp(e, n=n, dc=dc, pslot=pslot):
                        r = None
                        for kc in range(4):
                            r = e.matmul(half(*pslot), Wbr[:, n * 4 + kc, dc * 128:(dc + 1) * 128],
                                         yT[:, n * 4 + kc, :], start=(kc == 0), stop=(kc == 3))
                        return r
                    sc.add('pe', mmp, reads=['Wbr'] + [('yT', n * 4 + kc) for kc in range(4)],
                           writes=[hk(*pslot)])

                    def mmg(e, n=n, dc=dc, gslot=gslot, b=b):
                        r = None
                        for k in range(8):
                            c0 = 2048 + n * 1024 + dc * 128
                            r = e.matmul(half(*gslot), Wz[:, k, c0:c0 + 128], ht[b][:, k, :], start=(k == 0),
                                         stop=(k == 7))
                        return r
                    sc.add('pe', mmg, reads=['Wz', ('ht', b)], writes=[hk(*gslot)])
                    sc.add('act', lambda e, n=n, dc=dc, gslot=gslot: e.activation(
                        gs[n][:, :], half(*gslot), AF.Sigmoid, bias=bm_sb[:, n * 8 + dc:n * 8 + dc + 1], scale=1.0),
                        reads=[hk(*gslot), ('par', 1)], writes=[('gs', n)])
                sc.add('dve', lambda e: e.tensor_tensor(acc[0][:, :], half(4, 0), gs[0][:, :], ALU.mult),
                       reads=[hk(4, 0), ('gs', 0)], writes=[('acc', 0)])
                sc.add('dve', lambda e: e.tensor_tensor(acc[1][:, :], half(4, 1), gs[1][:, :], ALU.mult),
                       reads=[hk(4, 1), ('gs', 1)], writes=[('acc', 1)])
                sc.add('pool', lambda e: e.tensor_tensor(acc[0][:, :], acc[0][:, :], acc[1][:, :], ALU.add),
                       reads=[('acc', 0), ('acc', 1)], writes=[('acc', 0)])
                sc.add('dve', lambda e: e.tensor_tensor(acc[1][:, :], half(5, 0), gs[2][:, :], ALU.mult),
                       reads=[hk(5, 0), ('gs', 2)], writes=[('acc', 1)])
                sc.add('pool', lambda e, dc=dc: e.tensor_tensor(mg[:, dc, :], acc[0][:, :], acc[1][:, :], ALU.add),
                       reads=[('acc', 0), ('acc', 1)], writes=[('mg', dc)])
            for dc in range(8):
                s = dc % 2

                def mmo(e, dc=dc, s=s):
                    r = None
                    for k in range(8):
                        r = e.matmul(half(7, s), Wout[:, k, dc * 128:(dc + 1) * 128], mg[:, k, :], start=(k == 0),
                                     stop=(k == 7))
                    return r
                sc.add('pe', mmo, reads=['Wout'] + [('mg', k) for k in range(8)], writes=[hk(7, s)])
                sc.add('dve', lambda e, dc=dc, s=s: e.tensor_tensor(xt[:, dc, :], xt[:, dc, :], half(7, s), ALU.add),
                       reads=['xt', hk(7, s)], writes=[('xn', dc)])
            allxn = [('xn', dc) for dc in range(8)]
            if not last:
                sc.add('sp', lambda e, t0=t0, t1=t1: e.dma_start(out=xov[:, :, t0:t1], in_=xt[:, :, :]),
                       reads=allxn, writes=[('xo', it)], slot='xo')

        def finalnorm(it):
            b = it % 2
            t0, t1 = it * TT, (it + 1) * TT
            for k in range(8):
                s = k % 2
                sc.add('act', lambda e, k=k, s=s: e.activation(sqs[s][:, :], xt[:, k, :], AF.Square),
                       reads=[('xn', k)], writes=[('sqs', s)])
                sc.add('pe', lambda e, k=k, s=s: e.matmul(half(1, 1), ones_f[:, :], sqs[s][:, :], start=(k == 0),
                                                         stop=(k == 7)),
                       reads=[('sqs', s), 'ones_f'], writes=[hk(1, 1)])
            sc.add('act', lambda e: e.activation(rstd[:, :], half(1, 1), AF.Ln, bias=EPS, scale=1.0 / D),
                   reads=[hk(1, 1)], writes=['rstd'])
            sc.add('act', lambda e: e.activation(rstd[:, :], rstd[:, :], AF.Exp, scale=-0.5), reads=['rstd'],
                   writes=['rstd'])
            for k in range(8):
                sc.add('dve', lambda e, k=k: e.scalar_tensor_tensor(hout[:, k, :], xt[:, k, :], ng_sb[:, k:k + 1],
                                                                   rstd[:, :], ALU.mult, ALU.mult),
                       reads=[('xn', k), 'rstd', ('par', 3)], writes=['hout'])
            if last:
                sc.add('sp', lambda e, t0=t0, t1=t1: e.dma_start(out=xov[:, :, t0:t1], in_=hout[:, :, :]),
                       reads=['hout'], writes=[('xo', it)], slot='xo')
            else:
                sc.add('sp', lambda e, t0=t0, t1=t1: e.dma_start(out=hov[:, :, t0:t1], in_=hout[:, :, :]),
                       reads=['hout'], writes=[('ho', it)], slot='ho')

        def load_x(it):
            t0, t1 = it * TT, (it + 1) * TT
            sc.add('sp', lambda e: e.dma_start(out=xt[:, :, :], in_=xv[:, :, t0:t1]),
                   writes=['xt'] + [('xn', dc) for dc in range(8)], slot='xt')

        load_ht(0)
        load_o(0)
        load_x(0)
        phase12(0)
        for it in range(NTT):
            if it + 1 < NTT:
                load_ht(it + 1)
                load_o(it + 1)
            merge_out(it)
            if it + 1 < NTT:
                phase12(it + 1)
            finalnorm(it)
            if it + 1 < NTT:
                load_x(it + 1)
        sc.close()
    return nc


def mixer_inputs(c, hT, w_in_l, b_fg_l, conv_w_l, a_log_l, dt_bias_l):
    hd, half = c // 2, c % 2
    wf = np.concatenate([w_in_l[:, OFF['aq'] + c * 64:OFF['aq'] + (c + 1) * 64],
                         w_in_l[:, OFF['ak'] + c * 64:OFF['ak'] + (c + 1) * 64],
                         w_in_l[:, OFF['av'] + c * 64:OFF['av'] + (c + 1) * 64],
                         w_in_l[:, OFF['af'] + c:OFF['af'] + c + 1]], axis=1)
    vo = hd * 128 + half * 64
    wg = np.concatenate([w_in_l[:, OFF['bq'] + hd * 128:OFF['bq'] + (hd + 1) * 128],
                         w_in_l[:, OFF['bk'] + hd * 128:OFF['bk'] + (hd + 1) * 128],
                         w_in_l[:, OFF['bv'] + vo:OFF['bv'] + vo + 64],
                         w_in_l[:, OFF['ba'] + hd:OFF['ba'] + hd + 1],
                         w_in_l[:, OFF['bb'] + hd:OFF['bb'] + hd + 1]], axis=1)
    cw = np.zeros((128, 12), np.float32)
    cw[:, 0:4] = conv_w_l[:, hd * 128:(hd + 1) * 128].T
    cw[:, 4:8] = conv_w_l[:, 512 + hd * 128:512 + (hd + 1) * 128].T
    cw[0:64, 8:12] = conv_w_l[:, 1024 + vo:1024 + vo + 64].T
    gpar = np.empty((128, 2), np.float32)
    gpar[:, 0] = a_log_l[hd]
    gpar[:, 1] = dt_bias_l[hd]
    return dict(hT=hT, wf=np.ascontiguousarray(wf), bfg=np.full((128, 1), b_fg_l[c], np.float32),
                wg=np.ascontiguousarray(wg), cw=cw, gpar=gpar)


def _lay8(v):
    return np.ascontiguousarray(np.asarray(v, np.float32).reshape(-1, 128).T)


_PROGS = {}


def _prog(name, fn):
    if name not in _PROGS:
        _PROGS[name] = fn()
    return _PROGS[name]


def kernel(x, mem, norm_g, w_in, b_fg, b_merge, conv_w, a_log, dt_bias, gdn_norm_g, mem_norm_g, w_mem_kv,
           w_branch, w_out, final_norm_g):
    f = lambda a: np.asarray(a, np.float32)
    x, mem, norm_g, w_in, b_fg, b_merge, conv_w = map(f, (x, mem, norm_g, w_in, b_fg, b_merge, conv_w))
    a_log, dt_bias, gdn_norm_g, mem_norm_g = map(f, (a_log, dt_bias, gdn_norm_g, mem_norm_g))
    w_mem_kv, w_branch, w_out, final_norm_g = map(f, (w_mem_kv, w_branch, w_out, final_norm_g))
    S = x.shape[1]
    TS = S // NCORES
    cores = list(range(NCORES))
    xT = np.ascontiguousarray(x[0].T)
    memT = np.ascontiguousarray(mem[0].T)
    sh = lambda a, c: np.ascontiguousarray(a[:, c * TS:(c + 1) * TS])
    ncP = _prog('P', lambda: build_P(TS))
    res = run_bass_kernel_spmd(ncP, [dict(xT=sh(xT, c), g=_lay8(norm_g[0])) for c in cores], core_ids=cores)
    hT = np.concatenate([np.asarray(r["hT"]) for r in res.results], axis=1)
    depth = w_in.shape[0]
    for l in range(depth):
        last = (l == depth - 1)
        ncM = _prog('M', lambda: build_M(S))
        hTc = np.ascontiguousarray(hT)
        res = run_bass_kernel_spmd(
            ncM, [mixer_inputs(c, hTc, w_in[l], b_fg[l], conv_w[l], a_log[l], dt_bias[l]) for c in cores],
            core_ids=cores)
        oaT = np.concatenate([np.asarray(r["oaT"]) for r in res.results], axis=0)
        obT = np.concatenate([np.asarray(r["obT"]) for r in res.results], axis=0)
        ncT = _prog('T%d' % int(last), lambda: build_T(TS, last))
        ng = final_norm_g if last else norm_g[l + 1]
        maps = []
        for c in cores:
            maps.append(dict(xT=sh(xT, c), hT=sh(hT, c), oaT=sh(oaT, c), obT=sh(obT, c),
                             w_in=np.ascontiguousarray(w_in[l]),
                             w_br=np.ascontiguousarray(w_branch[l].reshape(1536, D)),
                             w_out=np.ascontiguousarray(w_out[l]), w_kv=np.ascontiguousarray(w_mem_kv[l]),
                             memT=memT, mem_g=_lay8(mem_norm_g[l]), b_mg=_lay8(b_merge[l]),
                             gdn_g=np.ascontiguousarray(gdn_norm_g[l].reshape(128, 1)), next_g=_lay8(ng)))
        res = run_bass_kernel_spmd(ncT, maps, core_ids=cores)
        xT = np.concatenate([np.asarray(r["xoT"]) for r in res.results], axis=1)
        if not last:
            hT = np.concatenate([np.asarray(r["hoT"]) for r in res.results], axis=1)
    out = np.ascontiguousarray(xT.T).reshape(1, S, D).astype(np.float32)
    return out
```

```python
import contextlib
import numpy as np
import ml_dtypes
import concourse.bass as bass
import concourse.mybir as mybir
from concourse.bass_utils import run_bass_kernel_spmd

F32 = mybir.dt.float32
BF16 = mybir.dt.bfloat16
AF = mybir.ActivationFunctionType
ALU = mybir.AluOpType

D = 1024
S_FULL = 16384
NCORES = 8
EPS = 1e-6
N_IN = 8208
import os as _os
SAME_ENGINE_SYNC = bool(int(_os.environ.get('SAME_SYNC', '1')))
OFF = dict(aq=0, ak=512, av=1024, af=1536, az=1544, bq=2056, bk=2568, bv=3080,
           ba=3592, bb=3596, bz=3600, mq=4112, mz=4624, gates=5136)


def _is_psum_key(k):
    if isinstance(k, str):
        return k.startswith('ps')
    if isinstance(k, tuple) and len(k) >= 2:
        return k[0] in ('pb', 'pS', 'pO') or k[1] == 'ps'
    return False


class Sched:
    ENGS = ['pe', 'act', 'dve', 'pool', 'sp']

    def __init__(self, nc, same_engine_sync=None):
        if same_engine_sync is None:
            same_engine_sync = SAME_ENGINE_SYNC
        self.nc = nc
        self.ops = []
        self.lastw = {}
        self.readers = {}
        self.slot_count = {}
        self.same = same_engine_sync
        self.stack = contextlib.ExitStack()
        self.esem = {e: self.stack.enter_context(nc.semaphore("sem_" + e)) for e in self.ENGS}
        self.ssem = {}
        self.cnt = {e: 0 for e in self.ENGS}

    def _needs_same(self, eng):
        if eng == 'pe':
            return False
        if eng == 'pool':
            return True
        return self.same

    def add(self, eng, fn, reads=(), writes=(), slot=None):
        op = dict(eng=eng, fn=fn, deps=[], slot=slot, inc=False, id=len(self.ops))
        deps = {}
        for k in reads:
            w = self.lastw.get(k)
            if w is not None:
                deps[w['id']] = w
            if _is_psum_key(k):
                for r in self.readers.get(k, ()):
                    if r['eng'] != eng:
                        deps[r['id']] = r
        for k in writes:
            w = self.lastw.get(k)
            if w is not None:
                deps[w['id']] = w
            for r in self.readers.get(k, ()):
                deps[r['id']] = r
        for d in deps.values():
            if d is op:
                continue
            op['deps'].append(d)
            if d['slot'] is None:
                if d['eng'] != eng or self._needs_same(eng) or slot is not None:
                    d['inc'] = True
        for k in writes:
            self.lastw[k] = op
            self.readers[k] = []
        for k in reads:
            self.readers.setdefault(k, []).append(op)
        if slot is not None:
            if slot not in self.ssem:
                self.ssem[slot] = self.stack.enter_context(self.nc.semaphore("sl_%d" % len(self.ssem)))
            self.slot_count[slot] = self.slot_count.get(slot, 0) + 1
            op['slot_val'] = self.slot_count[slot] * 16
        self.ops.append(op)
        return op

    def flush(self):
        nc = self.nc
        for op in self.ops:
            if op['slot'] is None and op['inc']:
                self.cnt[op['eng']] += 1
                op['count'] = self.cnt[op['eng']]
        ops = self.ops
        esem, ssem = self.esem, self.ssem
        final = dict(self.slot_count)
        with nc.Block() as block:
            def run(ename, eng):
                known = {}
                for op in ops:
                    if op['eng'] != ename:
                        continue
                    waits = {}
                    for d in op['deps']:
                        if d['slot'] is not None:
                            key = ('s', d['slot'])
                            v = d['slot_val']
                            sem = ssem[d['slot']]
                        else:
                            if d['eng'] == ename and op['slot'] is None and not self._needs_same(ename):
                                continue
                            key = ('e', d['eng'])
                            v = d['count']
                            sem = esem[d['eng']]
                        if waits.get(key, (None, -1))[1] < v:
                            waits[key] = (sem, v)
                    for key, (sem, v) in waits.items():
                        if known.get(key, -1) >= v:
                            continue
                        known[key] = v
                        eng.wait_ge(sem, v)
                    ins = op['fn'](eng)
                    if op['slot'] is not None:
                        ins.then_inc(ssem[op['slot']], 16)
                    elif op['inc']:
                        ins.then_inc(esem[ename], 1)
                if ename == 'sp':
                    for s, n in final.items():
                        eng.wait_ge(ssem[s], n * 16)

            block.tensor(lambda e: run('pe', e))
            block.scalar(lambda e: run('act', e))
            block.vector(lambda e: run('dve', e))
            block.gpsimd(lambda e: run('pool', e))
            block.sync(lambda e: run('sp', e))
        self.ops = []
        self.lastw = {}
        self.readers = {}

    def collective(self, kind, op, src_ap, dst_ap, reads=(), writes=(), slot='cc', ncores=NCORES):
        self.flush()
        if slot not in self.ssem:
            self.ssem[slot] = self.stack.enter_context(self.nc.semaphore("sl_%d" % len(self.ssem)))
        self.slot_count[slot] = self.slot_count.get(slot, 0) + 1
        ins = self.nc.gpsimd.collective_compute(kind, op, replica_groups=[list(range(ncores))],
                                                ins=[src_ap], outs=[dst_ap])
        ins.then_inc(self.ssem[slot], 16)
        pseudo = dict(eng='pool', fn=None, deps=[], slot=slot, inc=False, id=-1,
                      slot_val=self.slot_count[slot] * 16)
        for k in writes:
            self.lastw[k] = pseudo
            self.readers[k] = []

    def close(self):
        self.flush()
        self.stack.close()


_NAME = [0]


class Ctx:
    def __init__(self, nc):
        self.nc = nc
        self.st = contextlib.ExitStack()

    def sb(self, shape, dt, name=None):
        _NAME[0] += 1
        return self.st.enter_context(self.nc.sbuf_tensor(name or ("t%d" % _NAME[0]), list(shape), dt))

    def ps(self, shape, dt, name=None):
        _NAME[0] += 1
        return self.st.enter_context(self.nc.psum_tensor(name or ("p%d" % _NAME[0]), list(shape), dt))


def emit_rmsnorm(sc, x_sb, xkey, g_sb, ones_f, out_sb, outkey, TT, sq, ps, rstd, tag, dim=D, gkey='g'):
    for k in range(8):
        sc.add('act', lambda e, k=k: e.activation(sq[:, k, :], x_sb[:, k, :], AF.Square),
               reads=[xkey], writes=[(tag, 'sq', k)])

    def mm(e):
        r = None
        for k in range(8):
            r = e.matmul(ps[:, 0:TT], ones_f[:, :], sq[:, k, :], start=(k == 0), stop=(k == 7))
        return r
    sc.add('pe', mm, reads=[(tag, 'sq', k) for k in range(8)] + ['ones_f'], writes=[(tag, 'ps')])
    sc.add('act', lambda e: e.activation(rstd[:, :], ps[:, 0:TT], AF.Sqrt, bias=EPS, scale=1.0 / dim),
           reads=[(tag, 'ps')], writes=[(tag, 'rstd')])
    sc.add('dve', lambda e: e.reciprocal(rstd[:, :], rstd[:, :]),
           reads=[(tag, 'rstd')], writes=[(tag, 'rstd')])
    for k in range(8):
        sc.add('dve',
               lambda e, k=k: e.scalar_tensor_tensor(out_sb[:, k, :], x_sb[:, k, :], g_sb[:, k:k + 1],
                                                     rstd[:, :], ALU.mult, ALU.mult),
               reads=[xkey, (tag, 'rstd'), gkey], writes=[outkey])


def build_P(TS):
    nc = bass.Bass("TRN2", target_bir_lowering=False)
    xT = nc.dram_tensor("xT", [D, TS], F32, kind="ExternalInput").ap()
    g = nc.dram_tensor("g", [128, 8], F32, kind="ExternalInput").ap()
    hT = nc.dram_tensor("hT", [D, TS], BF16, kind="ExternalOutput").ap()
    TT = 512
    cx = Ctx(nc)
    with cx.st:
        sc = Sched(nc)
        ones_f = cx.sb([128, 128], F32)
        g_sb = cx.sb([128, 8], F32)
        xs = [cx.sb([128, 8, TT], F32) for _ in range(2)]
        hs = [cx.sb([128, 8, TT], BF16) for _ in range(2)]
        sq = cx.sb([128, 8, TT], F32)
        rstd = cx.sb([128, TT], F32)
        ps = cx.ps([128, 512], F32)
        sc.add('pool', lambda e: e.memset(ones_f[:, :], 1.0), writes=['ones_f'])
        sc.add('sp', lambda e: e.dma_start(out=g_sb[:, :], in_=g[:, :]), writes=['g'], slot='g')
        xv = xT.rearrange("(k p) t -> p k t", p=128)
        hv = hT.rearrange("(k p) t -> p k t", p=128)
        for i in range(TS // TT):
            b = i % 2
            sc.add('sp', lambda e, i=i, b=b: e.dma_start(out=xs[b][:, :, :], in_=xv[:, :, i * TT:(i + 1) * TT]),
                   writes=[('x', b)], slot=('x', b))
            emit_rmsnorm(sc, xs[b], ('x', b), g_sb, ones_f, hs[b], ('h', b), TT, sq, ps, rstd, 'n')
            sc.add('sp', lambda e, i=i, b=b: e.dma_start(out=hv[:, :, i * TT:(i + 1) * TT], in_=hs[b][:, :, :]),
                   reads=[('h', b)], writes=[('hout', i)], slot=('ho', b))
        sc.close()
    return nc


def make_consts(sc, cx):
    c = {}
    c['ones'] = cx.sb([128, 128], F32)
    c['ident'] = cx.sb([128, 128], F32)
    c['uincl'] = cx.sb([128, 128], F32)
    c['ustrict'] = cx.sb([128, 128], F32)
    c['e0'] = cx.sb([128, 128], F32)
    c['ones_bf'] = cx.sb([128, 128], BF16)
    c['ident_bf'] = cx.sb([128, 128], BF16)
    c['zeros'] = cx.sb([128, 128], F32)
    sc.add('pool', lambda e: e.memset(c['ones'][:, :], 1.0), writes=['c_ones'])
    sc.add('pool', lambda e: e.memset(c['zeros'][:, :], 0.0), writes=['c_zeros'])
    sc.add('pool', lambda e: e.memset(c['ones_bf'][:, :], 1.0), writes=['c_ones_bf'])
    sc.add('pool', lambda e: e.affine_select(c['ident'][:, :], c['zeros'][:, :], [[1, 128]], ALU.not_equal, 1.0,
                                             base=0, channel_multiplier=-1),
           reads=['c_zeros'], writes=['c_ident'])
    sc.add('pool', lambda e: e.tensor_copy(c['ident_bf'][:, :], c['ident'][:, :]),
           reads=['c_ident'], writes=['c_ident_bf'])
    sc.add('pool', lambda e: e.affine_select(c['uincl'][:, :], c['ones'][:, :], [[1, 128]], ALU.is_ge, 0.0,
                                             base=0, channel_multiplier=-1),
           reads=['c_ones'], writes=['c_uincl'])
    sc.add('pool', lambda e: e.affine_select(c['ustrict'][:, :], c['ones'][:, :], [[1, 128]], ALU.is_gt, 0.0,
                                             base=0, channel_multiplier=-1),
           reads=['c_ones'], writes=['c_ustrict'])
    sc.add('pool', lambda e: e.affine_select(c['e0'][:, :], c['ones'][:, :], [[0, 128]], ALU.is_ge, 0.0,
                                             base=0, channel_multiplier=-1),
           reads=['c_ones'], writes=['c_e0'])
    return c


def load_cast(sc, dst_bf, dstkey, src_ap, stage, stagekey, eng_dma='sp', eng_cast='pool', slot=None):
    sc.add(eng_dma, lambda e: e.dma_start(out=stage, in_=src_ap), writes=[stagekey], slot=slot or stagekey)
    sc.add(eng_cast, lambda e: e.tensor_copy(dst_bf, stage), reads=[stagekey], writes=[dstkey])


def fox_phase(nc, sc, cx0, c, S, hT, wf, bfg, oaT, scr, psb, stop=99):
    NT = S // 128
    NG = S // 512
    cx = Ctx(nc)
    with cx.st:
        wq = cx.sb([128, 8, 64], BF16)
        wk = cx.sb([128, 8, 64], BF16)
        wv = cx.sb([128, 8, 65], BF16)
        wst = cx.sb([128, 8, 193], F32)
        QT = cx.sb([65, S], BF16)
        KT = cx.sb([65, S], BF16)
        V = cx.sb([128, NT, 65], BF16)
        lfr = cx.sb([128, NT], F32)
        lfn = cx.sb([128, NT], F32)
        Fn = cx.sb([128, NT], F32)
        frefB = cx.sb([128, NG], F32)
        ctok = cx.sb([128, NT], F32)
        cTT = cx.sb([128, 128], BF16)
        totT = cx.sb([128, 1], F32)
        X = cx.sb([128, 128], F32)
        negb = cx.sb([128, 1], F32)
        biasg = [cx.sb([128, NT], F32) for _ in range(2)]
        ht = [cx.sb([128, 8, 512], BF16) for _ in range(2)]
        Pb = [cx.sb([128, 512], BF16) for _ in range(4)]
        oun = cx.sb([65, 512], F32)
        rl = cx.sb([65, 512], F32)
        ofin = [cx.sb([64, 512], F32) for _ in range(2)]

        sc.add('sp', lambda e: e.dma_start(out=wst[:, :, :], in_=wf.rearrange("(k p) c -> p k c", p=128)),
               writes=['wst'], slot='wst')
        sc.add('pool', lambda e: e.tensor_copy(wq[:, :, :], wst[:, :, 0:64]), reads=['wst'], writes=['wq'])
        sc.add('pool', lambda e: e.tensor_copy(wk[:, :, :], wst[:, :, 64:128]), reads=['wst'], writes=['wk'])
        sc.add('pool', lambda e: e.tensor_copy(wv[:, :, :], wst[:, :, 128:193]), reads=['wst'], writes=['wv'])
        sc.add('sp', lambda e: e.dma_start(out=negb[:, :], in_=bfg[:, :]), writes=['negb'], slot='negb')
        sc.add('dve', lambda e: e.tensor_scalar(negb[:, :], negb[:, :], -1.0, None, ALU.mult),
               reads=['negb'], writes=['negb'])
        sc.add('pool', lambda e: e.memset(KT[64:65, :], 1.0), writes=['KTrow'])
        sc.add('pool', lambda e: e.memset(V[:, :, 64:65], 1.0), writes=['Vones'])

        if stop <= 0:
            sc.flush()
            return
        hv = hT.rearrange("(k p) t -> p k t", p=128)
        psq, psk, psv = psb[0], psb[1], psb[2]
        for i in range(NG):
            b = i % 2
            sc.add('sp', lambda e, i=i, b=b: e.dma_start(out=ht[b][:, :, :], in_=hv[:, :, i * 512:(i + 1) * 512]),
                   writes=[('ht', b)], slot=('ht', b))

            def mmq(e, b=b):
                r = None
                for k in range(8):
                    r = e.matmul(psq[0:64, :], wq[:, k, :], ht[b][:, k, :], start=(k == 0), stop=(k == 7))
                return r
            import os
            DBG = int(os.environ.get('FOXDBG', '15'))
            if DBG & 1:
              sc.add('pe', mmq, reads=[('ht', b), 'wq'], writes=['psq'])
            if DBG & 1:
              sc.add('act', lambda e, i=i: e.activation(QT[0:64, i * 512:(i + 1) * 512], psq[0:64, :], AF.Copy,
                                                      scale=0.125),
                   reads=['psq'], writes=[('QT', i)])

            def mmk(e, b=b):
                r = None
                for k in range(8):
                    r = e.matmul(psk[0:64, :], wk[:, k, :], ht[b][:, k, :], start=(k == 0), stop=(k == 7))
                return r
            if DBG & 2:
              sc.add('pe', mmk, reads=[('ht', b), 'wk'], writes=['psk'])
              sc.add('dve', lambda e, i=i: e.tensor_copy(KT[0:64, i * 512:(i + 1) * 512], psk[0:64, :]),
                   reads=['psk'], writes=[('KT', i)])

            def mmv(e, b=b):
                r = None
                for sub in range(4):
                    for k in range(8):
                        r = e.matmul(psv[:, sub * 128:sub * 128 + 65], ht[b][:, k, sub * 128:(sub + 1) * 128],
                                     wv[:, k, :], start=(k == 0), stop=(k == 7))
                return r
            pv3 = psv[:, :].rearrange("p (s c) -> p s c", c=128)
            if DBG & 4:
              sc.add('pe', mmv, reads=[('ht', b), 'wv'], writes=['psv'])
              sc.add('dve', lambda e, i=i, pv3=pv3: e.tensor_copy(V[:, 4 * i:4 * i + 4, 0:64], pv3[:, :, 0:64]),
                   reads=['psv', 'Vones'], writes=[('V', i)])
            if DBG & 8:
              sc.add('dve', lambda e, i=i, pv3=pv3: e.tensor_copy(lfr[:, 4 * i:4 * i + 4], pv3[:, :, 64]),
                   reads=['psv'], writes=[('lfr', i)])

        if stop <= 1:
            sc.flush()
            return
        allfr = [('lfr', i) for i in range(NG)]
        sc.add('act', lambda e: e.activation(lfn[:, :], lfr[:, :], AF.Exp, bias=negb[:, 0:1], scale=-1.0),
               reads=allfr + ['negb'], writes=['lfn'])
        sc.add('act', lambda e: e.activation(lfn[:, :], lfn[:, :], AF.Ln, bias=1.0, scale=1.0),
               reads=['lfn'], writes=['lfn'])
        pt = psb[0]
        sc.add('pe', lambda e: e.matmul(pt[0:NT, 0:1], lfn[:, :], c['ones'][:, 0:1], start=True, stop=True),
               reads=['lfn', 'c_ones', 'psq'], writes=['psq'])
        sc.add('dve', lambda e: e.tensor_copy(totT[0:NT, :], pt[0:NT, 0:1]), reads=['psq'], writes=['totT'])
        sc.add('dve', lambda e: e.tensor_scalar(X[0:NT, 0:NT], c['ustrict'][0:NT, 0:NT], totT[0:NT, 0:1], None,
                                                ALU.mult),
               reads=['totT', 'c_ustrict'], writes=['X'])
        pf = psb[1]

        def mmF(e):
            e.matmul(pf[:, 0:NT], c['uincl'][:, :], lfn[:, :], start=True, stop=False)
            return e.matmul(pf[:, 0:NT], c['ones'][0:NT, :], X[0:NT, 0:NT], start=False, stop=True)
        sc.add('pe', mmF, reads=['lfn', 'X', 'c_uincl', 'c_ones', 'psk'], writes=['psk'])
        sc.add('dve', lambda e: e.tensor_copy(Fn[:, :], pf[:, 0:NT]), reads=['psk'], writes=['Fn'])
        pr = psb[2]
        sc.add('pe', lambda e: e.matmul(pr[:, 0:NG], c['e0'][:, :], Fn[:, 0:NT:4], start=True, stop=True),
               reads=['Fn', 'c_e0', 'psv'], writes=['psv'])
        sc.add('dve', lambda e: e.tensor_copy(frefB[:, :], pr[:, 0:NG]), reads=['psv'], writes=['frefB'])
        for r in range(4):
            sc.add('dve', lambda e, r=r: e.tensor_tensor(ctok[:, r:NT:4], frefB[:, :], Fn[:, r:NT:4], ALU.subtract),
                   reads=['frefB', 'Fn'], writes=[('ctok', r)])
        pc = psb[3]
        sc.add('pe', lambda e: e.transpose(pc[0:NT, 0:128], ctok[:, :], c['ident'][:, :]),
               reads=[('ctok', r) for r in range(4)] + ['c_ident'], writes=['ps3'])
        sc.add('dve', lambda e: e.tensor_copy(cTT[0:NT, :], pc[0:NT, 0:128]), reads=['ps3'], writes=['cTT'])
        sc.add('sp', lambda e: e.dma_start(out=scr[0:NT, :], in_=cTT[0:NT, :]), reads=['cTT'], writes=['scr'],
               slot='scr')
        sc.add('sp', lambda e: e.dma_start(out=QT[64:65, :], in_=scr[0:NT, :].rearrange("(o j) p -> o (j p)", o=1)),
               reads=['scr'], writes=['QTrow'], slot='qtrow')

        if stop <= 2:
            sc.flush()
            return
        sc.flush()
        LA = 3
        pS = [psb[0], psb[1], psb[2], psb[3]]
        pO = [psb[6], psb[7]]
        pbc = psb[4]
        blocks = []
        for g in range(NG):
            nj = 4 * g + 4
            for j in range(nj):
                r = j - 4 * g
                c0 = 0 if r < 0 else r * 128
                blocks.append((g, j, r, c0, 512 - c0, nj))
        NB = len(blocks)

        def emit_front(bi):
            g, j, r, c0, N, nj = blocks[bi]
            gb = g % 2
            sb_ = bi % 4
            if j == 0:
                sc.add('dve', lambda e: e.tensor_scalar(biasg[gb][:, 0:nj], Fn[:, 0:nj], frefB[:, g:g + 1], None,
                                                        ALU.subtract),
                       reads=['Fn', 'frefB'], writes=[('biasg', gb)])
            sc.add('pe', lambda e: e.matmul(pS[sb_][:, 0:N], KT[0:65, j * 128:(j + 1) * 128],
                                            QT[0:65, g * 512 + c0:(g + 1) * 512], start=True, stop=True),
                   reads=['QT', 'KT'], writes=[('pS', sb_)])
            sc.add('act', lambda e: e.activation(Pb[sb_][:, 0:N], pS[sb_][:, 0:N], AF.Exp,
                                                 bias=biasg[gb][:, j:j + 1], scale=1.0),
                   reads=[('pS', sb_), ('biasg', gb)], writes=[('P', sb_)])
            if r >= 0:
                sc.add('pool', lambda e: e.affine_select(Pb[sb_][:, 0:128], Pb[sb_][:, 0:128], [[1, 128]], ALU.is_ge,
                                                         0.0, base=0, channel_multiplier=-1),
                       reads=[('P', sb_)], writes=[('P', sb_)])

        def emit_back(bi):
            g, j, r, c0, N, nj = blocks[bi]
            gb = g % 2
            sb_ = bi % 4
            sc.add('pe', lambda e: e.matmul(pO[gb][0:65, c0:512], V[:, j, 0:65], Pb[sb_][:, 0:N], start=(j == 0),
                                            stop=(j == nj - 1), skip_group_check=True),
                   reads=[('P', sb_), 'V'], writes=[('pO', gb)])
            if j == nj - 1:
                sc.add('dve', lambda e: e.tensor_copy(oun[0:65, :], pO[gb][0:65, :]),
                       reads=[('pO', gb)], writes=['oun'])
                sc.add('dve', lambda e: e.reciprocal(rl[64:65, :], oun[64:65, :]), reads=['oun'], writes=['rl'])
                pending.append((bi + 6, g, gb))

        def emit_fin(g, gb):
            sc.add('pe', lambda e: e.matmul(pbc[0:64, :], c['ones'][64:65, 0:64], rl[64:65, :], start=True,
                                            stop=True),
                   reads=['rl', 'c_ones'], writes=[('pb', 4)])
            sc.add('dve', lambda e: e.tensor_tensor(ofin[gb][:, :], oun[0:64, :], pbc[0:64, :], ALU.mult),
                   reads=['oun', ('pb', 4)], writes=[('ofin', gb)])
            sc.add('sp', lambda e: e.dma_start(out=oaT[:, g * 512:(g + 1) * 512], in_=ofin[gb][:, :]),
                   reads=[('ofin', gb)], writes=[('oaT', g)], slot=('oa', gb))

        pending = []
        for bi in range(NB + LA):
            if bi < NB:
                emit_front(bi)
            if bi - LA >= 0:
                emit_back(bi - LA)
            while pending and pending[0][0] <= bi - LA:
                _, g_, gb_ = pending.pop(0)
                emit_fin(g_, gb_)
        for _, g_, gb_ in pending:
            emit_fin(g_, gb_)
        sc.flush()


def gdn_phase(nc, sc, c, S, hT, wg, cw, gpar, obT, psb):
    NSEG = S // 512
    A = sc.add
    cx = Ctx(nc)
    PB = lambda n: ('pb', n)
    with cx.st:
        f32t = lambda *sh: cx.sb(list(sh), F32)
        bft = lambda *sh: cx.sb(list(sh), BF16)
        wst = f32t(128, 8, 322)
        wq, wk, wv, wab = bft(128, 8, 128), bft(128, 8, 128), bft(128, 8, 64), bft(128, 8, 2)
        cw_sb, gp_sb = f32t(128, 12), f32t(128, 2)
        negA = f32t(128, 1)
        M_s, M_i = f32t(128, 4, 128), f32t(128, 4, 128)
        E63, E127, EL = f32t(128, 128), f32t(128, 128), f32t(128, 128)
        ht = [bft(128, 8, 512) for _ in range(2)]
        rq, rk, rv = f32t(128, 515), f32t(128, 515), f32t(64, 515)
        cq, ck = f32t(128, 512), f32t(128, 512)
        sq2, sq2b = bft(128, 512), bft(128, 512)
        rn, rnb = f32t(128, 512), f32t(128, 512)
        g_tok, G_tok, eG, ekd, glo = [f32t(128, 4) for _ in range(5)]
        diagG, EGrow, Dm, Gam, Gs, Gi = [f32t(128, 512) for _ in range(6)]
        ktok = f32t(128, 4, 128)
        Bb = [bft(128, 512) for _ in range(2)]
        Pb_ = [bft(128, 512) for _ in range(2)]
        S_f, S_b = f32t(128, 512), bft(128, 512)
        St_f, St_b = f32t(128, 64), bft(128, 64)
        vnew = bft(128, 64)
        o_sb = f32t(64, 512)

        A('sp', lambda e: e.dma_start(out=wst[:, :, :], in_=wg.rearrange("(k p) c -> p k c", p=128)),
          writes=['gwst'], slot='gwst')
        A('pool', lambda e: e.tensor_copy(wq[:, :, :], wst[:, :, 0:128]), reads=['gwst'], writes=['gwq'])
        A('pool', lambda e: e.tensor_copy(wk[:, :, :], wst[:, :, 128:256]), reads=['gwst'], writes=['gwk'])
        A('pool', lambda e: e.tensor_copy(wv[:, :, :], wst[:, :, 256:320]), reads=['gwst'], writes=['gwv'])
        A('pool', lambda e: e.tensor_copy(wab[:, :, :], wst[:, :, 320:322]), reads=['gwst'], writes=['gwab'])
        A('sp', lambda e: e.dma_start(out=cw_sb[:, :], in_=cw[:, :]), writes=['cw'], slot='cw')
        A('sp', lambda e: e.dma_start(out=gp_sb[:, :], in_=gpar[:, :]), writes=['gp'], slot='gp')
        A('act', lambda e: e.activation(negA[:, :], gp_sb[:, 0:1], AF.Exp), reads=['gp'], writes=['negA'])
        A('dve', lambda e: e.tensor_scalar(negA[:, :], negA[:, :], -1.0, None, ALU.mult), reads=['negA'],
          writes=['negA'])
        A('pool', lambda e: e.memset(M_s[:, :, :], 1.0), writes=['M_s'])
        A('pool', lambda e: e.memset(M_i[:, :, :], 1.0), writes=['M_i'])
        A('pool', lambda e: e.affine_select(M_s[:, :, :], M_s[:, :, :], [[0, 4], [1, 128]], ALU.is_gt, 0.0, base=0,
                                            channel_multiplier=-1), reads=['M_s'], writes=['M_s'])
        A('pool', lambda e: e.affine_select(M_i[:, :, :], M_i[:, :, :], [[0, 4], [1, 128]], ALU.is_ge, 0.0, base=0,
                                            channel_multiplier=-1), reads=['M_i'], writes=['M_i'])
        A('pool', lambda e: e.memset(M_s[0:64, :, 64:128], 0.0), reads=['M_s'], writes=['M_s'])
        A('pool', lambda e: e.memset(M_i[0:64, :, 64:128], 0.0), reads=['M_i'], writes=['M_i'])
        A('pool', lambda e: e.affine_select(E63[:, :], c['zeros'][:, :], [[0, 128]], ALU.not_equal, 1.0, base=-63,
                                            channel_multiplier=1), reads=['c_zeros'], writes=['E63'])
        A('pool', lambda e: e.affine_select(E127[:, :], c['zeros'][:, :], [[0, 128]], ALU.not_equal, 1.0, base=-127,
                                            channel_multiplier=1), reads=['c_zeros'], writes=['E127'])
        A('pool', lambda e: e.tensor_copy(EL[:, 0:64], E63[:, 0:64]), reads=['E63'], writes=['EL'])
        A('pool', lambda e: e.tensor_copy(EL[:, 64:128], E127[:, 64:128]), reads=['E127', 'EL'], writes=['EL'])
        A('pool', lambda e: e.memset(rq[:, 0:3], 0.0), writes=['rq'])
        A('pool', lambda e: e.memset(rk[:, 0:3], 0.0), writes=['rk'])
        A('pool', lambda e: e.memset(rv[:, 0:3], 0.0), writes=['rv'])
        A('pool', lambda e: e.memset(St_f[:, :], 0.0), writes=['St_f'])
        A('pool', lambda e: e.memset(St_b[:, :], 0.0), writes=['St_b'])

        hv = hT.rearrange("(k p) t -> p k t", p=128)
        ones, ident = c['ones'], c['ident']
        DEPTH = dict(qn_f=2, kn_f=2, qn_bf=2, kT_bf=2, cv=2, a_sb=2, b_sb=2, B_f=2, kg=2, vtok=2, bpos=2,
                     qdec=3, aqk=3, kdec=3, negbt=3, dl=3, dh=3, ybu=2, ywT=2)
        SHAPES = dict(qn_f=(F32, (128, 512)), kn_f=(F32, (128, 512)), qn_bf=(BF16, (128, 512)),
                      kT_bf=(BF16, (128, 512)), cv=(F32, (64, 512)), a_sb=(F32, (128, 4)), b_sb=(F32, (128, 4)),
                      B_f=(F32, (128, 512)), kg=(BF16, (128, 4, 128)), vtok=(BF16, (128, 4, 64)),
                      bpos=(F32, (128, 4)), qdec=(BF16, (128, 512)), aqk=(BF16, (128, 512)),
                      kdec=(BF16, (128, 4, 128)), negbt=(F32, (128, 4)), dl=(F32, (128, 4)), dh=(F32, (128, 4)),
                      ybu=(F32, (128, 4, 64)), ywT=(BF16, (128, 512)))
        ROT = {n: [cx.sb(list(SHAPES[n][1]), SHAPES[n][0]) for _ in range(DEPTH[n])] for n in DEPTH}

        def mkA(s, lst):
            def K(k):
                return (k, s % DEPTH[k]) if (isinstance(k, str) and k in DEPTH) else k

            def A_(eng, fn, reads=(), writes=(), slot=None):
                lst.append(((eng, fn), dict(reads=[K(k) for k in reads], writes=[K(k) for k in writes], slot=slot)))
            return A_

        def rb(s, n):
            return ROT[n][s % DEPTH[n]]

        def stage1(i, A):
            b = i % 2
            qn_f, kn_f, qn_bf, kT_bf, cv, a_sb, b_sb = [rb(i, n) for n in
                                                        ('qn_f', 'kn_f', 'qn_bf', 'kT_bf', 'cv', 'a_sb', 'b_sb')]
            A('sp', lambda e: e.dma_start(out=ht[b][:, :, :], in_=hv[:, :, i * 512:(i + 1) * 512]),
              writes=[('ght', b)], slot=('ght', b))
            for (w_, M, bank, raw, key) in ((wq, 128, 0, rq, 'rq'), (wk, 128, 1, rk, 'rk'), (wv, 64, 0, rv, 'rv')):
                def mm(e, w_=w_, M=M, bank=bank):
                    r = None
                    for k in range(8):
                        r = e.matmul(psb[bank][0:M, :], w_[:, k, :], ht[b][:, k, :], start=(k == 0), stop=(k == 7))
                    return r
                A('pe', mm, reads=[('ght', b), 'gwq', 'gwk', 'gwv'], writes=[PB(bank)])
                A('dve', lambda e, M=M, bank=bank, raw=raw: e.tensor_copy(raw[0:M, 3:515], psb[bank][0:M, :]),
                  reads=[PB(bank)], writes=[key])

            def mmab(e):
                r = None
                for sub in range(4):
                    for k in range(8):
                        r = e.matmul(psb[1][:, sub * 2:sub * 2 + 2], ht[b][:, k, sub * 128:(sub + 1) * 128],
                                     wab[:, k, :], start=(k == 0), stop=(k == 7))
                return r
            A('pe', mmab, reads=[('ght', b), 'gwab'], writes=[PB(1)])
            p3 = psb[1][:, 0:8].rearrange("p (s c) -> p s c", c=2)
            A('dve', lambda e: e.tensor_copy(a_sb[:, :], p3[:, :, 0]), reads=[PB(1)], writes=['a_sb'])
            A('dve', lambda e: e.tensor_copy(b_sb[:, :], p3[:, :, 1]), reads=[PB(1)], writes=['b_sb'])
            for which, (raw, cv_, M, key, ckey) in enumerate(((rq, cq, 128, 'rq', 'cq'), (rk, ck, 128, 'rk', 'ck'),
                                                              (rv, cv, 64, 'rv', 'cv'))):
                A('act', lambda e, raw=raw, cv_=cv_, M=M, which=which: e.activation(
                    cv_[0:M, :], raw[0:M, 0:512], AF.Copy, scale=cw_sb[0:M, which * 4:which * 4 + 1]),
                    reads=[key, 'cw'], writes=[ckey])
                for tap in range(1, 4):
                    A('dve', lambda e, raw=raw, cv_=cv_, M=M, which=which, tap=tap: e.scalar_tensor_tensor(
                        cv_[0:M, :], raw[0:M, tap:tap + 512], cw_sb[0:M, which * 4 + tap:which * 4 + tap + 1],
                        cv_[0:M, :], ALU.mult, ALU.add),
                        reads=[key, 'cw', ckey], writes=[ckey])
                A('pool', lambda e, raw=raw, M=M: e.tensor_copy(raw[0:M, 0:3], raw[0:M, 512:515]),
                  reads=[key, ckey], writes=[key])
                A('act', lambda e, cv_=cv_, M=M: e.activation(cv_[0:M, :], cv_[0:M, :], AF.Silu),
                  reads=[ckey], writes=[ckey])
            for (cv_, ckey, bank, outf, okey, mul) in ((cq, 'cq', 0, qn_f, 'qn_f', 128.0 ** -0.5),
                                                      (ck, 'ck', 1, kn_f, 'kn_f', 1.0)):
                sq_, rn_ = (sq2, rn) if bank == 0 else (sq2b, rnb)
                A('act', lambda e, cv_=cv_, sq_=sq_: e.activation(sq_[:, :], cv_[:, :], AF.Square), reads=[ckey],
                  writes=[('sq2', bank)])
                A('pe', lambda e, bank=bank, sq_=sq_: e.matmul(psb[bank][:, :], c['ones_bf'][:, :], sq_[:, :],
                                                               start=True, stop=True),
                  reads=[('sq2', bank), 'c_ones_bf'], writes=[PB(bank)])
                A('act', lambda e, bank=bank, rn_=rn_: e.activation(rn_[:, :], psb[bank][:, :], AF.Ln, bias=EPS,
                                                                    scale=1.0),
                  reads=[PB(bank)], writes=[('rn', bank)])
                A('act', lambda e, rn_=rn_: e.activation(rn_[:, :], rn_[:, :], AF.Exp, scale=-0.5),
                  reads=[('rn', bank)], writes=[('rn', bank)])
                A('dve', lambda e, cv_=cv_, outf=outf, mul=mul, rn_=rn_: e.scalar_tensor_tensor(
                    outf[:, :], cv_[:, :], mul, rn_[:, :], ALU.mult, ALU.mult), reads=[ckey, ('rn', bank)],
                    writes=[okey])
            A('act', lambda e: e.activation(qn_bf[:, :], qn_f[:, :], AF.Copy), reads=['qn_f'], writes=['qn_bf'])
            A('act', lambda e: e.activation(kT_bf[:, :], kn_f[:, :], AF.Copy), reads=['kn_f'], writes=['kT_bf'])

        def stage2(i, A):
            qn_f, kn_f, qn_bf, kT_bf, cv, a_sb, b_sb = [rb(i, n) for n in
                                                        ('qn_f', 'kn_f', 'qn_bf', 'kT_bf', 'cv', 'a_sb', 'b_sb')]
            B_f, kg, vtok, bpos = [rb(i, n) for n in ('B_f', 'kg', 'vtok', 'bpos')]
            qdec, aqk, kdec, negbt, dl, dh = [rb(i, n) for n in ('qdec', 'aqk', 'kdec', 'negbt', 'dl', 'dh')]
            A('act', lambda e: e.activation(g_tok[:, :], a_sb[:, :], AF.Exp, bias=gp_sb[:, 1:2], scale=1.0),
              reads=['a_sb', 'gp'], writes=['g_tok'])
            A('act', lambda e: e.activation(g_tok[:, :], g_tok[:, :], AF.Ln, bias=1.0, scale=1.0),
              reads=['g_tok'], writes=['g_tok'])
            A('dve', lambda e: e.tensor_scalar(g_tok[:, :], g_tok[:, :], negA[:, 0:1], None, ALU.mult),
              reads=['g_tok', 'negA'], writes=['g_tok'])
            A('act', lambda e: e.activation(bpos[:, :], b_sb[:, :], AF.Exp, scale=-1.0), reads=['b_sb'],
              writes=['bpos'])
            A('dve', lambda e: e.tensor_scalar(bpos[:, :], bpos[:, :], 1.0, None, ALU.add), reads=['bpos'],
              writes=['bpos'])
            A('dve', lambda e: e.reciprocal(bpos[:, :], bpos[:, :]), reads=['bpos'], writes=['bpos'])
            A('dve', lambda e: e.tensor_scalar(negbt[:, :], bpos[:, :], -1.0, None, ALU.mult), reads=['bpos'],
              writes=['negbt'])
            A('pe', lambda e: e.matmul(psb[2][:, 0:4], M_i[:, 0, :], g_tok[:, :], start=True, stop=True),
              reads=['g_tok', 'M_i'], writes=[PB(2)])
            A('dve', lambda e: e.tensor_copy(G_tok[:, :], psb[2][:, 0:4]), reads=[PB(2)], writes=['G_tok'])
            A('act', lambda e: e.activation(eG[:, :], G_tok[:, :], AF.Exp), reads=['G_tok'], writes=['eG'])
            A('pe', lambda e: e.matmul(psb[2][:, 0:4], EL[:, :], G_tok[:, :], start=True, stop=True),
              reads=['G_tok', 'EL'], writes=[PB(2)])
            A('dve', lambda e: e.tensor_tensor(glo[:, :], psb[2][:, 0:4], G_tok[:, :], ALU.subtract),
              reads=[PB(2), 'G_tok'], writes=['glo'])
            A('act', lambda e: e.activation(ekd[:, :], glo[:, :], AF.Exp), reads=['glo'], writes=['ekd'])
            A('pe', lambda e: e.matmul(psb[2][:, 0:4], E63[:, :], G_tok[:, :], start=True, stop=True),
              reads=['G_tok', 'E63'], writes=[PB(2)])
            A('dve', lambda e: e.tensor_copy(dl[:, :], psb[2][:, 0:4]), reads=[PB(2)], writes=['dl'])
            A('act', lambda e: e.activation(dl[:, :], dl[:, :], AF.Exp), reads=['dl'], writes=['dl'])
            A('pe', lambda e: e.matmul(psb[2][:, 0:4], E127[:, :], G_tok[:, :], start=True, stop=True),
              reads=['G_tok', 'E127'], writes=[PB(2)])
            A('dve', lambda e: e.tensor_copy(dh[:, :], psb[2][:, 0:4]), reads=[PB(2)], writes=['dh'])
            A('act', lambda e: e.activation(dh[:, :], dh[:, :], AF.Exp), reads=['dh'], writes=['dh'])
            for sub in range(4):
                A('dve', lambda e, sub=sub: e.tensor_scalar(diagG[:, sub * 128:(sub + 1) * 128], ident[:, :],
                                                            G_tok[:, sub:sub + 1], None, ALU.mult),
                  reads=['G_tok', 'c_ident'], writes=[('diagG', sub)])
                A('pe', lambda e, sub=sub: e.matmul(psb[3][:, sub * 128:(sub + 1) * 128], ones[:, :],
                                                    diagG[:, sub * 128:(sub + 1) * 128], start=True, stop=True),
                  reads=[('diagG', sub), 'c_ones'], writes=[PB(3)])
            A('act', lambda e: e.activation(EGrow[:, :], psb[3][:, :], AF.Exp), reads=[PB(3)], writes=['EGrow'])
            A('pool', lambda e: e.tensor_tensor(qdec[:, :], qn_f[:, :], EGrow[:, :], ALU.mult),
              reads=['qn_f', 'EGrow'], writes=['qdec'])
            for sub in range(4):
                A('dve', lambda e, sub=sub: e.tensor_scalar(Dm[:, sub * 128:(sub + 1) * 128],
                                                            psb[3][:, sub * 128:(sub + 1) * 128],
                                                            G_tok[:, sub:sub + 1], 0.0, ALU.subtract, ALU.min),
                  reads=[PB(3), 'G_tok'], writes=['Dm'])
            A('act', lambda e: e.activation(Gam[:, :], Dm[:, :], AF.Exp), reads=['Dm'], writes=['Gam'])
            A('pool', lambda e: e.tensor_tensor(Gs[:, :], Gam[:, :], M_s[:, :, :].rearrange("p s c -> p (s c)"),
                                                ALU.mult), reads=['Gam', 'M_s'], writes=['Gs'])
            A('pool', lambda e: e.tensor_tensor(Gi[:, :], Gam[:, :], M_i[:, :, :].rearrange("p s c -> p (s c)"),
                                                ALU.mult), reads=['Gam', 'M_i'], writes=['Gi'])
            for sub in range(4):
                A('pe', lambda e, sub=sub: e.transpose(psb[2][:, sub * 128:(sub + 1) * 128],
                                                       kn_f[:, sub * 128:(sub + 1) * 128], ident[:, :]),
                  reads=['kn_f', 'c_ident'], writes=[PB(2)])
            A('dve', lambda e: e.tensor_copy(ktok[:, :, :], psb[2][:, :].rearrange("p (s c) -> p s c", c=128)),
              reads=[PB(2)], writes=['ktok'])
            for sub in range(4):
                A('act', lambda e, sub=sub: e.activation(kg[:, sub, :], ktok[:, sub, :], AF.Copy,
                                                         scale=eG[:, sub:sub + 1]), reads=['ktok', 'eG'], writes=['kg'])
                A('act', lambda e, sub=sub: e.activation(kdec[:, sub, :], ktok[:, sub, :], AF.Copy,
                                                         scale=ekd[:, sub:sub + 1]), reads=['ktok', 'ekd'],
                  writes=['kdec'])
            for sub in range(4):
                A('pe', lambda e, sub=sub: e.transpose(psb[2][:, sub * 64:(sub + 1) * 64],
                                                       cv[0:64, sub * 128:(sub + 1) * 128], ident[0:64, 0:64]),
                  reads=['cv', 'c_ident'], writes=[PB(2)])
            A('dve', lambda e: e.tensor_copy(vtok[:, :, :], psb[2][:, 0:256].rearrange("p (s c) -> p s c", c=64)),
              reads=[PB(2)], writes=['vtok'])
            for sub in range(4):
                cs = slice(sub * 128, (sub + 1) * 128)
                A('pe', lambda e, cs=cs: e.matmul(psb[2][:, cs], kT_bf[:, cs], kT_bf[:, cs], start=True, stop=True),
                  reads=['kT_bf'], writes=[PB(2)])
                A('dve', lambda e, cs=cs, sub=sub: e.scalar_tensor_tensor(
                    B_f[:, cs], psb[2][:, cs], negbt[:, sub:sub + 1], Gs[:, cs], ALU.mult, ALU.mult),
                    reads=[PB(2), 'negbt', 'Gs'], writes=['B_f'])
            for sub in range(4):
                cs = slice(sub * 128, (sub + 1) * 128)
                A('pe', lambda e, cs=cs: e.matmul(psb[3][:, cs], kT_bf[:, cs], qn_bf[:, cs], start=True, stop=True),
                  reads=['kT_bf', 'qn_bf'], writes=[PB(3)])
            A('dve', lambda e: e.tensor_tensor(aqk[:, :], psb[3][:, :], Gi[:, :], ALU.mult), reads=[PB(3), 'Gi'],
              writes=['aqk'])

        def stage3(i, A):
            B_f, kg, vtok, bpos = [rb(i, n) for n in ('B_f', 'kg', 'vtok', 'bpos')]
            ybu, ywT = rb(i, 'ybu'), rb(i, 'ywT')
            A('act', lambda e: e.activation(Bb[0][:, :], B_f[:, :], AF.Copy), reads=['B_f'], writes=[('Bb', 0)])
            for sub in range(4):
                cs = slice(sub * 128, (sub + 1) * 128)
                A('pe', lambda e, cs=cs: e.transpose(psb[4][:, cs], B_f[:, cs], ident[:, :]),
                  reads=['B_f', 'c_ident'], writes=[PB(4)])
            A('dve', lambda e: e.tensor_copy(Pb_[0][:, :], psb[4][:, :]), reads=[PB(4)], writes=[('Pb', 0)])
            for sub in range(4):
                cs = slice(sub * 128, (sub + 1) * 128)
                A('pool', lambda e, cs=cs: e.tensor_tensor(S_f[:, cs], B_f[:, cs], ident[:, :], ALU.add),
                  reads=['B_f', 'c_ident'], writes=['S_f'])
            A('act', lambda e: e.activation(S_b[:, :], S_f[:, :], AF.Copy), reads=['S_f'], writes=['S_b'])
            for j in range(5):
                cur, nxt = j % 2, (j + 1) % 2
                for sub in range(4):
                    cs = slice(sub * 128, (sub + 1) * 128)
                    A('pe', lambda e, cs=cs, cur=cur: e.matmul(psb[5][:, cs], Pb_[cur][:, cs], Bb[cur][:, cs],
                                                               start=True, stop=True),
                      reads=[('Pb', cur), ('Bb', cur)], writes=[PB(5)])
                A('dve', lambda e, nxt=nxt: e.tensor_copy(Bb[nxt][:, :], psb[5][:, :]), reads=[PB(5)],
                  writes=[('Bb', nxt)])
                for sub in range(4):
                    cs = slice(sub * 128, (sub + 1) * 128)
                    A('pe', lambda e, cs=cs, cur=cur: e.matmul(psb[4][:, cs], Bb[cur][:, cs], Pb_[cur][:, cs],
                                                               start=True, stop=True),
                      reads=[('Pb', cur), ('Bb', cur)], writes=[PB(4)])
                A('act', lambda e, nxt=nxt: e.activation(Pb_[nxt][:, :], psb[4][:, :], AF.Copy), reads=[PB(4)],
                  writes=[('Pb', nxt)])
                for sub in range(4):
                    cs = slice(sub * 128, (sub + 1) * 128)
                    A('pe', lambda e, cs=cs, nxt=nxt: e.matmul(psb[5][:, cs], Pb_[nxt][:, cs], S_b[:, cs],
                                                               start=True, stop=True),
                      reads=[('Pb', nxt), 'S_b'], writes=[PB(5)])
                A('dve', lambda e: e.tensor_tensor(S_f[:, :], S_f[:, :], psb[5][:, :], ALU.add),
                  reads=['S_f', PB(5)], writes=['S_f'])
                A('act', lambda e: e.activation(S_b[:, :], S_f[:, :], AF.Copy), reads=['S_f'], writes=['S_b'])
            for sub in range(4):
                cs = slice(sub * 128, (sub + 1) * 128)
                A('pe', lambda e, cs=cs, sub=sub: e.matmul(psb[4][:, sub * 64:(sub + 1) * 64], S_b[:, cs],
                                                           vtok[:, sub, :], start=True, stop=True),
                  reads=['S_b', 'vtok'], writes=[PB(4)])
            for sub in range(4):
                A('dve', lambda e, sub=sub: e.tensor_scalar(ybu[:, sub, :], psb[4][:, sub * 64:(sub + 1) * 64],
                                                            bpos[:, sub:sub + 1], None, ALU.mult),
                  reads=[PB(4), 'bpos'], writes=['ybu'])
            for sub in range(4):
                cs = slice(sub * 128, (sub + 1) * 128)
                A('pe', lambda e, cs=cs, sub=sub: e.matmul(psb[5][:, cs], kg[:, sub, :], S_b[:, cs], start=True,
                                                           stop=True),
                  reads=['S_b', 'kg'], writes=[PB(5)])
            A('dve', lambda e: e.tensor_copy(ywT[:, :], psb[5][:, :]), reads=[PB(5)], writes=['ywT'])

        def stage4(i, A):
            qdec, aqk, kdec, negbt, dl, dh = [rb(i, n) for n in ('qdec', 'aqk', 'kdec', 'negbt', 'dl', 'dh')]
            ybu, ywT = rb(i, 'ybu'), rb(i, 'ywT')
            for ch in range(8):
                sub, hf = ch // 2, ch % 2
                rs = slice(hf * 64, hf * 64 + 64)
                cs = slice(sub * 128, (sub + 1) * 128)
                cc = slice(ch * 64, (ch + 1) * 64)
                A('pe', lambda e, cs=cs: e.matmul(psb[6][:, 0:64], ywT[:, cs], St_b[:, :], start=True, stop=True),
                  reads=['ywT', 'St_b'], writes=[PB(6)])
                A('dve', lambda e, rs=rs, sub=sub: e.scalar_tensor_tensor(
                    vnew[rs, :], psb[6][rs, 0:64], negbt[rs, sub:sub + 1], ybu[rs, sub, :], ALU.mult, ALU.add),
                    reads=[PB(6), 'negbt', 'ybu'], writes=['vnew'])

                def mmo(e, cc=cc, rs=rs):
                    e.matmul(psb[7][0:64, cc], St_b[:, :], qdec[:, cc], start=True, stop=False)
                    return e.matmul(psb[7][0:64, cc], vnew[rs, :], aqk[rs, cc], start=False, stop=True)
                A('pe', mmo, reads=['St_b', 'qdec', 'vnew', 'aqk'], writes=[PB(7)])
                A('pe', lambda e, rs=rs, sub=sub: e.matmul(psb[6][:, 64:128], kdec[rs, sub, :], vnew[rs, :],
                                                           start=True, stop=True),
                  reads=['kdec', 'vnew'], writes=[PB(6)])
                dsc = (dl if hf == 0 else dh)
                A('dve', lambda e, sub=sub, dsc=dsc: e.scalar_tensor_tensor(
                    St_b[:, :], St_f[:, :], dsc[:, sub:sub + 1], psb[6][:, 64:128], ALU.mult, ALU.add),
                    reads=['St_f', PB(6), 'dl', 'dh'], writes=['St_b'])
                A('dve', lambda e, sub=sub, dsc=dsc: e.scalar_tensor_tensor(
                    St_f[:, :], St_f[:, :], dsc[:, sub:sub + 1], psb[6][:, 64:128], ALU.mult, ALU.add),
                    reads=['St_f', PB(6), 'dl', 'dh'], writes=['St_f'])
            A('act', lambda e: e.activation(o_sb[:, :], psb[7][0:64, :], AF.Copy), reads=[PB(7)], writes=['o_sb'])
            A('sp', lambda e: e.dma_start(out=obT[:, i * 512:(i + 1) * 512], in_=o_sb[:, :]),
              reads=['o_sb'], writes=[('obT', i)], slot='ob')

        def merge_lists(lists):
            lists = [l for l in lists if l]
            pos = [0] * len(lists)
            out = []
            while True:
                best, bf = None, None
                for li, l in enumerate(lists):
                    if pos[li] < len(l):
                        fr = pos[li] / len(l)
                        if bf is None or fr < bf:
                            best, bf = li, fr
                if best is None:
                    break
                out.append(lists[best][pos[best]])
                pos[best] += 1
            return out

        stages = (stage1, stage2, stage3, stage4)
        for t in range(NSEG + 3):
            lists = []
            for si, st_ in enumerate(stages):
                s = t - si
                if 0 <= s < NSEG:
                    lst = []
                    st_(s, mkA(s, lst))
                    lists.append(lst)
            for (a_, k_) in merge_lists(lists[::-1]):
                sc.add(*a_, **k_)
        sc.flush()


def build_M(S, do_fox=True, do_gdn=True, stop=99):
    nc = bass.Bass("TRN2", target_bir_lowering=False)
    hT = nc.dram_tensor("hT", [D, S], BF16, kind="ExternalInput").ap()
    wf = nc.dram_tensor("wf", [D, 193], F32, kind="ExternalInput").ap()
    bfg = nc.dram_tensor("bfg", [128, 1], F32, kind="ExternalInput").ap()
    wg = nc.dram_tensor("wg", [D, 322], F32, kind="ExternalInput").ap()
    cw = nc.dram_tensor("cw", [128, 12], F32, kind="ExternalInput").ap()
    gpar = nc.dram_tensor("gpar", [128, 2], F32, kind="ExternalInput").ap()
    oaT = nc.dram_tensor("oaT", [64, S], F32, kind="ExternalOutput").ap()
    obT = nc.dram_tensor("obT", [64, S], F32, kind="ExternalOutput").ap()
    scr = nc.dram_tensor("scr", [128, 128], BF16).ap()
    cx = Ctx(nc)
    with cx.st:
        sc = Sched(nc)
        c = make_consts(sc, cx)
        psb = [cx.ps([128, 512], F32) for _ in range(8)]
        if do_gdn:
            gdn_phase(nc, sc, c, S, hT, wg, cw, gpar, obT, psb)
        if do_fox:
            fox_phase(nc, sc, cx, c, S, hT, wf, bfg, oaT, scr, psb, stop=stop)
        sc.close()
    return nc


def build_T(TS, last):
    nc = bass.Bass("TRN2", target_bir_lowering=False)
    TT = 256
    NTT = TS // TT
    xT = nc.dram_tensor("xT", [D, TS], F32, kind="ExternalInput").ap()
    hT = nc.dram_tensor("hT", [D, TS], BF16, kind="ExternalInput").ap()
    oaT = nc.dram_tensor("oaT", [512, TS], F32, kind="ExternalInput").ap()
    obT = nc.dram_tensor("obT", [512, TS], F32, kind="ExternalInput").ap()
    w_in = nc.dram_tensor("w_in", [D, N_IN], F32, kind="ExternalInput").ap()
    w_br = nc.dram_tensor("w_br", [1536, D], F32, kind="ExternalInput").ap()
    w_out = nc.dram_tensor("w_out", [D, D], F32, kind="ExternalInput").ap()
    w_kv = nc.dram_tensor("w_kv", [D, 1024], F32, kind="ExternalInput").ap()
    memT = nc.dram_tensor("memT", [D, 256], F32, kind="ExternalInput").ap()
    mem_g = nc.dram_tensor("mem_g", [128, 8], F32, kind="ExternalInput").ap()
    b_mg = nc.dram_tensor("b_mg", [128, 24], F32, kind="ExternalInput").ap()
    gdn_g = nc.dram_tensor("gdn_g", [128, 1], F32, kind="ExternalInput").ap()
    next_g = nc.dram_tensor("next_g", [128, 8], F32, kind="ExternalInput").ap()
    xoT = nc.dram_tensor("xoT", [D, TS], F32, kind="ExternalOutput").ap()
    if not last:
        hoT = nc.dram_tensor("hoT", [D, TS], BF16, kind="ExternalOutput").ap()
    cx = Ctx(nc)
    with cx.st:
        sc = Sched(nc)
        ones_f = cx.sb([128, 128], F32)
        ones_bf = cx.sb([128, 128], BF16)
        sc.add('pool', lambda e: e.memset(ones_f[:, :], 1.0), writes=['ones_f'])
        sc.add('pool', lambda e: e.memset(ones_bf[:, :], 1.0), writes=['ones_bf'])
        psb = [cx.ps([128, 512], F32) for _ in range(8)]

        BM = {(0, 0): 0, (0, 1): 1, (1, 0): 2, (1, 1): 2, (2, 0): 3, (2, 1): 4, (3, 0): 5, (3, 1): 6,
              (4, 0): 2, (4, 1): 3, (5, 0): 4, (5, 1): 5, (6, 0): 6, (6, 1): 7, (7, 0): 0, (7, 1): 1}

        def half(bk, h):
            return psb[BM[(bk, h)]][:, 0:TT]

        def hk(bk, h):
            return ('pb', BM[(bk, h)])
        Wz = cx.sb([128, 8, 5120], BF16)
        Wbr = cx.sb([128, 12, 1024], BF16)
        Wout = cx.sb([128, 8, 1024], BF16)
        mkT = cx.sb([128, 4, 256], BF16)
        mv = cx.sb([128, 2, 512], BF16)
        memg_sb = cx.sb([128, 8], F32)
        bm_sb = cx.sb([128, 24], F32)
        gg_sb = cx.sb([128, 1], F32)
        ng_sb = cx.sb([128, 8], F32)
        for i, (dst, srcap) in enumerate([(memg_sb, mem_g), (bm_sb, b_mg), (gg_sb, gdn_g), (ng_sb, next_g)]):
            sc.add('sp', lambda e, dst=dst, srcap=srcap: e.dma_start(out=dst[:, :], in_=srcap[:, :]),
                   writes=[('par', i)], slot=('par', i))
        def ldw(dst, dkey, srcap):
            sc.add('pool', lambda e: e.dma_start(out=dst, in_=srcap), writes=[dkey], slot=('w', dkey[0], dkey[1]))

        pcx = Ctx(nc)
        with pcx.st:
            Wkv = pcx.sb([128, 8, 1024], BF16)
            mt = pcx.sb([128, 8, 256], F32)
            mn = pcx.sb([128, 8, 256], BF16)
            sqm = pcx.sb([128, 8, 256], F32)
            rstm = pcx.sb([128, 256], F32)
            for k in range(8):
                ldw(Wkv[:, k, :], ('Wkv', '', k), w_kv[k * 128:(k + 1) * 128, :])
            sc.add('sp', lambda e: e.dma_start(out=mt[:, :, :], in_=memT.rearrange("(k p) m -> p k m", p=128)),
                   writes=['mt'], slot='mt')
            sc.ops[-1]
            saved = {'g': None}
            emit_rmsnorm(sc, mt, 'mt', memg_sb, ones_f, mn, 'mn', 256, sqm, psb[0], rstm, 'mnorm', gkey=('par', 0))
            for hh in range(4):
                def mmk(e, hh=hh):
                    r = None
                    for k in range(8):
                        r = e.matmul(half(1, hh % 2), Wkv[:, k, hh * 128:(hh + 1) * 128], mn[:, k, :],
                                     start=(k == 0), stop=(k == 7))
                    return r
                sc.add('pe', mmk, reads=[('Wkv', '', k) for k in range(8)] + ['mn'], writes=[hk(1, hh % 2)])
                sc.add('dve', lambda e, hh=hh: e.tensor_copy(mkT[:, hh, :], half(1, hh % 2)),
                       reads=[hk(1, hh % 2)], writes=['mkT'])
            for mc in range(2):
                def mmv(e, mc=mc):
                    r = None
                    for k in range(8):
                        r = e.matmul(psb[2 + mc][:, :], mn[:, k, mc * 128:(mc + 1) * 128], Wkv[:, k, 512:1024],
                                     start=(k == 0), stop=(k == 7))
                    return r
                sc.add('pe', mmv, reads=[('Wkv', '', k) for k in range(8)] + ['mn'], writes=[('pb', 2 + mc)])
                sc.add('dve', lambda e, mc=mc: e.tensor_copy(mv[:, mc, :], psb[2 + mc][:, :]),
                       reads=[('pb', 2 + mc)], writes=['mv'])
            sc.flush()

        def wzname(col0):
            if col0 < 512:
                return 'az'
            if col0 < 1024:
                return 'bz'
            if col0 < 2048:
                return 'mqz'
            return 'g%d' % ((col0 - 2048) // 1024)

        def wzkey(col0):
            return [('Wz', wzname(col0), k) for k in range(8)]
        late = []

        def ldw_late(*a):
            late.append(a)
        for k in range(8):
            ldw(Wz[:, k, 1024:2048], ('Wz', 'mqz', k), w_in[k * 128:(k + 1) * 128, OFF['mq']:OFF['mq'] + 1024])
        for k in range(8):
            ldw(Wz[:, k, 0:512], ('Wz', 'az', k), w_in[k * 128:(k + 1) * 128, OFF['az']:OFF['az'] + 512])
            ldw(Wz[:, k, 512:1024], ('Wz', 'bz', k), w_in[k * 128:(k + 1) * 128, OFF['bz']:OFF['bz'] + 512])
        for cb in range(1, 4):
            for k in range(8):
                ldw_late(Wz[:, k, 1024 + cb * 1024:2048 + cb * 1024], ('Wz', 'g%d' % (cb - 1), k),
                         w_in[k * 128:(k + 1) * 128, OFF['mq'] + cb * 1024:OFF['mq'] + (cb + 1) * 1024])
            for k in range(4 * (cb - 1), 4 * cb):
                ldw_late(Wbr[:, k, :], ('Wbr', cb - 1, k), w_br[k * 128:(k + 1) * 128, :])
        for k in range(8):
            ldw_late(Wout[:, k, :], ('Wout', '', k), w_out[k * 128:(k + 1) * 128, :])

        ht = [cx.sb([128, 8, TT], BF16) for _ in range(2)]
        xt = cx.sb([128, 8, TT], F32)
        oat = cx.sb([128, 4, TT], F32)
        obt = cx.sb([128, 4, TT], F32)
        yT = cx.sb([128, 12, TT], BF16)
        mg = cx.sb([128, 8, TT], BF16)
        hout = cx.sb([128, 8, TT], BF16 if not last else F32)
        sqs = [cx.sb([128, TT], F32) for _ in range(2)]
        sil = [cx.sb([128, TT], F32) for _ in range(2)]
        tmpB = [cx.sb([128, TT], F32) for _ in range(4)]
        tmpM = [cx.sb([128, TT], F32) for _ in range(4)]
        rstd = cx.sb([128, TT], F32)
        rden = cx.sb([128, TT], F32)
        mqs = cx.sb([128, TT], BF16)
        pT = [cx.sb([128, TT], BF16) for _ in range(2)]
        gs = [cx.sb([128, TT], F32) for _ in range(3)]
        acc = [cx.sb([128, TT], F32) for _ in range(2)]
        hv = hT.rearrange("(k p) t -> p k t", p=128)
        xv = xT.rearrange("(k p) t -> p k t", p=128)
        oav = oaT.rearrange("(k p) t -> p k t", p=128)
        obv = obT.rearrange("(k p) t -> p k t", p=128)
        xov = xoT.rearrange("(k p) t -> p k t", p=128)
        if not last:
            hov = hoT.rearrange("(k p) t -> p k t", p=128)

        zcnt = [0]

        def zproj(col0, b):
            s = zcnt[0] % 2
            zcnt[0] += 1
            dst = half(0, s)

            def mm(e):
                r = None
                for k in range(8):
                    r = e.matmul(dst, Wz[:, k, col0:col0 + 128], ht[b][:, k, :], start=(k == 0), stop=(k == 7))
                return r
            sc.add('pe', mm, reads=wzkey(col0) + [('ht', b)], writes=[hk(0, s)])
            return dst, hk(0, s)

        def load_ht(it_):
            b_ = it_ % 2
            sc.add('sp', lambda e: e.dma_start(out=ht[b_][:, :, :], in_=hv[:, :, it_ * TT:(it_ + 1) * TT]),
                   writes=[('ht', b_)], slot=('ht', b_))

        def load_o(it_):
            sc.add('sp', lambda e: e.dma_start(out=oat[:, :, :], in_=oav[:, :, it_ * TT:(it_ + 1) * TT]),
                   writes=['oat'], slot='oat')
            sc.add('sp', lambda e: e.dma_start(out=obt[:, :, :], in_=obv[:, :, it_ * TT:(it_ + 1) * TT]),
                   writes=['obt'], slot='obt')

        def phase12(it):
            b = it % 2
            t0, t1 = it * TT, (it + 1) * TT
            for hd in range(4):
                s = hd % 2
                sbk = 7
                sc.add('act', lambda e, hd=hd, s=s: e.activation(sqs[s][:, :], obt[:, hd, :], AF.Square),
                       reads=['obt'], writes=[('sqs', s)])
                sc.add('pe', lambda e, s=s, sbk=sbk: e.matmul(psb[sbk][:, 0:TT], ones_f[:, :], sqs[s][:, :],
                                                              start=True, stop=True),
                       reads=[('sqs', s), 'ones_f'], writes=[('pb', sbk)])
                sc.add('act', lambda e, hd=hd, sbk=sbk: e.activation(tmpB[hd][:, :], psb[sbk][:, 0:TT], AF.Ln,
                                                                     bias=EPS, scale=1.0 / 128),
                       reads=[('pb', sbk)], writes=[('tmpB', hd)])
                sc.add('act', lambda e, hd=hd: e.activation(tmpB[hd][:, :], tmpB[hd][:, :], AF.Exp, scale=-0.5),
                       reads=[('tmpB', hd)], writes=[('tmpB', hd)])
                sc.add('dve', lambda e, hd=hd: e.scalar_tensor_tensor(tmpB[hd][:, :], obt[:, hd, :], gg_sb[:, 0:1],
                                                                     tmpB[hd][:, :], ALU.mult, ALU.mult),
                       reads=['obt', ('tmpB', hd), ('par', 2)], writes=[('tmpB', hd)])
            for hh in range(4):
                zp, zk = zproj(1024 + hh * 128, b)
                sc.add('dve', lambda e, zp=zp: e.tensor_copy(mqs[:, :], zp), reads=[zk], writes=['mqs'])
                for mc in range(2):
                    sc.add('pe', lambda e, hh=hh, mc=mc: e.matmul(half(2, mc), mkT[:, hh, mc * 128:(mc + 1) * 128],
                                                                 mqs[:, :], start=True, stop=True),
                           reads=['mkT', 'mqs'], writes=[hk(2, mc)])
                    sc.add('act', lambda e, mc=mc: e.activation(pT[mc][:, :], half(2, mc), AF.Exp,
                                                                scale=128.0 ** -0.5),
                           reads=[hk(2, mc)], writes=[('pT', mc)])

                def mmn(e, hh=hh):
                    e.matmul(half(3, 0), mv[:, 0, hh * 128:(hh + 1) * 128], pT[0][:, :], start=True, stop=False)
                    return e.matmul(half(3, 0), mv[:, 1, hh * 128:(hh + 1) * 128], pT[1][:, :], start=False,
                                    stop=True)
                sc.add('pe', mmn, reads=['mv', ('pT', 0), ('pT', 1)], writes=[hk(3, 0)])

                def mmd(e):
                    e.matmul(half(3, 1), ones_bf[:, :], pT[0][:, :], start=True, stop=False)
                    return e.matmul(half(3, 1), ones_bf[:, :], pT[1][:, :], start=False, stop=True)
                sc.add('pe', mmd, reads=['ones_bf', ('pT', 0), ('pT', 1)], writes=[hk(3, 1)])
                sc.add('dve', lambda e: e.reciprocal(rden[:, :], half(3, 1)), reads=[hk(3, 1)], writes=['rden'])
                sc.add('dve', lambda e, hh=hh: e.tensor_tensor(tmpM[hh][:, :], half(3, 0), rden[:, :], ALU.mult),
                       reads=[hk(3, 0), 'rden'], writes=[('tmpM', hh)])
            for ci in range(12):
                s = ci % 2
                col0 = [0, 512, 1536][ci // 4] + (ci % 4) * 128
                zp, zk = zproj(col0, b)
                sc.add('act', lambda e, zp=zp, s=s: e.activation(sil[s][:, :], zp, AF.Silu),
                       reads=[zk], writes=[('sil', s)])
                if ci < 4:
                    srcb, skey = oat[:, ci, :], 'oat'
                elif ci < 8:
                    srcb, skey = tmpB[ci - 4][:, :], ('tmpB', ci - 4)
                else:
                    srcb, skey = tmpM[ci - 8][:, :], ('tmpM', ci - 8)
                sc.add('pool', lambda e, ci=ci, s=s, srcb=srcb: e.tensor_tensor(yT[:, ci, :], srcb, sil[s][:, :],
                                                                               ALU.mult),
                       reads=[skey, ('sil', s)], writes=[('yT', ci)])

        def merge_out(it):
            b = it % 2
            t0, t1 = it * TT, (it + 1) * TT
            for dc in range(8):
                for n in range(3):
                    pslot = [(4, 0), (4, 1), (5, 0)][n]
                    gslot = [(5, 1), (6, 0), (6, 1)][n]

                    def mmp(e, n=n, dc=dc, pslot=pslot):
                        r = None
                        for kc in range(4):
                            r = e.matmul(half(*pslot), Wbr[:, n * 4 + kc, dc * 128:(dc + 1) * 128],
                                         yT[:, n * 4 + kc, :], start=(kc == 0), stop=(kc == 3))
                        return r
                    sc.add('pe', mmp, reads=[('Wbr', n, n * 4 + kc) for kc in range(4)] + [('yT', n * 4 + kc) for kc in range(4)],
                           writes=[hk(*pslot)])

                    def mmg(e, n=n, dc=dc, gslot=gslot, b=b):
                        r = None
                        for k in range(8):
                            c0 = 2048 + n * 1024 + dc * 128
                            r = e.matmul(half(*gslot), Wz[:, k, c0:c0 + 128], ht[b][:, k, :], start=(k == 0),
                                         stop=(k == 7))
                        return r
                    sc.add('pe', mmg, reads=[('Wz', 'g%d' % n, k) for k in range(8)] + [('ht', b)], writes=[hk(*gslot)])
                    sc.add('act', lambda e, n=n, dc=dc, gslot=gslot: e.activation(
                        gs[n][:, :], half(*gslot), AF.Sigmoid, bias=bm_sb[:, n * 8 + dc:n * 8 + dc + 1], scale=1.0),
                        reads=[hk(*gslot), ('par', 1)], writes=[('gs', n)])
                sc.add('dve', lambda e: e.tensor_tensor(acc[0][:, :], half(4, 0), gs[0][:, :], ALU.mult),
                       reads=[hk(4, 0), ('gs', 0)], writes=[('acc', 0)])
                sc.add('dve', lambda e: e.tensor_tensor(acc[1][:, :], half(4, 1), gs[1][:, :], ALU.mult),
                       reads=[hk(4, 1), ('gs', 1)], writes=[('acc', 1)])
                sc.add('pool', lambda e: e.tensor_tensor(acc[0][:, :], acc[0][:, :], acc[1][:, :], ALU.add),
                       reads=[('acc', 0), ('acc', 1)], writes=[('acc', 0)])
                sc.add('dve', lambda e: e.tensor_tensor(acc[1][:, :], half(5, 0), gs[2][:, :], ALU.mult),
                       reads=[hk(5, 0), ('gs', 2)], writes=[('acc', 1)])
                sc.add('pool', lambda e, dc=dc: e.tensor_tensor(mg[:, dc, :], acc[0][:, :], acc[1][:, :], ALU.add),
                       reads=[('acc', 0), ('acc', 1)], writes=[('mg', dc)])
            for dc in range(8):
                s = dc % 2

                def mmo(e, dc=dc, s=s):
                    r = None
                    for k in range(8):
                        r = e.matmul(half(7, s), Wout[:, k, dc * 128:(dc + 1) * 128], mg[:, k, :], start=(k == 0),
                                     stop=(k == 7))
                    return r
                sc.add('pe', mmo, reads=[('Wout', '', k) for k in range(8)] + [('mg', k) for k in range(8)], writes=[hk(7, s)])
                sc.add('dve', lambda e, dc=dc, s=s: e.tensor_tensor(xt[:, dc, :], xt[:, dc, :], half(7, s), ALU.add),
                       reads=['xt', hk(7, s)], writes=[('xn', dc)])
            allxn = [('xn', dc) for dc in range(8)]
            if not last:
                sc.add('sp', lambda e, t0=t0, t1=t1: e.dma_start(out=xov[:, :, t0:t1], in_=xt[:, :, :]),
                       reads=allxn, writes=[('xo', it)], slot='xo')

        def finalnorm(it):
            b = it % 2
            t0, t1 = it * TT, (it + 1) * TT
            for k in range(8):
                s = k % 2
                sc.add('act', lambda e, k=k, s=s: e.activation(sqs[s][:, :], xt[:, k, :], AF.Square),
                       reads=[('xn', k)], writes=[('sqs', s)])
                sc.add('pe', lambda e, k=k, s=s: e.matmul(half(1, 1), ones_f[:, :], sqs[s][:, :], start=(k == 0),
                                                         stop=(k == 7)),
                       reads=[('sqs', s), 'ones_f'], writes=[hk(1, 1)])
            sc.add('act', lambda e: e.activation(rstd[:, :], half(1, 1), AF.Ln, bias=EPS, scale=1.0 / D),
                   reads=[hk(1, 1)], writes=['rstd'])
            sc.add('act', lambda e: e.activation(rstd[:, :], rstd[:, :], AF.Exp, scale=-0.5), reads=['rstd'],
                   writes=['rstd'])
            for k in range(8):
                sc.add('dve', lambda e, k=k: e.scalar_tensor_tensor(hout[:, k, :], xt[:, k, :], ng_sb[:, k:k + 1],
                                                                   rstd[:, :], ALU.mult, ALU.mult),
                       reads=[('xn', k), 'rstd', ('par', 3)], writes=['hout'])
            if last:
                sc.add('sp', lambda e, t0=t0, t1=t1: e.dma_start(out=xov[:, :, t0:t1], in_=hout[:, :, :]),
                       reads=['hout'], writes=[('xo', it)], slot='xo')
            else:
                sc.add('sp', lambda e, t0=t0, t1=t1: e.dma_start(out=hov[:, :, t0:t1], in_=hout[:, :, :]),
                       reads=['hout'], writes=[('ho', it)], slot='ho')

        def load_x(it):
            t0, t1 = it * TT, (it + 1) * TT
            sc.add('sp', lambda e: e.dma_start(out=xt[:, :, :], in_=xv[:, :, t0:t1]),
                   writes=['xt'] + [('xn', dc) for dc in range(8)], slot='xt')

        load_ht(0)
        load_o(0)
        load_x(0)
        phase12(0)
        for a_ in late:
            ldw(*a_)
        for it in range(NTT):
            if it + 1 < NTT:
                load_ht(it + 1)
                load_o(it + 1)
            merge_out(it)
            if it + 1 < NTT:
                phase12(it + 1)
            finalnorm(it)
            if it + 1 < NTT:
                load_x(it + 1)
        sc.close()
    return nc


def mixer_inputs(c, hT, w_in_l, b_fg_l, conv_w_l, a_log_l, dt_bias_l):
    hd, half = c // 2, c % 2
    wf = np.concatenate([w_in_l[:, OFF['aq'] + c * 64:OFF['aq'] + (c + 1) * 64],
                         w_in_l[:, OFF['ak'] + c * 64:OFF['ak'] + (c + 1) * 64],
                         w_in_l[:, OFF['av'] + c * 64:OFF['av'] + (c + 1) * 64],
                         w_in_l[:, OFF['af'] + c:OFF['af'] + c + 1]], axis=1)
    vo = hd * 128 + half * 64
    wg = np.concatenate([w_in_l[:, OFF['bq'] + hd * 128:OFF['bq'] + (hd + 1) * 128],
                         w_in_l[:, OFF['bk'] + hd * 128:OFF['bk'] + (hd + 1) * 128],
                         w_in_l[:, OFF['bv'] + vo:OFF['bv'] + vo + 64],
                         w_in_l[:, OFF['ba'] + hd:OFF['ba'] + hd + 1],
                         w_in_l[:, OFF['bb'] + hd:OFF['bb'] + hd + 1]], axis=1)
    cw = np.zeros((128, 12), np.float32)
    cw[:, 0:4] = conv_w_l[:, hd * 128:(hd + 1) * 128].T
    cw[:, 4:8] = conv_w_l[:, 512 + hd * 128:512 + (hd + 1) * 128].T
    cw[0:64, 8:12] = conv_w_l[:, 1024 + vo:1024 + vo + 64].T
    gpar = np.empty((128, 2), np.float32)
    gpar[:, 0] = a_log_l[hd]
    gpar[:, 1] = dt_bias_l[hd]
    return dict(hT=hT, wf=np.ascontiguousarray(wf), bfg=np.full((128, 1), b_fg_l[c], np.float32),
                wg=np.ascontiguousarray(wg), cw=cw, gpar=gpar)


def _lay8(v):
    return np.ascontiguousarray(np.asarray(v, np.float32).reshape(-1, 128).T)


_PROGS = {}


def _prog(name, fn):
    if name not in _PROGS:
        _PROGS[name] = fn()
    return _PROGS[name]


def kernel(x, mem, norm_g, w_in, b_fg, b_merge, conv_w, a_log, dt_bias, gdn_norm_g, mem_norm_g, w_mem_kv,
           w_branch, w_out, final_norm_g):
    f = lambda a: np.asarray(a, np.float32)
    x, mem, norm_g, w_in, b_fg, b_merge, conv_w = map(f, (x, mem, norm_g, w_in, b_fg, b_merge, conv_w))
    a_log, dt_bias, gdn_norm_g, mem_norm_g = map(f, (a_log, dt_bias, gdn_norm_g, mem_norm_g))
    w_mem_kv, w_branch, w_out, final_norm_g = map(f, (w_mem_kv, w_branch, w_out, final_norm_g))
    S = x.shape[1]
    TS = S // NCORES
    cores = list(range(NCORES))
    xT = np.ascontiguousarray(x[0].T)
    memT = np.ascontiguousarray(mem[0].T)
    sh = lambda a, c: np.ascontiguousarray(a[:, c * TS:(c + 1) * TS])
    ncP = _prog('P', lambda: build_P(TS))
    res = run_bass_kernel_spmd(ncP, [dict(xT=sh(xT, c), g=_lay8(norm_g[0])) for c in cores], core_ids=cores)
    hT = np.concatenate([np.asarray(r["hT"]) for r in res.results], axis=1)
    depth = w_in.shape[0]
    for l in range(depth):
        last = (l == depth - 1)
        ncM = _prog('M', lambda: build_M(S))
        hTc = np.ascontiguousarray(hT)
        res = run_bass_kernel_spmd(
            ncM, [mixer_inputs(c, hTc, w_in[l], b_fg[l], conv_w[l], a_log[l], dt_bias[l]) for c in cores],
            core_ids=cores)
        oaT = np.concatenate([np.asarray(r["oaT"]) for r in res.results], axis=0)
        obT = np.concatenate([np.asarray(r["obT"]) for r in res.results], axis=0)
        ncT = _prog('T%d' % int(last), lambda: build_T(TS, last))
        ng = final_norm_g if last else norm_g[l + 1]
        maps = []
        for c in cores:
            maps.append(dict(xT=sh(xT, c), hT=sh(hT, c), oaT=sh(oaT, c), obT=sh(obT, c),
                             w_in=np.ascontiguousarray(w_in[l]),
                             w_br=np.ascontiguousarray(w_branch[l].reshape(1536, D)),
                             w_out=np.ascontiguousarray(w_out[l]), w_kv=np.ascontiguousarray(w_mem_kv[l]),
                             memT=memT, mem_g=_lay8(mem_norm_g[l]), b_mg=_lay8(b_merge[l]),
                             gdn_g=np.ascontiguousarray(gdn_norm_g[l].reshape(128, 1)), next_g=_lay8(ng)))
        res = run_bass_kernel_spmd(ncT, maps, core_ids=cores)
        xT = np.concatenate([np.asarray(r["xoT"]) for r in res.results], axis=1)
        if not last:
            hT = np.concatenate([np.asarray(r["hoT"]) for r in res.results], axis=1)
    out = np.ascontiguousarray(xT.T).reshape(1, S, D).astype(np.float32)
    return out
```

```python
import contextlib
import numpy as np
import ml_dtypes
import concourse.bass as bass
import concourse.mybir as mybir
from concourse.bass_utils import run_bass_kernel_spmd

F32 = mybir.dt.float32
BF16 = mybir.dt.bfloat16
AF = mybir.ActivationFunctionType
ALU = mybir.AluOpType

D = 1024
S_FULL = 16384
NCORES = 8
EPS = 1e-6
N_IN = 8208
import os as _os
SAME_ENGINE_SYNC = bool(int(_os.environ.get('SAME_SYNC', '1')))
OFF = dict(aq=0, ak=512, av=1024, af=1536, az=1544, bq=2056, bk=2568, bv=3080,
           ba=3592, bb=3596, bz=3600, mq=4112, mz=4624, gates=5136)


def _is_psum_key(k):
    if isinstance(k, str):
        return k.startswith('ps')
    if isinstance(k, tuple) and len(k) >= 2:
        return k[0] in ('pb', 'pS', 'pO') or k[1] == 'ps'
    return False


class Sched:
    ENGS = ['pe', 'act', 'dve', 'pool', 'sp']

    def __init__(self, nc, same_engine_sync=None):
        if same_engine_sync is None:
            same_engine_sync = SAME_ENGINE_SYNC
        self.nc = nc
        self.ops = []
        self.lastw = {}
        self.readers = {}
        self.slot_count = {}
        self.same = same_engine_sync
        self.stack = contextlib.ExitStack()
        self.esem = {e: self.stack.enter_context(nc.semaphore("sem_" + e)) for e in self.ENGS}
        self.ssem = {}
        self.cnt = {e: 0 for e in self.ENGS}

    def _needs_same(self, eng):
        if eng == 'pe':
            return False
        if eng == 'pool':
            return True
        return self.same

    def add(self, eng, fn, reads=(), writes=(), slot=None):
        op = dict(eng=eng, fn=fn, deps=[], slot=slot, inc=False, id=len(self.ops))
        deps = {}
        for k in reads:
            w = self.lastw.get(k)
            if w is not None:
                deps[w['id']] = w
            if _is_psum_key(k):
                for r in self.readers.get(k, ()):
                    if r['eng'] != eng:
                        deps[r['id']] = r
        for k in writes:
            w = self.lastw.get(k)
            if w is not None:
                deps[w['id']] = w
            for r in self.readers.get(k, ()):
                deps[r['id']] = r
        for d in deps.values():
            if d is op:
                continue
            op['deps'].append(d)
            if d['slot'] is None:
                if d['eng'] != eng or self._needs_same(eng) or slot is not None:
                    d['inc'] = True
        for k in writes:
            self.lastw[k] = op
            self.readers[k] = []
        for k in reads:
            self.readers.setdefault(k, []).append(op)
        if slot is not None:
            if slot not in self.ssem:
                self.ssem[slot] = self.stack.enter_context(self.nc.semaphore("sl_%d" % len(self.ssem)))
            self.slot_count[slot] = self.slot_count.get(slot, 0) + 1
            op['slot_val'] = self.slot_count[slot] * 16
        self.ops.append(op)
        return op

    def flush(self):
        nc = self.nc
        for op in self.ops:
            if op['slot'] is None and op['inc']:
                self.cnt[op['eng']] += 1
                op['count'] = self.cnt[op['eng']]
        ops = self.ops
        esem, ssem = self.esem, self.ssem
        final = dict(self.slot_count)
        with nc.Block() as block:
            def run(ename, eng):
                known = {}
                for op in ops:
                    if op['eng'] != ename:
                        continue
                    waits = {}
                    for d in op['deps']:
                        if d['slot'] is not None:
                            key = ('s', d['slot'])
                            v = d['slot_val']
                            sem = ssem[d['slot']]
                        else:
                            if d['eng'] == ename and op['slot'] is None and not self._needs_same(ename):
                                continue
                            key = ('e', d['eng'])
                            v = d['count']
                            sem = esem[d['eng']]
                        if waits.get(key, (None, -1))[1] < v:
                            waits[key] = (sem, v)
                    for key, (sem, v) in waits.items():
                        if known.get(key, -1) >= v:
                            continue
                        known[key] = v
                        eng.wait_ge(sem, v)
                    ins = op['fn'](eng)
                    if op['slot'] is not None:
                        ins.then_inc(ssem[op['slot']], 16)
                    elif op['inc']:
                        ins.then_inc(esem[ename], 1)
                if ename == 'sp':
                    for s, n in final.items():
                        eng.wait_ge(ssem[s], n * 16)

            block.tensor(lambda e: run('pe', e))
            block.scalar(lambda e: run('act', e))
            block.vector(lambda e: run('dve', e))
            block.gpsimd(lambda e: run('pool', e))
            block.sync(lambda e: run('sp', e))
        self.ops = []
        self.lastw = {}
        self.readers = {}

    def collective(self, kind, op, src_ap, dst_ap, reads=(), writes=(), slot='cc', ncores=NCORES):
        self.flush()
        if slot not in self.ssem:
            self.ssem[slot] = self.stack.enter_context(self.nc.semaphore("sl_%d" % len(self.ssem)))
        self.slot_count[slot] = self.slot_count.get(slot, 0) + 1
        ins = self.nc.gpsimd.collective_compute(kind, op, replica_groups=[list(range(ncores))],
                                                ins=[src_ap], outs=[dst_ap])
        ins.then_inc(self.ssem[slot], 16)
        pseudo = dict(eng='pool', fn=None, deps=[], slot=slot, inc=False, id=-1,
                      slot_val=self.slot_count[slot] * 16)
        for k in writes:
            self.lastw[k] = pseudo
            self.readers[k] = []

    def close(self):
        self.flush()
        self.stack.close()


_NAME = [0]


class Ctx:
    def __init__(self, nc):
        self.nc = nc
        self.st = contextlib.ExitStack()

    def sb(self, shape, dt, name=None):
        _NAME[0] += 1
        return self.st.enter_context(self.nc.sbuf_tensor(name or ("t%d" % _NAME[0]), list(shape), dt))

    def ps(self, shape, dt, name=None):
        _NAME[0] += 1
        return self.st.enter_context(self.nc.psum_tensor(name or ("p%d" % _NAME[0]), list(shape), dt))


def emit_rmsnorm(sc, x_sb, xkey, g_sb, ones_f, out_sb, outkey, TT, sq, ps, rstd, tag, dim=D, gkey='g',
                 oneskey='ones_f'):
    for k in range(8):
        sc.add('act', lambda e, k=k: e.activation(sq[:, k, :], x_sb[:, k, :], AF.Square),
               reads=[xkey], writes=[(tag, 'sq', k)])

    def mm(e):
        r = None
        for k in range(8):
            r = e.matmul(ps[:, 0:TT], ones_f[:, :], sq[:, k, :], start=(k == 0), stop=(k == 7))
        return r
    sc.add('pe', mm, reads=[(tag, 'sq', k) for k in range(8)] + [oneskey], writes=[(tag, 'ps')])
    sc.add('act', lambda e: e.activation(rstd[:, :], ps[:, 0:TT], AF.Ln, bias=EPS, scale=1.0 / dim),
           reads=[(tag, 'ps')], writes=[(tag, 'rstd')])
    sc.add('act', lambda e: e.activation(rstd[:, :], rstd[:, :], AF.Exp, scale=-0.5),
           reads=[(tag, 'rstd')], writes=[(tag, 'rstd')])
    for k in range(8):
        sc.add('dve',
               lambda e, k=k: e.scalar_tensor_tensor(out_sb[:, k, :], x_sb[:, k, :], g_sb[:, k:k + 1],
                                                     rstd[:, :], ALU.mult, ALU.mult),
               reads=[xkey, (tag, 'rstd'), gkey], writes=[outkey])


def build_P(TS):
    nc = bass.Bass("TRN2", target_bir_lowering=False)
    xT = nc.dram_tensor("xT", [D, TS], F32, kind="ExternalInput").ap()
    g = nc.dram_tensor("g", [128, 8], F32, kind="ExternalInput").ap()
    hT = nc.dram_tensor("hT", [D, TS], BF16, kind="ExternalOutput").ap()
    TT = 512
    cx = Ctx(nc)
    with cx.st:
        sc = Sched(nc)
        ones_f = cx.sb([128, 128], BF16)
        g_sb = cx.sb([128, 8], F32)
        xs = [cx.sb([128, 8, TT], F32) for _ in range(2)]
        hs = [cx.sb([128, 8, TT], BF16) for _ in range(2)]
        sq = cx.sb([128, 8, TT], BF16)
        rstd = cx.sb([128, TT], F32)
        ps = cx.ps([128, 512], F32)
        sc.add('pool', lambda e: e.memset(ones_f[:, :], 1.0), writes=['ones_f'])
        sc.add('sp', lambda e: e.dma_start(out=g_sb[:, :], in_=g[:, :]), writes=['g'], slot='g')
        xv = xT.rearrange("(k p) t -> p k t", p=128)
        hv = hT.rearrange("(k p) t -> p k t", p=128)
        for i in range(TS // TT):
            b = i % 2
            sc.add('sp', lambda e, i=i, b=b: e.dma_start(out=xs[b][:, :, :], in_=xv[:, :, i * TT:(i + 1) * TT]),
                   writes=[('x', b)], slot=('x', b))
            emit_rmsnorm(sc, xs[b], ('x', b), g_sb, ones_f, hs[b], ('h', b), TT, sq, ps, rstd, 'n')
            sc.add('sp', lambda e, i=i, b=b: e.dma_start(out=hv[:, :, i * TT:(i + 1) * TT], in_=hs[b][:, :, :]),
                   reads=[('h', b)], writes=[('hout', i)], slot=('ho', b))
        sc.close()
    return nc


def make_consts(sc, cx):
    c = {}
    c['ones'] = cx.sb([128, 128], F32)
    c['ident'] = cx.sb([128, 128], F32)
    c['uincl'] = cx.sb([128, 128], F32)
    c['ustrict'] = cx.sb([128, 128], F32)
    c['e0'] = cx.sb([128, 128], F32)
    c['ones_bf'] = cx.sb([128, 128], BF16)
    c['ident_bf'] = cx.sb([128, 128], BF16)
    c['zeros'] = cx.sb([128, 128], F32)
    sc.add('pool', lambda e: e.memset(c['ones'][:, :], 1.0), writes=['c_ones'])
    sc.add('pool', lambda e: e.memset(c['zeros'][:, :], 0.0), writes=['c_zeros'])
    sc.add('pool', lambda e: e.memset(c['ones_bf'][:, :], 1.0), writes=['c_ones_bf'])
    sc.add('pool', lambda e: e.affine_select(c['ident'][:, :], c['zeros'][:, :], [[1, 128]], ALU.not_equal, 1.0,
                                             base=0, channel_multiplier=-1),
           reads=['c_zeros'], writes=['c_ident'])
    sc.add('pool', lambda e: e.tensor_copy(c['ident_bf'][:, :], c['ident'][:, :]),
           reads=['c_ident'], writes=['c_ident_bf'])
    sc.add('pool', lambda e: e.affine_select(c['uincl'][:, :], c['ones'][:, :], [[1, 128]], ALU.is_ge, 0.0,
                                             base=0, channel_multiplier=-1),
           reads=['c_ones'], writes=['c_uincl'])
    sc.add('pool', lambda e: e.affine_select(c['ustrict'][:, :], c['ones'][:, :], [[1, 128]], ALU.is_gt, 0.0,
                                             base=0, channel_multiplier=-1),
           reads=['c_ones'], writes=['c_ustrict'])
    sc.add('pool', lambda e: e.affine_select(c['e0'][:, :], c['ones'][:, :], [[0, 128]], ALU.is_ge, 0.0,
                                             base=0, channel_multiplier=-1),
           reads=['c_ones'], writes=['c_e0'])
    return c


def load_cast(sc, dst_bf, dstkey, src_ap, stage, stagekey, eng_dma='sp', eng_cast='pool', slot=None):
    sc.add(eng_dma, lambda e: e.dma_start(out=stage, in_=src_ap), writes=[stagekey], slot=slot or stagekey)
    sc.add(eng_cast, lambda e: e.tensor_copy(dst_bf, stage), reads=[stagekey], writes=[dstkey])


def fox_phase(nc, sc, cx0, c, S, hT, wf, bfg, oaT, scr, psb, stop=99):
    NT = S // 128
    NG = S // 512
    cx = Ctx(nc)
    with cx.st:
        wq = cx.sb([128, 8, 64], BF16)
        wk = cx.sb([128, 8, 64], BF16)
        wv = cx.sb([128, 8, 65], BF16)
        QT = cx.sb([65, S], BF16)
        KT = cx.sb([65, S], BF16)
        V = cx.sb([128, NT, 65], BF16)
        lfr = cx.sb([128, NT], F32)
        lfn = cx.sb([128, NT], F32)
        Fn = cx.sb([128, NT], F32)
        frefB = cx.sb([128, NG], F32)
        ctok = cx.sb([128, NT], F32)
        cTT = cx.sb([128, 128], BF16)
        totT = cx.sb([128, 1], F32)
        X = cx.sb([128, 128], F32)
        negb = cx.sb([128, 1], F32)
        biasg = [cx.sb([128, NT], F32) for _ in range(2)]
        ht = [cx.sb([128, 8, 512], BF16) for _ in range(2)]
        Pb = [cx.sb([128, 512], BF16) for _ in range(5)]
        oun = cx.sb([65, 512], F32)
        rl = cx.sb([65, 512], F32)
        ofin = [cx.sb([64, 512], F32) for _ in range(2)]

        wfv = wf.rearrange("(k p) c -> p k c", p=128)
        sc.add('pool', lambda e: e.dma_start(out=wq[:, :, :], in_=wfv[:, :, 0:64]), writes=['wq'], slot='wq')
        sc.add('pool', lambda e: e.dma_start(out=wk[:, :, :], in_=wfv[:, :, 64:128]), writes=['wk'], slot='wk')
        sc.add('pool', lambda e: e.dma_start(out=wv[:, :, :], in_=wfv[:, :, 128:193]), writes=['wv'], slot='wv')
        sc.add('sp', lambda e: e.dma_start(out=negb[:, :], in_=bfg[:, :]), writes=['negb'], slot='negb')
        sc.add('dve', lambda e: e.tensor_scalar(negb[:, :], negb[:, :], -1.0, None, ALU.mult),
               reads=['negb'], writes=['negb'])
        sc.add('pool', lambda e: e.memset(KT[64:65, :], 1.0), writes=['KTrow'])
        sc.add('pool', lambda e: e.memset(V[:, :, 64:65], 1.0), writes=['Vones'])

        if stop <= 0:
            sc.flush()
            return
        hv = hT.rearrange("(k p) t -> p k t", p=128)
        psq, psk, psv = psb[0], psb[1], psb[2]
        for i in range(NG):
            b = i % 2
            sc.add('sp', lambda e, i=i, b=b: e.dma_start(out=ht[b][:, :, :], in_=hv[:, :, i * 512:(i + 1) * 512]),
                   writes=[('ht', b)], slot=('ht', b))

            def mmq(e, b=b):
                r = None
                for k in range(8):
                    r = e.matmul(psq[0:64, :], wq[:, k, :], ht[b][:, k, :], start=(k == 0), stop=(k == 7))
                return r
            import os
            DBG = int(os.environ.get('FOXDBG', '15'))
            if DBG & 1:
              sc.add('pe', mmq, reads=[('ht', b), 'wq'], writes=['psq'])
            if DBG & 1:
              sc.add('act', lambda e, i=i: e.activation(QT[0:64, i * 512:(i + 1) * 512], psq[0:64, :], AF.Copy,
                                                      scale=0.125),
                   reads=['psq'], writes=[('QT', i)])

            def mmk(e, b=b):
                r = None
                for k in range(8):
                    r = e.matmul(psk[0:64, :], wk[:, k, :], ht[b][:, k, :], start=(k == 0), stop=(k == 7))
                return r
            if DBG & 2:
              sc.add('pe', mmk, reads=[('ht', b), 'wk'], writes=['psk'])
              sc.add('dve', lambda e, i=i: e.tensor_copy(KT[0:64, i * 512:(i + 1) * 512], psk[0:64, :]),
                   reads=['psk'], writes=[('KT', i)])

            def mmv(e, b=b):
                r = None
                for sub in range(4):
                    for k in range(8):
                        r = e.matmul(psv[:, sub * 128:sub * 128 + 65], ht[b][:, k, sub * 128:(sub + 1) * 128],
                                     wv[:, k, :], start=(k == 0), stop=(k == 7))
                return r
            pv3 = psv[:, :].rearrange("p (s c) -> p s c", c=128)
            if DBG & 4:
              sc.add('pe', mmv, reads=[('ht', b), 'wv'], writes=['psv'])
              sc.add('dve', lambda e, i=i, pv3=pv3: e.tensor_copy(V[:, 4 * i:4 * i + 4, 0:64], pv3[:, :, 0:64]),
                   reads=['psv', 'Vones'], writes=[('V', i)])
            if DBG & 8:
              sc.add('dve', lambda e, i=i, pv3=pv3: e.tensor_copy(lfr[:, 4 * i:4 * i + 4], pv3[:, :, 64]),
                   reads=['psv'], writes=[('lfr', i)])

        if stop <= 1:
            sc.flush()
            return
        allfr = [('lfr', i) for i in range(NG)]
        sc.add('act', lambda e: e.activation(lfn[:, :], lfr[:, :], AF.Exp, bias=negb[:, 0:1], scale=-1.0),
               reads=allfr + ['negb'], writes=['lfn'])
        sc.add('act', lambda e: e.activation(lfn[:, :], lfn[:, :], AF.Ln, bias=1.0, scale=1.0),
               reads=['lfn'], writes=['lfn'])
        pt = psb[0]
        sc.add('pe', lambda e: e.matmul(pt[0:NT, 0:1], lfn[:, :], c['ones'][:, 0:1], start=True, stop=True),
               reads=['lfn', 'c_ones', 'psq'], writes=['psq'])
        sc.add('dve', lambda e: e.tensor_copy(totT[0:NT, :], pt[0:NT, 0:1]), reads=['psq'], writes=['totT'])
        sc.add('dve', lambda e: e.tensor_scalar(X[0:NT, 0:NT], c['ustrict'][0:NT, 0:NT], totT[0:NT, 0:1], None,
                                                ALU.mult),
               reads=['totT', 'c_ustrict'], writes=['X'])
        pf = psb[1]

        def mmF(e):
            e.matmul(pf[:, 0:NT], c['uincl'][:, :], lfn[:, :], start=True, stop=False)
            return e.matmul(pf[:, 0:NT], c['ones'][0:NT, :], X[0:NT, 0:NT], start=False, stop=True)
        sc.add('pe', mmF, reads=['lfn', 'X', 'c_uincl', 'c_ones', 'psk'], writes=['psk'])
        sc.add('dve', lambda e: e.tensor_copy(Fn[:, :], pf[:, 0:NT]), reads=['psk'], writes=['Fn'])
        pr = psb[2]
        sc.add('pe', lambda e: e.matmul(pr[:, 0:NG], c['e0'][:, :], Fn[:, 0:NT:4], start=True, stop=True),
               reads=['Fn', 'c_e0', 'psv'], writes=['psv'])
        sc.add('dve', lambda e: e.tensor_copy(frefB[:, :], pr[:, 0:NG]), reads=['psv'], writes=['frefB'])
        for r in range(4):
            sc.add('dve', lambda e, r=r: e.tensor_tensor(ctok[:, r:NT:4], frefB[:, :], Fn[:, r:NT:4], ALU.subtract),
                   reads=['frefB', 'Fn'], writes=[('ctok', r)])
        pc = psb[3]
        sc.add('pe', lambda e: e.transpose(pc[0:NT, 0:128], ctok[:, :], c['ident'][:, :]),
               reads=[('ctok', r) for r in range(4)] + ['c_ident'], writes=['ps3'])
        sc.add('dve', lambda e: e.tensor_copy(cTT[0:NT, :], pc[0:NT, 0:128]), reads=['ps3'], writes=['cTT'])
        sc.add('sp', lambda e: e.dma_start(out=scr[0:NT, :], in_=cTT[0:NT, :]), reads=['cTT'], writes=['scr'],
               slot='scr')
        sc.add('sp', lambda e: e.dma_start(out=QT[64:65, :], in_=scr[0:NT, :].rearrange("(o j) p -> o (j p)", o=1)),
               reads=['scr'], writes=['QTrow'], slot='qtrow')

        if stop <= 2:
            sc.flush()
            return
        sc.flush()
        LA = 4
        pS = [psb[0], psb[1], psb[2], psb[3], psb[5]]
        pO = [psb[6], psb[7]]
        pbc = psb[4]
        blocks = []
        for g in range(NG):
            nj = 4 * g + 4
            for j in range(nj):
                r = j - 4 * g
                c0 = 0 if r < 0 else r * 128
                blocks.append((g, j, r, c0, 512 - c0, nj))
        NB = len(blocks)

        def emit_front(bi):
            g, j, r, c0, N, nj = blocks[bi]
            gb = g % 2
            sb_ = bi % 5
            if j == 0:
                sc.add('dve', lambda e: e.tensor_scalar(biasg[gb][:, 0:nj], Fn[:, 0:nj], frefB[:, g:g + 1], None,
                                                        ALU.subtract),
                       reads=['Fn', 'frefB'], writes=[('biasg', gb)])
            sc.add('pe', lambda e: e.matmul(pS[sb_][:, 0:N], KT[0:65, j * 128:(j + 1) * 128],
                                            QT[0:65, g * 512 + c0:(g + 1) * 512], start=True, stop=True),
                   reads=['QT', 'KT'], writes=[('pS', sb_)])
            sc.add('act', lambda e: e.activation(Pb[sb_][:, 0:N], pS[sb_][:, 0:N], AF.Exp,
                                                 bias=biasg[gb][:, j:j + 1], scale=1.0),
                   reads=[('pS', sb_), ('biasg', gb)], writes=[('P', sb_)])
            if r >= 0:
                sc.add('pool', lambda e: e.affine_select(Pb[sb_][:, 0:128], Pb[sb_][:, 0:128], [[1, 128]], ALU.is_ge,
                                                         0.0, base=0, channel_multiplier=-1),
                       reads=[('P', sb_)], writes=[('P', sb_)])

        def emit_back(bi):
            g, j, r, c0, N, nj = blocks[bi]
            gb = g % 2
            sb_ = bi % 5
            sc.add('pe', lambda e: e.matmul(pO[gb][0:65, c0:512], V[:, j, 0:65], Pb[sb_][:, 0:N], start=(j == 0),
                                            stop=(j == nj - 1), skip_group_check=True),
                   reads=[('P', sb_), 'V'], writes=[('pO', gb)])
            if j == nj - 1:
                sc.add('dve', lambda e: e.tensor_copy(oun[0:65, :], pO[gb][0:65, :]),
                       reads=[('pO', gb)], writes=['oun'])
                sc.add('dve', lambda e: e.reciprocal(rl[64:65, :], oun[64:65, :]), reads=['oun'], writes=['rl'])
                pending.append((bi + 6, g, gb))

        def emit_fin(g, gb):
            sc.add('pe', lambda e: e.matmul(pbc[0:64, :], c['ones'][64:65, 0:64], rl[64:65, :], start=True,
                                            stop=True),
                   reads=['rl', 'c_ones'], writes=[('pb', 4)])
            sc.add('dve', lambda e: e.tensor_tensor(ofin[gb][:, :], oun[0:64, :], pbc[0:64, :], ALU.mult),
                   reads=['oun', ('pb', 4)], writes=[('ofin', gb)])
            sc.add('sp', lambda e: e.dma_start(out=oaT[:, g * 512:(g + 1) * 512], in_=ofin[gb][:, :]),
                   reads=[('ofin', gb)], writes=[('oaT', g)], slot=('oa', gb))

        pending = []
        for bi in range(NB + LA):
            if bi < NB:
                emit_front(bi)
            if bi - LA >= 0:
                emit_back(bi - LA)
            while pending and pending[0][0] <= bi - LA:
                _, g_, gb_ = pending.pop(0)
                emit_fin(g_, gb_)
        for _, g_, gb_ in pending:
            emit_fin(g_, gb_)
        sc.flush()


def gdn_phase(nc, sc, c, S, hT, wg, cw, gpar, obT, psb):
    NSEG = S // 512
    A = sc.add
    cx = Ctx(nc)
    PB = lambda n: ('pb', n)
    with cx.st:
        f32t = lambda *sh: cx.sb(list(sh), F32)
        bft = lambda *sh: cx.sb(list(sh), BF16)
        wq, wk, wv, wab = bft(128, 8, 128), bft(128, 8, 128), bft(128, 8, 64), bft(128, 8, 2)
        cw_sb, gp_sb = f32t(128, 12), f32t(128, 2)
        negA = f32t(128, 1)
        M_s, M_i = f32t(128, 4, 128), f32t(128, 4, 128)
        E63, E127, EL = f32t(128, 128), f32t(128, 128), f32t(128, 128)
        ht = [bft(128, 8, 512) for _ in range(2)]
        rq, rk, rv = f32t(128, 515), f32t(128, 515), f32t(64, 515)
        cq, ck = f32t(128, 512), f32t(128, 512)
        sq2, sq2b = bft(128, 512), bft(128, 512)
        rn, rnb = f32t(128, 512), f32t(128, 512)
        g_tok, G_tok, eG, ekd, glo = [f32t(128, 4) for _ in range(5)]
        diagG, EGrow, Dm, Gam, Gs, Gi = [f32t(128, 512) for _ in range(6)]
        ktok = f32t(128, 4, 128)
        Bb = [bft(128, 512) for _ in range(2)]
        Pb_ = [bft(128, 512) for _ in range(2)]
        S_f, S_b = f32t(128, 512), bft(128, 512)
        St_f, St_b = f32t(128, 64), bft(128, 64)
        vnew = bft(128, 64)
        o_sb = f32t(64, 512)

        wgv = wg.rearrange("(k p) c -> p k c", p=128)
        A('pool', lambda e: e.dma_start(out=wq[:, :, :], in_=wgv[:, :, 0:128]), writes=['gwq'], slot='gwq')
        A('pool', lambda e: e.dma_start(out=wk[:, :, :], in_=wgv[:, :, 128:256]), writes=['gwk'], slot='gwk')
        A('pool', lambda e: e.dma_start(out=wv[:, :, :], in_=wgv[:, :, 256:320]), writes=['gwv'], slot='gwv')
        A('pool', lambda e: e.dma_start(out=wab[:, :, :], in_=wgv[:, :, 320:322]), writes=['gwab'], slot='gwab')
        A('sp', lambda e: e.dma_start(out=cw_sb[:, :], in_=cw[:, :]), writes=['cw'], slot='cw')
        A('sp', lambda e: e.dma_start(out=gp_sb[:, :], in_=gpar[:, :]), writes=['gp'], slot='gp')
        A('act', lambda e: e.activation(negA[:, :], gp_sb[:, 0:1], AF.Exp), reads=['gp'], writes=['negA'])
        A('dve', lambda e: e.tensor_scalar(negA[:, :], negA[:, :], -1.0, None, ALU.mult), reads=['negA'],
          writes=['negA'])
        A('pool', lambda e: e.memset(M_s[:, :, :], 1.0), writes=['M_s'])
        A('pool', lambda e: e.memset(M_i[:, :, :], 1.0), writes=['M_i'])
        A('pool', lambda e: e.affine_select(M_s[:, :, :], M_s[:, :, :], [[0, 4], [1, 128]], ALU.is_gt, 0.0, base=0,
                                            channel_multiplier=-1), reads=['M_s'], writes=['M_s'])
        A('pool', lambda e: e.affine_select(M_i[:, :, :], M_i[:, :, :], [[0, 4], [1, 128]], ALU.is_ge, 0.0, base=0,
                                            channel_multiplier=-1), reads=['M_i'], writes=['M_i'])
        A('pool', lambda e: e.memset(M_s[0:64, :, 64:128], 0.0), reads=['M_s'], writes=['M_s'])
        A('pool', lambda e: e.memset(M_i[0:64, :, 64:128], 0.0), reads=['M_i'], writes=['M_i'])
        A('pool', lambda e: e.affine_select(E63[:, :], c['zeros'][:, :], [[0, 128]], ALU.not_equal, 1.0, base=-63,
                                            channel_multiplier=1), reads=['c_zeros'], writes=['E63'])
        A('pool', lambda e: e.affine_select(E127[:, :], c['zeros'][:, :], [[0, 128]], ALU.not_equal, 1.0, base=-127,
                                            channel_multiplier=1), reads=['c_zeros'], writes=['E127'])
        A('pool', lambda e: e.tensor_copy(EL[:, 0:64], E63[:, 0:64]), reads=['E63'], writes=['EL'])
        A('pool', lambda e: e.tensor_copy(EL[:, 64:128], E127[:, 64:128]), reads=['E127', 'EL'], writes=['EL'])
        A('pool', lambda e: e.memset(rq[:, 0:3], 0.0), writes=['rq'])
        A('pool', lambda e: e.memset(rk[:, 0:3], 0.0), writes=['rk'])
        A('pool', lambda e: e.memset(rv[:, 0:3], 0.0), writes=['rv'])
        A('pool', lambda e: e.memset(St_f[:, :], 0.0), writes=['St_f'])
        A('pool', lambda e: e.memset(St_b[:, :], 0.0), writes=['St_b'])

        hv = hT.rearrange("(k p) t -> p k t", p=128)
        ones, ident = c['ones'], c['ident']
        DEPTH = dict(qn_f=2, kn_f=2, qn_bf=2, kT_bf=2, cv=2, a_sb=2, b_sb=2, B_f=2, kg=2, vtok=2, bpos=2,
                     qdec=3, aqk=3, kdec=3, negbt=3, dl=3, dh=3, ybu=2, ywT=2)
        SHAPES = dict(qn_f=(F32, (128, 512)), kn_f=(F32, (128, 512)), qn_bf=(BF16, (128, 512)),
                      kT_bf=(BF16, (128, 512)), cv=(F32, (64, 512)), a_sb=(F32, (128, 4)), b_sb=(F32, (128, 4)),
                      B_f=(F32, (128, 512)), kg=(BF16, (128, 4, 128)), vtok=(BF16, (128, 4, 64)),
                      bpos=(F32, (128, 4)), qdec=(BF16, (128, 512)), aqk=(BF16, (128, 512)),
                      kdec=(BF16, (128, 4, 128)), negbt=(F32, (128, 4)), dl=(F32, (128, 4)), dh=(F32, (128, 4)),
                      ybu=(F32, (128, 4, 64)), ywT=(BF16, (128, 512)))
        ROT = {n: [cx.sb(list(SHAPES[n][1]), SHAPES[n][0]) for _ in range(DEPTH[n])] for n in DEPTH}

        def mkA(s, lst):
            def K(k):
                return (k, s % DEPTH[k]) if (isinstance(k, str) and k in DEPTH) else k

            def A_(eng, fn, reads=(), writes=(), slot=None):
                lst.append(((eng, fn), dict(reads=[K(k) for k in reads], writes=[K(k) for k in writes], slot=slot)))
            return A_

        def rb(s, n):
            return ROT[n][s % DEPTH[n]]

        def stage1(i, A):
            b = i % 2
            qn_f, kn_f, qn_bf, kT_bf, cv, a_sb, b_sb = [rb(i, n) for n in
                                                        ('qn_f', 'kn_f', 'qn_bf', 'kT_bf', 'cv', 'a_sb', 'b_sb')]
            A('sp', lambda e: e.dma_start(out=ht[b][:, :, :], in_=hv[:, :, i * 512:(i + 1) * 512]),
              writes=[('ght', b)], slot=('ght', b))
            for (w_, M, bank, raw, key) in ((wq, 128, 0, rq, 'rq'), (wk, 128, 1, rk, 'rk'), (wv, 64, 0, rv, 'rv')):
                def mm(e, w_=w_, M=M, bank=bank):
                    r = None
                    for k in range(8):
                        r = e.matmul(psb[bank][0:M, :], w_[:, k, :], ht[b][:, k, :], start=(k == 0), stop=(k == 7))
                    return r
                A('pe', mm, reads=[('ght', b), 'gwq', 'gwk', 'gwv'], writes=[PB(bank)])
                A('dve', lambda e, M=M, bank=bank, raw=raw: e.tensor_copy(raw[0:M, 3:515], psb[bank][0:M, :]),
                  reads=[PB(bank)], writes=[key])

            def mmab(e):
                r = None
                for sub in range(4):
                    for k in range(8):
                        r = e.matmul(psb[1][:, sub * 2:sub * 2 + 2], ht[b][:, k, sub * 128:(sub + 1) * 128],
                                     wab[:, k, :], start=(k == 0), stop=(k == 7))
                return r
            A('pe', mmab, reads=[('ght', b), 'gwab'], writes=[PB(1)])
            p3 = psb[1][:, 0:8].rearrange("p (s c) -> p s c", c=2)
            A('dve', lambda e: e.tensor_copy(a_sb[:, :], p3[:, :, 0]), reads=[PB(1)], writes=['a_sb'])
            A('dve', lambda e: e.tensor_copy(b_sb[:, :], p3[:, :, 1]), reads=[PB(1)], writes=['b_sb'])
            for which, (raw, cv_, M, key, ckey) in enumerate(((rq, cq, 128, 'rq', 'cq'), (rk, ck, 128, 'rk', 'ck'),
                                                              (rv, cv, 64, 'rv', 'cv'))):
                A('act', lambda e, raw=raw, cv_=cv_, M=M, which=which: e.activation(
                    cv_[0:M, :], raw[0:M, 0:512], AF.Copy, scale=cw_sb[0:M, which * 4:which * 4 + 1]),
                    reads=[key, 'cw'], writes=[ckey])
                for tap in range(1, 4):
                    A('dve', lambda e, raw=raw, cv_=cv_, M=M, which=which, tap=tap: e.scalar_tensor_tensor(
                        cv_[0:M, :], raw[0:M, tap:tap + 512], cw_sb[0:M, which * 4 + tap:which * 4 + tap + 1],
                        cv_[0:M, :], ALU.mult, ALU.add),
                        reads=[key, 'cw', ckey], writes=[ckey])
                A('pool', lambda e, raw=raw, M=M: e.tensor_copy(raw[0:M, 0:3], raw[0:M, 512:515]),
                  reads=[key, ckey], writes=[key])
                A('act', lambda e, cv_=cv_, M=M: e.activation(cv_[0:M, :], cv_[0:M, :], AF.Silu),
                  reads=[ckey], writes=[ckey])
            for (cv_, ckey, bank, outf, okey, mul) in ((cq, 'cq', 0, qn_f, 'qn_f', 128.0 ** -0.5),
                                                      (ck, 'ck', 1, kn_f, 'kn_f', 1.0)):
                sq_, rn_ = (sq2, rn) if bank == 0 else (sq2b, rnb)
                A('act', lambda e, cv_=cv_, sq_=sq_: e.activation(sq_[:, :], cv_[:, :], AF.Square), reads=[ckey],
                  writes=[('sq2', bank)])
                A('pe', lambda e, bank=bank, sq_=sq_: e.matmul(psb[bank][:, :], c['ones_bf'][:, :], sq_[:, :],
                                                               start=True, stop=True),
                  reads=[('sq2', bank), 'c_ones_bf'], writes=[PB(bank)])
                A('act', lambda e, bank=bank, rn_=rn_: e.activation(rn_[:, :], psb[bank][:, :], AF.Ln, bias=EPS,
                                                                    scale=1.0),
                  reads=[PB(bank)], writes=[('rn', bank)])
                A('act', lambda e, rn_=rn_: e.activation(rn_[:, :], rn_[:, :], AF.Exp, scale=-0.5),
                  reads=[('rn', bank)], writes=[('rn', bank)])
                A('dve', lambda e, cv_=cv_, outf=outf, mul=mul, rn_=rn_: e.scalar_tensor_tensor(
                    outf[:, :], cv_[:, :], mul, rn_[:, :], ALU.mult, ALU.mult), reads=[ckey, ('rn', bank)],
                    writes=[okey])
            A('act', lambda e: e.activation(qn_bf[:, :], qn_f[:, :], AF.Copy), reads=['qn_f'], writes=['qn_bf'])
            A('act', lambda e: e.activation(kT_bf[:, :], kn_f[:, :], AF.Copy), reads=['kn_f'], writes=['kT_bf'])

        def stage2(i, A):
            qn_f, kn_f, qn_bf, kT_bf, cv, a_sb, b_sb = [rb(i, n) for n in
                                                        ('qn_f', 'kn_f', 'qn_bf', 'kT_bf', 'cv', 'a_sb', 'b_sb')]
            B_f, kg, vtok, bpos = [rb(i, n) for n in ('B_f', 'kg', 'vtok', 'bpos')]
            qdec, aqk, kdec, negbt, dl, dh = [rb(i, n) for n in ('qdec', 'aqk', 'kdec', 'negbt', 'dl', 'dh')]
            A('act', lambda e: e.activation(g_tok[:, :], a_sb[:, :], AF.Exp, bias=gp_sb[:, 1:2], scale=1.0),
              reads=['a_sb', 'gp'], writes=['g_tok'])
            A('act', lambda e: e.activation(g_tok[:, :], g_tok[:, :], AF.Ln, bias=1.0, scale=1.0),
              reads=['g_tok'], writes=['g_tok'])
            A('dve', lambda e: e.tensor_scalar(g_tok[:, :], g_tok[:, :], negA[:, 0:1], None, ALU.mult),
              reads=['g_tok', 'negA'], writes=['g_tok'])
            A('act', lambda e: e.activation(bpos[:, :], b_sb[:, :], AF.Exp, scale=-1.0), reads=['b_sb'],
              writes=['bpos'])
            A('dve', lambda e: e.tensor_scalar(bpos[:, :], bpos[:, :], 1.0, None, ALU.add), reads=['bpos'],
              writes=['bpos'])
            A('dve', lambda e: e.reciprocal(bpos[:, :], bpos[:, :]), reads=['bpos'], writes=['bpos'])
            A('dve', lambda e: e.tensor_scalar(negbt[:, :], bpos[:, :], -1.0, None, ALU.mult), reads=['bpos'],
              writes=['negbt'])
            A('pe', lambda e: e.matmul(psb[2][:, 0:4], M_i[:, 0, :], g_tok[:, :], start=True, stop=True),
              reads=['g_tok', 'M_i'], writes=[PB(2)])
            A('dve', lambda e: e.tensor_copy(G_tok[:, :], psb[2][:, 0:4]), reads=[PB(2)], writes=['G_tok'])
            A('act', lambda e: e.activation(eG[:, :], G_tok[:, :], AF.Exp), reads=['G_tok'], writes=['eG'])
            A('pe', lambda e: e.matmul(psb[2][:, 0:4], EL[:, :], G_tok[:, :], start=True, stop=True),
              reads=['G_tok', 'EL'], writes=[PB(2)])
            A('dve', lambda e: e.tensor_tensor(glo[:, :], psb[2][:, 0:4], G_tok[:, :], ALU.subtract),
              reads=[PB(2), 'G_tok'], writes=['glo'])
            A('act', lambda e: e.activation(ekd[:, :], glo[:, :], AF.Exp), reads=['glo'], writes=['ekd'])
            A('pe', lambda e: e.matmul(psb[2][:, 0:4], E63[:, :], G_tok[:, :], start=True, stop=True),
              reads=['G_tok', 'E63'], writes=[PB(2)])
            A('dve', lambda e: e.tensor_copy(dl[:, :], psb[2][:, 0:4]), reads=[PB(2)], writes=['dl'])
            A('act', lambda e: e.activation(dl[:, :], dl[:, :], AF.Exp), reads=['dl'], writes=['dl'])
            A('pe', lambda e: e.matmul(psb[2][:, 0:4], E127[:, :], G_tok[:, :], start=True, stop=True),
              reads=['G_tok', 'E127'], writes=[PB(2)])
            A('dve', lambda e: e.tensor_copy(dh[:, :], psb[2][:, 0:4]), reads=[PB(2)], writes=['dh'])
            A('act', lambda e: e.activation(dh[:, :], dh[:, :], AF.Exp), reads=['dh'], writes=['dh'])
            for sub in range(4):
                A('dve', lambda e, sub=sub: e.tensor_scalar(diagG[:, sub * 128:(sub + 1) * 128], ident[:, :],
                                                            G_tok[:, sub:sub + 1], None, ALU.mult),
                  reads=['G_tok', 'c_ident'], writes=[('diagG', sub)])
                A('pe', lambda e, sub=sub: e.matmul(psb[3][:, sub * 128:(sub + 1) * 128], ones[:, :],
                                                    diagG[:, sub * 128:(sub + 1) * 128], start=True, stop=True),
                  reads=[('diagG', sub), 'c_ones'], writes=[PB(3)])
            A('act', lambda e: e.activation(EGrow[:, :], psb[3][:, :], AF.Exp), reads=[PB(3)], writes=['EGrow'])
            A('pool', lambda e: e.tensor_tensor(qdec[:, :], qn_f[:, :], EGrow[:, :], ALU.mult),
              reads=['qn_f', 'EGrow'], writes=['qdec'])
            for sub in range(4):
                A('dve', lambda e, sub=sub: e.tensor_scalar(Dm[:, sub * 128:(sub + 1) * 128],
                                                            psb[3][:, sub * 128:(sub + 1) * 128],
                                                            G_tok[:, sub:sub + 1], 0.0, ALU.subtract, ALU.min),
                  reads=[PB(3), 'G_tok'], writes=['Dm'])
            A('act', lambda e: e.activation(Gam[:, :], Dm[:, :], AF.Exp), reads=['Dm'], writes=['Gam'])
            A('pool', lambda e: e.tensor_tensor(Gs[:, :], Gam[:, :], M_s[:, :, :].rearrange("p s c -> p (s c)"),
                                                ALU.mult), reads=['Gam', 'M_s'], writes=['Gs'])
            A('pool', lambda e: e.tensor_tensor(Gi[:, :], Gam[:, :], M_i[:, :, :].rearrange("p s c -> p (s c)"),
                                                ALU.mult), reads=['Gam', 'M_i'], writes=['Gi'])
            for sub in range(4):
                A('pe', lambda e, sub=sub: e.transpose(psb[2][:, sub * 128:(sub + 1) * 128],
                                                       kn_f[:, sub * 128:(sub + 1) * 128], ident[:, :]),
                  reads=['kn_f', 'c_ident'], writes=[PB(2)])
            A('dve', lambda e: e.tensor_copy(ktok[:, :, :], psb[2][:, :].rearrange("p (s c) -> p s c", c=128)),
              reads=[PB(2)], writes=['ktok'])
            for sub in range(4):
                A('act', lambda e, sub=sub: e.activation(kg[:, sub, :], ktok[:, sub, :], AF.Copy,
                                                         scale=eG[:, sub:sub + 1]), reads=['ktok', 'eG'], writes=['kg'])
                A('act', lambda e, sub=sub: e.activation(kdec[:, sub, :], ktok[:, sub, :], AF.Copy,
                                                         scale=ekd[:, sub:sub + 1]), reads=['ktok', 'ekd'],
                  writes=['kdec'])
            for sub in range(4):
                A('pe', lambda e, sub=sub: e.transpose(psb[2][:, sub * 64:(sub + 1) * 64],
                                                       cv[0:64, sub * 128:(sub + 1) * 128], ident[0:64, 0:64]),
                  reads=['cv', 'c_ident'], writes=[PB(2)])
            A('dve', lambda e: e.tensor_copy(vtok[:, :, :], psb[2][:, 0:256].rearrange("p (s c) -> p s c", c=64)),
              reads=[PB(2)], writes=['vtok'])
            for sub in range(4):
                cs = slice(sub * 128, (sub + 1) * 128)
                A('pe', lambda e, cs=cs: e.matmul(psb[2][:, cs], kT_bf[:, cs], kT_bf[:, cs], start=True, stop=True),
                  reads=['kT_bf'], writes=[PB(2)])
                A('dve', lambda e, cs=cs, sub=sub: e.scalar_tensor_tensor(
                    B_f[:, cs], psb[2][:, cs], negbt[:, sub:sub + 1], Gs[:, cs], ALU.mult, ALU.mult),
                    reads=[PB(2), 'negbt', 'Gs'], writes=['B_f'])
            for sub in range(4):
                cs = slice(sub * 128, (sub + 1) * 128)
                A('pe', lambda e, cs=cs: e.matmul(psb[3][:, cs], kT_bf[:, cs], qn_bf[:, cs], start=True, stop=True),
                  reads=['kT_bf', 'qn_bf'], writes=[PB(3)])
            A('dve', lambda e: e.tensor_tensor(aqk[:, :], psb[3][:, :], Gi[:, :], ALU.mult), reads=[PB(3), 'Gi'],
              writes=['aqk'])

        def stage3(i, A):
            B_f, kg, vtok, bpos = [rb(i, n) for n in ('B_f', 'kg', 'vtok', 'bpos')]
            ybu, ywT = rb(i, 'ybu'), rb(i, 'ywT')
            A('act', lambda e: e.activation(Bb[0][:, :], B_f[:, :], AF.Copy), reads=['B_f'], writes=[('Bb', 0)])
            for sub in range(4):
                cs = slice(sub * 128, (sub + 1) * 128)
                A('pe', lambda e, cs=cs: e.transpose(psb[4][:, cs], B_f[:, cs], ident[:, :]),
                  reads=['B_f', 'c_ident'], writes=[PB(4)])
            A('dve', lambda e: e.tensor_copy(Pb_[0][:, :], psb[4][:, :]), reads=[PB(4)], writes=[('Pb', 0)])
            for sub in range(4):
                cs = slice(sub * 128, (sub + 1) * 128)
                A('pool', lambda e, cs=cs: e.tensor_tensor(S_f[:, cs], B_f[:, cs], ident[:, :], ALU.add),
                  reads=['B_f', 'c_ident'], writes=['S_f'])
            A('act', lambda e: e.activation(S_b[:, :], S_f[:, :], AF.Copy), reads=['S_f'], writes=['S_b'])
            for j in range(5):
                cur, nxt = j % 2, (j + 1) % 2
                for sub in range(4):
                    cs = slice(sub * 128, (sub + 1) * 128)
                    A('pe', lambda e, cs=cs, cur=cur: e.matmul(psb[5][:, cs], Pb_[cur][:, cs], Bb[cur][:, cs],
                                                               start=True, stop=True),
                      reads=[('Pb', cur), ('Bb', cur)], writes=[PB(5)])
                A('dve', lambda e, nxt=nxt: e.tensor_copy(Bb[nxt][:, :], psb[5][:, :]), reads=[PB(5)],
                  writes=[('Bb', nxt)])
                for sub in range(4):
                    cs = slice(sub * 128, (sub + 1) * 128)
                    A('pe', lambda e, cs=cs, cur=cur: e.matmul(psb[4][:, cs], Bb[cur][:, cs], Pb_[cur][:, cs],
                                                               start=True, stop=True),
                      reads=[('Pb', cur), ('Bb', cur)], writes=[PB(4)])
                A('act', lambda e, nxt=nxt: e.activation(Pb_[nxt][:, :], psb[4][:, :], AF.Copy), reads=[PB(4)],
                  writes=[('Pb', nxt)])
                for sub in range(4):
                    cs = slice(sub * 128, (sub + 1) * 128)
                    A('pe', lambda e, cs=cs, nxt=nxt: e.matmul(psb[5][:, cs], Pb_[nxt][:, cs], S_b[:, cs],
                                                               start=True, stop=True),
                      reads=[('Pb', nxt), 'S_b'], writes=[PB(5)])
                A('dve', lambda e: e.tensor_tensor(S_f[:, :], S_f[:, :], psb[5][:, :], ALU.add),
                  reads=['S_f', PB(5)], writes=['S_f'])
                A('act', lambda e: e.activation(S_b[:, :], S_f[:, :], AF.Copy), reads=['S_f'], writes=['S_b'])
            for sub in range(4):
                cs = slice(sub * 128, (sub + 1) * 128)
                A('pe', lambda e, cs=cs, sub=sub: e.matmul(psb[4][:, sub * 64:(sub + 1) * 64], S_b[:, cs],
                                                           vtok[:, sub, :], start=True, stop=True),
                  reads=['S_b', 'vtok'], writes=[PB(4)])
            for sub in range(4):
                A('dve', lambda e, sub=sub: e.tensor_scalar(ybu[:, sub, :], psb[4][:, sub * 64:(sub + 1) * 64],
                                                            bpos[:, sub:sub + 1], None, ALU.mult),
                  reads=[PB(4), 'bpos'], writes=['ybu'])
            for sub in range(4):
                cs = slice(sub * 128, (sub + 1) * 128)
                A('pe', lambda e, cs=cs, sub=sub: e.matmul(psb[5][:, cs], kg[:, sub, :], S_b[:, cs], start=True,
                                                           stop=True),
                  reads=['S_b', 'kg'], writes=[PB(5)])
            A('dve', lambda e: e.tensor_copy(ywT[:, :], psb[5][:, :]), reads=[PB(5)], writes=['ywT'])

        def stage4(i, A):
            qdec, aqk, kdec, negbt, dl, dh = [rb(i, n) for n in ('qdec', 'aqk', 'kdec', 'negbt', 'dl', 'dh')]
            ybu, ywT = rb(i, 'ybu'), rb(i, 'ywT')
            for ch in range(8):
                sub, hf = ch // 2, ch % 2
                rs = slice(hf * 64, hf * 64 + 64)
                cs = slice(sub * 128, (sub + 1) * 128)
                cc = slice(ch * 64, (ch + 1) * 64)
                A('pe', lambda e, cs=cs: e.matmul(psb[6][:, 0:64], ywT[:, cs], St_b[:, :], start=True, stop=True),
                  reads=['ywT', 'St_b'], writes=[PB(6)])
                A('dve', lambda e, rs=rs, sub=sub: e.scalar_tensor_tensor(
                    vnew[rs, :], psb[6][rs, 0:64], negbt[rs, sub:sub + 1], ybu[rs, sub, :], ALU.mult, ALU.add),
                    reads=[PB(6), 'negbt', 'ybu'], writes=['vnew'])

                def mmo(e, cc=cc, rs=rs):
                    e.matmul(psb[7][0:64, cc], St_b[:, :], qdec[:, cc], start=True, stop=False)
                    return e.matmul(psb[7][0:64, cc], vnew[rs, :], aqk[rs, cc], start=False, stop=True)
                A('pe', mmo, reads=['St_b', 'qdec', 'vnew', 'aqk'], writes=[PB(7)])
                A('pe', lambda e, rs=rs, sub=sub: e.matmul(psb[6][:, 64:128], kdec[rs, sub, :], vnew[rs, :],
                                                           start=True, stop=True),
                  reads=['kdec', 'vnew'], writes=[PB(6)])
                dsc = (dl if hf == 0 else dh)
                A('dve', lambda e, sub=sub, dsc=dsc: e.scalar_tensor_tensor(
                    St_b[:, :], St_f[:, :], dsc[:, sub:sub + 1], psb[6][:, 64:128], ALU.mult, ALU.add),
                    reads=['St_f', PB(6), 'dl', 'dh'], writes=['St_b'])
                A('dve', lambda e, sub=sub, dsc=dsc: e.scalar_tensor_tensor(
                    St_f[:, :], St_f[:, :], dsc[:, sub:sub + 1], psb[6][:, 64:128], ALU.mult, ALU.add),
                    reads=['St_f', PB(6), 'dl', 'dh'], writes=['St_f'])
            A('act', lambda e: e.activation(o_sb[:, :], psb[7][0:64, :], AF.Copy), reads=[PB(7)], writes=['o_sb'])
            A('sp', lambda e: e.dma_start(out=obT[:, i * 512:(i + 1) * 512], in_=o_sb[:, :]),
              reads=['o_sb'], writes=[('obT', i)], slot='ob')

        def merge_lists(lists):
            lists = [l for l in lists if l]
            pos = [0] * len(lists)
            out = []
            while True:
                best, bf = None, None
                for li, l in enumerate(lists):
                    if pos[li] < len(l):
                        fr = pos[li] / len(l)
                        if bf is None or fr < bf:
                            best, bf = li, fr
                if best is None:
                    break
                out.append(lists[best][pos[best]])
                pos[best] += 1
            return out

        stages = (stage1, stage2, stage3, stage4)
        for t in range(NSEG + 3):
            lists = []
            for si, st_ in enumerate(stages):
                s = t - si
                if 0 <= s < NSEG:
                    lst = []
                    st_(s, mkA(s, lst))
                    lists.append(lst)
            for (a_, k_) in merge_lists(lists[::-1]):
                sc.add(*a_, **k_)
        sc.flush()


def build_M(S, do_fox=True, do_gdn=True, stop=99):
    nc = bass.Bass("TRN2", target_bir_lowering=False)
    hT = nc.dram_tensor("hT", [D, S], BF16, kind="ExternalInput").ap()
    wf = nc.dram_tensor("wf", [D, 193], F32, kind="ExternalInput").ap()
    bfg = nc.dram_tensor("bfg", [128, 1], F32, kind="ExternalInput").ap()
    wg = nc.dram_tensor("wg", [D, 322], F32, kind="ExternalInput").ap()
    cw = nc.dram_tensor("cw", [128, 12], F32, kind="ExternalInput").ap()
    gpar = nc.dram_tensor("gpar", [128, 2], F32, kind="ExternalInput").ap()
    oaT = nc.dram_tensor("oaT", [64, S], F32, kind="ExternalOutput").ap()
    obT = nc.dram_tensor("obT", [64, S], F32, kind="ExternalOutput").ap()
    scr = nc.dram_tensor("scr", [128, 128], BF16).ap()
    cx = Ctx(nc)
    with cx.st:
        sc = Sched(nc)
        c = make_consts(sc, cx)
        psb = [cx.ps([128, 512], F32) for _ in range(8)]
        if do_gdn:
            gdn_phase(nc, sc, c, S, hT, wg, cw, gpar, obT, psb)
        if do_fox:
            fox_phase(nc, sc, cx, c, S, hT, wf, bfg, oaT, scr, psb, stop=stop)
        sc.close()
    return nc


def build_T(TS, last):
    nc = bass.Bass("TRN2", target_bir_lowering=False)
    TT = 256
    NTT = TS // TT
    xT = nc.dram_tensor("xT", [D, TS], F32, kind="ExternalInput").ap()
    hT = nc.dram_tensor("hT", [D, TS], BF16, kind="ExternalInput").ap()
    oaT = nc.dram_tensor("oaT", [512, TS], F32, kind="ExternalInput").ap()
    obT = nc.dram_tensor("obT", [512, TS], F32, kind="ExternalInput").ap()
    w_in = nc.dram_tensor("w_in", [D, N_IN], F32, kind="ExternalInput").ap()
    w_br = nc.dram_tensor("w_br", [1536, D], F32, kind="ExternalInput").ap()
    w_out = nc.dram_tensor("w_out", [D, D], F32, kind="ExternalInput").ap()
    w_kv = nc.dram_tensor("w_kv", [D, 1024], F32, kind="ExternalInput").ap()
    memT = nc.dram_tensor("memT", [D, 256], F32, kind="ExternalInput").ap()
    mem_g = nc.dram_tensor("mem_g", [128, 8], F32, kind="ExternalInput").ap()
    b_mg = nc.dram_tensor("b_mg", [128, 24], F32, kind="ExternalInput").ap()
    gdn_g = nc.dram_tensor("gdn_g", [128, 1], F32, kind="ExternalInput").ap()
    next_g = nc.dram_tensor("next_g", [128, 8], F32, kind="ExternalInput").ap()
    xoT = nc.dram_tensor("xoT", [D, TS], F32, kind="ExternalOutput").ap()
    if not last:
        hoT = nc.dram_tensor("hoT", [D, TS], BF16, kind="ExternalOutput").ap()
    cx = Ctx(nc)
    with cx.st:
        sc = Sched(nc)
        ones_f = cx.sb([128, 128], F32)
        ones_bf = cx.sb([128, 128], BF16)
        sc.add('pool', lambda e: e.memset(ones_f[:, :], 1.0), writes=['ones_f'])
        sc.add('pool', lambda e: e.memset(ones_bf[:, :], 1.0), writes=['ones_bf'])
        psb = [cx.ps([128, 512], F32) for _ in range(8)]

        BM = {(0, 0): 0, (0, 1): 1, (1, 0): 2, (1, 1): 2, (2, 0): 3, (2, 1): 4, (3, 0): 5, (3, 1): 6,
              (4, 0): 2, (4, 1): 3, (5, 0): 4, (5, 1): 5, (6, 0): 6, (6, 1): 7, (7, 0): 0, (7, 1): 1}

        def half(bk, h):
            return psb[BM[(bk, h)]][:, 0:TT]

        def hk(bk, h):
            return ('pb', BM[(bk, h)])
        Wz = cx.sb([128, 8, 5120], BF16)
        Wbr = cx.sb([128, 12, 1024], BF16)
        Wout = cx.sb([128, 8, 1024], BF16)
        mkT = cx.sb([128, 4, 256], BF16)
        mv = cx.sb([128, 2, 512], BF16)
        memg_sb = cx.sb([128, 8], F32)
        bm_sb = cx.sb([128, 24], F32)
        gg_sb = cx.sb([128, 1], F32)
        ng_sb = cx.sb([128, 8], F32)
        for i, (dst, srcap) in enumerate([(memg_sb, mem_g), (bm_sb, b_mg), (gg_sb, gdn_g), (ng_sb, next_g)]):
            sc.add('sp', lambda e, dst=dst, srcap=srcap: e.dma_start(out=dst[:, :], in_=srcap[:, :]),
                   writes=[('par', i)], slot=('par', i))
        def ldw(dst, dkey, srcap):
            sc.add('pool', lambda e: e.dma_start(out=dst, in_=srcap), writes=[dkey], slot=('w', dkey[0], dkey[1]))

        pcx = Ctx(nc)
        with pcx.st:
            Wkv = pcx.sb([128, 8, 1024], BF16)
            mt = pcx.sb([128, 8, 256], F32)
            mn = pcx.sb([128, 8, 256], BF16)
            sqm = pcx.sb([128, 8, 256], BF16)
            rstm = pcx.sb([128, 256], F32)
            for k in range(8):
                ldw(Wkv[:, k, :], ('Wkv', '', k), w_kv[k * 128:(k + 1) * 128, :])
            sc.add('sp', lambda e: e.dma_start(out=mt[:, :, :], in_=memT.rearrange("(k p) m -> p k m", p=128)),
                   writes=['mt'], slot='mt')
            sc.ops[-1]
            saved = {'g': None}
            emit_rmsnorm(sc, mt, 'mt', memg_sb, ones_bf, mn, 'mn', 256, sqm, psb[0], rstm, 'mnorm', gkey=('par', 0),
                         oneskey='ones_bf')
            for hh in range(4):
                def mmk(e, hh=hh):
                    r = None
                    for k in range(8):
                        r = e.matmul(half(1, hh % 2), Wkv[:, k, hh * 128:(hh + 1) * 128], mn[:, k, :],
                                     start=(k == 0), stop=(k == 7))
                    return r
                sc.add('pe', mmk, reads=[('Wkv', '', k) for k in range(8)] + ['mn'], writes=[hk(1, hh % 2)])
                sc.add('dve', lambda e, hh=hh: e.tensor_copy(mkT[:, hh, :], half(1, hh % 2)),
                       reads=[hk(1, hh % 2)], writes=['mkT'])
            for mc in range(2):
                def mmv(e, mc=mc):
                    r = None
                    for k in range(8):
                        r = e.matmul(psb[2 + mc][:, :], mn[:, k, mc * 128:(mc + 1) * 128], Wkv[:, k, 512:1024],
                                     start=(k == 0), stop=(k == 7))
                    return r
                sc.add('pe', mmv, reads=[('Wkv', '', k) for k in range(8)] + ['mn'], writes=[('pb', 2 + mc)])
                sc.add('dve', lambda e, mc=mc: e.tensor_copy(mv[:, mc, :], psb[2 + mc][:, :]),
                       reads=[('pb', 2 + mc)], writes=['mv'])
            sc.flush()

        def wzname(col0):
            if col0 < 512:
                return 'az'
            if col0 < 1024:
                return 'bz'
            if col0 < 2048:
                return 'mqz'
            return 'g%d' % ((col0 - 2048) // 1024)

        def wzkey(col0):
            return [('Wz', wzname(col0), k) for k in range(8)]
        late = []

        def ldw_late(*a):
            late.append(a)
        for k in range(8):
            ldw(Wz[:, k, 1024:2048], ('Wz', 'mqz', k), w_in[k * 128:(k + 1) * 128, OFF['mq']:OFF['mq'] + 1024])
        for k in range(8):
            ldw(Wz[:, k, 0:512], ('Wz', 'az', k), w_in[k * 128:(k + 1) * 128, OFF['az']:OFF['az'] + 512])
            ldw(Wz[:, k, 512:1024], ('Wz', 'bz', k), w_in[k * 128:(k + 1) * 128, OFF['bz']:OFF['bz'] + 512])
        for cb in range(1, 4):
            for k in range(8):
                ldw_late(Wz[:, k, 1024 + cb * 1024:2048 + cb * 1024], ('Wz', 'g%d' % (cb - 1), k),
                         w_in[k * 128:(k + 1) * 128, OFF['mq'] + cb * 1024:OFF['mq'] + (cb + 1) * 1024])
            for k in range(4 * (cb - 1), 4 * cb):
                ldw_late(Wbr[:, k, :], ('Wbr', cb - 1, k), w_br[k * 128:(k + 1) * 128, :])
        for k in range(8):
            ldw_late(Wout[:, k, :], ('Wout', '', k), w_out[k * 128:(k + 1) * 128, :])

        ht = [cx.sb([128, 8, TT], BF16) for _ in range(2)]
        xt = cx.sb([128, 8, TT], F32)
        oat = cx.sb([128, 4, TT], F32)
        obt = cx.sb([128, 4, TT], F32)
        yT = cx.sb([128, 12, TT], BF16)
        mg = cx.sb([128, 8, TT], BF16)
        hout = cx.sb([128, 8, TT], BF16 if not last else F32)
        sqs = [cx.sb([128, TT], BF16) for _ in range(2)]
        sil = [cx.sb([128, TT], F32) for _ in range(2)]
        tmpB = [cx.sb([128, TT], F32) for _ in range(4)]
        tmpM = [cx.sb([128, TT], F32) for _ in range(4)]
        rstd = cx.sb([128, TT], F32)
        rden = cx.sb([128, TT], F32)
        mqs = cx.sb([128, TT], BF16)
        pT = [cx.sb([128, TT], BF16) for _ in range(2)]
        gs = [cx.sb([128, TT], F32) for _ in range(3)]
        acc = [cx.sb([128, TT], F32) for _ in range(2)]
        hv = hT.rearrange("(k p) t -> p k t", p=128)
        xv = xT.rearrange("(k p) t -> p k t", p=128)
        oav = oaT.rearrange("(k p) t -> p k t", p=128)
        obv = obT.rearrange("(k p) t -> p k t", p=128)
        xov = xoT.rearrange("(k p) t -> p k t", p=128)
        if not last:
            hov = hoT.rearrange("(k p) t -> p k t", p=128)

        zcnt = [0]

        def zproj(col0, b):
            s = zcnt[0] % 2
            zcnt[0] += 1
            dst = half(0, s)

            def mm(e):
                r = None
                for k in range(8):
                    r = e.matmul(dst, Wz[:, k, col0:col0 + 128], ht[b][:, k, :], start=(k == 0), stop=(k == 7))
                return r
            sc.add('pe', mm, reads=wzkey(col0) + [('ht', b)], writes=[hk(0, s)])
            return dst, hk(0, s)

        def load_ht(it_):
            b_ = it_ % 2
            sc.add('sp', lambda e: e.dma_start(out=ht[b_][:, :, :], in_=hv[:, :, it_ * TT:(it_ + 1) * TT]),
                   writes=[('ht', b_)], slot=('ht', b_))

        def load_o(it_):
            sc.add('sp', lambda e: e.dma_start(out=oat[:, :, :], in_=oav[:, :, it_ * TT:(it_ + 1) * TT]),
                   writes=['oat'], slot='oat')
            sc.add('sp', lambda e: e.dma_start(out=obt[:, :, :], in_=obv[:, :, it_ * TT:(it_ + 1) * TT]),
                   writes=['obt'], slot='obt')

        def phase12(it):
            b = it % 2
            t0, t1 = it * TT, (it + 1) * TT
            for hd in range(4):
                s = hd % 2
                sbk = 7
                sc.add('act', lambda e, hd=hd, s=s: e.activation(sqs[s][:, :], obt[:, hd, :], AF.Square),
                       reads=['obt'], writes=[('sqs', s)])
                sc.add('pe', lambda e, s=s, sbk=sbk: e.matmul(psb[sbk][:, 0:TT], ones_bf[:, :], sqs[s][:, :],
                                                              start=True, stop=True),
                       reads=[('sqs', s), 'ones_bf'], writes=[('pb', sbk)])
                sc.add('act', lambda e, hd=hd, sbk=sbk: e.activation(tmpB[hd][:, :], psb[sbk][:, 0:TT], AF.Ln,
                                                                     bias=EPS, scale=1.0 / 128),
                       reads=[('pb', sbk)], writes=[('tmpB', hd)])
                sc.add('act', lambda e, hd=hd: e.activation(tmpB[hd][:, :], tmpB[hd][:, :], AF.Exp, scale=-0.5),
                       reads=[('tmpB', hd)], writes=[('tmpB', hd)])
                sc.add('dve', lambda e, hd=hd: e.scalar_tensor_tensor(tmpB[hd][:, :], obt[:, hd, :], gg_sb[:, 0:1],
                                                                     tmpB[hd][:, :], ALU.mult, ALU.mult),
                       reads=['obt', ('tmpB', hd), ('par', 2)], writes=[('tmpB', hd)])
            for hh in range(4):
                zp, zk = zproj(1024 + hh * 128, b)
                sc.add('dve', lambda e, zp=zp: e.tensor_copy(mqs[:, :], zp), reads=[zk], writes=['mqs'])
                for mc in range(2):
                    sc.add('pe', lambda e, hh=hh, mc=mc: e.matmul(half(2, mc), mkT[:, hh, mc * 128:(mc + 1) * 128],
                                                                 mqs[:, :], start=True, stop=True),
                           reads=['mkT', 'mqs'], writes=[hk(2, mc)])
                    sc.add('act', lambda e, mc=mc: e.activation(pT[mc][:, :], half(2, mc), AF.Exp,
                                                                scale=128.0 ** -0.5),
                           reads=[hk(2, mc)], writes=[('pT', mc)])

                def mmn(e, hh=hh):
                    e.matmul(half(3, 0), mv[:, 0, hh * 128:(hh + 1) * 128], pT[0][:, :], start=True, stop=False)
                    return e.matmul(half(3, 0), mv[:, 1, hh * 128:(hh + 1) * 128], pT[1][:, :], start=False,
                                    stop=True)
                sc.add('pe', mmn, reads=['mv', ('pT', 0), ('pT', 1)], writes=[hk(3, 0)])

                def mmd(e):
                    e.matmul(half(3, 1), ones_bf[:, :], pT[0][:, :], start=True, stop=False)
                    return e.matmul(half(3, 1), ones_bf[:, :], pT[1][:, :], start=False, stop=True)
                sc.add('pe', mmd, reads=['ones_bf', ('pT', 0), ('pT', 1)], writes=[hk(3, 1)])
                sc.add('dve', lambda e: e.reciprocal(rden[:, :], half(3, 1)), reads=[hk(3, 1)], writes=['rden'])
                sc.add('dve', lambda e, hh=hh: e.tensor_tensor(tmpM[hh][:, :], half(3, 0), rden[:, :], ALU.mult),
                       reads=[hk(3, 0), 'rden'], writes=[('tmpM', hh)])
            for ci in range(12):
                s = ci % 2
                col0 = [0, 512, 1536][ci // 4] + (ci % 4) * 128
                zp, zk = zproj(col0, b)
                sc.add('act', lambda e, zp=zp, s=s: e.activation(sil[s][:, :], zp, AF.Silu),
                       reads=[zk], writes=[('sil', s)])
                if ci < 4:
                    srcb, skey = oat[:, ci, :], 'oat'
                elif ci < 8:
                    srcb, skey = tmpB[ci - 4][:, :], ('tmpB', ci - 4)
                else:
                    srcb, skey = tmpM[ci - 8][:, :], ('tmpM', ci - 8)
                sc.add('pool', lambda e, ci=ci, s=s, srcb=srcb: e.tensor_tensor(yT[:, ci, :], srcb, sil[s][:, :],
                                                                               ALU.mult),
                       reads=[skey, ('sil', s)], writes=[('yT', ci)])

        def merge_out(it):
            b = it % 2
            t0, t1 = it * TT, (it + 1) * TT
            for dc in range(8):
                for n in range(3):
                    pslot = [(4, 0), (4, 1), (5, 0)][n]
                    gslot = [(5, 1), (6, 0), (6, 1)][n]

                    def mmp(e, n=n, dc=dc, pslot=pslot):
                        r = None
                        for kc in range(4):
                            r = e.matmul(half(*pslot), Wbr[:, n * 4 + kc, dc * 128:(dc + 1) * 128],
                                         yT[:, n * 4 + kc, :], start=(kc == 0), stop=(kc == 3))
                        return r
                    sc.add('pe', mmp, reads=[('Wbr', n, n * 4 + kc) for kc in range(4)] + [('yT', n * 4 + kc) for kc in range(4)],
                           writes=[hk(*pslot)])

                    def mmg(e, n=n, dc=dc, gslot=gslot, b=b):
                        r = None
                        for k in range(8):
                            c0 = 2048 + n * 1024 + dc * 128
                            r = e.matmul(half(*gslot), Wz[:, k, c0:c0 + 128], ht[b][:, k, :], start=(k == 0),
                                         stop=(k == 7))
                        return r
                    sc.add('pe', mmg, reads=[('Wz', 'g%d' % n, k) for k in range(8)] + [('ht', b)], writes=[hk(*gslot)])
                    sc.add('act', lambda e, n=n, dc=dc, gslot=gslot: e.activation(
                        gs[n][:, :], half(*gslot), AF.Sigmoid, bias=bm_sb[:, n * 8 + dc:n * 8 + dc + 1], scale=1.0),
                        reads=[hk(*gslot), ('par', 1)], writes=[('gs', n)])
                sc.add('dve', lambda e: e.tensor_tensor(acc[0][:, :], half(4, 0), gs[0][:, :], ALU.mult),
                       reads=[hk(4, 0), ('gs', 0)], writes=[('acc', 0)])
                sc.add('dve', lambda e: e.tensor_tensor(acc[1][:, :], half(4, 1), gs[1][:, :], ALU.mult),
                       reads=[hk(4, 1), ('gs', 1)], writes=[('acc', 1)])
                sc.add('pool', lambda e: e.tensor_tensor(acc[0][:, :], acc[0][:, :], acc[1][:, :], ALU.add),
                       reads=[('acc', 0), ('acc', 1)], writes=[('acc', 0)])
                sc.add('dve', lambda e: e.tensor_tensor(acc[1][:, :], half(5, 0), gs[2][:, :], ALU.mult),
                       reads=[hk(5, 0), ('gs', 2)], writes=[('acc', 1)])
                sc.add('pool', lambda e, dc=dc: e.tensor_tensor(mg[:, dc, :], acc[0][:, :], acc[1][:, :], ALU.add),
                       reads=[('acc', 0), ('acc', 1)], writes=[('mg', dc)])
            for dc in range(8):
                s = dc % 2

                def mmo(e, dc=dc, s=s):
                    r = None
                    for k in range(8):
                        r = e.matmul(half(7, s), Wout[:, k, dc * 128:(dc + 1) * 128], mg[:, k, :], start=(k == 0),
                                     stop=(k == 7))
                    return r
                sc.add('pe', mmo, reads=[('Wout', '', k) for k in range(8)] + [('mg', k) for k in range(8)], writes=[hk(7, s)])
                sc.add('dve', lambda e, dc=dc, s=s: e.tensor_tensor(xt[:, dc, :], xt[:, dc, :], half(7, s), ALU.add),
                       reads=['xt', hk(7, s)], writes=[('xn', dc)])
            allxn = [('xn', dc) for dc in range(8)]
            if not last:
                sc.add('sp', lambda e, t0=t0, t1=t1: e.dma_start(out=xov[:, :, t0:t1], in_=xt[:, :, :]),
                       reads=allxn, writes=[('xo', it)], slot='xo')

        def finalnorm(it):
            b = it % 2
            t0, t1 = it * TT, (it + 1) * TT
            for k in range(8):
                s = k % 2
                sc.add('act', lambda e, k=k, s=s: e.activation(sqs[s][:, :], xt[:, k, :], AF.Square),
                       reads=[('xn', k)], writes=[('sqs', s)])
                sc.add('pe', lambda e, k=k, s=s: e.matmul(half(1, 1), ones_bf[:, :], sqs[s][:, :], start=(k == 0),
                                                         stop=(k == 7)),
                       reads=[('sqs', s), 'ones_bf'], writes=[hk(1, 1)])
            sc.add('act', lambda e: e.activation(rstd[:, :], half(1, 1), AF.Ln, bias=EPS, scale=1.0 / D),
                   reads=[hk(1, 1)], writes=['rstd'])
            sc.add('act', lambda e: e.activation(rstd[:, :], rstd[:, :], AF.Exp, scale=-0.5), reads=['rstd'],
                   writes=['rstd'])
            for k in range(8):
                sc.add('dve', lambda e, k=k: e.scalar_tensor_tensor(hout[:, k, :], xt[:, k, :], ng_sb[:, k:k + 1],
                                                                   rstd[:, :], ALU.mult, ALU.mult),
                       reads=[('xn', k), 'rstd', ('par', 3)], writes=['hout'])
            if last:
                sc.add('sp', lambda e, t0=t0, t1=t1: e.dma_start(out=xov[:, :, t0:t1], in_=hout[:, :, :]),
                       reads=['hout'], writes=[('xo', it)], slot='xo')
            else:
                sc.add('sp', lambda e, t0=t0, t1=t1: e.dma_start(out=hov[:, :, t0:t1], in_=hout[:, :, :]),
                       reads=['hout'], writes=[('ho', it)], slot='ho')

        def load_x(it):
            t0, t1 = it * TT, (it + 1) * TT
            sc.add('sp', lambda e: e.dma_start(out=xt[:, :, :], in_=xv[:, :, t0:t1]),
                   writes=['xt'] + [('xn', dc) for dc in range(8)], slot='xt')

        load_ht(0)
        load_o(0)
        load_x(0)
        phase12(0)
        for a_ in late:
            ldw(*a_)
        for it in range(NTT):
            if it + 1 < NTT:
                load_ht(it + 1)
                load_o(it + 1)
            merge_out(it)
            if it + 1 < NTT:
                phase12(it + 1)
            finalnorm(it)
            if it + 1 < NTT:
                load_x(it + 1)
        sc.close()
    return nc


def mixer_inputs(c, hT, w_in_l, b_fg_l, conv_w_l, a_log_l, dt_bias_l):
    hd, half = c // 2, c % 2
    wf = np.concatenate([w_in_l[:, OFF['aq'] + c * 64:OFF['aq'] + (c + 1) * 64],
                         w_in_l[:, OFF['ak'] + c * 64:OFF['ak'] + (c + 1) * 64],
                         w_in_l[:, OFF['av'] + c * 64:OFF['av'] + (c + 1) * 64],
                         w_in_l[:, OFF['af'] + c:OFF['af'] + c + 1]], axis=1)
    vo = hd * 128 + half * 64
    wg = np.concatenate([w_in_l[:, OFF['bq'] + hd * 128:OFF['bq'] + (hd + 1) * 128],
                         w_in_l[:, OFF['bk'] + hd * 128:OFF['bk'] + (hd + 1) * 128],
                         w_in_l[:, OFF['bv'] + vo:OFF['bv'] + vo + 64],
                         w_in_l[:, OFF['ba'] + hd:OFF['ba'] + hd + 1],
                         w_in_l[:, OFF['bb'] + hd:OFF['bb'] + hd + 1]], axis=1)
    cw = np.zeros((128, 12), np.float32)
    cw[:, 0:4] = conv_w_l[:, hd * 128:(hd + 1) * 128].T
    cw[:, 4:8] = conv_w_l[:, 512 + hd * 128:512 + (hd + 1) * 128].T
    cw[0:64, 8:12] = conv_w_l[:, 1024 + vo:1024 + vo + 64].T
    gpar = np.empty((128, 2), np.float32)
    gpar[:, 0] = a_log_l[hd]
    gpar[:, 1] = dt_bias_l[hd]
    return dict(hT=hT, wf=np.ascontiguousarray(wf), bfg=np.full((128, 1), b_fg_l[c], np.float32),
                wg=np.ascontiguousarray(wg), cw=cw, gpar=gpar)


def _lay8(v):
    return np.ascontiguousarray(np.asarray(v, np.float32).reshape(-1, 128).T)


_PROGS = {}


def _prog(name, fn):
    if name not in _PROGS:
        _PROGS[name] = fn()
    return _PROGS[name]


def kernel(x, mem, norm_g, w_in, b_fg, b_merge, conv_w, a_log, dt_bias, gdn_norm_g, mem_norm_g, w_mem_kv,
           w_branch, w_out, final_norm_g):
    f = lambda a: np.asarray(a, np.float32)
    x, mem, norm_g, w_in, b_fg, b_merge, conv_w = map(f, (x, mem, norm_g, w_in, b_fg, b_merge, conv_w))
    a_log, dt_bias, gdn_norm_g, mem_norm_g = map(f, (a_log, dt_bias, gdn_norm_g, mem_norm_g))
    w_mem_kv, w_branch, w_out, final_norm_g = map(f, (w_mem_kv, w_branch, w_out, final_norm_g))
    S = x.shape[1]
    TS = S // NCORES
    cores = list(range(NCORES))
    xT = np.ascontiguousarray(x[0].T)
    memT = np.ascontiguousarray(mem[0].T)
    sh = lambda a, c: np.ascontiguousarray(a[:, c * TS:(c + 1) * TS])
    ncP = _prog('P', lambda: build_P(TS))
    res = run_bass_kernel_spmd(ncP, [dict(xT=sh(xT, c), g=_lay8(norm_g[0])) for c in cores], core_ids=cores)
    hT = np.concatenate([np.asarray(r["hT"]) for r in res.results], axis=1)
    depth = w_in.shape[0]
    for l in range(depth):
        last = (l == depth - 1)
        ncM = _prog('M', lambda: build_M(S))
        hTc = np.ascontiguousarray(hT)
        res = run_bass_kernel_spmd(
            ncM, [mixer_inputs(c, hTc, w_in[l], b_fg[l], conv_w[l], a_log[l], dt_bias[l]) for c in cores],
            core_ids=cores)
        oaT = np.concatenate([np.asarray(r["oaT"]) for r in res.results], axis=0)
        obT = np.concatenate([np.asarray(r["obT"]) for r in res.results], axis=0)
        ncT = _prog('T%d' % int(last), lambda: build_T(TS, last))
        ng = final_norm_g if last else norm_g[l + 1]
        maps = []
        for c in cores:
            maps.append(dict(xT=sh(xT, c), hT=sh(hT, c), oaT=sh(oaT, c), obT=sh(obT, c),
                             w_in=np.ascontiguousarray(w_in[l]),
                             w_br=np.ascontiguousarray(w_branch[l].reshape(1536, D)),
                             w_out=np.ascontiguousarray(w_out[l]), w_kv=np.ascontiguousarray(w_mem_kv[l]),
                             memT=memT, mem_g=_lay8(mem_norm_g[l]), b_mg=_lay8(b_merge[l]),
                             gdn_g=np.ascontiguousarray(gdn_norm_g[l].reshape(128, 1)), next_g=_lay8(ng)))
        res = run_bass_kernel_spmd(ncT, maps, core_ids=cores)
        xT = np.concatenate([np.asarray(r["xoT"]) for r in res.results], axis=1)
        if not last:
            hT = np.concatenate([np.asarray(r["hoT"]) for r in res.results], axis=1)
    out = np.ascontiguousarray(xT.T).reshape(1, S, D).astype(np.float32)
    return out
```

```python
import contextlib
import numpy as np
import ml_dtypes
import concourse.bass as bass
import concourse.mybir as mybir
from concourse.bass_utils import run_bass_kernel_spmd

F32 = mybir.dt.float32
BF16 = mybir.dt.bfloat16
AF = mybir.ActivationFunctionType
ALU = mybir.AluOpType

D = 1024
S_FULL = 16384
NCORES = 8
EPS = 1e-6
N_IN = 8208
import os as _os
SAME_ENGINE_SYNC = bool(int(_os.environ.get('SAME_SYNC', '1')))
OFF = dict(aq=0, ak=512, av=1024, af=1536, az=1544, bq=2056, bk=2568, bv=3080,
           ba=3592, bb=3596, bz=3600, mq=4112, mz=4624, gates=5136)


def _is_psum_key(k):
    if isinstance(k, str):
        return k.startswith('ps')
    if isinstance(k, tuple) and len(k) >= 2:
        return k[0] in ('pb', 'pS', 'pO') or k[1] == 'ps'
    return False


class Sched:
    ENGS = ['pe', 'act', 'dve', 'pool', 'sp']

    def __init__(self, nc, same_engine_sync=None):
        if same_engine_sync is None:
            same_engine_sync = SAME_ENGINE_SYNC
        self.nc = nc
        self.ops = []
        self.lastw = {}
        self.readers = {}
        self.slot_count = {}
        self.same = same_engine_sync
        self.stack = contextlib.ExitStack()
        self.esem = {e: self.stack.enter_context(nc.semaphore("sem_" + e)) for e in self.ENGS}
        self.ssem = {}
        self.cnt = {e: 0 for e in self.ENGS}

    def _needs_same(self, eng):
        if eng == 'pe':
            return False
        if eng == 'pool':
            return True
        return self.same

    def add(self, eng, fn, reads=(), writes=(), slot=None):
        op = dict(eng=eng, fn=fn, deps=[], slot=slot, inc=False, id=len(self.ops))
        deps = {}
        for k in reads:
            w = self.lastw.get(k)
            if w is not None:
                deps[w['id']] = w
            if _is_psum_key(k):
                for r in self.readers.get(k, ()):
                    if r['eng'] != eng:
                        deps[r['id']] = r
        for k in writes:
            w = self.lastw.get(k)
            if w is not None:
                deps[w['id']] = w
            for r in self.readers.get(k, ()):
                deps[r['id']] = r
        for d in deps.values():
            if d is op:
                continue
            op['deps'].append(d)
            if d['slot'] is None:
                if d['eng'] != eng or self._needs_same(eng) or slot is not None:
                    d['inc'] = True
        for k in writes:
            self.lastw[k] = op
            self.readers[k] = []
        for k in reads:
            self.readers.setdefault(k, []).append(op)
        if slot is not None:
            if slot not in self.ssem:
                self.ssem[slot] = self.stack.enter_context(self.nc.semaphore("sl_%d" % len(self.ssem)))
            self.slot_count[slot] = self.slot_count.get(slot, 0) + 1
            op['slot_val'] = self.slot_count[slot] * 16
        self.ops.append(op)
        return op

    def flush(self):
        nc = self.nc
        for op in self.ops:
            if op['slot'] is None and op['inc']:
                self.cnt[op['eng']] += 1
                op['count'] = self.cnt[op['eng']]
        ops = self.ops
        esem, ssem = self.esem, self.ssem
        final = dict(self.slot_count)
        with nc.Block() as block:
            def run(ename, eng):
                known = {}
                for op in ops:
                    if op['eng'] != ename:
                        continue
                    waits = {}
                    for d in op['deps']:
                        if d['slot'] is not None:
                            key = ('s', d['slot'])
                            v = d['slot_val']
                            sem = ssem[d['slot']]
                        else:
                            if d['eng'] == ename and op['slot'] is None and not self._needs_same(ename):
                                continue
                            key = ('e', d['eng'])
                            v = d['count']
                            sem = esem[d['eng']]
                        if waits.get(key, (None, -1))[1] < v:
                            waits[key] = (sem, v)
                    for key, (sem, v) in waits.items():
                        if known.get(key, -1) >= v:
                            continue
                        known[key] = v
                        eng.wait_ge(sem, v)
                    ins = op['fn'](eng)
                    if op['slot'] is not None:
                        ins.then_inc(ssem[op['slot']], 16)
                    elif op['inc']:
                        ins.then_inc(esem[ename], 1)
                if ename == 'sp':
                    for s, n in final.items():
                        eng.wait_ge(ssem[s], n * 16)

            block.tensor(lambda e: run('pe', e))
            block.scalar(lambda e: run('act', e))
            block.vector(lambda e: run('dve', e))
            block.gpsimd(lambda e: run('pool', e))
            block.sync(lambda e: run('sp', e))
        self.ops = []
        self.lastw = {}
        self.readers = {}

    def collective(self, kind, op, src_ap, dst_ap, reads=(), writes=(), slot='cc', ncores=NCORES):
        self.flush()
        if slot not in self.ssem:
            self.ssem[slot] = self.stack.enter_context(self.nc.semaphore("sl_%d" % len(self.ssem)))
        self.slot_count[slot] = self.slot_count.get(slot, 0) + 1
        ins = self.nc.gpsimd.collective_compute(kind, op, replica_groups=[list(range(ncores))],
                                                ins=[src_ap], outs=[dst_ap])
        ins.then_inc(self.ssem[slot], 16)
        pseudo = dict(eng='pool', fn=None, deps=[], slot=slot, inc=False, id=-1,
                      slot_val=self.slot_count[slot] * 16)
        for k in writes:
            self.lastw[k] = pseudo
            self.readers[k] = []

    def close(self):
        self.flush()
        self.stack.close()


_NAME = [0]


class Ctx:
    def __init__(self, nc):
        self.nc = nc
        self.st = contextlib.ExitStack()

    def sb(self, shape, dt, name=None):
        _NAME[0] += 1
        return self.st.enter_context(self.nc.sbuf_tensor(name or ("t%d" % _NAME[0]), list(shape), dt))

    def ps(self, shape, dt, name=None):
        _NAME[0] += 1
        return self.st.enter_context(self.nc.psum_tensor(name or ("p%d" % _NAME[0]), list(shape), dt))


def emit_rmsnorm(sc, x_sb, xkey, g_sb, ones_f, out_sb, outkey, TT, sq, ps, rstd, tag, dim=D, gkey='g',
                 oneskey='ones_f'):
    for k in range(8):
        sc.add('act', lambda e, k=k: e.activation(sq[:, k, :], x_sb[:, k, :], AF.Square),
               reads=[xkey], writes=[(tag, 'sq', k)])

    def mm(e):
        r = None
        for k in range(8):
            r = e.matmul(ps[:, 0:TT], ones_f[:, :], sq[:, k, :], start=(k == 0), stop=(k == 7))
        return r
    sc.add('pe', mm, reads=[(tag, 'sq', k) for k in range(8)] + [oneskey], writes=[(tag, 'ps')])
    sc.add('act', lambda e: e.activation(rstd[:, :], ps[:, 0:TT], AF.Ln, bias=EPS, scale=1.0 / dim),
           reads=[(tag, 'ps')], writes=[(tag, 'rstd')])
    sc.add('act', lambda e: e.activation(rstd[:, :], rstd[:, :], AF.Exp, scale=-0.5),
           reads=[(tag, 'rstd')], writes=[(tag, 'rstd')])
    for k in range(8):
        sc.add('dve',
               lambda e, k=k: e.scalar_tensor_tensor(out_sb[:, k, :], x_sb[:, k, :], g_sb[:, k:k + 1],
                                                     rstd[:, :], ALU.mult, ALU.mult),
               reads=[xkey, (tag, 'rstd'), gkey], writes=[outkey])


def build_P(TS):
    nc = bass.Bass("TRN2", target_bir_lowering=False)
    xT = nc.dram_tensor("xT", [D, TS], F32, kind="ExternalInput").ap()
    g = nc.dram_tensor("g", [128, 8], F32, kind="ExternalInput").ap()
    hT = nc.dram_tensor("hT", [D, TS], BF16, kind="ExternalOutput").ap()
    TT = 512
    cx = Ctx(nc)
    with cx.st:
        sc = Sched(nc)
        ones_f = cx.sb([128, 128], BF16)
        g_sb = cx.sb([128, 8], F32)
        xs = [cx.sb([128, 8, TT], F32) for _ in range(2)]
        hs = [cx.sb([128, 8, TT], BF16) for _ in range(2)]
        sq = cx.sb([128, 8, TT], BF16)
        rstd = cx.sb([128, TT], F32)
        ps = cx.ps([128, 512], F32)
        sc.add('pool', lambda e: e.memset(ones_f[:, :], 1.0), writes=['ones_f'])
        sc.add('sp', lambda e: e.dma_start(out=g_sb[:, :], in_=g[:, :]), writes=['g'], slot='g')
        xv = xT.rearrange("(k p) t -> p k t", p=128)
        hv = hT.rearrange("(k p) t -> p k t", p=128)
        for i in range(TS // TT):
            b = i % 2
            sc.add('sp', lambda e, i=i, b=b: e.dma_start(out=xs[b][:, :, :], in_=xv[:, :, i * TT:(i + 1) * TT]),
                   writes=[('x', b)], slot=('x', b))
            emit_rmsnorm(sc, xs[b], ('x', b), g_sb, ones_f, hs[b], ('h', b), TT, sq, ps, rstd, 'n')
            sc.add('sp', lambda e, i=i, b=b: e.dma_start(out=hv[:, :, i * TT:(i + 1) * TT], in_=hs[b][:, :, :]),
                   reads=[('h', b)], writes=[('hout', i)], slot=('ho', b))
        sc.close()
    return nc


def make_consts(sc, cx):
    c = {}
    c['ones'] = cx.sb([128, 128], F32)
    c['ident'] = cx.sb([128, 128], F32)
    c['uincl'] = cx.sb([128, 128], F32)
    c['ustrict'] = cx.sb([128, 128], F32)
    c['e0'] = cx.sb([128, 128], F32)
    c['ones_bf'] = cx.sb([128, 128], BF16)
    c['ident_bf'] = cx.sb([128, 128], BF16)
    c['zeros'] = cx.sb([128, 128], F32)
    sc.add('pool', lambda e: e.memset(c['ones'][:, :], 1.0), writes=['c_ones'])
    sc.add('pool', lambda e: e.memset(c['zeros'][:, :], 0.0), writes=['c_zeros'])
    sc.add('pool', lambda e: e.memset(c['ones_bf'][:, :], 1.0), writes=['c_ones_bf'])
    sc.add('pool', lambda e: e.affine_select(c['ident'][:, :], c['zeros'][:, :], [[1, 128]], ALU.not_equal, 1.0,
                                             base=0, channel_multiplier=-1),
           reads=['c_zeros'], writes=['c_ident'])
    sc.add('pool', lambda e: e.tensor_copy(c['ident_bf'][:, :], c['ident'][:, :]),
           reads=['c_ident'], writes=['c_ident_bf'])
    sc.add('pool', lambda e: e.affine_select(c['uincl'][:, :], c['ones'][:, :], [[1, 128]], ALU.is_ge, 0.0,
                                             base=0, channel_multiplier=-1),
           reads=['c_ones'], writes=['c_uincl'])
    sc.add('pool', lambda e: e.affine_select(c['ustrict'][:, :], c['ones'][:, :], [[1, 128]], ALU.is_gt, 0.0,
                                             base=0, channel_multiplier=-1),
           reads=['c_ones'], writes=['c_ustrict'])
    sc.add('pool', lambda e: e.affine_select(c['e0'][:, :], c['ones'][:, :], [[0, 128]], ALU.is_ge, 0.0,
                                             base=0, channel_multiplier=-1),
           reads=['c_ones'], writes=['c_e0'])
    return c


def load_cast(sc, dst_bf, dstkey, src_ap, stage, stagekey, eng_dma='sp', eng_cast='pool', slot=None):
    sc.add(eng_dma, lambda e: e.dma_start(out=stage, in_=src_ap), writes=[stagekey], slot=slot or stagekey)
    sc.add(eng_cast, lambda e: e.tensor_copy(dst_bf, stage), reads=[stagekey], writes=[dstkey])


def fox_phase(nc, sc, cx0, c, S, hT, wf, bfg, oaT, scr, psb, stop=99):
    NT = S // 128
    NG = S // 512
    cx = Ctx(nc)
    with cx.st:
        wq = cx.sb([128, 8, 64], BF16)
        wk = cx.sb([128, 8, 64], BF16)
        wv = cx.sb([128, 8, 65], BF16)
        QT = cx.sb([65, S], BF16)
        KT = cx.sb([65, S], BF16)
        V = cx.sb([128, NT, 65], BF16)
        lfr = cx.sb([128, NT], F32)
        lfn = cx.sb([128, NT], F32)
        Fn = cx.sb([128, NT], F32)
        frefB = cx.sb([128, NG], F32)
        ctok = cx.sb([128, NT], F32)
        cTT = cx.sb([128, 128], BF16)
        totT = cx.sb([128, 1], F32)
        X = cx.sb([128, 128], F32)
        negb = cx.sb([128, 1], F32)
        biasg = [cx.sb([128, NT], F32) for _ in range(2)]
        ht = [cx.sb([128, 8, 512], BF16) for _ in range(2)]
        Pb = [cx.sb([128, 512], BF16) for _ in range(5)]
        oun = cx.sb([65, 512], F32)
        rl = cx.sb([65, 512], F32)
        ofin = [cx.sb([64, 512], F32) for _ in range(2)]

        wfv = wf.rearrange("(k p) c -> p k c", p=128)
        sc.add('pool', lambda e: e.dma_start(out=wq[:, :, :], in_=wfv[:, :, 0:64]), writes=['wq'], slot='wq')
        sc.add('pool', lambda e: e.dma_start(out=wk[:, :, :], in_=wfv[:, :, 64:128]), writes=['wk'], slot='wk')
        sc.add('pool', lambda e: e.dma_start(out=wv[:, :, :], in_=wfv[:, :, 128:193]), writes=['wv'], slot='wv')
        sc.add('sp', lambda e: e.dma_start(out=negb[:, :], in_=bfg[:, :]), writes=['negb'], slot='negb')
        sc.add('dve', lambda e: e.tensor_scalar(negb[:, :], negb[:, :], -1.0, None, ALU.mult),
               reads=['negb'], writes=['negb'])
        sc.add('pool', lambda e: e.memset(KT[64:65, :], 1.0), writes=['KTrow'])
        sc.add('pool', lambda e: e.memset(V[:, :, 64:65], 1.0), writes=['Vones'])

        if stop <= 0:
            sc.flush()
            return
        hv = hT.rearrange("(k p) t -> p k t", p=128)
        psq, psk, psv = psb[0], psb[1], psb[2]
        for i in range(NG):
            b = i % 2
            sc.add('sp', lambda e, i=i, b=b: e.dma_start(out=ht[b][:, :, :], in_=hv[:, :, i * 512:(i + 1) * 512]),
                   writes=[('ht', b)], slot=('ht', b))

            def mmq(e, b=b):
                r = None
                for k in range(8):
                    r = e.matmul(psq[0:64, :], wq[:, k, :], ht[b][:, k, :], start=(k == 0), stop=(k == 7))
                return r
            import os
            DBG = int(os.environ.get('FOXDBG', '15'))
            if DBG & 1:
              sc.add('pe', mmq, reads=[('ht', b), 'wq'], writes=['psq'])
            if DBG & 1:
              sc.add('act', lambda e, i=i: e.activation(QT[0:64, i * 512:(i + 1) * 512], psq[0:64, :], AF.Copy,
                                                      scale=0.125),
                   reads=['psq'], writes=[('QT', i)])

            def mmk(e, b=b):
                r = None
                for k in range(8):
                    r = e.matmul(psk[0:64, :], wk[:, k, :], ht[b][:, k, :], start=(k == 0), stop=(k == 7))
                return r
            if DBG & 2:
              sc.add('pe', mmk, reads=[('ht', b), 'wk'], writes=['psk'])
              sc.add('dve', lambda e, i=i: e.tensor_copy(KT[0:64, i * 512:(i + 1) * 512], psk[0:64, :]),
                   reads=['psk'], writes=[('KT', i)])

            def mmv(e, b=b):
                r = None
                for sub in range(4):
                    for k in range(8):
                        r = e.matmul(psv[:, sub * 128:sub * 128 + 65], ht[b][:, k, sub * 128:(sub + 1) * 128],
                                     wv[:, k, :], start=(k == 0), stop=(k == 7))
                return r
            pv3 = psv[:, :].rearrange("p (s c) -> p s c", c=128)
            if DBG & 4:
              sc.add('pe', mmv, reads=[('ht', b), 'wv'], writes=['psv'])
              sc.add('dve', lambda e, i=i, pv3=pv3: e.tensor_copy(V[:, 4 * i:4 * i + 4, 0:64], pv3[:, :, 0:64]),
                   reads=['psv', 'Vones'], writes=[('V', i)])
            if DBG & 8:
              sc.add('dve', lambda e, i=i, pv3=pv3: e.tensor_copy(lfr[:, 4 * i:4 * i + 4], pv3[:, :, 64]),
                   reads=['psv'], writes=[('lfr', i)])

        if stop <= 1:
            sc.flush()
            return
        allfr = [('lfr', i) for i in range(NG)]
        sc.add('act', lambda e: e.activation(lfn[:, :], lfr[:, :], AF.Exp, bias=negb[:, 0:1], scale=-1.0),
               reads=allfr + ['negb'], writes=['lfn'])
        sc.add('act', lambda e: e.activation(lfn[:, :], lfn[:, :], AF.Ln, bias=1.0, scale=1.0),
               reads=['lfn'], writes=['lfn'])
        pt = psb[0]
        sc.add('pe', lambda e: e.matmul(pt[0:NT, 0:1], lfn[:, :], c['ones'][:, 0:1], start=True, stop=True),
               reads=['lfn', 'c_ones', 'psq'], writes=['psq'])
        sc.add('dve', lambda e: e.tensor_copy(totT[0:NT, :], pt[0:NT, 0:1]), reads=['psq'], writes=['totT'])
        sc.add('dve', lambda e: e.tensor_scalar(X[0:NT, 0:NT], c['ustrict'][0:NT, 0:NT], totT[0:NT, 0:1], None,
                                                ALU.mult),
               reads=['totT', 'c_ustrict'], writes=['X'])
        pf = psb[1]

        def mmF(e):
            e.matmul(pf[:, 0:NT], c['uincl'][:, :], lfn[:, :], start=True, stop=False)
            return e.matmul(pf[:, 0:NT], c['ones'][0:NT, :], X[0:NT, 0:NT], start=False, stop=True)
        sc.add('pe', mmF, reads=['lfn', 'X', 'c_uincl', 'c_ones', 'psk'], writes=['psk'])
        sc.add('dve', lambda e: e.tensor_copy(Fn[:, :], pf[:, 0:NT]), reads=['psk'], writes=['Fn'])
        pr = psb[2]
        sc.add('pe', lambda e: e.matmul(pr[:, 0:NG], c['e0'][:, :], Fn[:, 0:NT:4], start=True, stop=True),
               reads=['Fn', 'c_e0', 'psv'], writes=['psv'])
        sc.add('dve', lambda e: e.tensor_copy(frefB[:, :], pr[:, 0:NG]), reads=['psv'], writes=['frefB'])
        for r in range(4):
            sc.add('dve', lambda e, r=r: e.tensor_tensor(ctok[:, r:NT:4], frefB[:, :], Fn[:, r:NT:4], ALU.subtract),
                   reads=['frefB', 'Fn'], writes=[('ctok', r)])
        pc = psb[3]
        sc.add('pe', lambda e: e.transpose(pc[0:NT, 0:128], ctok[:, :], c['ident'][:, :]),
               reads=[('ctok', r) for r in range(4)] + ['c_ident'], writes=['ps3'])
        sc.add('dve', lambda e: e.tensor_copy(cTT[0:NT, :], pc[0:NT, 0:128]), reads=['ps3'], writes=['cTT'])
        sc.add('sp', lambda e: e.dma_start(out=scr[0:NT, :], in_=cTT[0:NT, :]), reads=['cTT'], writes=['scr'],
               slot='scr')
        sc.add('sp', lambda e: e.dma_start(out=QT[64:65, :], in_=scr[0:NT, :].rearrange("(o j) p -> o (j p)", o=1)),
               reads=['scr'], writes=['QTrow'], slot='qtrow')

        if stop <= 2:
            sc.flush()
            return
        sc.flush()
        LA = 4
        pS = [psb[0], psb[1], psb[2], psb[3], psb[5]]
        pO = [psb[6], psb[7]]
        pbc = psb[4]
        blocks = []
        for g in range(NG):
            nj = 4 * g + 4
            for j in range(nj):
                r = j - 4 * g
                c0 = 0 if r < 0 else r * 128
                blocks.append((g, j, r, c0, 512 - c0, nj))
        NB = len(blocks)

        def emit_front(bi):
            g, j, r, c0, N, nj = blocks[bi]
            gb = g % 2
            sb_ = bi % 5
            if j == 0:
                sc.add('dve', lambda e: e.tensor_scalar(biasg[gb][:, 0:nj], Fn[:, 0:nj], frefB[:, g:g + 1], None,
                                                        ALU.subtract),
                       reads=['Fn', 'frefB'], writes=[('biasg', gb)])
            sc.add('pe', lambda e: e.matmul(pS[sb_][:, 0:N], KT[0:65, j * 128:(j + 1) * 128],
                                            QT[0:65, g * 512 + c0:(g + 1) * 512], start=True, stop=True),
                   reads=['QT', 'KT'], writes=[('pS', sb_)])
            sc.add('act', lambda e: e.activation(Pb[sb_][:, 0:N], pS[sb_][:, 0:N], AF.Exp,
                                                 bias=biasg[gb][:, j:j + 1], scale=1.0),
                   reads=[('pS', sb_), ('biasg', gb)], writes=[('P', sb_)])
            if r >= 0:
                sc.add('pool', lambda e: e.affine_select(Pb[sb_][:, 0:128], Pb[sb_][:, 0:128], [[1, 128]], ALU.is_ge,
                                                         0.0, base=0, channel_multiplier=-1),
                       reads=[('P', sb_)], writes=[('P', sb_)])

        def emit_back(bi):
            g, j, r, c0, N, nj = blocks[bi]
            gb = g % 2
            sb_ = bi % 5
            sc.add('pe', lambda e: e.matmul(pO[gb][0:65, c0:512], V[:, j, 0:65], Pb[sb_][:, 0:N], start=(j == 0),
                                            stop=(j == nj - 1), skip_group_check=True),
                   reads=[('P', sb_), 'V'], writes=[('pO', gb)])
            if j == nj - 1:
                sc.add('dve', lambda e: e.tensor_copy(oun[0:65, :], pO[gb][0:65, :]),
                       reads=[('pO', gb)], writes=['oun'])
                sc.add('dve', lambda e: e.reciprocal(rl[64:65, :], oun[64:65, :]), reads=['oun'], writes=['rl'])
                pending.append((bi + 6, g, gb))

        def emit_fin(g, gb):
            sc.add('pe', lambda e: e.matmul(pbc[0:64, :], c['ones'][64:65, 0:64], rl[64:65, :], start=True,
                                            stop=True),
                   reads=['rl', 'c_ones'], writes=[('pb', 4)])
            sc.add('dve', lambda e: e.tensor_tensor(ofin[gb][:, :], oun[0:64, :], pbc[0:64, :], ALU.mult),
                   reads=['oun', ('pb', 4)], writes=[('ofin', gb)])
            sc.add('sp', lambda e: e.dma_start(out=oaT[:, g * 512:(g + 1) * 512], in_=ofin[gb][:, :]),
                   reads=[('ofin', gb)], writes=[('oaT', g)], slot=('oa', gb))

        pending = []
        for bi in range(NB + LA):
            if bi < NB:
                emit_front(bi)
            if bi - LA >= 0:
                emit_back(bi - LA)
            while pending and pending[0][0] <= bi - LA:
                _, g_, gb_ = pending.pop(0)
                emit_fin(g_, gb_)
        for _, g_, gb_ in pending:
            emit_fin(g_, gb_)
        sc.flush()


def gdn_phase(nc, sc, c, S, hT, wg, cw, gpar, obT, psb):
    NSEG = S // 512
    A = sc.add
    cx = Ctx(nc)
    PB = lambda n: ('pb', n)
    with cx.st:
        f32t = lambda *sh: cx.sb(list(sh), F32)
        bft = lambda *sh: cx.sb(list(sh), BF16)
        wq, wk, wv, wab = bft(128, 8, 128), bft(128, 8, 128), bft(128, 8, 64), bft(128, 8, 2)
        cw_sb, gp_sb = f32t(128, 12), f32t(128, 2)
        negA = f32t(128, 1)
        M_s, M_i = f32t(128, 4, 128), f32t(128, 4, 128)
        E63, E127, EL = f32t(128, 128), f32t(128, 128), f32t(128, 128)
        ht = [bft(128, 8, 512) for _ in range(2)]
        rq, rk, rv = f32t(128, 515), f32t(128, 515), f32t(64, 515)
        cq, ck = f32t(128, 512), f32t(128, 512)
        sq2, sq2b = bft(128, 512), bft(128, 512)
        rn, rnb = f32t(128, 512), f32t(128, 512)
        g_tok, G_tok, eG, ekd, glo = [f32t(128, 4) for _ in range(5)]
        diagG, EGrow, Dm, Gam, Gs, Gi = [f32t(128, 512) for _ in range(6)]
        ktok = f32t(128, 4, 128)
        Bb = [bft(128, 512) for _ in range(2)]
        Pb_ = [bft(128, 512) for _ in range(2)]
        S_f, S_b = f32t(128, 512), bft(128, 512)
        St_f, St_b = f32t(128, 64), bft(128, 64)
        vnew = bft(128, 64)
        o_sb = f32t(64, 512)

        wgv = wg.rearrange("(k p) c -> p k c", p=128)
        A('pool', lambda e: e.dma_start(out=wq[:, :, :], in_=wgv[:, :, 0:128]), writes=['gwq'], slot='gwq')
        A('pool', lambda e: e.dma_start(out=wk[:, :, :], in_=wgv[:, :, 128:256]), writes=['gwk'], slot='gwk')
        A('pool', lambda e: e.dma_start(out=wv[:, :, :], in_=wgv[:, :, 256:320]), writes=['gwv'], slot='gwv')
        A('pool', lambda e: e.dma_start(out=wab[:, :, :], in_=wgv[:, :, 320:322]), writes=['gwab'], slot='gwab')
        A('sp', lambda e: e.dma_start(out=cw_sb[:, :], in_=cw[:, :]), writes=['cw'], slot='cw')
        A('sp', lambda e: e.dma_start(out=gp_sb[:, :], in_=gpar[:, :]), writes=['gp'], slot='gp')
        A('act', lambda e: e.activation(negA[:, :], gp_sb[:, 0:1], AF.Exp), reads=['gp'], writes=['negA'])
        A('dve', lambda e: e.tensor_scalar(negA[:, :], negA[:, :], -1.0, None, ALU.mult), reads=['negA'],
          writes=['negA'])
        A('pool', lambda e: e.memset(M_s[:, :, :], 1.0), writes=['M_s'])
        A('pool', lambda e: e.memset(M_i[:, :, :], 1.0), writes=['M_i'])
        A('pool', lambda e: e.affine_select(M_s[:, :, :], M_s[:, :, :], [[0, 4], [1, 128]], ALU.is_gt, 0.0, base=0,
                                            channel_multiplier=-1), reads=['M_s'], writes=['M_s'])
        A('pool', lambda e: e.affine_select(M_i[:, :, :], M_i[:, :, :], [[0, 4], [1, 128]], ALU.is_ge, 0.0, base=0,
                                            channel_multiplier=-1), reads=['M_i'], writes=['M_i'])
        A('pool', lambda e: e.memset(M_s[0:64, :, 64:128], 0.0), reads=['M_s'], writes=['M_s'])
        A('pool', lambda e: e.memset(M_i[0:64, :, 64:128], 0.0), reads=['M_i'], writes=['M_i'])
        A('pool', lambda e: e.affine_select(E63[:, :], c['zeros'][:, :], [[0, 128]], ALU.not_equal, 1.0, base=-63,
                                            channel_multiplier=1), reads=['c_zeros'], writes=['E63'])
        A('pool', lambda e: e.affine_select(E127[:, :], c['zeros'][:, :], [[0, 128]], ALU.not_equal, 1.0, base=-127,
                                            channel_multiplier=1), reads=['c_zeros'], writes=['E127'])
        A('pool', lambda e: e.tensor_copy(EL[:, 0:64], E63[:, 0:64]), reads=['E63'], writes=['EL'])
        A('pool', lambda e: e.tensor_copy(EL[:, 64:128], E127[:, 64:128]), reads=['E127', 'EL'], writes=['EL'])
        A('pool', lambda e: e.memset(rq[:, 0:3], 0.0), writes=['rq'])
        A('pool', lambda e: e.memset(rk[:, 0:3], 0.0), writes=['rk'])
        A('pool', lambda e: e.memset(rv[:, 0:3], 0.0), writes=['rv'])
        A('pool', lambda e: e.memset(St_f[:, :], 0.0), writes=['St_f'])
        A('pool', lambda e: e.memset(St_b[:, :], 0.0), writes=['St_b'])

        hv = hT.rearrange("(k p) t -> p k t", p=128)
        ones, ident = c['ones'], c['ident']
        DEPTH = dict(qn_f=2, kn_f=2, qn_bf=2, kT_bf=2, cv=2, a_sb=2, b_sb=2, B_f=2, kg=2, vtok=2, bpos=2,
                     qdec=3, aqk=3, kdec=3, negbt=3, dl=3, dh=3, ybu=2, ywT=2)
        SHAPES = dict(qn_f=(F32, (128, 512)), kn_f=(F32, (128, 512)), qn_bf=(BF16, (128, 512)),
                      kT_bf=(BF16, (128, 512)), cv=(F32, (64, 512)), a_sb=(F32, (128, 4)), b_sb=(F32, (128, 4)),
                      B_f=(F32, (128, 512)), kg=(BF16, (128, 4, 128)), vtok=(BF16, (128, 4, 64)),
                      bpos=(F32, (128, 4)), qdec=(BF16, (128, 512)), aqk=(BF16, (128, 512)),
                      kdec=(BF16, (128, 4, 128)), negbt=(F32, (128, 4)), dl=(F32, (128, 4)), dh=(F32, (128, 4)),
                      ybu=(F32, (128, 4, 64)), ywT=(BF16, (128, 512)))
        ROT = {n: [cx.sb(list(SHAPES[n][1]), SHAPES[n][0]) for _ in range(DEPTH[n])] for n in DEPTH}

        def mkA(s, lst):
            def K(k):
                return (k, s % DEPTH[k]) if (isinstance(k, str) and k in DEPTH) else k

            def A_(eng, fn, reads=(), writes=(), slot=None):
                lst.append(((eng, fn), dict(reads=[K(k) for k in reads], writes=[K(k) for k in writes], slot=slot)))
            return A_

        def rb(s, n):
            return ROT[n][s % DEPTH[n]]

        def stage1(i, A):
            b = i % 2
            qn_f, kn_f, qn_bf, kT_bf, cv, a_sb, b_sb = [rb(i, n) for n in
                                                        ('qn_f', 'kn_f', 'qn_bf', 'kT_bf', 'cv', 'a_sb', 'b_sb')]
            A('sp', lambda e: e.dma_start(out=ht[b][:, :, :], in_=hv[:, :, i * 512:(i + 1) * 512]),
              writes=[('ght', b)], slot=('ght', b))
            for (w_, M, bank, raw, key) in ((wq, 128, 0, rq, 'rq'), (wk, 128, 1, rk, 'rk'), (wv, 64, 0, rv, 'rv')):
                def mm(e, w_=w_, M=M, bank=bank):
                    r = None
                    for k in range(8):
                        r = e.matmul(psb[bank][0:M, :], w_[:, k, :], ht[b][:, k, :], start=(k == 0), stop=(k == 7))
                    return r
                A('pe', mm, reads=[('ght', b), 'gwq', 'gwk', 'gwv'], writes=[PB(bank)])
                A('dve', lambda e, M=M, bank=bank, raw=raw: e.tensor_copy(raw[0:M, 3:515], psb[bank][0:M, :]),
                  reads=[PB(bank)], writes=[key])

            def mmab(e):
                r = None
                for sub in range(4):
                    for k in range(8):
                        r = e.matmul(psb[1][:, sub * 2:sub * 2 + 2], ht[b][:, k, sub * 128:(sub + 1) * 128],
                                     wab[:, k, :], start=(k == 0), stop=(k == 7))
                return r
            A('pe', mmab, reads=[('ght', b), 'gwab'], writes=[PB(1)])
            p3 = psb[1][:, 0:8].rearrange("p (s c) -> p s c", c=2)
            A('dve', lambda e: e.tensor_copy(a_sb[:, :], p3[:, :, 0]), reads=[PB(1)], writes=['a_sb'])
            A('dve', lambda e: e.tensor_copy(b_sb[:, :], p3[:, :, 1]), reads=[PB(1)], writes=['b_sb'])
            for which, (raw, cv_, M, key, ckey) in enumerate(((rq, cq, 128, 'rq', 'cq'), (rk, ck, 128, 'rk', 'ck'),
                                                              (rv, cv, 64, 'rv', 'cv'))):
                A('act', lambda e, raw=raw, cv_=cv_, M=M, which=which: e.activation(
                    cv_[0:M, :], raw[0:M, 0:512], AF.Copy, scale=cw_sb[0:M, which * 4:which * 4 + 1]),
                    reads=[key, 'cw'], writes=[ckey])
                for tap in range(1, 4):
                    A('dve', lambda e, raw=raw, cv_=cv_, M=M, which=which, tap=tap: e.scalar_tensor_tensor(
                        cv_[0:M, :], raw[0:M, tap:tap + 512], cw_sb[0:M, which * 4 + tap:which * 4 + tap + 1],
                        cv_[0:M, :], ALU.mult, ALU.add),
                        reads=[key, 'cw', ckey], writes=[ckey])
                A('pool', lambda e, raw=raw, M=M: e.tensor_copy(raw[0:M, 0:3], raw[0:M, 512:515]),
                  reads=[key, ckey], writes=[key])
                A('act', lambda e, cv_=cv_, M=M: e.activation(cv_[0:M, :], cv_[0:M, :], AF.Silu),
                  reads=[ckey], writes=[ckey])
            for (cv_, ckey, bank, outf, okey, mul) in ((cq, 'cq', 0, qn_f, 'qn_f', 128.0 ** -0.5),
                                                      (ck, 'ck', 1, kn_f, 'kn_f', 1.0)):
                sq_, rn_ = (sq2, rn) if bank == 0 else (sq2b, rnb)
                A('act', lambda e, cv_=cv_, sq_=sq_: e.activation(sq_[:, :], cv_[:, :], AF.Square), reads=[ckey],
                  writes=[('sq2', bank)])
                A('pe', lambda e, bank=bank, sq_=sq_: e.matmul(psb[bank][:, :], c['ones_bf'][:, :], sq_[:, :],
                                                               start=True, stop=True),
                  reads=[('sq2', bank), 'c_ones_bf'], writes=[PB(bank)])
                A('act', lambda e, bank=bank, rn_=rn_: e.activation(rn_[:, :], psb[bank][:, :], AF.Ln, bias=EPS,
                                                                    scale=1.0),
                  reads=[PB(bank)], writes=[('rn', bank)])
                A('act', lambda e, rn_=rn_: e.activation(rn_[:, :], rn_[:, :], AF.Exp, scale=-0.5),
                  reads=[('rn', bank)], writes=[('rn', bank)])
                A('dve', lambda e, cv_=cv_, outf=outf, mul=mul, rn_=rn_: e.scalar_tensor_tensor(
                    outf[:, :], cv_[:, :], mul, rn_[:, :], ALU.mult, ALU.mult), reads=[ckey, ('rn', bank)],
                    writes=[okey])
            A('act', lambda e: e.activation(qn_bf[:, :], qn_f[:, :], AF.Copy), reads=['qn_f'], writes=['qn_bf'])
            A('act', lambda e: e.activation(kT_bf[:, :], kn_f[:, :], AF.Copy), reads=['kn_f'], writes=['kT_bf'])

        def stage2(i, A):
            qn_f, kn_f, qn_bf, kT_bf, cv, a_sb, b_sb = [rb(i, n) for n in
                                                        ('qn_f', 'kn_f', 'qn_bf', 'kT_bf', 'cv', 'a_sb', 'b_sb')]
            B_f, kg, vtok, bpos = [rb(i, n) for n in ('B_f', 'kg', 'vtok', 'bpos')]
            qdec, aqk, kdec, negbt, dl, dh = [rb(i, n) for n in ('qdec', 'aqk', 'kdec', 'negbt', 'dl', 'dh')]
            A('act', lambda e: e.activation(g_tok[:, :], a_sb[:, :], AF.Exp, bias=gp_sb[:, 1:2], scale=1.0),
              reads=['a_sb', 'gp'], writes=['g_tok'])
            A('act', lambda e: e.activation(g_tok[:, :], g_tok[:, :], AF.Ln, bias=1.0, scale=1.0),
              reads=['g_tok'], writes=['g_tok'])
            A('dve', lambda e: e.tensor_scalar(g_tok[:, :], g_tok[:, :], negA[:, 0:1], None, ALU.mult),
              reads=['g_tok', 'negA'], writes=['g_tok'])
            A('act', lambda e: e.activation(bpos[:, :], b_sb[:, :], AF.Exp, scale=-1.0), reads=['b_sb'],
              writes=['bpos'])
            A('dve', lambda e: e.tensor_scalar(bpos[:, :], bpos[:, :], 1.0, None, ALU.add), reads=['bpos'],
              writes=['bpos'])
            A('dve', lambda e: e.reciprocal(bpos[:, :], bpos[:, :]), reads=['bpos'], writes=['bpos'])
            A('dve', lambda e: e.tensor_scalar(negbt[:, :], bpos[:, :], -1.0, None, ALU.mult), reads=['bpos'],
              writes=['negbt'])
            A('pe', lambda e: e.matmul(psb[2][:, 0:4], M_i[:, 0, :], g_tok[:, :], start=True, stop=True),
              reads=['g_tok', 'M_i'], writes=[PB(2)])
            A('dve', lambda e: e.tensor_copy(G_tok[:, :], psb[2][:, 0:4]), reads=[PB(2)], writes=['G_tok'])
            A('act', lambda e: e.activation(eG[:, :], G_tok[:, :], AF.Exp), reads=['G_tok'], writes=['eG'])
            A('pe', lambda e: e.matmul(psb[2][:, 0:4], EL[:, :], G_tok[:, :], start=True, stop=True),
              reads=['G_tok', 'EL'], writes=[PB(2)])
            A('dve', lambda e: e.tensor_tensor(glo[:, :], psb[2][:, 0:4], G_tok[:, :], ALU.subtract),
              reads=[PB(2), 'G_tok'], writes=['glo'])
            A('act', lambda e: e.activation(ekd[:, :], glo[:, :], AF.Exp), reads=['glo'], writes=['ekd'])
            A('pe', lambda e: e.matmul(psb[2][:, 0:4], E63[:, :], G_tok[:, :], start=True, stop=True),
              reads=['G_tok', 'E63'], writes=[PB(2)])
            A('dve', lambda e: e.tensor_copy(dl[:, :], psb[2][:, 0:4]), reads=[PB(2)], writes=['dl'])
            A('act', lambda e: e.activation(dl[:, :], dl[:, :], AF.Exp), reads=['dl'], writes=['dl'])
            A('pe', lambda e: e.matmul(psb[2][:, 0:4], E127[:, :], G_tok[:, :], start=True, stop=True),
              reads=['G_tok', 'E127'], writes=[PB(2)])
            A('dve', lambda e: e.tensor_copy(dh[:, :], psb[2][:, 0:4]), reads=[PB(2)], writes=['dh'])
            A('act', lambda e: e.activation(dh[:, :], dh[:, :], AF.Exp), reads=['dh'], writes=['dh'])
            for sub in range(4):
                A('dve', lambda e, sub=sub: e.tensor_scalar(diagG[:, sub * 128:(sub + 1) * 128], ident[:, :],
                                                            G_tok[:, sub:sub + 1], None, ALU.mult),
                  reads=['G_tok', 'c_ident'], writes=[('diagG', sub)])
                A('pe', lambda e, sub=sub: e.matmul(psb[3][:, sub * 128:(sub + 1) * 128], ones[:, :],
                                                    diagG[:, sub * 128:(sub + 1) * 128], start=True, stop=True),
                  reads=[('diagG', sub), 'c_ones'], writes=[PB(3)])
            A('act', lambda e: e.activation(EGrow[:, :], psb[3][:, :], AF.Exp), reads=[PB(3)], writes=['EGrow'])
            A('pool', lambda e: e.tensor_tensor(qdec[:, :], qn_f[:, :], EGrow[:, :], ALU.mult),
              reads=['qn_f', 'EGrow'], writes=['qdec'])
            for sub in range(4):
                A('dve', lambda e, sub=sub: e.tensor_scalar(Dm[:, sub * 128:(sub + 1) * 128],
                                                            psb[3][:, sub * 128:(sub + 1) * 128],
                                                            G_tok[:, sub:sub + 1], 0.0, ALU.subtract, ALU.min),
                  reads=[PB(3), 'G_tok'], writes=['Dm'])
            A('act', lambda e: e.activation(Gam[:, :], Dm[:, :], AF.Exp), reads=['Dm'], writes=['Gam'])
            A('pool', lambda e: e.tensor_tensor(Gs[:, :], Gam[:, :], M_s[:, :, :].rearrange("p s c -> p (s c)"),
                                                ALU.mult), reads=['Gam', 'M_s'], writes=['Gs'])
            A('pool', lambda e: e.tensor_tensor(Gi[:, :], Gam[:, :], M_i[:, :, :].rearrange("p s c -> p (s c)"),
                                                ALU.mult), reads=['Gam', 'M_i'], writes=['Gi'])
            for sub in range(4):
                A('pe', lambda e, sub=sub: e.transpose(psb[2][:, sub * 128:(sub + 1) * 128],
                                                       kn_f[:, sub * 128:(sub + 1) * 128], ident[:, :]),
                  reads=['kn_f', 'c_ident'], writes=[PB(2)])
            A('dve', lambda e: e.tensor_copy(ktok[:, :, :], psb[2][:, :].rearrange("p (s c) -> p s c", c=128)),
              reads=[PB(2)], writes=['ktok'])
            for sub in range(4):
                A('act', lambda e, sub=sub: e.activation(kg[:, sub, :], ktok[:, sub, :], AF.Copy,
                                                         scale=eG[:, sub:sub + 1]), reads=['ktok', 'eG'], writes=['kg'])
                A('act', lambda e, sub=sub: e.activation(kdec[:, sub, :], ktok[:, sub, :], AF.Copy,
                                                         scale=ekd[:, sub:sub + 1]), reads=['ktok', 'ekd'],
                  writes=['kdec'])
            for sub in range(4):
                A('pe', lambda e, sub=sub: e.transpose(psb[2][:, sub * 64:(sub + 1) * 64],
                                                       cv[0:64, sub * 128:(sub + 1) * 128], ident[0:64, 0:64]),
                  reads=['cv', 'c_ident'], writes=[PB(2)])
            A('dve', lambda e: e.tensor_copy(vtok[:, :, :], psb[2][:, 0:256].rearrange("p (s c) -> p s c", c=64)),
              reads=[PB(2)], writes=['vtok'])
            for sub in range(4):
                cs = slice(sub * 128, (sub + 1) * 128)
                A('pe', lambda e, cs=cs: e.matmul(psb[2][:, cs], kT_bf[:, cs], kT_bf[:, cs], start=True, stop=True),
                  reads=['kT_bf'], writes=[PB(2)])
                A('dve', lambda e, cs=cs, sub=sub: e.scalar_tensor_tensor(
                    B_f[:, cs], psb[2][:, cs], negbt[:, sub:sub + 1], Gs[:, cs], ALU.mult, ALU.mult),
                    reads=[PB(2), 'negbt', 'Gs'], writes=['B_f'])
            for sub in range(4):
                cs = slice(sub * 128, (sub + 1) * 128)
                A('pe', lambda e, cs=cs: e.matmul(psb[3][:, cs], kT_bf[:, cs], qn_bf[:, cs], start=True, stop=True),
                  reads=['kT_bf', 'qn_bf'], writes=[PB(3)])
            A('dve', lambda e: e.tensor_tensor(aqk[:, :], psb[3][:, :], Gi[:, :], ALU.mult), reads=[PB(3), 'Gi'],
              writes=['aqk'])

        def stage3(i, A):
            B_f, kg, vtok, bpos = [rb(i, n) for n in ('B_f', 'kg', 'vtok', 'bpos')]
            ybu, ywT = rb(i, 'ybu'), rb(i, 'ywT')
            A('act', lambda e: e.activation(Bb[0][:, :], B_f[:, :], AF.Copy), reads=['B_f'], writes=[('Bb', 0)])
            for sub in range(4):
                cs = slice(sub * 128, (sub + 1) * 128)
                A('pe', lambda e, cs=cs: e.transpose(psb[4][:, cs], B_f[:, cs], ident[:, :]),
                  reads=['B_f', 'c_ident'], writes=[PB(4)])
            A('dve', lambda e: e.tensor_copy(Pb_[0][:, :], psb[4][:, :]), reads=[PB(4)], writes=[('Pb', 0)])
            for sub in range(4):
                cs = slice(sub * 128, (sub + 1) * 128)
                A('pool', lambda e, cs=cs: e.tensor_tensor(S_f[:, cs], B_f[:, cs], ident[:, :], ALU.add),
                  reads=['B_f', 'c_ident'], writes=['S_f'])
            A('act', lambda e: e.activation(S_b[:, :], S_f[:, :], AF.Copy), reads=['S_f'], writes=['S_b'])
            for j in range(5):
                cur, nxt = j % 2, (j + 1) % 2
                for sub in range(4):
                    cs = slice(sub * 128, (sub + 1) * 128)
                    A('pe', lambda e, cs=cs, cur=cur: e.matmul(psb[5][:, cs], Pb_[cur][:, cs], Bb[cur][:, cs],
                                                               start=True, stop=True),
                      reads=[('Pb', cur), ('Bb', cur)], writes=[PB(5)])
                A('dve', lambda e, nxt=nxt: e.tensor_copy(Bb[nxt][:, :], psb[5][:, :]), reads=[PB(5)],
                  writes=[('Bb', nxt)])
                for sub in range(4):
                    cs = slice(sub * 128, (sub + 1) * 128)
                    A('pe', lambda e, cs=cs, cur=cur: e.matmul(psb[4][:, cs], Bb[cur][:, cs], Pb_[cur][:, cs],
                                                               start=True, stop=True),
                      reads=[('Pb', cur), ('Bb', cur)], writes=[PB(4)])
                A('act', lambda e, nxt=nxt: e.activation(Pb_[nxt][:, :], psb[4][:, :], AF.Copy), reads=[PB(4)],
                  writes=[('Pb', nxt)])
                for sub in range(4):
                    cs = slice(sub * 128, (sub + 1) * 128)
                    A('pe', lambda e, cs=cs, nxt=nxt: e.matmul(psb[5][:, cs], Pb_[nxt][:, cs], S_b[:, cs],
                                                               start=True, stop=True),
                      reads=[('Pb', nxt), 'S_b'], writes=[PB(5)])
                A('dve', lambda e: e.tensor_tensor(S_f[:, :], S_f[:, :], psb[5][:, :], ALU.add),
                  reads=['S_f', PB(5)], writes=['S_f'])
                A('act', lambda e: e.activation(S_b[:, :], S_f[:, :], AF.Copy), reads=['S_f'], writes=['S_b'])
            for sub in range(4):
                cs = slice(sub * 128, (sub + 1) * 128)
                A('pe', lambda e, cs=cs, sub=sub: e.matmul(psb[4][:, sub * 64:(sub + 1) * 64], S_b[:, cs],
                                                           vtok[:, sub, :], start=True, stop=True),
                  reads=['S_b', 'vtok'], writes=[PB(4)])
            for sub in range(4):
                A('dve', lambda e, sub=sub: e.tensor_scalar(ybu[:, sub, :], psb[4][:, sub * 64:(sub + 1) * 64],
                                                            bpos[:, sub:sub + 1], None, ALU.mult),
                  reads=[PB(4), 'bpos'], writes=['ybu'])
            for sub in range(4):
                cs = slice(sub * 128, (sub + 1) * 128)
                A('pe', lambda e, cs=cs, sub=sub: e.matmul(psb[5][:, cs], kg[:, sub, :], S_b[:, cs], start=True,
                                                           stop=True),
                  reads=['S_b', 'kg'], writes=[PB(5)])
            A('dve', lambda e: e.tensor_copy(ywT[:, :], psb[5][:, :]), reads=[PB(5)], writes=['ywT'])

        def stage4(i, A):
            qdec, aqk, kdec, negbt, dl, dh = [rb(i, n) for n in ('qdec', 'aqk', 'kdec', 'negbt', 'dl', 'dh')]
            ybu, ywT = rb(i, 'ybu'), rb(i, 'ywT')
            for ch in range(8):
                sub, hf = ch // 2, ch % 2
                rs = slice(hf * 64, hf * 64 + 64)
                cs = slice(sub * 128, (sub + 1) * 128)
                cc = slice(ch * 64, (ch + 1) * 64)
                A('pe', lambda e, cs=cs: e.matmul(psb[6][:, 0:64], ywT[:, cs], St_b[:, :], start=True, stop=True),
                  reads=['ywT', 'St_b'], writes=[PB(6)])
                A('dve', lambda e, rs=rs, sub=sub: e.scalar_tensor_tensor(
                    vnew[rs, :], psb[6][rs, 0:64], negbt[rs, sub:sub + 1], ybu[rs, sub, :], ALU.mult, ALU.add),
                    reads=[PB(6), 'negbt', 'ybu'], writes=['vnew'])

                def mmo(e, cc=cc, rs=rs):
                    e.matmul(psb[7][0:64, cc], St_b[:, :], qdec[:, cc], start=True, stop=False)
                    return e.matmul(psb[7][0:64, cc], vnew[rs, :], aqk[rs, cc], start=False, stop=True)
                A('pe', mmo, reads=['St_b', 'qdec', 'vnew', 'aqk'], writes=[PB(7)])
                A('pe', lambda e, rs=rs, sub=sub: e.matmul(psb[6][:, 64:128], kdec[rs, sub, :], vnew[rs, :],
                                                           start=True, stop=True),
                  reads=['kdec', 'vnew'], writes=[PB(6)])
                dsc = (dl if hf == 0 else dh)
                A('dve', lambda e, sub=sub, dsc=dsc: e.scalar_tensor_tensor(
                    St_b[:, :], St_f[:, :], dsc[:, sub:sub + 1], psb[6][:, 64:128], ALU.mult, ALU.add),
                    reads=['St_f', PB(6), 'dl', 'dh'], writes=['St_b'])
                A('dve', lambda e, sub=sub, dsc=dsc: e.scalar_tensor_tensor(
                    St_f[:, :], St_f[:, :], dsc[:, sub:sub + 1], psb[6][:, 64:128], ALU.mult, ALU.add),
                    reads=['St_f', PB(6), 'dl', 'dh'], writes=['St_f'])
            A('act', lambda e: e.activation(o_sb[:, :], psb[7][0:64, :], AF.Copy), reads=[PB(7)], writes=['o_sb'])
            A('sp', lambda e: e.dma_start(out=obT[:, i * 512:(i + 1) * 512], in_=o_sb[:, :]),
              reads=['o_sb'], writes=[('obT', i)], slot='ob')

        def merge_lists(lists):
            lists = [l for l in lists if l]
            pos = [0] * len(lists)
            out = []
            while True:
                best, bf = None, None
                for li, l in enumerate(lists):
                    if pos[li] < len(l):
                        fr = pos[li] / len(l)
                        if bf is None or fr < bf:
                            best, bf = li, fr
                if best is None:
                    break
                out.append(lists[best][pos[best]])
                pos[best] += 1
            return out

        stages = (stage1, stage2, stage3, stage4)
        for t in range(NSEG + 3):
            lists = []
            for si, st_ in enumerate(stages):
                s = t - si
                if 0 <= s < NSEG:
                    lst = []
                    st_(s, mkA(s, lst))
                    lists.append(lst)
            for (a_, k_) in merge_lists(lists[::-1]):
                sc.add(*a_, **k_)
        sc.flush()


def build_M(S, do_fox=True, do_gdn=True, stop=99):
    nc = bass.Bass("TRN2", target_bir_lowering=False)
    hT = nc.dram_tensor("hT", [D, S], BF16, kind="ExternalInput").ap()
    wf = nc.dram_tensor("wf", [D, 193], F32, kind="ExternalInput").ap()
    bfg = nc.dram_tensor("bfg", [128, 1], F32, kind="ExternalInput").ap()
    wg = nc.dram_tensor("wg", [D, 322], F32, kind="ExternalInput").ap()
    cw = nc.dram_tensor("cw", [128, 12], F32, kind="ExternalInput").ap()
    gpar = nc.dram_tensor("gpar", [128, 2], F32, kind="ExternalInput").ap()
    oaT = nc.dram_tensor("oaT", [64, S], F32, kind="ExternalOutput").ap()
    obT = nc.dram_tensor("obT", [64, S], F32, kind="ExternalOutput").ap()
    scr = nc.dram_tensor("scr", [128, 128], BF16).ap()
    cx = Ctx(nc)
    with cx.st:
        sc = Sched(nc)
        c = make_consts(sc, cx)
        psb = [cx.ps([128, 512], F32) for _ in range(8)]
        if do_gdn:
            gdn_phase(nc, sc, c, S, hT, wg, cw, gpar, obT, psb)
        if do_fox:
            fox_phase(nc, sc, cx, c, S, hT, wf, bfg, oaT, scr, psb, stop=stop)
        sc.close()
    return nc


def build_T(TS, last):
    nc = bass.Bass("TRN2", target_bir_lowering=False)
    TT = 256
    NTT = TS // TT
    xT = nc.dram_tensor("xT", [D, TS], F32, kind="ExternalInput").ap()
    hT = nc.dram_tensor("hT", [D, TS], BF16, kind="ExternalInput").ap()
    oaT = nc.dram_tensor("oaT", [512, TS], F32, kind="ExternalInput").ap()
    obT = nc.dram_tensor("obT", [512, TS], F32, kind="ExternalInput").ap()
    w_in = nc.dram_tensor("w_in", [D, N_IN], F32, kind="ExternalInput").ap()
    w_br = nc.dram_tensor("w_br", [1536, D], F32, kind="ExternalInput").ap()
    w_out = nc.dram_tensor("w_out", [D, D], F32, kind="ExternalInput").ap()
    w_kv = nc.dram_tensor("w_kv", [D, 1024], F32, kind="ExternalInput").ap()
    memT = nc.dram_tensor("memT", [D, 256], F32, kind="ExternalInput").ap()
    mem_g = nc.dram_tensor("mem_g", [128, 8], F32, kind="ExternalInput").ap()
    b_mg = nc.dram_tensor("b_mg", [128, 24], F32, kind="ExternalInput").ap()
    gdn_g = nc.dram_tensor("gdn_g", [128, 1], F32, kind="ExternalInput").ap()
    next_g = nc.dram_tensor("next_g", [128, 8], F32, kind="ExternalInput").ap()
    xoT = nc.dram_tensor("xoT", [D, TS], F32, kind="ExternalOutput").ap()
    if not last:
        hoT = nc.dram_tensor("hoT", [D, TS], BF16, kind="ExternalOutput").ap()
    cx = Ctx(nc)
    with cx.st:
        sc = Sched(nc)
        ones_f = cx.sb([128, 128], F32)
        ones_bf = cx.sb([128, 128], BF16)
        sc.add('pool', lambda e: e.memset(ones_f[:, :], 1.0), writes=['ones_f'])
        sc.add('pool', lambda e: e.memset(ones_bf[:, :], 1.0), writes=['ones_bf'])
        psb = [cx.ps([128, 512], F32) for _ in range(8)]

        BM = {(0, 0): 0, (0, 1): 1, (1, 0): 2, (1, 1): 2, (2, 0): 3, (2, 1): 4, (3, 0): 5, (3, 1): 6,
              (4, 0): 2, (4, 1): 3, (5, 0): 4, (5, 1): 5, (6, 0): 6, (6, 1): 7, (7, 0): 0, (7, 1): 1}

        def half(bk, h):
            return psb[BM[(bk, h)]][:, 0:TT]

        def hk(bk, h):
            return ('pb', BM[(bk, h)])
        Wz = cx.sb([128, 8, 5120], BF16)
        Wbr = cx.sb([128, 12, 1024], BF16)
        Wout = cx.sb([128, 8, 1024], BF16)
        mkT = cx.sb([128, 4, 256], BF16)
        mv = cx.sb([128, 2, 512], BF16)
        memg_sb = cx.sb([128, 8], F32)
        bm_sb = cx.sb([128, 24], F32)
        gg_sb = cx.sb([128, 1], F32)
        ng_sb = cx.sb([128, 8], F32)
        for i, (dst, srcap) in enumerate([(memg_sb, mem_g), (bm_sb, b_mg), (gg_sb, gdn_g), (ng_sb, next_g)]):
            sc.add('sp', lambda e, dst=dst, srcap=srcap: e.dma_start(out=dst[:, :], in_=srcap[:, :]),
                   writes=[('par', i)], slot=('par', i))
        def ldw(dst, dkey, srcap):
            sc.add('pool', lambda e: e.dma_start(out=dst, in_=srcap), writes=[dkey], slot=('w', dkey[0], dkey[1]))

        pcx = Ctx(nc)
        with pcx.st:
            Wkv = pcx.sb([128, 8, 1024], BF16)
            mt = pcx.sb([128, 8, 256], F32)
            mn = pcx.sb([128, 8, 256], BF16)
            sqm = pcx.sb([128, 8, 256], BF16)
            rstm = pcx.sb([128, 256], F32)
            for k in range(8):
                ldw(Wkv[:, k, :], ('Wkv', '', k), w_kv[k * 128:(k + 1) * 128, :])
            sc.add('sp', lambda e: e.dma_start(out=mt[:, :, :], in_=memT.rearrange("(k p) m -> p k m", p=128)),
                   writes=['mt'], slot='mt')
            sc.ops[-1]
            saved = {'g': None}
            emit_rmsnorm(sc, mt, 'mt', memg_sb, ones_bf, mn, 'mn', 256, sqm, psb[0], rstm, 'mnorm', gkey=('par', 0),
                         oneskey='ones_bf')
            for hh in range(4):
                def mmk(e, hh=hh):
                    r = None
                    for k in range(8):
                        r = e.matmul(half(1, hh % 2), Wkv[:, k, hh * 128:(hh + 1) * 128], mn[:, k, :],
                                     start=(k == 0), stop=(k == 7))
                    return r
                sc.add('pe', mmk, reads=[('Wkv', '', k) for k in range(8)] + ['mn'], writes=[hk(1, hh % 2)])
                sc.add('dve', lambda e, hh=hh: e.tensor_copy(mkT[:, hh, :], half(1, hh % 2)),
                       reads=[hk(1, hh % 2)], writes=['mkT'])
            for mc in range(2):
                def mmv(e, mc=mc):
                    r = None
                    for k in range(8):
                        r = e.matmul(psb[2 + mc][:, :], mn[:, k, mc * 128:(mc + 1) * 128], Wkv[:, k, 512:1024],
                                     start=(k == 0), stop=(k == 7))
                    return r
                sc.add('pe', mmv, reads=[('Wkv', '', k) for k in range(8)] + ['mn'], writes=[('pb', 2 + mc)])
                sc.add('dve', lambda e, mc=mc: e.tensor_copy(mv[:, mc, :], psb[2 + mc][:, :]),
                       reads=[('pb', 2 + mc)], writes=['mv'])
            sc.flush()

        def wzname(col0):
            if col0 < 512:
                return 'az'
            if col0 < 1024:
                return 'bz'
            if col0 < 2048:
                return 'mqz'
            return 'g%d' % ((col0 - 2048) // 1024)

        def wzkey(col0):
            return [('Wz', wzname(col0), k) for k in range(8)]
        late = []

        def ldw_late(*a):
            late.append(a)
        for k in range(8):
            ldw(Wz[:, k, 1024:2048], ('Wz', 'mqz', k), w_in[k * 128:(k + 1) * 128, OFF['mq']:OFF['mq'] + 1024])
        for k in range(8):
            ldw(Wz[:, k, 0:512], ('Wz', 'az', k), w_in[k * 128:(k + 1) * 128, OFF['az']:OFF['az'] + 512])
            ldw(Wz[:, k, 512:1024], ('Wz', 'bz', k), w_in[k * 128:(k + 1) * 128, OFF['bz']:OFF['bz'] + 512])
        for cb in range(1, 4):
            for k in range(8):
                ldw_late(Wz[:, k, 1024 + cb * 1024:2048 + cb * 1024], ('Wz', 'g%d' % (cb - 1), k),
                         w_in[k * 128:(k + 1) * 128, OFF['mq'] + cb * 1024:OFF['mq'] + (cb + 1) * 1024])
            for k in range(4 * (cb - 1), 4 * cb):
                ldw_late(Wbr[:, k, :], ('Wbr', cb - 1, k), w_br[k * 128:(k + 1) * 128, :])
        for k in range(8):
            ldw_late(Wout[:, k, :], ('Wout', '', k), w_out[k * 128:(k + 1) * 128, :])

        ht = [cx.sb([128, 8, TT], BF16) for _ in range(2)]
        xt = cx.sb([128, 8, TT], F32)
        oat = cx.sb([128, 4, TT], F32)
        obt = cx.sb([128, 4, TT], F32)
        yT = cx.sb([128, 12, TT], BF16)
        mg = cx.sb([128, 8, TT], BF16)
        hout = cx.sb([128, 8, TT], BF16 if not last else F32)
        sqs = [cx.sb([128, TT], BF16) for _ in range(2)]
        sil = [cx.sb([128, TT], F32) for _ in range(2)]
        tmpB2 = [cx.sb([128, 2 * TT], F32) for _ in range(2)]
        sqs4 = [cx.sb([128, TT], BF16) for _ in range(4)]
        tmpM = [cx.sb([128, TT], F32) for _ in range(4)]
        rstd = cx.sb([128, TT], F32)
        rden = cx.sb([128, TT], F32)
        mqs = cx.sb([128, TT], BF16)
        pT = [cx.sb([128, TT], BF16) for _ in range(2)]
        gs = [cx.sb([128, TT], F32) for _ in range(3)]
        acc = [cx.sb([128, TT], F32) for _ in range(2)]
        hv = hT.rearrange("(k p) t -> p k t", p=128)
        xv = xT.rearrange("(k p) t -> p k t", p=128)
        oav = oaT.rearrange("(k p) t -> p k t", p=128)
        obv = obT.rearrange("(k p) t -> p k t", p=128)
        xov = xoT.rearrange("(k p) t -> p k t", p=128)
        if not last:
            hov = hoT.rearrange("(k p) t -> p k t", p=128)

        zcnt = [0]

        def zproj(col0, b):
            s = zcnt[0] % 2
            zcnt[0] += 1
            dst = half(0, s)

            def mm(e):
                r = None
                for k in range(8):
                    r = e.matmul(dst, Wz[:, k, col0:col0 + 128], ht[b][:, k, :], start=(k == 0), stop=(k == 7))
                return r
            sc.add('pe', mm, reads=wzkey(col0) + [('ht', b)], writes=[hk(0, s)])
            return dst, hk(0, s)

        def load_ht(it_):
            b_ = it_ % 2
            sc.add('sp', lambda e: e.dma_start(out=ht[b_][:, :, :], in_=hv[:, :, it_ * TT:(it_ + 1) * TT]),
                   writes=[('ht', b_)], slot=('ht', b_))

        def load_o(it_):
            sc.add('sp', lambda e: e.dma_start(out=oat[:, :, :], in_=oav[:, :, it_ * TT:(it_ + 1) * TT]),
                   writes=['oat'], slot='oat')
            sc.add('sp', lambda e: e.dma_start(out=obt[:, :, :], in_=obv[:, :, it_ * TT:(it_ + 1) * TT]),
                   writes=['obt'], slot='obt')

        def phase12(it):
            b = it % 2
            t0, t1 = it * TT, (it + 1) * TT
            for hd in range(4):
                sc.add('act', lambda e, hd=hd: e.activation(sqs4[hd][:, :], obt[:, hd, :], AF.Square),
                       reads=['obt'], writes=[('sqs4', hd)])
            for pr in range(2):
                bk = 7 if pr == 0 else 6
                for h2 in range(2):
                    hd = pr * 2 + h2
                    sc.add('pe', lambda e, hd=hd, h2=h2, bk=bk: e.matmul(psb[bk][:, h2 * TT:(h2 + 1) * TT],
                                                                        ones_bf[:, :], sqs4[hd][:, :], start=True,
                                                                        stop=True),
                           reads=[('sqs4', hd), 'ones_bf'], writes=[('pb', bk)])
                sc.add('act', lambda e, pr=pr, bk=bk: e.activation(tmpB2[pr][:, :], psb[bk][:, 0:2 * TT], AF.Ln,
                                                                   bias=EPS, scale=1.0 / 128),
                       reads=[('pb', bk)], writes=[('tmpB2', pr), ('tmpBh', 2 * pr), ('tmpBh', 2 * pr + 1)])
                sc.add('act', lambda e, pr=pr: e.activation(tmpB2[pr][:, :], tmpB2[pr][:, :], AF.Exp, scale=-0.5),
                       reads=[('tmpB2', pr)], writes=[('tmpB2', pr)])
            for hd in range(4):
                pr, h2 = hd // 2, hd % 2
                sc.add('dve', lambda e, hd=hd, pr=pr, h2=h2: e.scalar_tensor_tensor(
                    tmpB2[pr][:, h2 * TT:(h2 + 1) * TT], obt[:, hd, :], gg_sb[:, 0:1],
                    tmpB2[pr][:, h2 * TT:(h2 + 1) * TT], ALU.mult, ALU.mult),
                    reads=['obt', ('tmpB2', pr), ('par', 2)], writes=[('tmpBh', hd)])
            for hh in range(4):
                zp, zk = zproj(1024 + hh * 128, b)
                sc.add('dve', lambda e, zp=zp: e.tensor_copy(mqs[:, :], zp), reads=[zk], writes=['mqs'])
                for mc in range(2):
                    sc.add('pe', lambda e, hh=hh, mc=mc: e.matmul(half(2, mc), mkT[:, hh, mc * 128:(mc + 1) * 128],
                                                                 mqs[:, :], start=True, stop=True),
                           reads=['mkT', 'mqs'], writes=[hk(2, mc)])
                    sc.add('act', lambda e, mc=mc: e.activation(pT[mc][:, :], half(2, mc), AF.Exp,
                                                                scale=128.0 ** -0.5),
                           reads=[hk(2, mc)], writes=[('pT', mc)])

                def mmn(e, hh=hh):
                    e.matmul(half(3, 0), mv[:, 0, hh * 128:(hh + 1) * 128], pT[0][:, :], start=True, stop=False)
                    return e.matmul(half(3, 0), mv[:, 1, hh * 128:(hh + 1) * 128], pT[1][:, :], start=False,
                                    stop=True)
                sc.add('pe', mmn, reads=['mv', ('pT', 0), ('pT', 1)], writes=[hk(3, 0)])

                def mmd(e):
                    e.matmul(half(3, 1), ones_bf[:, :], pT[0][:, :], start=True, stop=False)
                    return e.matmul(half(3, 1), ones_bf[:, :], pT[1][:, :], start=False, stop=True)
                sc.add('pe', mmd, reads=['ones_bf', ('pT', 0), ('pT', 1)], writes=[hk(3, 1)])
                sc.add('act', lambda e: e.activation(rden[:, :], half(3, 1), AF.Ln), reads=[hk(3, 1)],
                       writes=['rden'])
                sc.add('act', lambda e: e.activation(rden[:, :], rden[:, :], AF.Exp, scale=-1.0), reads=['rden'],
                       writes=['rden'])
                sc.add('dve', lambda e, hh=hh: e.tensor_tensor(tmpM[hh][:, :], half(3, 0), rden[:, :], ALU.mult),
                       reads=[hk(3, 0), 'rden'], writes=[('tmpM', hh)])
            for ci in range(12):
                s = ci % 2
                col0 = [0, 512, 1536][ci // 4] + (ci % 4) * 128
                zp, zk = zproj(col0, b)
                sc.add('act', lambda e, zp=zp, s=s: e.activation(sil[s][:, :], zp, AF.Silu),
                       reads=[zk], writes=[('sil', s)])
                if ci < 4:
                    srcb, skey = oat[:, ci, :], 'oat'
                elif ci < 8:
                    srcb = tmpB2[(ci - 4) // 2][:, ((ci - 4) % 2) * TT:((ci - 4) % 2 + 1) * TT]
                    skey = ('tmpBh', ci - 4)
                else:
                    srcb, skey = tmpM[ci - 8][:, :], ('tmpM', ci - 8)
                sc.add('pool', lambda e, ci=ci, s=s, srcb=srcb: e.tensor_tensor(yT[:, ci, :], srcb, sil[s][:, :],
                                                                               ALU.mult),
                       reads=[skey, ('sil', s)], writes=[('yT', ci)])

        def merge_out(it):
            b = it % 2
            t0, t1 = it * TT, (it + 1) * TT
            for dc in range(8):
                for n in range(3):
                    pslot = [(4, 0), (4, 1), (5, 0)][n]
                    gslot = [(5, 1), (6, 0), (6, 1)][n]

                    def mmp(e, n=n, dc=dc, pslot=pslot):
                        r = None
                        for kc in range(4):
                            r = e.matmul(half(*pslot), Wbr[:, n * 4 + kc, dc * 128:(dc + 1) * 128],
                                         yT[:, n * 4 + kc, :], start=(kc == 0), stop=(kc == 3))
                        return r
                    sc.add('pe', mmp, reads=[('Wbr', n, n * 4 + kc) for kc in range(4)] + [('yT', n * 4 + kc) for kc in range(4)],
                           writes=[hk(*pslot)])

                    def mmg(e, n=n, dc=dc, gslot=gslot, b=b):
                        r = None
                        for k in range(8):
                            c0 = 2048 + n * 1024 + dc * 128
                            r = e.matmul(half(*gslot), Wz[:, k, c0:c0 + 128], ht[b][:, k, :], start=(k == 0),
                                         stop=(k == 7))
                        return r
                    sc.add('pe', mmg, reads=[('Wz', 'g%d' % n, k) for k in range(8)] + [('ht', b)], writes=[hk(*gslot)])
                    sc.add('act', lambda e, n=n, dc=dc, gslot=gslot: e.activation(
                        gs[n][:, :], half(*gslot), AF.Sigmoid, bias=bm_sb[:, n * 8 + dc:n * 8 + dc + 1], scale=1.0),
                        reads=[hk(*gslot), ('par', 1)], writes=[('gs', n)])
                sc.add('dve', lambda e: e.tensor_tensor(acc[0][:, :], half(4, 0), gs[0][:, :], ALU.mult),
                       reads=[hk(4, 0), ('gs', 0)], writes=[('acc', 0)])
                sc.add('dve', lambda e: e.tensor_tensor(acc[1][:, :], half(4, 1), gs[1][:, :], ALU.mult),
                       reads=[hk(4, 1), ('gs', 1)], writes=[('acc', 1)])
                sc.add('pool', lambda e: e.tensor_tensor(acc[0][:, :], acc[0][:, :], acc[1][:, :], ALU.add),
                       reads=[('acc', 0), ('acc', 1)], writes=[('acc', 0)])
                sc.add('dve', lambda e: e.tensor_tensor(acc[1][:, :], half(5, 0), gs[2][:, :], ALU.mult),
                       reads=[hk(5, 0), ('gs', 2)], writes=[('acc', 1)])
                sc.add('pool', lambda e, dc=dc: e.tensor_tensor(mg[:, dc, :], acc[0][:, :], acc[1][:, :], ALU.add),
                       reads=[('acc', 0), ('acc', 1)], writes=[('mg', dc)])
            for dc in range(8):
                s = dc % 2

                def mmo(e, dc=dc, s=s):
                    r = None
                    for k in range(8):
                        r = e.matmul(half(7, s), Wout[:, k, dc * 128:(dc + 1) * 128], mg[:, k, :], start=(k == 0),
                                     stop=(k == 7))
                    return r
                sc.add('pe', mmo, reads=[('Wout', '', k) for k in range(8)] + [('mg', k) for k in range(8)], writes=[hk(7, s)])
                sc.add('dve', lambda e, dc=dc, s=s: e.tensor_tensor(xt[:, dc, :], xt[:, dc, :], half(7, s), ALU.add),
                       reads=['xt', hk(7, s)], writes=[('xn', dc)])
            allxn = [('xn', dc) for dc in range(8)]
            if not last:
                sc.add('sp', lambda e, t0=t0, t1=t1: e.dma_start(out=xov[:, :, t0:t1], in_=xt[:, :, :]),
                       reads=allxn, writes=[('xo', it)], slot='xo')

        def finalnorm(it):
            b = it % 2
            t0, t1 = it * TT, (it + 1) * TT
            for k in range(8):
                s = k % 2
                sc.add('act', lambda e, k=k, s=s: e.activation(sqs[s][:, :], xt[:, k, :], AF.Square),
                       reads=[('xn', k)], writes=[('sqs', s)])
                sc.add('pe', lambda e, k=k, s=s: e.matmul(half(1, 1), ones_bf[:, :], sqs[s][:, :], start=(k == 0),
                                                         stop=(k == 7)),
                       reads=[('sqs', s), 'ones_bf'], writes=[hk(1, 1)])
            sc.add('act', lambda e: e.activation(rstd[:, :], half(1, 1), AF.Ln, bias=EPS, scale=1.0 / D),
                   reads=[hk(1, 1)], writes=['rstd'])
            sc.add('act', lambda e: e.activation(rstd[:, :], rstd[:, :], AF.Exp, scale=-0.5), reads=['rstd'],
                   writes=['rstd'])
            for k in range(8):
                sc.add('dve', lambda e, k=k: e.scalar_tensor_tensor(hout[:, k, :], xt[:, k, :], ng_sb[:, k:k + 1],
                                                                   rstd[:, :], ALU.mult, ALU.mult),
                       reads=[('xn', k), 'rstd', ('par', 3)], writes=['hout'])
            if last:
                sc.add('sp', lambda e, t0=t0, t1=t1: e.dma_start(out=xov[:, :, t0:t1], in_=hout[:, :, :]),
                       reads=['hout'], writes=[('xo', it)], slot='xo')
            else:
                sc.add('sp', lambda e, t0=t0, t1=t1: e.dma_start(out=hov[:, :, t0:t1], in_=hout[:, :, :]),
                       reads=['hout'], writes=[('ho', it)], slot='ho')

        def load_x(it):
            t0, t1 = it * TT, (it + 1) * TT
            sc.add('sp', lambda e: e.dma_start(out=xt[:, :, :], in_=xv[:, :, t0:t1]),
                   writes=['xt'] + [('xn', dc) for dc in range(8)], slot='xt')

        load_ht(0)
        load_o(0)
        load_x(0)
        phase12(0)
        for a_ in late:
            ldw(*a_)
        for it in range(NTT):
            if it + 1 < NTT:
                load_ht(it + 1)
                load_o(it + 1)
            merge_out(it)
            if it + 1 < NTT:
                phase12(it + 1)
            finalnorm(it)
            if it + 1 < NTT:
                load_x(it + 1)
        sc.close()
    return nc


def mixer_inputs(c, hT, w_in_l, b_fg_l, conv_w_l, a_log_l, dt_bias_l):
    hd, half = c // 2, c % 2
    wf = np.concatenate([w_in_l[:, OFF['aq'] + c * 64:OFF['aq'] + (c + 1) * 64],
                         w_in_l[:, OFF['ak'] + c * 64:OFF['ak'] + (c + 1) * 64],
                         w_in_l[:, OFF['av'] + c * 64:OFF['av'] + (c + 1) * 64],
                         w_in_l[:, OFF['af'] + c:OFF['af'] + c + 1]], axis=1)
    vo = hd * 128 + half * 64
    wg = np.concatenate([w_in_l[:, OFF['bq'] + hd * 128:OFF['bq'] + (hd + 1) * 128],
                         w_in_l[:, OFF['bk'] + hd * 128:OFF['bk'] + (hd + 1) * 128],
                         w_in_l[:, OFF['bv'] + vo:OFF['bv'] + vo + 64],
                         w_in_l[:, OFF['ba'] + hd:OFF['ba'] + hd + 1],
                         w_in_l[:, OFF['bb'] + hd:OFF['bb'] + hd + 1]], axis=1)
    cw = np.zeros((128, 12), np.float32)
    cw[:, 0:4] = conv_w_l[:, hd * 128:(hd + 1) * 128].T
    cw[:, 4:8] = conv_w_l[:, 512 + hd * 128:512 + (hd + 1) * 128].T
    cw[0:64, 8:12] = conv_w_l[:, 1024 + vo:1024 + vo + 64].T
    gpar = np.empty((128, 2), np.float32)
    gpar[:, 0] = a_log_l[hd]
    gpar[:, 1] = dt_bias_l[hd]
    return dict(hT=hT, wf=np.ascontiguousarray(wf), bfg=np.full((128, 1), b_fg_l[c], np.float32),
                wg=np.ascontiguousarray(wg), cw=cw, gpar=gpar)


def _lay8(v):
    return np.ascontiguousarray(np.asarray(v, np.float32).reshape(-1, 128).T)


_PROGS = {}


def _prog(name, fn):
    if name not in _PROGS:
        _PROGS[name] = fn()
    return _PROGS[name]


def kernel(x, mem, norm_g, w_in, b_fg, b_merge, conv_w, a_log, dt_bias, gdn_norm_g, mem_norm_g, w_mem_kv,
           w_branch, w_out, final_norm_g):
    f = lambda a: np.asarray(a, np.float32)
    x, mem, norm_g, w_in, b_fg, b_merge, conv_w = map(f, (x, mem, norm_g, w_in, b_fg, b_merge, conv_w))
    a_log, dt_bias, gdn_norm_g, mem_norm_g = map(f, (a_log, dt_bias, gdn_norm_g, mem_norm_g))
    w_mem_kv, w_branch, w_out, final_norm_g = map(f, (w_mem_kv, w_branch, w_out, final_norm_g))
    S = x.shape[1]
    TS = S // NCORES
    cores = list(range(NCORES))
    xT = np.ascontiguousarray(x[0].T)
    memT = np.ascontiguousarray(mem[0].T)
    sh = lambda a, c: np.ascontiguousarray(a[:, c * TS:(c + 1) * TS])
    ncP = _prog('P', lambda: build_P(TS))
    res = run_bass_kernel_spmd(ncP, [dict(xT=sh(xT, c), g=_lay8(norm_g[0])) for c in cores], core_ids=cores)
    hT = np.concatenate([np.asarray(r["hT"]) for r in res.results], axis=1)
    depth = w_in.shape[0]
    for l in range(depth):
        last = (l == depth - 1)
        ncM = _prog('M', lambda: build_M(S))
        hTc = np.ascontiguousarray(hT)
        res = run_bass_kernel_spmd(
            ncM, [mixer_inputs(c, hTc, w_in[l], b_fg[l], conv_w[l], a_log[l], dt_bias[l]) for c in cores],
            core_ids=cores)
        oaT = np.concatenate([np.asarray(r["oaT"]) for r in res.results], axis=0)
        obT = np.concatenate([np.asarray(r["obT"]) for r in res.results], axis=0)
        ncT = _prog('T%d' % int(last), lambda: build_T(TS, last))
        ng = final_norm_g if last else norm_g[l + 1]
        maps = []
        for c in cores:
            maps.append(dict(xT=sh(xT, c), hT=sh(hT, c), oaT=sh(oaT, c), obT=sh(obT, c),
                             w_in=np.ascontiguousarray(w_in[l]),
                             w_br=np.ascontiguousarray(w_branch[l].reshape(1536, D)),
                             w_out=np.ascontiguousarray(w_out[l]), w_kv=np.ascontiguousarray(w_mem_kv[l]),
                             memT=memT, mem_g=_lay8(mem_norm_g[l]), b_mg=_lay8(b_merge[l]),
                             gdn_g=np.ascontiguousarray(gdn_norm_g[l].reshape(128, 1)), next_g=_lay8(ng)))
        res = run_bass_kernel_spmd(ncT, maps, core_ids=cores)
        xT = np.concatenate([np.asarray(r["xoT"]) for r in res.results], axis=1)
        if not last:
            hT = np.concatenate([np.asarray(r["hoT"]) for r in res.results], axis=1)
    out = np.ascontiguousarray(xT.T).reshape(1, S, D).astype(np.float32)
    return out
```

```python
import contextlib
import numpy as np
import ml_dtypes
import concourse.bass as bass
import concourse.mybir as mybir
from concourse.bass_utils import run_bass_kernel_spmd

F32 = mybir.dt.float32
BF16 = mybir.dt.bfloat16
AF = mybir.ActivationFunctionType
ALU = mybir.AluOpType

D = 1024
S_FULL = 16384
NCORES = 8
EPS = 1e-6
N_IN = 8208
import os as _os
SAME_ENGINE_SYNC = bool(int(_os.environ.get('SAME_SYNC', '1')))
OFF = dict(aq=0, ak=512, av=1024, af=1536, az=1544, bq=2056, bk=2568, bv=3080,
           ba=3592, bb=3596, bz=3600, mq=4112, mz=4624, gates=5136)


def _is_psum_key(k):
    if isinstance(k, str):
        return k.startswith('ps')
    if isinstance(k, tuple) and len(k) >= 2:
        return k[0] in ('pb', 'pS', 'pO') or k[1] == 'ps'
    return False


class Sched:
    ENGS = ['pe', 'act', 'dve', 'pool', 'sp']

    def __init__(self, nc, same_engine_sync=None):
        if same_engine_sync is None:
            same_engine_sync = SAME_ENGINE_SYNC
        self.nc = nc
        self.ops = []
        self.lastw = {}
        self.readers = {}
        self.slot_count = {}
        self.same = same_engine_sync
        self.stack = contextlib.ExitStack()
        self.esem = {e: self.stack.enter_context(nc.semaphore("sem_" + e)) for e in self.ENGS}
        self.ssem = {}
        self.cnt = {e: 0 for e in self.ENGS}

    def _needs_same(self, eng):
        if eng == 'pe':
            return False
        if eng == 'pool':
            return True
        return self.same

    def add(self, eng, fn, reads=(), writes=(), slot=None):
        op = dict(eng=eng, fn=fn, deps=[], slot=slot, inc=False, id=len(self.ops))
        deps = {}
        for k in reads:
            w = self.lastw.get(k)
            if w is not None:
                deps[w['id']] = w
            if _is_psum_key(k):
                for r in self.readers.get(k, ()):
                    if r['eng'] != eng:
                        deps[r['id']] = r
        for k in writes:
            w = self.lastw.get(k)
            if w is not None:
                deps[w['id']] = w
            for r in self.readers.get(k, ()):
                deps[r['id']] = r
        for d in deps.values():
            if d is op:
                continue
            op['deps'].append(d)
            if d['slot'] is None:
                if d['eng'] != eng or self._needs_same(eng) or slot is not None:
                    d['inc'] = True
        for k in writes:
            self.lastw[k] = op
            self.readers[k] = []
        for k in reads:
            self.readers.setdefault(k, []).append(op)
        if slot is not None:
            if slot not in self.ssem:
                self.ssem[slot] = self.stack.enter_context(self.nc.semaphore("sl_%d" % len(self.ssem)))
            self.slot_count[slot] = self.slot_count.get(slot, 0) + 1
            op['slot_val'] = self.slot_count[slot] * 16
        self.ops.append(op)
        return op

    def flush(self):
        nc = self.nc
        for op in self.ops:
            if op['slot'] is None and op['inc']:
                self.cnt[op['eng']] += 1
                op['count'] = self.cnt[op['eng']]
        ops = self.ops
        esem, ssem = self.esem, self.ssem
        final = dict(self.slot_count)
        with nc.Block() as block:
            def run(ename, eng):
                known = {}
                for op in ops:
                    if op['eng'] != ename:
                        continue
                    waits = {}
                    for d in op['deps']:
                        if d['slot'] is not None:
                            key = ('s', d['slot'])
                            v = d['slot_val']
                            sem = ssem[d['slot']]
                        else:
                            if d['eng'] == ename and op['slot'] is None and not self._needs_same(ename):
                                continue
                            key = ('e', d['eng'])
                            v = d['count']
                            sem = esem[d['eng']]
                        if waits.get(key, (None, -1))[1] < v:
                            waits[key] = (sem, v)
                    for key, (sem, v) in waits.items():
                        if known.get(key, -1) >= v:
                            continue
                        known[key] = v
                        eng.wait_ge(sem, v)
                    ins = op['fn'](eng)
                    if op['slot'] is not None:
                        ins.then_inc(ssem[op['slot']], 16)
                    elif op['inc']:
                        ins.then_inc(esem[ename], 1)
                if ename == 'sp':
                    for s, n in final.items():
                        eng.wait_ge(ssem[s], n * 16)

            block.tensor(lambda e: run('pe', e))
            block.scalar(lambda e: run('act', e))
            block.vector(lambda e: run('dve', e))
            block.gpsimd(lambda e: run('pool', e))
            block.sync(lambda e: run('sp', e))
        self.ops = []
        self.lastw = {}
        self.readers = {}

    def collective(self, kind, op, src_ap, dst_ap, reads=(), writes=(), slot='cc', ncores=NCORES):
        self.flush()
        if slot not in self.ssem:
            self.ssem[slot] = self.stack.enter_context(self.nc.semaphore("sl_%d" % len(self.ssem)))
        self.slot_count[slot] = self.slot_count.get(slot, 0) + 1
        ins = self.nc.gpsimd.collective_compute(kind, op, replica_groups=[list(range(ncores))],
                                                ins=[src_ap], outs=[dst_ap])
        ins.then_inc(self.ssem[slot], 16)
        pseudo = dict(eng='pool', fn=None, deps=[], slot=slot, inc=False, id=-1,
                      slot_val=self.slot_count[slot] * 16)
        for k in writes:
            self.lastw[k] = pseudo
            self.readers[k] = []

    def close(self):
        self.flush()
        self.stack.close()


_NAME = [0]


class Ctx:
    def __init__(self, nc):
        self.nc = nc
        self.st = contextlib.ExitStack()

    def sb(self, shape, dt, name=None):
        _NAME[0] += 1
        return self.st.enter_context(self.nc.sbuf_tensor(name or ("t%d" % _NAME[0]), list(shape), dt))

    def ps(self, shape, dt, name=None):
        _NAME[0] += 1
        return self.st.enter_context(self.nc.psum_tensor(name or ("p%d" % _NAME[0]), list(shape), dt))


def emit_rmsnorm(sc, x_sb, xkey, g_sb, ones_f, out_sb, outkey, TT, sq, ps, rstd, tag, dim=D, gkey='g',
                 oneskey='ones_f'):
    for k in range(8):
        sc.add('act', lambda e, k=k: e.activation(sq[:, k, :], x_sb[:, k, :], AF.Square),
               reads=[xkey], writes=[(tag, 'sq', k)])

    def mm(e):
        r = None
        for k in range(8):
            r = e.matmul(ps[:, 0:TT], ones_f[:, :], sq[:, k, :], start=(k == 0), stop=(k == 7))
        return r
    sc.add('pe', mm, reads=[(tag, 'sq', k) for k in range(8)] + [oneskey], writes=[(tag, 'ps')])
    sc.add('act', lambda e: e.activation(rstd[:, :], ps[:, 0:TT], AF.Ln, bias=EPS, scale=1.0 / dim),
           reads=[(tag, 'ps')], writes=[(tag, 'rstd')])
    sc.add('act', lambda e: e.activation(rstd[:, :], rstd[:, :], AF.Exp, scale=-0.5),
           reads=[(tag, 'rstd')], writes=[(tag, 'rstd')])
    for k in range(8):
        sc.add('dve',
               lambda e, k=k: e.scalar_tensor_tensor(out_sb[:, k, :], x_sb[:, k, :], g_sb[:, k:k + 1],
                                                     rstd[:, :], ALU.mult, ALU.mult),
               reads=[xkey, (tag, 'rstd'), gkey], writes=[outkey])


def build_P(TS):
    nc = bass.Bass("TRN2", target_bir_lowering=False)
    xT = nc.dram_tensor("xT", [D, TS], F32, kind="ExternalInput").ap()
    g = nc.dram_tensor("g", [128, 8], F32, kind="ExternalInput").ap()
    hT = nc.dram_tensor("hT", [D, TS], BF16, kind="ExternalOutput").ap()
    TT = 512
    cx = Ctx(nc)
    with cx.st:
        sc = Sched(nc)
        ones_f = cx.sb([128, 128], BF16)
        g_sb = cx.sb([128, 8], F32)
        xs = [cx.sb([128, 8, TT], F32) for _ in range(2)]
        hs = [cx.sb([128, 8, TT], BF16) for _ in range(2)]
        sq = cx.sb([128, 8, TT], BF16)
        rstd = cx.sb([128, TT], F32)
        ps = cx.ps([128, 512], F32)
        sc.add('pool', lambda e: e.memset(ones_f[:, :], 1.0), writes=['ones_f'])
        sc.add('sp', lambda e: e.dma_start(out=g_sb[:, :], in_=g[:, :]), writes=['g'], slot='g')
        xv = xT.rearrange("(k p) t -> p k t", p=128)
        hv = hT.rearrange("(k p) t -> p k t", p=128)
        for i in range(TS // TT):
            b = i % 2
            sc.add('sp', lambda e, i=i, b=b: e.dma_start(out=xs[b][:, :, :], in_=xv[:, :, i * TT:(i + 1) * TT]),
                   writes=[('x', b)], slot=('x', b))
            emit_rmsnorm(sc, xs[b], ('x', b), g_sb, ones_f, hs[b], ('h', b), TT, sq, ps, rstd, 'n')
            sc.add('sp', lambda e, i=i, b=b: e.dma_start(out=hv[:, :, i * TT:(i + 1) * TT], in_=hs[b][:, :, :]),
                   reads=[('h', b)], writes=[('hout', i)], slot=('ho', b))
        sc.close()
    return nc


def make_consts(sc, cx):
    c = {}
    c['ones'] = cx.sb([128, 128], F32)
    c['ident'] = cx.sb([128, 128], F32)
    c['uincl'] = cx.sb([128, 128], F32)
    c['ustrict'] = cx.sb([128, 128], F32)
    c['e0'] = cx.sb([128, 128], F32)
    c['ones_bf'] = cx.sb([128, 128], BF16)
    c['ident_bf'] = cx.sb([128, 128], BF16)
    c['zeros'] = cx.sb([128, 128], F32)
    sc.add('pool', lambda e: e.memset(c['ones'][:, :], 1.0), writes=['c_ones'])
    sc.add('pool', lambda e: e.memset(c['zeros'][:, :], 0.0), writes=['c_zeros'])
    sc.add('pool', lambda e: e.memset(c['ones_bf'][:, :], 1.0), writes=['c_ones_bf'])
    sc.add('pool', lambda e: e.affine_select(c['ident'][:, :], c['zeros'][:, :], [[1, 128]], ALU.not_equal, 1.0,
                                             base=0, channel_multiplier=-1),
           reads=['c_zeros'], writes=['c_ident'])
    sc.add('pool', lambda e: e.tensor_copy(c['ident_bf'][:, :], c['ident'][:, :]),
           reads=['c_ident'], writes=['c_ident_bf'])
    sc.add('pool', lambda e: e.affine_select(c['uincl'][:, :], c['ones'][:, :], [[1, 128]], ALU.is_ge, 0.0,
                                             base=0, channel_multiplier=-1),
           reads=['c_ones'], writes=['c_uincl'])
    sc.add('pool', lambda e: e.affine_select(c['ustrict'][:, :], c['ones'][:, :], [[1, 128]], ALU.is_gt, 0.0,
                                             base=0, channel_multiplier=-1),
           reads=['c_ones'], writes=['c_ustrict'])
    sc.add('pool', lambda e: e.affine_select(c['e0'][:, :], c['ones'][:, :], [[0, 128]], ALU.is_ge, 0.0,
                                             base=0, channel_multiplier=-1),
           reads=['c_ones'], writes=['c_e0'])
    return c


def load_cast(sc, dst_bf, dstkey, src_ap, stage, stagekey, eng_dma='sp', eng_cast='pool', slot=None):
    sc.add(eng_dma, lambda e: e.dma_start(out=stage, in_=src_ap), writes=[stagekey], slot=slot or stagekey)
    sc.add(eng_cast, lambda e: e.tensor_copy(dst_bf, stage), reads=[stagekey], writes=[dstkey])


def fox_phase(nc, sc, cx0, c, S, hT, wf, bfg, oaT, scr, psb, stop=99):
    NT = S // 128
    NG = S // 512
    cx = Ctx(nc)
    with cx.st:
        wqk = cx.sb([128, 8, 128], BF16)
        wv = cx.sb([128, 8, 65], BF16)
        QT = cx.sb([65, S], BF16)
        KT = cx.sb([65, S], BF16)
        V = cx.sb([128, NT, 65], BF16)
        lfr = cx.sb([128, NT], F32)
        lfn = cx.sb([128, NT], F32)
        Fn = cx.sb([128, NT], F32)
        frefB = cx.sb([128, NG], F32)
        ctok = cx.sb([128, NT], F32)
        cTT = cx.sb([128, 128], BF16)
        totT = cx.sb([128, 1], F32)
        X = cx.sb([128, 128], F32)
        negb = cx.sb([128, 1], F32)
        biasg = [cx.sb([128, NT], F32) for _ in range(2)]
        ht = [cx.sb([128, 8, 512], BF16) for _ in range(2)]
        Pb = [cx.sb([128, 512], BF16) for _ in range(5)]
        oun = cx.sb([65, 512], F32)
        rl = cx.sb([65, 512], F32)
        ofin = [cx.sb([64, 512], F32) for _ in range(2)]

        wfv = wf.rearrange("(k p) c -> p k c", p=128)
        sc.add('pool', lambda e: e.dma_start(out=wqk[:, :, :], in_=wfv[:, :, 0:128]), writes=['wq'], slot='wq')
        sc.add('pool', lambda e: e.dma_start(out=wv[:, :, :], in_=wfv[:, :, 128:193]), writes=['wv'], slot='wv')
        sc.add('sp', lambda e: e.dma_start(out=negb[:, :], in_=bfg[:, :]), writes=['negb'], slot='negb')
        sc.add('dve', lambda e: e.tensor_scalar(negb[:, :], negb[:, :], -1.0, None, ALU.mult),
               reads=['negb'], writes=['negb'])
        sc.add('pool', lambda e: e.memset(KT[64:65, :], 1.0), writes=['KTrow'])
        sc.add('pool', lambda e: e.memset(V[:, :, 64:65], 1.0), writes=['Vones'])

        if stop <= 0:
            sc.flush()
            return
        hv = hT.rearrange("(k p) t -> p k t", p=128)
        psq, psk, psv = psb[0], psb[1], psb[2]
        for i in range(NG):
            b = i % 2
            sc.add('sp', lambda e, i=i, b=b: e.dma_start(out=ht[b][:, :, :], in_=hv[:, :, i * 512:(i + 1) * 512]),
                   writes=[('ht', b)], slot=('ht', b))

            def mmq(e, b=b):
                r = None
                for k in range(8):
                    r = e.matmul(psq[:, :], wqk[:, k, :], ht[b][:, k, :], start=(k == 0), stop=(k == 7))
                return r
            sc.add('pe', mmq, reads=[('ht', b), 'wq'], writes=['psq'])
            sc.add('act', lambda e, i=i: e.activation(QT[0:64, i * 512:(i + 1) * 512], psq[0:64, :], AF.Copy,
                                                      scale=0.125),
                   reads=['psq'], writes=[('QT', i)])
            sc.add('dve', lambda e, i=i: e.tensor_copy(KT[0:64, i * 512:(i + 1) * 512], psq[64:128, :]),
                   reads=['psq'], writes=[('KT', i)])

            def mmv(e, b=b):
                r = None
                for sub in range(4):
                    for k in range(8):
                        r = e.matmul(psv[:, sub * 128:sub * 128 + 65], ht[b][:, k, sub * 128:(sub + 1) * 128],
                                     wv[:, k, :], start=(k == 0), stop=(k == 7))
                return r
            pv3 = psv[:, :].rearrange("p (s c) -> p s c", c=128)
            sc.add('pe', mmv, reads=[('ht', b), 'wv'], writes=['psv'])
            sc.add('dve', lambda e, i=i, pv3=pv3: e.tensor_copy(V[:, 4 * i:4 * i + 4, 0:64], pv3[:, :, 0:64]),
                   reads=['psv', 'Vones'], writes=[('V', i)])
            sc.add('dve', lambda e, i=i, pv3=pv3: e.tensor_copy(lfr[:, 4 * i:4 * i + 4], pv3[:, :, 64]),
                   reads=['psv'], writes=[('lfr', i)])

        if stop <= 1:
            sc.flush()
            return
        allfr = [('lfr', i) for i in range(NG)]
        sc.add('act', lambda e: e.activation(lfn[:, :], lfr[:, :], AF.Exp, bias=negb[:, 0:1], scale=-1.0),
               reads=allfr + ['negb'], writes=['lfn'])
        sc.add('act', lambda e: e.activation(lfn[:, :], lfn[:, :], AF.Ln, bias=1.0, scale=1.0),
               reads=['lfn'], writes=['lfn'])
        pt = psb[0]
        sc.add('pe', lambda e: e.matmul(pt[0:NT, 0:1], lfn[:, :], c['ones'][:, 0:1], start=True, stop=True),
               reads=['lfn', 'c_ones', 'psq'], writes=['psq'])
        sc.add('dve', lambda e: e.tensor_copy(totT[0:NT, :], pt[0:NT, 0:1]), reads=['psq'], writes=['totT'])
        sc.add('dve', lambda e: e.tensor_scalar(X[0:NT, 0:NT], c['ustrict'][0:NT, 0:NT], totT[0:NT, 0:1], None,
                                                ALU.mult),
               reads=['totT', 'c_ustrict'], writes=['X'])
        pf = psb[1]

        def mmF(e):
            e.matmul(pf[:, 0:NT], c['uincl'][:, :], lfn[:, :], start=True, stop=False)
            return e.matmul(pf[:, 0:NT], c['ones'][0:NT, :], X[0:NT, 0:NT], start=False, stop=True)
        sc.add('pe', mmF, reads=['lfn', 'X', 'c_uincl', 'c_ones', 'psk'], writes=['psk'])
        sc.add('dve', lambda e: e.tensor_copy(Fn[:, :], pf[:, 0:NT]), reads=['psk'], writes=['Fn'])
        pr = psb[2]
        sc.add('pe', lambda e: e.matmul(pr[:, 0:NG], c['e0'][:, :], Fn[:, 0:NT:4], start=True, stop=True),
               reads=['Fn', 'c_e0', 'psv'], writes=['psv'])
        sc.add('dve', lambda e: e.tensor_copy(frefB[:, :], pr[:, 0:NG]), reads=['psv'], writes=['frefB'])
        for r in range(4):
            sc.add('dve', lambda e, r=r: e.tensor_tensor(ctok[:, r:NT:4], frefB[:, :], Fn[:, r:NT:4], ALU.subtract),
                   reads=['frefB', 'Fn'], writes=[('ctok', r)])
        pc = psb[3]
        sc.add('pe', lambda e: e.transpose(pc[0:NT, 0:128], ctok[:, :], c['ident'][:, :]),
               reads=[('ctok', r) for r in range(4)] + ['c_ident'], writes=['ps3'])
        sc.add('dve', lambda e: e.tensor_copy(cTT[0:NT, :], pc[0:NT, 0:128]), reads=['ps3'], writes=['cTT'])
        sc.add('sp', lambda e: e.dma_start(out=scr[0:NT, :], in_=cTT[0:NT, :]), reads=['cTT'], writes=['scr'],
               slot='scr')
        sc.add('sp', lambda e: e.dma_start(out=QT[64:65, :], in_=scr[0:NT, :].rearrange("(o j) p -> o (j p)", o=1)),
               reads=['scr'], writes=['QTrow'], slot='qtrow')

        if stop <= 2:
            sc.flush()
            return
        sc.flush()
        LA = 4
        pS = [psb[0], psb[1], psb[2], psb[3], psb[5]]
        pO = [psb[6], psb[7]]
        pbc = psb[4]
        blocks = []
        for g in range(NG):
            nj = 4 * g + 4
            for j in range(nj):
                r = j - 4 * g
                c0 = 0 if r < 0 else r * 128
                blocks.append((g, j, r, c0, 512 - c0, nj))
        NB = len(blocks)

        def emit_front(bi):
            g, j, r, c0, N, nj = blocks[bi]
            gb = g % 2
            sb_ = bi % 5
            if j == 0:
                sc.add('dve', lambda e: e.tensor_scalar(biasg[gb][:, 0:nj], Fn[:, 0:nj], frefB[:, g:g + 1], None,
                                                        ALU.subtract),
                       reads=['Fn', 'frefB'], writes=[('biasg', gb)])
            sc.add('pe', lambda e: e.matmul(pS[sb_][:, 0:N], KT[0:65, j * 128:(j + 1) * 128],
                                            QT[0:65, g * 512 + c0:(g + 1) * 512], start=True, stop=True),
                   reads=['QT', 'KT'], writes=[('pS', sb_)])
            sc.add('act', lambda e: e.activation(Pb[sb_][:, 0:N], pS[sb_][:, 0:N], AF.Exp,
                                                 bias=biasg[gb][:, j:j + 1], scale=1.0),
                   reads=[('pS', sb_), ('biasg', gb)], writes=[('P', sb_)])
            if r >= 0:
                sc.add('pool', lambda e: e.affine_select(Pb[sb_][:, 0:128], Pb[sb_][:, 0:128], [[1, 128]], ALU.is_ge,
                                                         0.0, base=0, channel_multiplier=-1),
                       reads=[('P', sb_)], writes=[('P', sb_)])

        def emit_back(bi):
            g, j, r, c0, N, nj = blocks[bi]
            gb = g % 2
            sb_ = bi % 5
            sc.add('pe', lambda e: e.matmul(pO[gb][0:65, c0:512], V[:, j, 0:65], Pb[sb_][:, 0:N], start=(j == 0),
                                            stop=(j == nj - 1), skip_group_check=True),
                   reads=[('P', sb_), 'V'], writes=[('pO', gb)])
            if j == nj - 1:
                sc.add('dve', lambda e: e.tensor_copy(oun[0:65, :], pO[gb][0:65, :]),
                       reads=[('pO', gb)], writes=['oun'])
                sc.add('dve', lambda e: e.reciprocal(rl[64:65, :], oun[64:65, :]), reads=['oun'], writes=['rl'])
                pending.append((bi + 6, g, gb))

        def emit_fin(g, gb):
            sc.add('pe', lambda e: e.matmul(pbc[0:64, :], c['ones'][64:65, 0:64], rl[64:65, :], start=True,
                                            stop=True),
                   reads=['rl', 'c_ones'], writes=[('pb', 4)])
            sc.add('dve', lambda e: e.tensor_tensor(ofin[gb][:, :], oun[0:64, :], pbc[0:64, :], ALU.mult),
                   reads=['oun', ('pb', 4)], writes=[('ofin', gb)])
            sc.add('sp', lambda e: e.dma_start(out=oaT[:, g * 512:(g + 1) * 512], in_=ofin[gb][:, :]),
                   reads=[('ofin', gb)], writes=[('oaT', g)], slot=('oa', gb))

        pending = []
        for bi in range(NB + LA):
            if bi < NB:
                emit_front(bi)
            if bi - LA >= 0:
                emit_back(bi - LA)
            while pending and pending[0][0] <= bi - LA:
                _, g_, gb_ = pending.pop(0)
                emit_fin(g_, gb_)
        for _, g_, gb_ in pending:
            emit_fin(g_, gb_)
        sc.flush()


def gdn_phase(nc, sc, c, S, hT, wg, cw, gpar, obT, psb):
    NSEG = S // 512
    A = sc.add
    cx = Ctx(nc)
    PB = lambda n: ('pb', n)
    with cx.st:
        f32t = lambda *sh: cx.sb(list(sh), F32)
        bft = lambda *sh: cx.sb(list(sh), BF16)
        wq, wk, wv, wab = bft(128, 8, 128), bft(128, 8, 128), bft(128, 8, 64), bft(128, 8, 2)
        cw_sb, gp_sb = f32t(128, 12), f32t(128, 2)
        negA = f32t(128, 1)
        M_s, M_i = f32t(128, 4, 128), f32t(128, 4, 128)
        E63, E127, EL = f32t(128, 128), f32t(128, 128), f32t(128, 128)
        ht = [bft(128, 8, 512) for _ in range(2)]
        rq, rk, rv = f32t(128, 515), f32t(128, 515), f32t(64, 515)
        cq, ck = f32t(128, 512), f32t(128, 512)
        sq2, sq2b = bft(128, 512), bft(128, 512)
        rn, rnb = f32t(128, 512), f32t(128, 512)
        g_tok, G_tok, eG, ekd, glo = [f32t(128, 4) for _ in range(5)]
        diagG, EGrow, Dm, Gam, Gs, Gi = [f32t(128, 512) for _ in range(6)]
        ktok = f32t(128, 4, 128)
        Bb = [bft(128, 512) for _ in range(2)]
        Pb_ = [bft(128, 512) for _ in range(2)]
        S_f, S_b = f32t(128, 512), bft(128, 512)
        St_f, St_b = f32t(128, 64), bft(128, 64)
        vnew = bft(128, 64)
        o_sb = f32t(64, 512)

        wgv = wg.rearrange("(k p) c -> p k c", p=128)
        A('pool', lambda e: e.dma_start(out=wq[:, :, :], in_=wgv[:, :, 0:128]), writes=['gwq'], slot='gwq')
        A('pool', lambda e: e.dma_start(out=wk[:, :, :], in_=wgv[:, :, 128:256]), writes=['gwk'], slot='gwk')
        A('pool', lambda e: e.dma_start(out=wv[:, :, :], in_=wgv[:, :, 256:320]), writes=['gwv'], slot='gwv')
        A('pool', lambda e: e.dma_start(out=wab[:, :, :], in_=wgv[:, :, 320:322]), writes=['gwab'], slot='gwab')
        A('sp', lambda e: e.dma_start(out=cw_sb[:, :], in_=cw[:, :]), writes=['cw'], slot='cw')
        A('sp', lambda e: e.dma_start(out=gp_sb[:, :], in_=gpar[:, :]), writes=['gp'], slot='gp')
        A('act', lambda e: e.activation(negA[:, :], gp_sb[:, 0:1], AF.Exp), reads=['gp'], writes=['negA'])
        A('dve', lambda e: e.tensor_scalar(negA[:, :], negA[:, :], -1.0, None, ALU.mult), reads=['negA'],
          writes=['negA'])
        A('pool', lambda e: e.memset(M_s[:, :, :], 1.0), writes=['M_s'])
        A('pool', lambda e: e.memset(M_i[:, :, :], 1.0), writes=['M_i'])
        A('pool', lambda e: e.affine_select(M_s[:, :, :], M_s[:, :, :], [[0, 4], [1, 128]], ALU.is_gt, 0.0, base=0,
                                            channel_multiplier=-1), reads=['M_s'], writes=['M_s'])
        A('pool', lambda e: e.affine_select(M_i[:, :, :], M_i[:, :, :], [[0, 4], [1, 128]], ALU.is_ge, 0.0, base=0,
                                            channel_multiplier=-1), reads=['M_i'], writes=['M_i'])
        A('pool', lambda e: e.memset(M_s[0:64, :, 64:128], 0.0), reads=['M_s'], writes=['M_s'])
        A('pool', lambda e: e.memset(M_i[0:64, :, 64:128], 0.0), reads=['M_i'], writes=['M_i'])
        A('pool', lambda e: e.affine_select(E63[:, :], c['zeros'][:, :], [[0, 128]], ALU.not_equal, 1.0, base=-63,
                                            channel_multiplier=1), reads=['c_zeros'], writes=['E63'])
        A('pool', lambda e: e.affine_select(E127[:, :], c['zeros'][:, :], [[0, 128]], ALU.not_equal, 1.0, base=-127,
                                            channel_multiplier=1), reads=['c_zeros'], writes=['E127'])
        A('pool', lambda e: e.tensor_copy(EL[:, 0:64], E63[:, 0:64]), reads=['E63'], writes=['EL'])
        A('pool', lambda e: e.tensor_copy(EL[:, 64:128], E127[:, 64:128]), reads=['E127', 'EL'], writes=['EL'])
        A('pool', lambda e: e.memset(rq[:, 0:3], 0.0), writes=['rq'])
        A('pool', lambda e: e.memset(rk[:, 0:3], 0.0), writes=['rk'])
        A('pool', lambda e: e.memset(rv[:, 0:3], 0.0), writes=['rv'])
        A('pool', lambda e: e.memset(St_f[:, :], 0.0), writes=['St_f'])
        A('pool', lambda e: e.memset(St_b[:, :], 0.0), writes=['St_b'])

        hv = hT.rearrange("(k p) t -> p k t", p=128)
        ones, ident = c['ones'], c['ident']
        DEPTH = dict(qn_f=2, kn_f=2, qn_bf=2, kT_bf=2, cv=2, a_sb=2, b_sb=2, B_f=2, kg=2, vtok=2, bpos=2,
                     qdec=3, aqk=3, kdec=3, negbt=3, dl=3, dh=3, ybu=2, ywT=2)
        SHAPES = dict(qn_f=(F32, (128, 512)), kn_f=(F32, (128, 512)), qn_bf=(BF16, (128, 512)),
                      kT_bf=(BF16, (128, 512)), cv=(F32, (64, 512)), a_sb=(F32, (128, 4)), b_sb=(F32, (128, 4)),
                      B_f=(F32, (128, 512)), kg=(BF16, (128, 4, 128)), vtok=(BF16, (128, 4, 64)),
                      bpos=(F32, (128, 4)), qdec=(BF16, (128, 512)), aqk=(BF16, (128, 512)),
                      kdec=(BF16, (128, 4, 128)), negbt=(F32, (128, 4)), dl=(F32, (128, 4)), dh=(F32, (128, 4)),
                      ybu=(F32, (128, 4, 64)), ywT=(BF16, (128, 512)))
        ROT = {n: [cx.sb(list(SHAPES[n][1]), SHAPES[n][0]) for _ in range(DEPTH[n])] for n in DEPTH}

        def mkA(s, lst):
            def K(k):
                return (k, s % DEPTH[k]) if (isinstance(k, str) and k in DEPTH) else k

            def A_(eng, fn, reads=(), writes=(), slot=None):
                lst.append(((eng, fn), dict(reads=[K(k) for k in reads], writes=[K(k) for k in writes], slot=slot)))
            return A_

        def rb(s, n):
            return ROT[n][s % DEPTH[n]]

        def stage1(i, A):
            b = i % 2
            qn_f, kn_f, qn_bf, kT_bf, cv, a_sb, b_sb = [rb(i, n) for n in
                                                        ('qn_f', 'kn_f', 'qn_bf', 'kT_bf', 'cv', 'a_sb', 'b_sb')]
            A('sp', lambda e: e.dma_start(out=ht[b][:, :, :], in_=hv[:, :, i * 512:(i + 1) * 512]),
              writes=[('ght', b)], slot=('ght', b))
            for (w_, M, bank, raw, key) in ((wq, 128, 0, rq, 'rq'), (wk, 128, 1, rk, 'rk'), (wv, 64, 0, rv, 'rv')):
                def mm(e, w_=w_, M=M, bank=bank):
                    r = None
                    for k in range(8):
                        r = e.matmul(psb[bank][0:M, :], w_[:, k, :], ht[b][:, k, :], start=(k == 0), stop=(k == 7))
                    return r
                A('pe', mm, reads=[('ght', b), 'gwq', 'gwk', 'gwv'], writes=[PB(bank)])
                A('dve', lambda e, M=M, bank=bank, raw=raw: e.tensor_copy(raw[0:M, 3:515], psb[bank][0:M, :]),
                  reads=[PB(bank)], writes=[key])

            def mmab(e):
                r = None
                for sub in range(4):
                    for k in range(8):
                        r = e.matmul(psb[1][:, sub * 2:sub * 2 + 2], ht[b][:, k, sub * 128:(sub + 1) * 128],
                                     wab[:, k, :], start=(k == 0), stop=(k == 7))
                return r
            A('pe', mmab, reads=[('ght', b), 'gwab'], writes=[PB(1)])
            p3 = psb[1][:, 0:8].rearrange("p (s c) -> p s c", c=2)
            A('dve', lambda e: e.tensor_copy(a_sb[:, :], p3[:, :, 0]), reads=[PB(1)], writes=['a_sb'])
            A('dve', lambda e: e.tensor_copy(b_sb[:, :], p3[:, :, 1]), reads=[PB(1)], writes=['b_sb'])
            for which, (raw, cv_, M, key, ckey) in enumerate(((rq, cq, 128, 'rq', 'cq'), (rk, ck, 128, 'rk', 'ck'),
                                                              (rv, cv, 64, 'rv', 'cv'))):
                A('act', lambda e, raw=raw, cv_=cv_, M=M, which=which: e.activation(
                    cv_[0:M, :], raw[0:M, 0:512], AF.Copy, scale=cw_sb[0:M, which * 4:which * 4 + 1]),
                    reads=[key, 'cw'], writes=[ckey])
                for tap in range(1, 4):
                    A('dve', lambda e, raw=raw, cv_=cv_, M=M, which=which, tap=tap: e.scalar_tensor_tensor(
                        cv_[0:M, :], raw[0:M, tap:tap + 512], cw_sb[0:M, which * 4 + tap:which * 4 + tap + 1],
                        cv_[0:M, :], ALU.mult, ALU.add),
                        reads=[key, 'cw', ckey], writes=[ckey])
                A('pool', lambda e, raw=raw, M=M: e.tensor_copy(raw[0:M, 0:3], raw[0:M, 512:515]),
                  reads=[key, ckey], writes=[key])
                A('act', lambda e, cv_=cv_, M=M: e.activation(cv_[0:M, :], cv_[0:M, :], AF.Silu),
                  reads=[ckey], writes=[ckey])
            for (cv_, ckey, bank, outf, okey, mul) in ((cq, 'cq', 0, qn_f, 'qn_f', 128.0 ** -0.5),
                                                      (ck, 'ck', 1, kn_f, 'kn_f', 1.0)):
                sq_, rn_ = (sq2, rn) if bank == 0 else (sq2b, rnb)
                A('act', lambda e, cv_=cv_, sq_=sq_: e.activation(sq_[:, :], cv_[:, :], AF.Square), reads=[ckey],
                  writes=[('sq2', bank)])
                A('pe', lambda e, bank=bank, sq_=sq_: e.matmul(psb[bank][:, :], c['ones_bf'][:, :], sq_[:, :],
                                                               start=True, stop=True),
                  reads=[('sq2', bank), 'c_ones_bf'], writes=[PB(bank)])
                A('act', lambda e, bank=bank, rn_=rn_: e.activation(rn_[:, :], psb[bank][:, :], AF.Ln, bias=EPS,
                                                                    scale=1.0),
                  reads=[PB(bank)], writes=[('rn', bank)])
                A('act', lambda e, rn_=rn_: e.activation(rn_[:, :], rn_[:, :], AF.Exp, scale=-0.5),
                  reads=[('rn', bank)], writes=[('rn', bank)])
                A('dve', lambda e, cv_=cv_, outf=outf, mul=mul, rn_=rn_: e.scalar_tensor_tensor(
                    outf[:, :], cv_[:, :], mul, rn_[:, :], ALU.mult, ALU.mult), reads=[ckey, ('rn', bank)],
                    writes=[okey])
            A('act', lambda e: e.activation(qn_bf[:, :], qn_f[:, :], AF.Copy), reads=['qn_f'], writes=['qn_bf'])
            A('act', lambda e: e.activation(kT_bf[:, :], kn_f[:, :], AF.Copy), reads=['kn_f'], writes=['kT_bf'])

        def stage2(i, A):
            qn_f, kn_f, qn_bf, kT_bf, cv, a_sb, b_sb = [rb(i, n) for n in
                                                        ('qn_f', 'kn_f', 'qn_bf', 'kT_bf', 'cv', 'a_sb', 'b_sb')]
            B_f, kg, vtok, bpos = [rb(i, n) for n in ('B_f', 'kg', 'vtok', 'bpos')]
            qdec, aqk, kdec, negbt, dl, dh = [rb(i, n) for n in ('qdec', 'aqk', 'kdec', 'negbt', 'dl', 'dh')]
            A('act', lambda e: e.activation(g_tok[:, :], a_sb[:, :], AF.Exp, bias=gp_sb[:, 1:2], scale=1.0),
              reads=['a_sb', 'gp'], writes=['g_tok'])
            A('act', lambda e: e.activation(g_tok[:, :], g_tok[:, :], AF.Ln, bias=1.0, scale=1.0),
              reads=['g_tok'], writes=['g_tok'])
            A('dve', lambda e: e.tensor_scalar(g_tok[:, :], g_tok[:, :], negA[:, 0:1], None, ALU.mult),
              reads=['g_tok', 'negA'], writes=['g_tok'])
            A('act', lambda e: e.activation(bpos[:, :], b_sb[:, :], AF.Exp, scale=-1.0), reads=['b_sb'],
              writes=['bpos'])
            A('dve', lambda e: e.tensor_scalar(bpos[:, :], bpos[:, :], 1.0, None, ALU.add), reads=['bpos'],
              writes=['bpos'])
            A('dve', lambda e: e.reciprocal(bpos[:, :], bpos[:, :]), reads=['bpos'], writes=['bpos'])
            A('dve', lambda e: e.tensor_scalar(negbt[:, :], bpos[:, :], -1.0, None, ALU.mult), reads=['bpos'],
              writes=['negbt'])
            A('pe', lambda e: e.matmul(psb[2][:, 0:4], M_i[:, 0, :], g_tok[:, :], start=True, stop=True),
              reads=['g_tok', 'M_i'], writes=[PB(2)])
            A('dve', lambda e: e.tensor_copy(G_tok[:, :], psb[2][:, 0:4]), reads=[PB(2)], writes=['G_tok'])
            A('act', lambda e: e.activation(eG[:, :], G_tok[:, :], AF.Exp), reads=['G_tok'], writes=['eG'])
            A('pe', lambda e: e.matmul(psb[2][:, 0:4], EL[:, :], G_tok[:, :], start=True, stop=True),
              reads=['G_tok', 'EL'], writes=[PB(2)])
            A('dve', lambda e: e.tensor_tensor(glo[:, :], psb[2][:, 0:4], G_tok[:, :], ALU.subtract),
              reads=[PB(2), 'G_tok'], writes=['glo'])
            A('act', lambda e: e.activation(ekd[:, :], glo[:, :], AF.Exp), reads=['glo'], writes=['ekd'])
            A('pe', lambda e: e.matmul(psb[2][:, 0:4], E63[:, :], G_tok[:, :], start=True, stop=True),
              reads=['G_tok', 'E63'], writes=[PB(2)])
            A('dve', lambda e: e.tensor_copy(dl[:, :], psb[2][:, 0:4]), reads=[PB(2)], writes=['dl'])
            A('act', lambda e: e.activation(dl[:, :], dl[:, :], AF.Exp), reads=['dl'], writes=['dl'])
            A('pe', lambda e: e.matmul(psb[2][:, 0:4], E127[:, :], G_tok[:, :], start=True, stop=True),
              reads=['G_tok', 'E127'], writes=[PB(2)])
            A('dve', lambda e: e.tensor_copy(dh[:, :], psb[2][:, 0:4]), reads=[PB(2)], writes=['dh'])
            A('act', lambda e: e.activation(dh[:, :], dh[:, :], AF.Exp), reads=['dh'], writes=['dh'])
            for sub in range(4):
                A('dve', lambda e, sub=sub: e.tensor_scalar(diagG[:, sub * 128:(sub + 1) * 128], ident[:, :],
                                                            G_tok[:, sub:sub + 1], None, ALU.mult),
                  reads=['G_tok', 'c_ident'], writes=[('diagG', sub)])
                A('pe', lambda e, sub=sub: e.matmul(psb[3][:, sub * 128:(sub + 1) * 128], ones[:, :],
                                                    diagG[:, sub * 128:(sub + 1) * 128], start=True, stop=True),
                  reads=[('diagG', sub), 'c_ones'], writes=[PB(3)])
            A('act', lambda e: e.activation(EGrow[:, :], psb[3][:, :], AF.Exp), reads=[PB(3)], writes=['EGrow'])
            A('pool', lambda e: e.tensor_tensor(qdec[:, :], qn_f[:, :], EGrow[:, :], ALU.mult),
              reads=['qn_f', 'EGrow'], writes=['qdec'])
            for sub in range(4):
                A('dve', lambda e, sub=sub: e.tensor_scalar(Dm[:, sub * 128:(sub + 1) * 128],
                                                            psb[3][:, sub * 128:(sub + 1) * 128],
                                                            G_tok[:, sub:sub + 1], 0.0, ALU.subtract, ALU.min),
                  reads=[PB(3), 'G_tok'], writes=['Dm'])
            A('act', lambda e: e.activation(Gam[:, :], Dm[:, :], AF.Exp), reads=['Dm'], writes=['Gam'])
            A('pool', lambda e: e.tensor_tensor(Gs[:, :], Gam[:, :], M_s[:, :, :].rearrange("p s c -> p (s c)"),
                                                ALU.mult), reads=['Gam', 'M_s'], writes=['Gs'])
            A('pool', lambda e: e.tensor_tensor(Gi[:, :], Gam[:, :], M_i[:, :, :].rearrange("p s c -> p (s c)"),
                                                ALU.mult), reads=['Gam', 'M_i'], writes=['Gi'])
            for sub in range(4):
                A('pe', lambda e, sub=sub: e.transpose(psb[2][:, sub * 128:(sub + 1) * 128],
                                                       kn_f[:, sub * 128:(sub + 1) * 128], ident[:, :]),
                  reads=['kn_f', 'c_ident'], writes=[PB(2)])
            A('dve', lambda e: e.tensor_copy(ktok[:, :, :], psb[2][:, :].rearrange("p (s c) -> p s c", c=128)),
              reads=[PB(2)], writes=['ktok'])
            for sub in range(4):
                A('act', lambda e, sub=sub: e.activation(kg[:, sub, :], ktok[:, sub, :], AF.Copy,
                                                         scale=eG[:, sub:sub + 1]), reads=['ktok', 'eG'], writes=['kg'])
                A('act', lambda e, sub=sub: e.activation(kdec[:, sub, :], ktok[:, sub, :], AF.Copy,
                                                         scale=ekd[:, sub:sub + 1]), reads=['ktok', 'ekd'],
                  writes=['kdec'])
            for sub in range(4):
                A('pe', lambda e, sub=sub: e.transpose(psb[2][:, sub * 64:(sub + 1) * 64],
                                                       cv[0:64, sub * 128:(sub + 1) * 128], ident[0:64, 0:64]),
                  reads=['cv', 'c_ident'], writes=[PB(2)])
            A('dve', lambda e: e.tensor_copy(vtok[:, :, :], psb[2][:, 0:256].rearrange("p (s c) -> p s c", c=64)),
              reads=[PB(2)], writes=['vtok'])
            for sub in range(4):
                cs = slice(sub * 128, (sub + 1) * 128)
                A('pe', lambda e, cs=cs: e.matmul(psb[2][:, cs], kT_bf[:, cs], kT_bf[:, cs], start=True, stop=True),
                  reads=['kT_bf'], writes=[PB(2)])
                A('dve', lambda e, cs=cs, sub=sub: e.scalar_tensor_tensor(
                    B_f[:, cs], psb[2][:, cs], negbt[:, sub:sub + 1], Gs[:, cs], ALU.mult, ALU.mult),
                    reads=[PB(2), 'negbt', 'Gs'], writes=['B_f'])
            for sub in range(4):
                cs = slice(sub * 128, (sub + 1) * 128)
                A('pe', lambda e, cs=cs: e.matmul(psb[3][:, cs], kT_bf[:, cs], qn_bf[:, cs], start=True, stop=True),
                  reads=['kT_bf', 'qn_bf'], writes=[PB(3)])
            A('dve', lambda e: e.tensor_tensor(aqk[:, :], psb[3][:, :], Gi[:, :], ALU.mult), reads=[PB(3), 'Gi'],
              writes=['aqk'])

        def stage3(i, A):
            B_f, kg, vtok, bpos = [rb(i, n) for n in ('B_f', 'kg', 'vtok', 'bpos')]
            ybu, ywT = rb(i, 'ybu'), rb(i, 'ywT')
            A('act', lambda e: e.activation(Bb[0][:, :], B_f[:, :], AF.Copy), reads=['B_f'], writes=[('Bb', 0)])
            for sub in range(4):
                cs = slice(sub * 128, (sub + 1) * 128)
                A('pe', lambda e, cs=cs: e.transpose(psb[4][:, cs], B_f[:, cs], ident[:, :]),
                  reads=['B_f', 'c_ident'], writes=[PB(4)])
            A('dve', lambda e: e.tensor_copy(Pb_[0][:, :], psb[4][:, :]), reads=[PB(4)], writes=[('Pb', 0)])
            for sub in range(4):
                cs = slice(sub * 128, (sub + 1) * 128)
                A('pool', lambda e, cs=cs: e.tensor_tensor(S_f[:, cs], B_f[:, cs], ident[:, :], ALU.add),
                  reads=['B_f', 'c_ident'], writes=['S_f'])
            A('act', lambda e: e.activation(S_b[:, :], S_f[:, :], AF.Copy), reads=['S_f'], writes=['S_b'])
            for j in range(5):
                cur, nxt = j % 2, (j + 1) % 2
                for sub in range(4):
                    cs = slice(sub * 128, (sub + 1) * 128)
                    A('pe', lambda e, cs=cs, cur=cur: e.matmul(psb[5][:, cs], Pb_[cur][:, cs], Bb[cur][:, cs],
                                                               start=True, stop=True),
                      reads=[('Pb', cur), ('Bb', cur)], writes=[PB(5)])
                A('dve', lambda e, nxt=nxt: e.tensor_copy(Bb[nxt][:, :], psb[5][:, :]), reads=[PB(5)],
                  writes=[('Bb', nxt)])
                for sub in range(4):
                    cs = slice(sub * 128, (sub + 1) * 128)
                    A('pe', lambda e, cs=cs, cur=cur: e.matmul(psb[4][:, cs], Bb[cur][:, cs], Pb_[cur][:, cs],
                                                               start=True, stop=True),
                      reads=[('Pb', cur), ('Bb', cur)], writes=[PB(4)])
                A('act', lambda e, nxt=nxt: e.activation(Pb_[nxt][:, :], psb[4][:, :], AF.Copy), reads=[PB(4)],
                  writes=[('Pb', nxt)])
                for sub in range(4):
                    cs = slice(sub * 128, (sub + 1) * 128)
                    A('pe', lambda e, cs=cs, nxt=nxt: e.matmul(psb[5][:, cs], Pb_[nxt][:, cs], S_b[:, cs],
                                                               start=True, stop=True),
                      reads=[('Pb', nxt), 'S_b'], writes=[PB(5)])
                A('dve', lambda e: e.tensor_tensor(S_f[:, :], S_f[:, :], psb[5][:, :], ALU.add),
                  reads=['S_f', PB(5)], writes=['S_f'])
                A('act', lambda e: e.activation(S_b[:, :], S_f[:, :], AF.Copy), reads=['S_f'], writes=['S_b'])
            for sub in range(4):
                cs = slice(sub * 128, (sub + 1) * 128)
                A('pe', lambda e, cs=cs, sub=sub: e.matmul(psb[4][:, sub * 64:(sub + 1) * 64], S_b[:, cs],
                                                           vtok[:, sub, :], start=True, stop=True),
                  reads=['S_b', 'vtok'], writes=[PB(4)])
            for sub in range(4):
                A('dve', lambda e, sub=sub: e.tensor_scalar(ybu[:, sub, :], psb[4][:, sub * 64:(sub + 1) * 64],
                                                            bpos[:, sub:sub + 1], None, ALU.mult),
                  reads=[PB(4), 'bpos'], writes=['ybu'])
            for sub in range(4):
                cs = slice(sub * 128, (sub + 1) * 128)
                A('pe', lambda e, cs=cs, sub=sub: e.matmul(psb[5][:, cs], kg[:, sub, :], S_b[:, cs], start=True,
                                                           stop=True),
                  reads=['S_b', 'kg'], writes=[PB(5)])
            A('dve', lambda e: e.tensor_copy(ywT[:, :], psb[5][:, :]), reads=[PB(5)], writes=['ywT'])

        def stage4(i, A):
            qdec, aqk, kdec, negbt, dl, dh = [rb(i, n) for n in ('qdec', 'aqk', 'kdec', 'negbt', 'dl', 'dh')]
            ybu, ywT = rb(i, 'ybu'), rb(i, 'ywT')
            for ch in range(8):
                sub, hf = ch // 2, ch % 2
                rs = slice(hf * 64, hf * 64 + 64)
                cs = slice(sub * 128, (sub + 1) * 128)
                cc = slice(ch * 64, (ch + 1) * 64)
                A('pe', lambda e, cs=cs: e.matmul(psb[6][:, 0:64], ywT[:, cs], St_b[:, :], start=True, stop=True),
                  reads=['ywT', 'St_b'], writes=[PB(6)])
                A('dve', lambda e, rs=rs, sub=sub: e.scalar_tensor_tensor(
                    vnew[rs, :], psb[6][rs, 0:64], negbt[rs, sub:sub + 1], ybu[rs, sub, :], ALU.mult, ALU.add),
                    reads=[PB(6), 'negbt', 'ybu'], writes=['vnew'])

                def mmo(e, cc=cc, rs=rs):
                    e.matmul(psb[7][0:64, cc], St_b[:, :], qdec[:, cc], start=True, stop=False)
                    return e.matmul(psb[7][0:64, cc], vnew[rs, :], aqk[rs, cc], start=False, stop=True)
                A('pe', mmo, reads=['St_b', 'qdec', 'vnew', 'aqk'], writes=[PB(7)])
                A('pe', lambda e, rs=rs, sub=sub: e.matmul(psb[6][:, 64:128], kdec[rs, sub, :], vnew[rs, :],
                                                           start=True, stop=True),
                  reads=['kdec', 'vnew'], writes=[PB(6)])
                dsc = (dl if hf == 0 else dh)
                A('dve', lambda e, sub=sub, dsc=dsc: e.scalar_tensor_tensor(
                    St_b[:, :], St_f[:, :], dsc[:, sub:sub + 1], psb[6][:, 64:128], ALU.mult, ALU.add),
                    reads=['St_f', PB(6), 'dl', 'dh'], writes=['St_b'])
                A('dve', lambda e, sub=sub, dsc=dsc: e.scalar_tensor_tensor(
                    St_f[:, :], St_f[:, :], dsc[:, sub:sub + 1], psb[6][:, 64:128], ALU.mult, ALU.add),
                    reads=['St_f', PB(6), 'dl', 'dh'], writes=['St_f'])
            A('act', lambda e: e.activation(o_sb[:, :], psb[7][0:64, :], AF.Copy), reads=[PB(7)], writes=['o_sb'])
            A('sp', lambda e: e.dma_start(out=obT[:, i * 512:(i + 1) * 512], in_=o_sb[:, :]),
              reads=['o_sb'], writes=[('obT', i)], slot='ob')

        def merge_lists(lists):
            lists = [l for l in lists if l]
            pos = [0] * len(lists)
            out = []
            while True:
                best, bf = None, None
                for li, l in enumerate(lists):
                    if pos[li] < len(l):
                        fr = pos[li] / len(l)
                        if bf is None or fr < bf:
                            best, bf = li, fr
                if best is None:
                    break
                out.append(lists[best][pos[best]])
                pos[best] += 1
            return out

        stages = (stage1, stage2, stage3, stage4)
        for t in range(NSEG + 3):
            lists = []
            for si, st_ in enumerate(stages):
                s = t - si
                if 0 <= s < NSEG:
                    lst = []
                    st_(s, mkA(s, lst))
                    lists.append(lst)
            for (a_, k_) in merge_lists(lists[::-1]):
                sc.add(*a_, **k_)
        sc.flush()


def build_M(S, do_fox=True, do_gdn=True, stop=99):
    nc = bass.Bass("TRN2", target_bir_lowering=False)
    hT = nc.dram_tensor("hT", [D, S], BF16, kind="ExternalInput").ap()
    wf = nc.dram_tensor("wf", [D, 193], F32, kind="ExternalInput").ap()
    bfg = nc.dram_tensor("bfg", [128, 1], F32, kind="ExternalInput").ap()
    wg = nc.dram_tensor("wg", [D, 322], F32, kind="ExternalInput").ap()
    cw = nc.dram_tensor("cw", [128, 12], F32, kind="ExternalInput").ap()
    gpar = nc.dram_tensor("gpar", [128, 2], F32, kind="ExternalInput").ap()
    oaT = nc.dram_tensor("oaT", [64, S], F32, kind="ExternalOutput").ap()
    obT = nc.dram_tensor("obT", [64, S], F32, kind="ExternalOutput").ap()
    scr = nc.dram_tensor("scr", [128, 128], BF16).ap()
    cx = Ctx(nc)
    with cx.st:
        sc = Sched(nc)
        c = make_consts(sc, cx)
        psb = [cx.ps([128, 512], F32) for _ in range(8)]
        if do_gdn:
            gdn_phase(nc, sc, c, S, hT, wg, cw, gpar, obT, psb)
        if do_fox:
            fox_phase(nc, sc, cx, c, S, hT, wf, bfg, oaT, scr, psb, stop=stop)
        sc.close()
    return nc


def build_T(TS, last):
    nc = bass.Bass("TRN2", target_bir_lowering=False)
    TT = 256
    NTT = TS // TT
    xT = nc.dram_tensor("xT", [D, TS], F32, kind="ExternalInput").ap()
    hT = nc.dram_tensor("hT", [D, TS], BF16, kind="ExternalInput").ap()
    oaT = nc.dram_tensor("oaT", [512, TS], F32, kind="ExternalInput").ap()
    obT = nc.dram_tensor("obT", [512, TS], F32, kind="ExternalInput").ap()
    w_in = nc.dram_tensor("w_in", [D, N_IN], F32, kind="ExternalInput").ap()
    w_br = nc.dram_tensor("w_br", [1536, D], F32, kind="ExternalInput").ap()
    w_out = nc.dram_tensor("w_out", [D, D], F32, kind="ExternalInput").ap()
    w_kv = nc.dram_tensor("w_kv", [D, 1024], F32, kind="ExternalInput").ap()
    memT = nc.dram_tensor("memT", [D, 256], F32, kind="ExternalInput").ap()
    mem_g = nc.dram_tensor("mem_g", [128, 8], F32, kind="ExternalInput").ap()
    b_mg = nc.dram_tensor("b_mg", [128, 24], F32, kind="ExternalInput").ap()
    gdn_g = nc.dram_tensor("gdn_g", [128, 1], F32, kind="ExternalInput").ap()
    next_g = nc.dram_tensor("next_g", [128, 8], F32, kind="ExternalInput").ap()
    xoT = nc.dram_tensor("xoT", [D, TS], F32, kind="ExternalOutput").ap()
    if not last:
        hoT = nc.dram_tensor("hoT", [D, TS], BF16, kind="ExternalOutput").ap()
    cx = Ctx(nc)
    with cx.st:
        sc = Sched(nc)
        ones_f = cx.sb([128, 128], F32)
        ones_bf = cx.sb([128, 128], BF16)
        sc.add('pool', lambda e: e.memset(ones_f[:, :], 1.0), writes=['ones_f'])
        sc.add('pool', lambda e: e.memset(ones_bf[:, :], 1.0), writes=['ones_bf'])
        psb = [cx.ps([128, 512], F32) for _ in range(8)]

        BM = {(0, 0): 0, (0, 1): 1, (1, 0): 2, (1, 1): 2, (2, 0): 3, (2, 1): 4, (3, 0): 5, (3, 1): 6,
              (4, 0): 2, (4, 1): 3, (5, 0): 4, (5, 1): 5, (6, 0): 6, (6, 1): 7, (7, 0): 0, (7, 1): 1}

        def half(bk, h):
            return psb[BM[(bk, h)]][:, 0:TT]

        def hk(bk, h):
            return ('pb', BM[(bk, h)])
        Wz = cx.sb([128, 8, 5120], BF16)
        Wbr = cx.sb([128, 12, 1024], BF16)
        Wout = cx.sb([128, 8, 1024], BF16)
        mkT = cx.sb([128, 4, 256], BF16)
        mv = cx.sb([128, 2, 512], BF16)
        memg_sb = cx.sb([128, 8], F32)
        bm_sb = cx.sb([128, 24], F32)
        gg_sb = cx.sb([128, 1], F32)
        ng_sb = cx.sb([128, 8], F32)
        for i, (dst, srcap) in enumerate([(memg_sb, mem_g), (bm_sb, b_mg), (gg_sb, gdn_g), (ng_sb, next_g)]):
            sc.add('sp', lambda e, dst=dst, srcap=srcap: e.dma_start(out=dst[:, :], in_=srcap[:, :]),
                   writes=[('par', i)], slot=('par', i))
        def ldw(dst, dkey, srcap):
            sc.add('pool', lambda e: e.dma_start(out=dst, in_=srcap), writes=[dkey], slot=('w', dkey[0], dkey[1]))

        pcx = Ctx(nc)
        with pcx.st:
            Wkv = pcx.sb([128, 8, 1024], BF16)
            mt = pcx.sb([128, 8, 256], F32)
            mn = pcx.sb([128, 8, 256], BF16)
            sqm = pcx.sb([128, 8, 256], BF16)
            rstm = pcx.sb([128, 256], F32)
            for k in range(8):
                ldw(Wkv[:, k, :], ('Wkv', '', k), w_kv[k * 128:(k + 1) * 128, :])
            sc.add('sp', lambda e: e.dma_start(out=mt[:, :, :], in_=memT.rearrange("(k p) m -> p k m", p=128)),
                   writes=['mt'], slot='mt')
            sc.ops[-1]
            saved = {'g': None}
            emit_rmsnorm(sc, mt, 'mt', memg_sb, ones_bf, mn, 'mn', 256, sqm, psb[0], rstm, 'mnorm', gkey=('par', 0),
                         oneskey='ones_bf')
            for hh in range(4):
                def mmk(e, hh=hh):
                    r = None
                    for k in range(8):
                        r = e.matmul(half(1, hh % 2), Wkv[:, k, hh * 128:(hh + 1) * 128], mn[:, k, :],
                                     start=(k == 0), stop=(k == 7))
                    return r
                sc.add('pe', mmk, reads=[('Wkv', '', k) for k in range(8)] + ['mn'], writes=[hk(1, hh % 2)])
                sc.add('dve', lambda e, hh=hh: e.tensor_copy(mkT[:, hh, :], half(1, hh % 2)),
                       reads=[hk(1, hh % 2)], writes=['mkT'])
            for mc in range(2):
                def mmv(e, mc=mc):
                    r = None
                    for k in range(8):
                        r = e.matmul(psb[2 + mc][:, :], mn[:, k, mc * 128:(mc + 1) * 128], Wkv[:, k, 512:1024],
                                     start=(k == 0), stop=(k == 7))
                    return r
                sc.add('pe', mmv, reads=[('Wkv', '', k) for k in range(8)] + ['mn'], writes=[('pb', 2 + mc)])
                sc.add('dve', lambda e, mc=mc: e.tensor_copy(mv[:, mc, :], psb[2 + mc][:, :]),
                       reads=[('pb', 2 + mc)], writes=['mv'])
            sc.flush()

        def wzname(col0):
            if col0 < 512:
                return 'az'
            if col0 < 1024:
                return 'bz'
            if col0 < 2048:
                return 'mqz'
            return 'g%d' % ((col0 - 2048) // 1024)

        def wzkey(col0):
            return [('Wz', wzname(col0), k) for k in range(8)]
        late = []

        def ldw_late(*a):
            late.append(a)
        for k in range(8):
            ldw(Wz[:, k, 1024:2048], ('Wz', 'mqz', k), w_in[k * 128:(k + 1) * 128, OFF['mq']:OFF['mq'] + 1024])
        for k in range(8):
            ldw(Wz[:, k, 0:512], ('Wz', 'az', k), w_in[k * 128:(k + 1) * 128, OFF['az']:OFF['az'] + 512])
            ldw(Wz[:, k, 512:1024], ('Wz', 'bz', k), w_in[k * 128:(k + 1) * 128, OFF['bz']:OFF['bz'] + 512])
        for cb in range(1, 4):
            for k in range(8):
                ldw_late(Wz[:, k, 1024 + cb * 1024:2048 + cb * 1024], ('Wz', 'g%d' % (cb - 1), k),
                         w_in[k * 128:(k + 1) * 128, OFF['mq'] + cb * 1024:OFF['mq'] + (cb + 1) * 1024])
            for k in range(4 * (cb - 1), 4 * cb):
                ldw_late(Wbr[:, k, :], ('Wbr', cb - 1, k), w_br[k * 128:(k + 1) * 128, :])
        for k in range(8):
            ldw_late(Wout[:, k, :], ('Wout', '', k), w_out[k * 128:(k + 1) * 128, :])

        ht = [cx.sb([128, 8, TT], BF16) for _ in range(2)]
        xt = cx.sb([128, 8, TT], F32)
        oat = cx.sb([128, 4, TT], F32)
        obt = cx.sb([128, 4, TT], F32)
        yT = cx.sb([128, 12, TT], BF16)
        mg = cx.sb([128, 8, TT], BF16)
        hout = cx.sb([128, 8, TT], BF16 if not last else F32)
        sqs = [cx.sb([128, TT], BF16) for _ in range(2)]
        sil = [cx.sb([128, TT], F32) for _ in range(2)]
        tmpB2 = [cx.sb([128, 2 * TT], F32) for _ in range(2)]
        sqs4 = [cx.sb([128, TT], BF16) for _ in range(4)]
        tmpM = [cx.sb([128, TT], F32) for _ in range(4)]
        rstd = cx.sb([128, TT], F32)
        rden = cx.sb([128, TT], F32)
        mqs = cx.sb([128, TT], BF16)
        pT = [cx.sb([128, TT], BF16) for _ in range(2)]
        gs = [cx.sb([128, TT], F32) for _ in range(3)]
        acc = [cx.sb([128, TT], F32) for _ in range(2)]
        hv = hT.rearrange("(k p) t -> p k t", p=128)
        xv = xT.rearrange("(k p) t -> p k t", p=128)
        oav = oaT.rearrange("(k p) t -> p k t", p=128)
        obv = obT.rearrange("(k p) t -> p k t", p=128)
        xov = xoT.rearrange("(k p) t -> p k t", p=128)
        if not last:
            hov = hoT.rearrange("(k p) t -> p k t", p=128)

        zcnt = [0]

        def zproj(col0, b):
            s = zcnt[0] % 2
            zcnt[0] += 1
            dst = half(0, s)

            def mm(e):
                r = None
                for k in range(8):
                    r = e.matmul(dst, Wz[:, k, col0:col0 + 128], ht[b][:, k, :], start=(k == 0), stop=(k == 7))
                return r
            sc.add('pe', mm, reads=wzkey(col0) + [('ht', b)], writes=[hk(0, s)])
            return dst, hk(0, s)

        def load_ht(it_):
            b_ = it_ % 2
            sc.add('sp', lambda e: e.dma_start(out=ht[b_][:, :, :], in_=hv[:, :, it_ * TT:(it_ + 1) * TT]),
                   writes=[('ht', b_)], slot=('ht', b_))

        def load_o(it_):
            sc.add('sp', lambda e: e.dma_start(out=oat[:, :, :], in_=oav[:, :, it_ * TT:(it_ + 1) * TT]),
                   writes=['oat'], slot='oat')
            sc.add('sp', lambda e: e.dma_start(out=obt[:, :, :], in_=obv[:, :, it_ * TT:(it_ + 1) * TT]),
                   writes=['obt'], slot='obt')

        def phase12(it):
            b = it % 2
            t0, t1 = it * TT, (it + 1) * TT
            for hd in range(4):
                sc.add('act', lambda e, hd=hd: e.activation(sqs4[hd][:, :], obt[:, hd, :], AF.Square),
                       reads=['obt'], writes=[('sqs4', hd)])
            for pr in range(2):
                bk = 7 if pr == 0 else 6
                for h2 in range(2):
                    hd = pr * 2 + h2
                    sc.add('pe', lambda e, hd=hd, h2=h2, bk=bk: e.matmul(psb[bk][:, h2 * TT:(h2 + 1) * TT],
                                                                        ones_bf[:, :], sqs4[hd][:, :], start=True,
                                                                        stop=True),
                           reads=[('sqs4', hd), 'ones_bf'], writes=[('pb', bk)])
                sc.add('act', lambda e, pr=pr, bk=bk: e.activation(tmpB2[pr][:, :], psb[bk][:, 0:2 * TT], AF.Ln,
                                                                   bias=EPS, scale=1.0 / 128),
                       reads=[('pb', bk)], writes=[('tmpB2', pr), ('tmpBh', 2 * pr), ('tmpBh', 2 * pr + 1)])
                sc.add('act', lambda e, pr=pr: e.activation(tmpB2[pr][:, :], tmpB2[pr][:, :], AF.Exp, scale=-0.5),
                       reads=[('tmpB2', pr)], writes=[('tmpB2', pr)])
            for hd in range(4):
                pr, h2 = hd // 2, hd % 2
                sc.add('dve', lambda e, hd=hd, pr=pr, h2=h2: e.scalar_tensor_tensor(
                    tmpB2[pr][:, h2 * TT:(h2 + 1) * TT], obt[:, hd, :], gg_sb[:, 0:1],
                    tmpB2[pr][:, h2 * TT:(h2 + 1) * TT], ALU.mult, ALU.mult),
                    reads=['obt', ('tmpB2', pr), ('par', 2)], writes=[('tmpBh', hd)])
            for hh in range(4):
                zp, zk = zproj(1024 + hh * 128, b)
                sc.add('dve', lambda e, zp=zp: e.tensor_copy(mqs[:, :], zp), reads=[zk], writes=['mqs'])
                for mc in range(2):
                    sc.add('pe', lambda e, hh=hh, mc=mc: e.matmul(half(2, mc), mkT[:, hh, mc * 128:(mc + 1) * 128],
                                                                 mqs[:, :], start=True, stop=True),
                           reads=['mkT', 'mqs'], writes=[hk(2, mc)])
                    sc.add('act', lambda e, mc=mc: e.activation(pT[mc][:, :], half(2, mc), AF.Exp,
                                                                scale=128.0 ** -0.5),
                           reads=[hk(2, mc)], writes=[('pT', mc)])

                def mmn(e, hh=hh):
                    e.matmul(half(3, 0), mv[:, 0, hh * 128:(hh + 1) * 128], pT[0][:, :], start=True, stop=False)
                    return e.matmul(half(3, 0), mv[:, 1, hh * 128:(hh + 1) * 128], pT[1][:, :], start=False,
                                    stop=True)
                sc.add('pe', mmn, reads=['mv', ('pT', 0), ('pT', 1)], writes=[hk(3, 0)])

                def mmd(e):
                    e.matmul(half(3, 1), ones_bf[:, :], pT[0][:, :], start=True, stop=False)
                    return e.matmul(half(3, 1), ones_bf[:, :], pT[1][:, :], start=False, stop=True)
                sc.add('pe', mmd, reads=['ones_bf', ('pT', 0), ('pT', 1)], writes=[hk(3, 1)])
                sc.add('act', lambda e: e.activation(rden[:, :], half(3, 1), AF.Ln), reads=[hk(3, 1)],
                       writes=['rden'])
                sc.add('act', lambda e: e.activation(rden[:, :], rden[:, :], AF.Exp, scale=-1.0), reads=['rden'],
                       writes=['rden'])
                sc.add('dve', lambda e, hh=hh: e.tensor_tensor(tmpM[hh][:, :], half(3, 0), rden[:, :], ALU.mult),
                       reads=[hk(3, 0), 'rden'], writes=[('tmpM', hh)])
            for ci in range(12):
                s = ci % 2
                col0 = [0, 512, 1536][ci // 4] + (ci % 4) * 128
                zp, zk = zproj(col0, b)
                sc.add('act', lambda e, zp=zp, s=s: e.activation(sil[s][:, :], zp, AF.Silu),
                       reads=[zk], writes=[('sil', s)])
                if ci < 4:
                    srcb, skey = oat[:, ci, :], 'oat'
                elif ci < 8:
                    srcb = tmpB2[(ci - 4) // 2][:, ((ci - 4) % 2) * TT:((ci - 4) % 2 + 1) * TT]
                    skey = ('tmpBh', ci - 4)
                else:
                    srcb, skey = tmpM[ci - 8][:, :], ('tmpM', ci - 8)
                sc.add('pool', lambda e, ci=ci, s=s, srcb=srcb: e.tensor_tensor(yT[:, ci, :], srcb, sil[s][:, :],
                                                                               ALU.mult),
                       reads=[skey, ('sil', s)], writes=[('yT', ci)])

        def merge_out(it):
            b = it % 2
            t0, t1 = it * TT, (it + 1) * TT
            for dc in range(8):
                for n in range(3):
                    pslot = [(4, 0), (4, 1), (5, 0)][n]
                    gslot = [(5, 1), (6, 0), (6, 1)][n]

                    def mmp(e, n=n, dc=dc, pslot=pslot):
                        r = None
                        for kc in range(4):
                            r = e.matmul(half(*pslot), Wbr[:, n * 4 + kc, dc * 128:(dc + 1) * 128],
                                         yT[:, n * 4 + kc, :], start=(kc == 0), stop=(kc == 3))
                        return r
                    sc.add('pe', mmp, reads=[('Wbr', n, n * 4 + kc) for kc in range(4)] + [('yT', n * 4 + kc) for kc in range(4)],
                           writes=[hk(*pslot)])

                    def mmg(e, n=n, dc=dc, gslot=gslot, b=b):
                        r = None
                        for k in range(8):
                            c0 = 2048 + n * 1024 + dc * 128
                            r = e.matmul(half(*gslot), Wz[:, k, c0:c0 + 128], ht[b][:, k, :], start=(k == 0),
                                         stop=(k == 7))
                        return r
                    sc.add('pe', mmg, reads=[('Wz', 'g%d' % n, k) for k in range(8)] + [('ht', b)], writes=[hk(*gslot)])
                    sc.add('act', lambda e, n=n, dc=dc, gslot=gslot: e.activation(
                        gs[n][:, :], half(*gslot), AF.Sigmoid, bias=bm_sb[:, n * 8 + dc:n * 8 + dc + 1], scale=1.0),
                        reads=[hk(*gslot), ('par', 1)], writes=[('gs', n)])
                sc.add('dve', lambda e: e.tensor_tensor(acc[0][:, :], half(4, 0), gs[0][:, :], ALU.mult),
                       reads=[hk(4, 0), ('gs', 0)], writes=[('acc', 0)])
                sc.add('dve', lambda e: e.tensor_tensor(acc[1][:, :], half(4, 1), gs[1][:, :], ALU.mult),
                       reads=[hk(4, 1), ('gs', 1)], writes=[('acc', 1)])
                sc.add('pool', lambda e: e.tensor_tensor(acc[0][:, :], acc[0][:, :], acc[1][:, :], ALU.add),
                       reads=[('acc', 0), ('acc', 1)], writes=[('acc', 0)])
                sc.add('dve', lambda e: e.tensor_tensor(acc[1][:, :], half(5, 0), gs[2][:, :], ALU.mult),
                       reads=[hk(5, 0), ('gs', 2)], writes=[('acc', 1)])
                sc.add('pool', lambda e, dc=dc: e.tensor_tensor(mg[:, dc, :], acc[0][:, :], acc[1][:, :], ALU.add),
                       reads=[('acc', 0), ('acc', 1)], writes=[('mg', dc)])
            for dc in range(8):
                s = dc % 2

                def mmo(e, dc=dc, s=s):
                    r = None
                    for k in range(8):
                        r = e.matmul(half(7, s), Wout[:, k, dc * 128:(dc + 1) * 128], mg[:, k, :], start=(k == 0),
                                     stop=(k == 7))
                    return r
                sc.add('pe', mmo, reads=[('Wout', '', k) for k in range(8)] + [('mg', k) for k in range(8)], writes=[hk(7, s)])
                sc.add('dve', lambda e, dc=dc, s=s: e.tensor_tensor(xt[:, dc, :], xt[:, dc, :], half(7, s), ALU.add),
                       reads=['xt', hk(7, s)], writes=[('xn', dc)])
            allxn = [('xn', dc) for dc in range(8)]
            if not last:
                sc.add('sp', lambda e, t0=t0, t1=t1: e.dma_start(out=xov[:, :, t0:t1], in_=xt[:, :, :]),
                       reads=allxn, writes=[('xo', it)], slot='xo')

        def finalnorm(it):
            b = it % 2
            t0, t1 = it * TT, (it + 1) * TT
            for k in range(8):
                s = k % 2
                sc.add('act', lambda e, k=k, s=s: e.activation(sqs[s][:, :], xt[:, k, :], AF.Square),
                       reads=[('xn', k)], writes=[('sqs', s)])
                sc.add('pe', lambda e, k=k, s=s: e.matmul(half(1, 1), ones_bf[:, :], sqs[s][:, :], start=(k == 0),
                                                         stop=(k == 7)),
                       reads=[('sqs', s), 'ones_bf'], writes=[hk(1, 1)])
            sc.add('act', lambda e: e.activation(rstd[:, :], half(1, 1), AF.Ln, bias=EPS, scale=1.0 / D),
                   reads=[hk(1, 1)], writes=['rstd'])
            sc.add('act', lambda e: e.activation(rstd[:, :], rstd[:, :], AF.Exp, scale=-0.5), reads=['rstd'],
                   writes=['rstd'])
            for k in range(8):
                sc.add('dve', lambda e, k=k: e.scalar_tensor_tensor(hout[:, k, :], xt[:, k, :], ng_sb[:, k:k + 1],
                                                                   rstd[:, :], ALU.mult, ALU.mult),
                       reads=[('xn', k), 'rstd', ('par', 3)], writes=['hout'])
            if last:
                sc.add('sp', lambda e, t0=t0, t1=t1: e.dma_start(out=xov[:, :, t0:t1], in_=hout[:, :, :]),
                       reads=['hout'], writes=[('xo', it)], slot='xo')
            else:
                sc.add('sp', lambda e, t0=t0, t1=t1: e.dma_start(out=hov[:, :, t0:t1], in_=hout[:, :, :]),
                       reads=['hout'], writes=[('ho', it)], slot='ho')

        def load_x(it):
            t0, t1 = it * TT, (it + 1) * TT
            sc.add('sp', lambda e: e.dma_start(out=xt[:, :, :], in_=xv[:, :, t0:t1]),
                   writes=['xt'] + [('xn', dc) for dc in range(8)], slot='xt')

        load_ht(0)
        load_o(0)
        load_x(0)
        phase12(0)
        for a_ in late:
            ldw(*a_)
        for it in range(NTT):
            if it + 1 < NTT:
                load_ht(it + 1)
                load_o(it + 1)
            merge_out(it)
            if it + 1 < NTT:
                phase12(it + 1)
            finalnorm(it)
            if it + 1 < NTT:
                load_x(it + 1)
        sc.close()
    return nc


def mixer_inputs(c, hT, w_in_l, b_fg_l, conv_w_l, a_log_l, dt_bias_l):
    hd, half = c // 2, c % 2
    wf = np.concatenate([w_in_l[:, OFF['aq'] + c * 64:OFF['aq'] + (c + 1) * 64],
                         w_in_l[:, OFF['ak'] + c * 64:OFF['ak'] + (c + 1) * 64],
                         w_in_l[:, OFF['av'] + c * 64:OFF['av'] + (c + 1) * 64],
                         w_in_l[:, OFF['af'] + c:OFF['af'] + c + 1]], axis=1)
    vo = hd * 128 + half * 64
    wg = np.concatenate([w_in_l[:, OFF['bq'] + hd * 128:OFF['bq'] + (hd + 1) * 128],
                         w_in_l[:, OFF['bk'] + hd * 128:OFF['bk'] + (hd + 1) * 128],
                         w_in_l[:, OFF['bv'] + vo:OFF['bv'] + vo + 64],
                         w_in_l[:, OFF['ba'] + hd:OFF['ba'] + hd + 1],
                         w_in_l[:, OFF['bb'] + hd:OFF['bb'] + hd + 1]], axis=1)
    cw = np.zeros((128, 12), np.float32)
    cw[:, 0:4] = conv_w_l[:, hd * 128:(hd + 1) * 128].T
    cw[:, 4:8] = conv_w_l[:, 512 + hd * 128:512 + (hd + 1) * 128].T
    cw[0:64, 8:12] = conv_w_l[:, 1024 + vo:1024 + vo + 64].T
    gpar = np.empty((128, 2), np.float32)
    gpar[:, 0] = a_log_l[hd]
    gpar[:, 1] = dt_bias_l[hd]
    return dict(hT=hT, wf=np.ascontiguousarray(wf), bfg=np.full((128, 1), b_fg_l[c], np.float32),
                wg=np.ascontiguousarray(wg), cw=cw, gpar=gpar)


def _lay8(v):
    return np.ascontiguousarray(np.asarray(v, np.float32).reshape(-1, 128).T)


_PROGS = {}


def _prog(name, fn):
    if name not in _PROGS:
        _PROGS[name] = fn()
    return _PROGS[name]


def kernel(x, mem, norm_g, w_in, b_fg, b_merge, conv_w, a_log, dt_bias, gdn_norm_g, mem_norm_g, w_mem_kv,
           w_branch, w_out, final_norm_g):
    f = lambda a: np.asarray(a, np.float32)
    x, mem, norm_g, w_in, b_fg, b_merge, conv_w = map(f, (x, mem, norm_g, w_in, b_fg, b_merge, conv_w))
    a_log, dt_bias, gdn_norm_g, mem_norm_g = map(f, (a_log, dt_bias, gdn_norm_g, mem_norm_g))
    w_mem_kv, w_branch, w_out, final_norm_g = map(f, (w_mem_kv, w_branch, w_out, final_norm_g))
    S = x.shape[1]
    TS = S // NCORES
    cores = list(range(NCORES))
    xT = np.ascontiguousarray(x[0].T)
    memT = np.ascontiguousarray(mem[0].T)
    sh = lambda a, c: np.ascontiguousarray(a[:, c * TS:(c + 1) * TS])
    ncP = _prog('P', lambda: build_P(TS))
    res = run_bass_kernel_spmd(ncP, [dict(xT=sh(xT, c), g=_lay8(norm_g[0])) for c in cores], core_ids=cores)
    hT = np.concatenate([np.asarray(r["hT"]) for r in res.results], axis=1)
    depth = w_in.shape[0]
    for l in range(depth):
        last = (l == depth - 1)
        ncM = _prog('M', lambda: build_M(S))
        hTc = np.ascontiguousarray(hT)
        res = run_bass_kernel_spmd(
            ncM, [mixer_inputs(c, hTc, w_in[l], b_fg[l], conv_w[l], a_log[l], dt_bias[l]) for c in cores],
            core_ids=cores)
        oaT = np.concatenate([np.asarray(r["oaT"]) for r in res.results], axis=0)
        obT = np.concatenate([np.asarray(r["obT"]) for r in res.results], axis=0)
        ncT = _prog('T%d' % int(last), lambda: build_T(TS, last))
        ng = final_norm_g if last else norm_g[l + 1]
        maps = []
        for c in cores:
            maps.append(dict(xT=sh(xT, c), hT=sh(hT, c), oaT=sh(oaT, c), obT=sh(obT, c),
                             w_in=np.ascontiguousarray(w_in[l]),
                             w_br=np.ascontiguousarray(w_branch[l].reshape(1536, D)),
                             w_out=np.ascontiguousarray(w_out[l]), w_kv=np.ascontiguousarray(w_mem_kv[l]),
                             memT=memT, mem_g=_lay8(mem_norm_g[l]), b_mg=_lay8(b_merge[l]),
                             gdn_g=np.ascontiguousarray(gdn_norm_g[l].reshape(128, 1)), next_g=_lay8(ng)))
        res = run_bass_kernel_spmd(ncT, maps, core_ids=cores)
        xT = np.concatenate([np.asarray(r["xoT"]) for r in res.results], axis=1)
        if not last:
            hT = np.concatenate([np.asarray(r["hoT"]) for r in res.results], axis=1)
    out = np.ascontiguousarray(xT.T).reshape(1, S, D).astype(np.float32)
    return out
```
